# Optimizing a Trainium2 kernel written in Bass

```python
import math
import jax
import jax.numpy as jnp
from jax import lax
import numpy as np

D_MODEL = 1024
BATCH = 8
SEQ = 2048
DEPTH = 4
DEC_BATCH = 128
DEC_SEQ = 1
PAST_LEN = 16384
PAGE_SIZE = 128

N_HYB_LAYERS = (DEPTH + 1) // 2
N_SSM_LAYERS = DEPTH // 2
CONV_WIDTH = 4
CHUNK = 64
NORM_EPS = 1e-6

GDN_HEADS = 4
GDN_KEY_DIM = 128
GDN_VAL_DIM = 128
GDN_QKV_DIM = GDN_HEADS * (2 * GDN_KEY_DIM + GDN_VAL_DIM)
GDN_OUT_DIM = GDN_HEADS * GDN_VAL_DIM

RET_HEADS = 4
RET_KEY_DIM = 64
RET_VAL_DIM = 128
RET_OUT_DIM = RET_HEADS * RET_VAL_DIM
ROPE_BASE = 10000.0

HYB_MIX_DIM = GDN_OUT_DIM + RET_OUT_DIM
HYB_IN_SIZES = (GDN_QKV_DIM, GDN_OUT_DIM, GDN_HEADS, GDN_HEADS,
                RET_HEADS * RET_KEY_DIM, RET_HEADS * RET_KEY_DIM, RET_OUT_DIM, RET_OUT_DIM)
HYB_IN_DIM = sum(HYB_IN_SIZES)

SSM_INNER = 2 * D_MODEL
SSM_HEAD_DIM = 64
SSM_HEADS = SSM_INNER // SSM_HEAD_DIM
SSM_GROUPS = 4
SSM_HEADS_PER_GROUP = SSM_HEADS // SSM_GROUPS
SSM_STATE = 128
SSM_CONV_DIM = SSM_INNER + 2 * SSM_GROUPS * SSM_STATE
SSM_IN_SIZES = (SSM_INNER, SSM_CONV_DIM, SSM_HEADS)
SSM_IN_DIM = sum(SSM_IN_SIZES)

MLP_HIDDEN = 4 * D_MODEL

kernel_name = 'hybrid_gdn_retnet_ssd_decoder_step'


def split_last(t, sizes):
    return jnp.split(t, np.cumsum(sizes)[:-1].tolist(), axis=-1)


def chunk_len(length):
    return CHUNK if length % CHUNK == 0 else length


def rms_norm(x, w):
    x32 = x.astype(jnp.float32)
    y = x32 * lax.rsqrt(jnp.mean(x32 * x32, axis=-1, keepdims=True) + NORM_EPS)
    return (y * w.astype(jnp.float32)).astype(x.dtype)


def l2_normalize(x):
    x32 = x.astype(jnp.float32)
    return x32 * lax.rsqrt(jnp.sum(x32 * x32, axis=-1, keepdims=True) + NORM_EPS)


def head_group_norm(x):
    x32 = x.astype(jnp.float32)
    xc = x32 - jnp.mean(x32, axis=-1, keepdims=True)
    return xc * lax.rsqrt(jnp.mean(xc * xc, axis=-1, keepdims=True) + NORM_EPS)


def rotary(t, pos):
    half = t.shape[-1] // 2
    inv_freq = ROPE_BASE ** (-jnp.arange(half, dtype=jnp.float32) / half)
    ang = pos.astype(jnp.float32)[:, None] * inv_freq[None, :]
    cos = jnp.cos(ang)[None, :, None, :]
    sin = jnp.sin(ang)[None, :, None, :]
    t32 = t.astype(jnp.float32)
    t1, t2 = t32[..., :half], t32[..., half:]
    return jnp.concatenate([t1 * cos - t2 * sin, t2 * cos + t1 * sin], axis=-1)


def causal_conv(x, buf, w, bias=None):
    l = x.shape[1]
    xp = jnp.concatenate([buf.astype(x.dtype), x], axis=1)
    out = xp[:, 0:l] * w[0]
    for i in range(1, w.shape[0]):
        out = out + xp[:, i:i + l] * w[i]
    if bias is not None:
        out = out + bias
    return out, xp[:, l:]


def decay_linear_attention(q, k, v, g, s0, chunk):
    dtype = s0.dtype
    b, l = q.shape[:2]
    out_shape = v.shape

    def blocks(t):
        return jnp.moveaxis(t.astype(jnp.float32).reshape(b, l // chunk, chunk, *t.shape[2:]), 1, 0)

    idx = jnp.arange(chunk)
    causal = (idx[:, None] >= idx[None, :])[None, :, :, None, None]

    def step(s, blk):
        qc, kc, vc, gc = blk
        gcum = jnp.cumsum(gc, axis=1)
        diff = gcum[:, :, None] - gcum[:, None, :]
        decay = jnp.where(causal, jnp.exp(jnp.where(causal, diff, 0.0)), 0.0)
        scores = jnp.einsum('bigk,bjgk->bijg', qc, kc)
        y_intra = jnp.einsum('bijgr,bjgrv->bigrv', scores[..., None] * decay, vc)
        y_inter = jnp.einsum('bigk,bgrkv->bigrv', qc, s) * jnp.exp(gcum)[..., None]
        g_end = gcum[:, -1]
        s_new = s * jnp.exp(g_end)[..., None, None] + jnp.einsum(
            'bjgk,bjgrv->bgrkv', kc, vc * jnp.exp(g_end[:, None] - gcum)[..., None])
        return s_new, y_intra + y_inter

    s_final, ys = lax.scan(step, s0.astype(jnp.float32), (blocks(q), blocks(k), blocks(v), blocks(g)))
    y = jnp.moveaxis(ys, 0, 1).reshape(out_shape)
    return y, s_final.astype(dtype)


def gated_delta_rule(q, k, v, g, beta, s0, chunk):
    dtype = s0.dtype
    b, l, h, _ = q.shape
    dv = v.shape[-1]

    def blocks(t):
        t = t.astype(jnp.float32).reshape(b, l // chunk, chunk, *t.shape[2:])
        return jnp.moveaxis(jnp.moveaxis(t, 1, 0), 2, 3)

    idx = jnp.arange(chunk)
    causal = idx[:, None] >= idx[None, :]
    strict = idx[:, None] > idx[None, :]
    eye = jnp.eye(chunk, dtype=jnp.float32)

    def step(s, blk):
        qc, kc, vc, gc, bc = blk
        gcum = jnp.cumsum(gc, axis=-1)
        diff = gcum[..., :, None] - gcum[..., None, :]
        decay = jnp.where(causal, jnp.exp(jnp.where(causal, diff, 0.0)), 0.0)
        kk = jnp.einsum('bhik,bhjk->bhij', kc, kc)
        lower = jnp.where(strict, bc[..., :, None] * kk * decay, 0.0)
        rhs = jnp.concatenate([vc * bc[..., None], kc * (bc * jnp.exp(gcum))[..., None]], axis=-1)
        sol = lax.linalg.triangular_solve(eye + lower, rhs, left_side=True, lower=True, unit_diagonal=True)
        u, w = sol[..., :dv], sol[..., dv:]
        delta = u - jnp.einsum('bhik,bhkv->bhiv', w, s)
        qk = jnp.einsum('bhik,bhjk->bhij', qc, kc) * decay
        y = (jnp.einsum('bhik,bhkv->bhiv', qc * jnp.exp(gcum)[..., None], s)
             + jnp.einsum('bhij,bhjv->bhiv', qk, delta))
        g_end = gcum[..., -1:]
        s_new = s * jnp.exp(g_end)[..., None] + jnp.einsum(
            'bhjk,bhjv->bhkv', kc * jnp.exp(g_end - gcum)[..., None], delta)
        return s_new, y

    s_final, ys = lax.scan(step, s0.astype(jnp.float32),
                           (blocks(q), blocks(k), blocks(v), blocks(g), blocks(beta)))
    y = jnp.swapaxes(jnp.moveaxis(ys, 0, 1), 2, 3).reshape(b, l, h, dv)
    return y, s_final.astype(dtype)


def hybrid_mixer(h, pos, s_gdn, c_gdn, s_ret, w_in, gdn_conv_w, gdn_a_log, gdn_dt_bias,
                 gdn_norm_w, ret_norm_w, w_out):
    b, l, _ = h.shape
    chunk = chunk_len(l)
    proj = jnp.einsum('bld,de->ble', h, w_in)
    qkv_a, z_a, beta_raw, a_raw, q_b, k_b, v_b, gate_b = split_last(proj, HYB_IN_SIZES)

    qkv_a, c_gdn_new = causal_conv(qkv_a, c_gdn, gdn_conv_w)
    qkv_a = jax.nn.silu(qkv_a)
    q_a, k_a, v_a = split_last(qkv_a, (GDN_HEADS * GDN_KEY_DIM, GDN_HEADS * GDN_KEY_DIM, GDN_OUT_DIM))
    q_a = l2_normalize(q_a.reshape(b, l, GDN_HEADS, GDN_KEY_DIM)) * GDN_KEY_DIM ** -0.5
    k_a = l2_normalize(k_a.reshape(b, l, GDN_HEADS, GDN_KEY_DIM))
    v_a = v_a.reshape(b, l, GDN_HEADS, GDN_VAL_DIM)
    beta = jax.nn.sigmoid(beta_raw.astype(jnp.float32))
    g_a = -jnp.exp(gdn_a_log.astype(jnp.float32)) * jax.nn.softplus(a_raw.astype(jnp.float32) + gdn_dt_bias)
    o_a, s_gdn_new = gated_delta_rule(q_a, k_a, v_a, g_a, beta, s_gdn, chunk)
    o_a = rms_norm(o_a, gdn_norm_w) * jax.nn.silu(z_a.astype(jnp.float32).reshape(b, l, GDN_HEADS, GDN_VAL_DIM))

    q_b = rotary(q_b.reshape(b, l, RET_HEADS, RET_KEY_DIM), pos)
    k_b = rotary(k_b.reshape(b, l, RET_HEADS, RET_KEY_DIM), pos) * RET_KEY_DIM ** -0.5
    v_b = v_b.reshape(b, l, RET_HEADS, 1, RET_VAL_DIM)
    log_gamma = jnp.log(1.0 - jnp.exp2(-5.0 - jnp.arange(RET_HEADS, dtype=jnp.float32)))
    g_b = jnp.broadcast_to(log_gamma[:, None], (b, l, RET_HEADS, 1))
    o_b, s_ret_new = decay_linear_attention(q_b, k_b, v_b, g_b, s_ret[:, :, None], chunk)
    o_b = head_group_norm(o_b.reshape(b, l, RET_HEADS, RET_VAL_DIM)).reshape(b, l, RET_OUT_DIM)
    o_b = o_b * ret_norm_w * jax.nn.silu(gate_b.astype(jnp.float32))

    mixed = jnp.concatenate([o_a.reshape(b, l, GDN_OUT_DIM), o_b], axis=-1).astype(h.dtype)
    out = jnp.einsum('ble,ed->bld', mixed, w_out)
    return out, s_gdn_new, c_gdn_new, s_ret_new[:, :, 0]


def ssd_mixer(h, s_ssm, c_ssm, w_in, conv_w, conv_b, dt_bias, a_log, d_skip, norm_w, w_out):
    b, l, _ = h.shape
    chunk = chunk_len(l)
    proj = jnp.einsum('bld,de->ble', h, w_in)
    z, xbc, dt_raw = split_last(proj, SSM_IN_SIZES)
    xbc, c_ssm_new = causal_conv(xbc, c_ssm, conv_w, conv_b)
    xbc = jax.nn.silu(xbc)
    xs, b_mat, c_mat = split_last(xbc, (SSM_INNER, SSM_GROUPS * SSM_STATE, SSM_GROUPS * SSM_STATE))
    dt = jax.nn.softplus(dt_raw.astype(jnp.float32) + dt_bias).reshape(b, l, SSM_GROUPS, SSM_HEADS_PER_GROUP)
    a = -jnp.exp(a_log.astype(jnp.float32)).reshape(SSM_GROUPS, SSM_HEADS_PER_GROUP)
    xh = xs.astype(jnp.float32).reshape(b, l, SSM_GROUPS, SSM_HEADS_PER_GROUP, SSM_HEAD_DIM)
    s0 = s_ssm.reshape(b, SSM_GROUPS, SSM_HEADS_PER_GROUP, SSM_STATE, SSM_HEAD_DIM)
    y, s_new = decay_linear_attention(
        c_mat.reshape(b, l, SSM_GROUPS, SSM_STATE), b_mat.reshape(b, l, SSM_GROUPS, SSM_STATE),
        xh * dt[..., None], dt * a, s0, chunk)
    y = y + d_skip.reshape(SSM_GROUPS, SSM_HEADS_PER_GROUP)[..., None] * xh
    y = y.reshape(b, l, SSM_INNER) * jax.nn.silu(z.astype(jnp.float32))
    y = rms_norm(y.reshape(b, l, SSM_GROUPS, SSM_INNER // SSM_GROUPS),
                 norm_w.reshape(SSM_GROUPS, SSM_INNER // SSM_GROUPS)).reshape(b, l, SSM_INNER)
    out = jnp.einsum('ble,ed->bld', y.astype(h.dtype), w_out)
    return out, s_new.reshape(b, SSM_HEADS, SSM_STATE, SSM_HEAD_DIM), c_ssm_new


def squared_relu_mlp(h, w1, w2):
    a = jax.nn.relu(jnp.einsum('bld,df->blf', h, w1))
    return jnp.einsum('blf,fd->bld', a * a, w2)


def trunk(x, pos, s_gdn, c_gdn, s_ret, s_ssm, c_ssm, norm_mix, norm_mlp, norm_final,
          w_in_hyb, gdn_conv_w, gdn_a_log, gdn_dt_bias, gdn_norm_w, ret_norm_w, w_out_hyb,
          w_in_ssm, ssm_conv_w, ssm_conv_b, ssm_dt_bias, ssm_a_log, ssm_d, ssm_norm_w, w_out_ssm,
          mlp_w1, mlp_w2):
    new_gdn, new_gdn_conv, new_ret, new_ssm, new_ssm_conv = [], [], [], [], []
    for layer in range(DEPTH):
        i = layer // 2
        h = rms_norm(x, norm_mix[layer])
        if layer % 2 == 0:
            out, sg, cg, sr = hybrid_mixer(h, pos, s_gdn[i], c_gdn[i], s_ret[i], w_in_hyb[i], gdn_conv_w[i],
                                           gdn_a_log[i], gdn_dt_bias[i], gdn_norm_w[i], ret_norm_w[i],
                                           w_out_hyb[i])
            new_gdn.append(sg)
            new_gdn_conv.append(cg)
            new_ret.append(sr)
        else:
            out, ss, cs = ssd_mixer(h, s_ssm[i], c_ssm[i], w_in_ssm[i], ssm_conv_w[i], ssm_conv_b[i],
                                    ssm_dt_bias[i], ssm_a_log[i], ssm_d[i], ssm_norm_w[i], w_out_ssm[i])
            new_ssm.append(ss)
            new_ssm_conv.append(cs)
        x = x + out
        x = x + squared_relu_mlp(rms_norm(x, norm_mlp[layer]), mlp_w1[layer], mlp_w2[layer])
    return (rms_norm(x, norm_final), jnp.stack(new_gdn), jnp.stack(new_gdn_conv), jnp.stack(new_ret),
            jnp.stack(new_ssm), jnp.stack(new_ssm_conv))


def setup_inputs(seed: int = 0) -> dict:
    key = jax.random.key(seed)
    ks = jax.random.split(key, 32)
    f32 = jnp.float32

    def normal(k, shape, scale):
        return scale * jax.random.normal(k, shape, f32)

    def gain(k, shape):
        return 1.0 + 0.01 * jax.random.normal(k, shape, f32)

    def dt_bias(k, shape):
        dt = jnp.exp(jax.random.uniform(k, shape, f32, math.log(1e-3), math.log(1e-1)))
        return dt + jnp.log(-jnp.expm1(-dt))

    def a_log(k, shape):
        return jnp.log(jax.random.uniform(k, shape, f32, 1.0, 16.0))

    return {
        'x_prompt': normal(ks[0], (BATCH, SEQ, D_MODEL), 1.0),
        'x_sample': normal(ks[1], (DEC_BATCH, DEC_SEQ, D_MODEL), 1.0),
        'state_gdn': normal(ks[2], (N_HYB_LAYERS, DEC_BATCH, GDN_HEADS, GDN_KEY_DIM, GDN_VAL_DIM), 0.1),
        'state_gdn_conv': normal(ks[3], (N_HYB_LAYERS, DEC_BATCH, CONV_WIDTH - 1, GDN_QKV_DIM), 1.0),
        'state_ret': normal(ks[4], (N_HYB_LAYERS, DEC_BATCH, RET_HEADS, RET_KEY_DIM, RET_VAL_DIM), 0.5),
        'state_ssm': normal(ks[5], (N_SSM_LAYERS, DEC_BATCH, SSM_HEADS, SSM_STATE, SSM_HEAD_DIM), 0.1),
        'state_ssm_conv': normal(ks[6], (N_SSM_LAYERS, DEC_BATCH, CONV_WIDTH - 1, SSM_CONV_DIM), 1.0),
        'norm_mix': gain(ks[7], (DEPTH, D_MODEL)),
        'norm_mlp': gain(ks[8], (DEPTH, D_MODEL)),
        'norm_final': gain(ks[9], (D_MODEL,)),
        'w_in_hyb': normal(ks[10], (N_HYB_LAYERS, D_MODEL, HYB_IN_DIM), D_MODEL ** -0.5),
        'gdn_conv_w': normal(ks[11], (N_HYB_LAYERS, CONV_WIDTH, GDN_QKV_DIM), CONV_WIDTH ** -0.5),
        'gdn_a_log': a_log(ks[12], (N_HYB_LAYERS, GDN_HEADS)),
        'gdn_dt_bias': dt_bias(ks[13], (N_HYB_LAYERS, GDN_HEADS)),
        'gdn_norm_w': gain(ks[14], (N_HYB_LAYERS, GDN_VAL_DIM)),
        'ret_norm_w': gain(ks[15], (N_HYB_LAYERS, RET_OUT_DIM)),
        'w_out_hyb': normal(ks[16], (N_HYB_LAYERS, HYB_MIX_DIM, D_MODEL), HYB_MIX_DIM ** -0.5),
        'w_in_ssm': normal(ks[17], (N_SSM_LAYERS, D_MODEL, SSM_IN_DIM), D_MODEL ** -0.5),
        'ssm_conv_w': normal(ks[18], (N_SSM_LAYERS, CONV_WIDTH, SSM_CONV_DIM), CONV_WIDTH ** -0.5),
        'ssm_conv_b': normal(ks[19], (N_SSM_LAYERS, SSM_CONV_DIM), 0.02),
        'ssm_dt_bias': dt_bias(ks[20], (N_SSM_LAYERS, SSM_HEADS)),
        'ssm_a_log': a_log(ks[21], (N_SSM_LAYERS, SSM_HEADS)),
        'ssm_d': 1.0 + 0.1 * jax.random.normal(ks[22], (N_SSM_LAYERS, SSM_HEADS), f32),
        'ssm_norm_w': gain(ks[23], (N_SSM_LAYERS, SSM_INNER)),
        'w_out_ssm': normal(ks[24], (N_SSM_LAYERS, SSM_INNER, D_MODEL), SSM_INNER ** -0.5),
        'mlp_w1': normal(ks[25], (DEPTH, D_MODEL, MLP_HIDDEN), D_MODEL ** -0.5),
        'mlp_w2': normal(ks[26], (DEPTH, MLP_HIDDEN, D_MODEL), 0.5 * MLP_HIDDEN ** -0.5),
    }


def reference(x_prompt, x_sample, state_gdn, state_gdn_conv, state_ret, state_ssm, state_ssm_conv,
              norm_mix, norm_mlp, norm_final, w_in_hyb, gdn_conv_w, gdn_a_log, gdn_dt_bias, gdn_norm_w,
              ret_norm_w, w_out_hyb, w_in_ssm, ssm_conv_w, ssm_conv_b, ssm_dt_bias, ssm_a_log, ssm_d,
              ssm_norm_w, w_out_ssm, mlp_w1, mlp_w2):
    bp, lp, _ = x_prompt.shape
    ls = x_sample.shape[1]
    dt = x_prompt.dtype
    z_gdn = jnp.zeros((N_HYB_LAYERS, bp, GDN_HEADS, GDN_KEY_DIM, GDN_VAL_DIM), dt)
    z_gdn_conv = jnp.zeros((N_HYB_LAYERS, bp, CONV_WIDTH - 1, GDN_QKV_DIM), dt)
    z_ret = jnp.zeros((N_HYB_LAYERS, bp, RET_HEADS, RET_KEY_DIM, RET_VAL_DIM), dt)
    z_ssm = jnp.zeros((N_SSM_LAYERS, bp, SSM_HEADS, SSM_STATE, SSM_HEAD_DIM), dt)
    z_ssm_conv = jnp.zeros((N_SSM_LAYERS, bp, CONV_WIDTH - 1, SSM_CONV_DIM), dt)
    pos_prompt = jnp.arange(lp, dtype=jnp.int32)
    pos_sample = PAST_LEN + jnp.arange(ls, dtype=jnp.int32)

    y_prompt, p_gdn, p_gdn_conv, p_ret, p_ssm, p_ssm_conv = trunk(
        x_prompt, pos_prompt, z_gdn, z_gdn_conv, z_ret, z_ssm, z_ssm_conv, norm_mix, norm_mlp, norm_final,
        w_in_hyb, gdn_conv_w, gdn_a_log, gdn_dt_bias, gdn_norm_w, ret_norm_w, w_out_hyb,
        w_in_ssm, ssm_conv_w, ssm_conv_b, ssm_dt_bias, ssm_a_log, ssm_d, ssm_norm_w, w_out_ssm,
        mlp_w1, mlp_w2)
    y_sample, s_gdn, s_gdn_conv, s_ret, s_ssm, s_ssm_conv = trunk(
        x_sample, pos_sample, state_gdn, state_gdn_conv, state_ret, state_ssm, state_ssm_conv,
        norm_mix, norm_mlp, norm_final,
        w_in_hyb, gdn_conv_w, gdn_a_log, gdn_dt_bias, gdn_norm_w, ret_norm_w, w_out_hyb,
        w_in_ssm, ssm_conv_w, ssm_conv_b, ssm_dt_bias, ssm_a_log, ssm_d, ssm_norm_w, w_out_ssm,
        mlp_w1, mlp_w2)
    return (y_prompt, y_sample, p_gdn, p_gdn_conv, p_ret, p_ssm, p_ssm_conv,
            s_gdn, s_gdn_conv, s_ret, s_ssm, s_ssm_conv)
```

```python
import contextlib
import math
import numpy as np
import concourse.bass as bass
import concourse.mybir as mybir
from concourse.bass_utils import run_bass_kernel_spmd

F32 = mybir.dt.float32
BF16 = mybir.dt.bfloat16
ALU = mybir.AluOpType
AF = mybir.ActivationFunctionType
AX = mybir.AxisListType

ENGS = ("sync", "scalar", "vector", "gpsimd", "tensor")
EPOCH = 30000
NCORES = 8
SEQ = 2048
NS = 16
NTOK = SEQ + NS
D = 1024
EPS = 1e-6
HYB_IN = 3592
SSM_IN = 5152
BLK = 256


def _esize(dt):
    return 2 if dt == BF16 else 4


def _box(ap):
    dims = ap.ap
    off = ap.offset
    sp = str(ap.space)
    es = _esize(ap.dtype)
    if sp in ("SB", "PSUM"):
        ps = dims[0][0]
        if ps == 0:
            ps = 1 << 30
        p0 = off // ps
        p1 = p0 + dims[0][1]
        f0 = off % ps
        ext = 1
        for st, cnt in dims[1:]:
            ext += (cnt - 1) * abs(st)
        if sp == "PSUM":
            return (sp + ap.name, (p0 // 32) * 32, ((p1 + 31) // 32) * 32, 0, 2048)
        return (sp + ap.name, p0, p1, f0 * es, (f0 + ext) * es)
    ext = 1
    for st, cnt in dims:
        ext += (cnt - 1) * abs(st)
    return (sp + ap.name, 0, 1, off * es, (off + ext) * es)


class Sched:
    def __init__(self, nc):
        self.nc = nc
        self.ops = []
        self.hist = {}
        self.chans = {}
        self.last_eng = {}
        self.pending_barrier = None

    def barrier(self):
        self.pending_barrier = (dict(self.last_eng), dict(self.last_chan_op()))
        self.barrier_seen = set()
        self.hist = {}

    def last_chan_op(self):
        d = {}
        for i, o in enumerate(self.ops):
            if o["chan"] is not None:
                d[o["chan"]] = i
        return d

    def add(self, eng, fn, reads=(), writes=(), chan=None):
        idx = len(self.ops)
        deps = {}
        rb = [_box(a) for a in reads]
        wb = [_box(a) for a in writes]
        for b in rb:
            isps = b[0].startswith("PSUM")
            for r in self.hist.get(b[0], ()):
                if r[0] < b[2] and b[1] < r[1] and r[2] < b[4] and b[3] < r[3]:
                    if r[4]:
                        deps[r[5]] = True
                    elif isps and r[5] < idx and self.ops[r[5]]["eng"] != eng:
                        deps[r[5]] = True
        for b in wb:
            for r in self.hist.get(b[0], ()):
                if r[0] < b[2] and b[1] < r[1] and r[2] < b[4] and b[3] < r[3]:
                    deps.setdefault(r[5], False)
        isdma = chan is not None
        if self.pending_barrier is not None and eng not in self.barrier_seen:
            self.barrier_seen.add(eng)
            le, lc = self.pending_barrier
            for e2, i2 in le.items():
                deps[i2] = True
            for c2, i2 in lc.items():
                deps[i2] = True
        for b in wb:
            lst = self.hist.setdefault(b[0], [])
            lst[:] = [r for r in lst if not (b[1] <= r[0] and r[1] <= b[2] and b[3] <= r[2] and r[3] <= b[4])]
            lst.append((b[1], b[2], b[3], b[4], True, idx))
        for b in rb:
            lst = self.hist.setdefault(b[0], [])
            if not isdma:
                lst[:] = [r for r in lst if not ((not r[4]) and r[5] < idx and self.ops[r[5]]["eng"] == eng and self.ops[r[5]]["chan"] is None
                                                 and b[1] <= r[0] and r[1] <= b[2] and b[3] <= r[2] and r[3] <= b[4])]
            lst.append((b[1], b[2], b[3], b[4], False, idx))
        if isdma:
            self.chans[chan] = self.chans.get(chan, 0) + 1
        else:
            self.last_eng[eng] = idx
        self.ops.append(dict(eng=eng, fn=fn, deps=deps, chan=chan))
        return idx

    def emit(self):
        nc = self.nc
        ops = self.ops
        need = [False] * len(ops)
        for c, o in enumerate(ops):
            kept = {}
            for p, raw in o["deps"].items():
                po = ops[p]
                if po["chan"] is None and po["eng"] == o["eng"] and o["chan"] is None:
                    if o["eng"] == "tensor":
                        continue
                kept[p] = raw
                need[p] = True
            o["deps"] = kept
        sigidx = {}
        cnt = {e: 0 for e in ENGS}
        for i, o in enumerate(ops):
            if o["chan"] is None and need[i]:
                cnt[o["eng"]] += 1
                sigidx[i] = cnt[o["eng"]]
        nep = {e: (cnt[e] + EPOCH - 1) // EPOCH for e in ENGS}
        self.cnt = cnt
        with contextlib.ExitStack() as st:
            esem = {e: [st.enter_context(nc.semaphore(f"s_{e}_{j}")) for j in range(max(1, nep[e]))] for e in ENGS}
            csem = {c: st.enter_context(nc.semaphore(f"c_{c}")) for c in self.chans}
            waited = {e: {} for e in ENGS}
            chan_issued = {c: 0 for c in self.chans}
            chan_tgt = {c: 0 for c in self.chans}
            plan = {e: [] for e in ENGS}
            for i, o in enumerate(ops):
                E = o["eng"]
                w = {}
                for p in o["deps"]:
                    po = ops[p]
                    if po["chan"] is None:
                        s = sigidx[p]
                        key = ("e", po["eng"], (s - 1) // EPOCH)
                        val = (s - 1) % EPOCH + 1
                    else:
                        ch = po["chan"]
                        tgt = chan_issued[ch]
                        chan_tgt[ch] = max(chan_tgt[ch], tgt)
                        key = ("c", ch)
                        val = 16 * tgt
                    if val > w.get(key, 0):
                        w[key] = val
                if o["chan"] is not None:
                    ch = o["chan"]
                    if chan_tgt[ch] > 0:
                        key = ("c", ch)
                        w[key] = max(w.get(key, 0), 16 * chan_tgt[ch])
                    chan_issued[ch] += 1
                waits = []
                for key, val in w.items():
                    if val > waited[E].get(key, 0):
                        waited[E][key] = val
                        sem = csem[key[1]] if key[0] == "c" else esem[key[1]][key[2]]
                        waits.append((sem, val))
                sig = None
                if o["chan"] is not None:
                    sig = (csem[o["chan"]], 16)
                elif i in sigidx:
                    s = sigidx[i]
                    sig = (esem[E][(s - 1) // EPOCH], 1)
                plan[E].append((o["fn"], waits, sig))
            fin = [(csem[ch], 16 * n) for ch, n in chan_issued.items() if n]
            self.n_instr = {e: len(plan[e]) for e in ENGS}
            with nc.Block() as block:
                def mk(E):
                    def body(eng):
                        for fn, waits, sig in plan[E]:
                            for sem, val in waits:
                                eng.wait_ge(sem, val)
                            ins = fn(eng)
                            if sig is not None:
                                ins.then_inc(sig[0], sig[1])
                        if E == "sync":
                            for sem, val in fin:
                                eng.wait_ge(sem, val)
                    return body
                block.sync(mk("sync"))
                block.scalar(mk("scalar"))
                block.vector(mk("vector"))
                block.gpsimd(mk("gpsimd"))
                block.tensor(mk("tensor"))


C_IDENT, C_U, C_NEG, C_INCL, C_STRICT, C_ONES, C_NONES, C_PT = [i * 128 for i in range(8)]
C_DMT = 8 * 128
C_QDEC = C_DMT + 512
C_KDEC = C_QDEC + 512
C_ID16 = C_KDEC + 256
C_ROPES = C_ID16 + 256
C_G128 = C_ROPES + 64
NCST = C_G128 + 4


def make_consts():
    c = np.zeros((128, NCST), np.float64)
    j = np.arange(128)[:, None]
    i = np.arange(128)[None, :]
    c[:, C_IDENT:C_IDENT + 128] = (j == i)
    c[:, C_U:C_U + 128] = (j <= i)
    c[:, C_NEG:C_NEG + 128] = np.where(i >= j, 0.0, -1e30)
    c[:, C_INCL:C_INCL + 128] = (i >= j)
    c[:, C_STRICT:C_STRICT + 128] = (i > j)
    c[:, C_ONES:C_ONES + 128] = 1.0
    c[:, C_NONES:C_NONES + 128] = -1.0
    PT = np.zeros((128, 128))
    for m in range(128):
        if m % 64 < 32:
            PT[m + 32, m] = -1.0
        else:
            PT[m - 32, m] = 1.0
    c[:, C_PT:C_PT + 128] = PT
    gam = 1.0 - 2.0 ** (-5.0 - np.arange(4))
    for h in range(4):
        c[:, C_DMT + h * 128:C_DMT + (h + 1) * 128] = np.where(i >= j, gam[h] ** np.maximum(i - j, 0), 0.0)
        c[:, C_KDEC + h * 64:C_KDEC + (h + 1) * 64] = (gam[h] ** (127 - j))
    for h in range(4):
        c[:, C_QDEC + h * 128:C_QDEC + (h + 1) * 128] = gam[h] ** (i + 1)
    c[:, C_ID16:C_ID16 + 256] = np.eye(16).reshape(1, 256)
    half = 32
    inv = (10000.0 ** (-np.arange(half, dtype=np.float32) / half)).astype(np.float32)
    ang = (np.float32(16384.0) * inv).astype(np.float32).astype(np.float64)
    c[:, C_ROPES:C_ROPES + 32] = np.cos(ang)[None, :]
    c[:, C_ROPES + 32:C_ROPES + 64] = np.sin(ang)[None, :]
    c[:, C_G128:C_G128 + 4] = gam[None, :] ** 128
    return c.astype(np.float32)


GAM = [1.0 - 2.0 ** (-5.0 - h) for h in range(4)]


def make_rope():
    half = 32
    inv = (10000.0 ** (-np.arange(half, dtype=np.float32) / half)).astype(np.float32)
    pos = np.concatenate([np.arange(SEQ), np.full(NS, 16384)]).astype(np.float32)
    ang = (pos[None, :] * inv[:, None]).astype(np.float32).astype(np.float64)
    r = np.zeros((2, 128, NTOK), np.float32)
    for p in range(128):
        r[0, p] = np.cos(ang[p % 32])
        r[1, p] = np.sin(ang[p % 32])
    return r


PK = {}


def _pk_layout():
    off = 0

    def put(name, n):
        nonlocal off
        PK[name] = off
        off += n
    put("nmix", 32)
    put("nmlp", 32)
    put("nfin", 8)
    for i in range(2):
        put(f"gcw{i}", 48)
        put(f"gdtb{i}", 4)
        put(f"galog{i}", 4)
        put(f"gnw{i}", 1)
        put(f"rnw{i}", 4)
        put(f"scw{i}", 96)
        put(f"scb{i}", 24)
        put(f"sdtb{i}", 32)
        put(f"salog{i}", 32)
        put(f"sD{i}", 16)
        put(f"sDrep{i}", 32)
        put(f"snw{i}", 16)
    return off


NPK = _pk_layout()


def make_pk(inp):
    pk = np.zeros((128, NPK), np.float32)

    def fm(v, nch):
        return np.ascontiguousarray(v.reshape(nch, 128).T)
    for l in range(4):
        pk[:, PK["nmix"] + l * 8:PK["nmix"] + (l + 1) * 8] = fm(inp["norm_mix"][l], 8)
        pk[:, PK["nmlp"] + l * 8:PK["nmlp"] + (l + 1) * 8] = fm(inp["norm_mlp"][l], 8)
    pk[:, PK["nfin"]:PK["nfin"] + 8] = fm(inp["norm_final"], 8)
    for i in range(2):
        cw = inp["gdn_conv_w"][i]
        pk[:, PK[f"gcw{i}"]:PK[f"gcw{i}"] + 48] = cw.reshape(4, 12, 128).transpose(2, 1, 0).reshape(128, 48)
        pk[:, PK[f"gdtb{i}"]:PK[f"gdtb{i}"] + 4] = inp["gdn_dt_bias"][i][None, :]
        pk[:, PK[f"galog{i}"]:PK[f"galog{i}"] + 4] = inp["gdn_a_log"][i][None, :]
        pk[:, PK[f"gnw{i}"]] = inp["gdn_norm_w"][i]
        pk[:, PK[f"rnw{i}"]:PK[f"rnw{i}"] + 4] = fm(inp["ret_norm_w"][i], 4)
        sw = inp["ssm_conv_w"][i]
        pk[:, PK[f"scw{i}"]:PK[f"scw{i}"] + 96] = sw.reshape(4, 24, 128).transpose(2, 1, 0).reshape(128, 96)
        pk[:, PK[f"scb{i}"]:PK[f"scb{i}"] + 24] = fm(inp["ssm_conv_b"][i], 24)
        pk[:, PK[f"sdtb{i}"]:PK[f"sdtb{i}"] + 32] = inp["ssm_dt_bias"][i][None, :]
        pk[:, PK[f"salog{i}"]:PK[f"salog{i}"] + 32] = inp["ssm_a_log"][i][None, :]
        pk[:, PK[f"sD{i}"]:PK[f"sD{i}"] + 16] = np.repeat(inp["ssm_d"][i].reshape(16, 2), 64, axis=1).T
        pk[:, PK[f"sDrep{i}"]:PK[f"sDrep{i}"] + 32] = inp["ssm_d"][i][None, :]
        pk[:, PK[f"snw{i}"]:PK[f"snw{i}"] + 16] = fm(inp["ssm_norm_w"][i], 16)
    return pk


class StopBuild(Exception):
    pass


class Builder:
    def chk(self, k):
        import os
        return int(os.environ.get("KH", "99")) == k

    def __init__(self, depth=4, do_sample=True, debug=False):
        self.depth = depth
        self.do_sample = do_sample
        self.debug = debug
        nc = bass.Bass("TRN2", target_bir_lowering=False)
        self.nc = nc
        self.S = Sched(nc)
        self._psi = 0
        self._u = 0

    def mm(self, out, lhsT, rhs, start=True, stop=True):
        self.S.add("tensor", lambda e: e.matmul(out, lhsT=lhsT, rhs=rhs, start=start, stop=stop),
                   reads=[lhsT, rhs], writes=[out])

    def tr(self, out, in_, ident):
        self.S.add("tensor", lambda e: e.transpose(out=out, in_=in_, identity=ident), reads=[in_, ident], writes=[out])

    def act(self, out, in_, func, bias=None, scale=None):
        if func == AF.Sqrt:
            self.act(out, in_, AF.Ln, bias=bias, scale=scale)
            self.act(out, out, AF.Exp, scale=-0.5)
            return
        rd = [in_]
        kw = {}
        if bias is not None:
            kw["bias"] = bias
            if not isinstance(bias, (int, float)):
                rd.append(bias)
        if scale is not None:
            kw["scale"] = scale
            if not isinstance(scale, (int, float)):
                rd.append(scale)
        self.S.add("scalar", lambda e: e.activation(out=out, in_=in_, func=func, **kw), reads=rd, writes=[out])

    def tt(self, eng, out, in0, in1, op):
        self.S.add(eng, lambda e: e.tensor_tensor(out=out, in0=in0, in1=in1, op=op), reads=[in0, in1], writes=[out])

    def ts(self, eng, out, in0, s1, op0, s2=None, op1=None):
        rd = [in0]
        if not isinstance(s1, (int, float)):
            rd.append(s1)
        if s2 is not None and not isinstance(s2, (int, float)):
            rd.append(s2)
        if op1 is None:
            self.S.add(eng, lambda e: e.tensor_scalar(out=out, in0=in0, scalar1=s1, scalar2=None, op0=op0), reads=rd, writes=[out])
        else:
            self.S.add(eng, lambda e: e.tensor_scalar(out=out, in0=in0, scalar1=s1, scalar2=s2, op0=op0, op1=op1), reads=rd, writes=[out])

    def stt(self, eng, out, in0, scalar, in1, op0, op1):
        rd = [in0, in1]
        if not isinstance(scalar, (int, float)):
            rd.append(scalar)
        self.S.add("vector", lambda e: e.scalar_tensor_tensor(out=out, in0=in0, scalar=scalar, in1=in1, op0=op0, op1=op1),
                   reads=rd, writes=[out])

    def cp(self, eng, out, in_):
        if eng == "scalar":
            self.S.add("scalar", lambda e: e.copy(out=out, in_=in_), reads=[in_], writes=[out])
        else:
            self.S.add(eng, lambda e: e.tensor_copy(out=out, in_=in_), reads=[in_], writes=[out])

    def recip(self, out, in_):
        return

    def memset(self, eng, out, val):
        self.S.add(eng, lambda e: e.memset(out, val), writes=[out])

    def dma(self, eng, out, in_, chan):
        self.S.add(eng, lambda e: e.dma_start(out=out, in_=in_), reads=[in_], writes=[out], chan=chan)

    def ps(self):
        self._psi = (self._psi + 1) % len(self.psf)
        return self.psf[self._psi]

    def psbf(self):
        self._u = (self._u + 1) % len(self.psb)
        return self.psb[self._u]

    def ve(self):
        self._u2 = getattr(self, "_u2", 0) + 1
        return "vector" if self._u2 % 2 else "gpsimd"

    def sb(self, st, name, shape, dt):
        self._nid = getattr(self, "_nid", 0) + 1
        return st.enter_context(self.nc.sbuf_tensor(f"s{self._nid}_{name}", shape, dt))

    def build(self):
        nc = self.nc
        dr = {}

        def din(name, shape):
            dr[name] = nc.dram_tensor(name, shape, F32, kind="ExternalInput").ap()

        def dout(name, shape):
            dr[name] = nc.dram_tensor(name, shape, F32, kind="ExternalOutput").ap()
        din("x_prompt", [SEQ, D])
        din("x_sample", [NS, D])
        din("state_gdn", [2, NS, 4, 128, 128])
        din("state_gdn_conv", [2, NS, 3, 1536])
        din("state_ret", [2, NS, 4, 64, 128])
        din("state_ssm", [2, NS, 32, 128, 64])
        din("state_ssm_conv", [2, NS, 3, 3072])
        din("w_in_hyb", [2, D, HYB_IN])
        din("w_out_hyb", [2, D, D])
        din("w_in_ssm", [2, D, SSM_IN])
        din("w_out_ssm", [2, 2048, D])
        din("mlp_w1", [4, D, 4096])
        din("mlp_w2", [4, 4096, D])
        din("gdn_conv_w", [2, 4, 1536])
        din("ssm_conv_w", [2, 4, 3072])
        din("ssm_conv_b", [2, 3072])
        din("pk", [128, NPK])
        din("cst", [128, NCST])
        din("rope", [2, 128, NTOK])
        dout("y_prompt", [SEQ, D])
        dout("y_sample", [NS, D])
        dout("p_gdn", [2, 4, 128, 128])
        dout("p_gdn_conv", [2, 3, 1536])
        dout("p_ret", [2, 4, 64, 128])
        dout("p_ssm", [2, 32, 128, 64])
        dout("p_ssm_conv", [2, 3, 3072])
        dout("s_gdn", [2, NS, 4, 128, 128])
        dout("s_gdn_conv", [2, NS, 3, 1536])
        dout("s_ret", [2, NS, 4, 64, 128])
        dout("s_ssm", [2, NS, 32, 128, 64])
        dout("s_ssm_conv", [2, NS, 3, 3072])
        if self.debug:
            dout("xs", [128, 8, NTOK])
        else:
            dr["xs"] = nc.dram_tensor("xs", [128, 8, NTOK], F32, kind="Internal").ap()
        self.dr = dr
        with contextlib.ExitStack() as top:
            self.psf = [top.enter_context(nc.psum_tensor(f"psf{i}", [128, 512], F32)) for i in range(6)]
            self.psb = [top.enter_context(nc.psum_tensor(f"psb{i}", [128, 1024], BF16)) for i in range(2)]
            cst = self.sb(top, "cst", [128, NCST], F32)
            pk = self.sb(top, "pk", [128, NPK], F32)
            self.cst, self.pk = cst, pk
            self.dma("sync", cst[:], dr["cst"], "cst")
            self.dma("sync", pk[:], dr["pk"], "cst")
            self.identb = self.sb(top, "identb", [128, 128], BF16)
            self.onesb = self.sb(top, "onesb", [128, 128], BF16)
            self.PTb = self.sb(top, "PTb", [128, 128], BF16)
            self.cp("vector", self.identb[:], cst[:, C_IDENT:C_IDENT + 128])
            self.cp("vector", self.onesb[:], cst[:, C_ONES:C_ONES + 128])
            self.cp("vector", self.PTb[:], cst[:, C_PT:C_PT + 128])
            self.ident = cst[:, C_IDENT:C_IDENT + 128]
            self.onesf = cst[:, C_ONES:C_ONES + 128]
            self.nonesf = cst[:, C_NONES:C_NONES + 128]
            self.U = cst[:, C_U:C_U + 128]
            self.zcol = self.sb(top, "zcol", [128, 4], F32)
            self.memset("gpsimd", self.zcol[:], 0.0)
            self.negA = self.sb(top, "negA", [128, 72], F32)
            for i in range(2):
                self.act(self.negA[:, i * 4:(i + 1) * 4], pk[:, PK[f"galog{i}"]:PK[f"galog{i}"] + 4], AF.Exp)
                self.act(self.negA[:, 8 + i * 32:8 + (i + 1) * 32], pk[:, PK[f"salog{i}"]:PK[f"salog{i}"] + 32], AF.Exp)
            self.ts("vector", self.negA[:], self.negA[:], -1.0, ALU.mult)

            self.prologue()
            for layer in range(self.depth):
                self.S.barrier()
                import os
                if "mix" in os.environ.get("KSKIP", ""):
                    pass
                elif layer % 2 == 0:
                    self.hybrid_phase(layer // 2, layer)
                else:
                    self.ssd_phase(layer // 2, layer)
                self.S.barrier()
                if "mlp" not in os.environ.get("KSKIP", ""):
                    self.mlp_phase(layer, last=(layer == self.depth - 1))
            self.S.emit()
        return nc

    def prologue(self):
        dr = self.dr
        with contextlib.ExitStack() as st:
            xin = [self.sb(st, f"xin{i}", [128, D], F32) for i in range(2)]
            stg = [self.sb(st, f"xstg{i}", [128, 8, BLK], F32) for i in range(2)]
            for blk in range(SEQ // BLK):
                sg = stg[blk % 2]
                for c in range(2):
                    t = blk * 2 + c
                    xi = xin[t % 2]
                    self.dma("sync", xi[:], dr["x_prompt"][t * 128:(t + 1) * 128, :], f"xin{t % 2}")
                    for half in range(2):
                        p = self.ps()
                        for q in range(4):
                            dc = half * 4 + q
                            self.tr(p[:, q * 128:(q + 1) * 128], xi[:, dc * 128:(dc + 1) * 128], self.ident)
                        src = p[:].rearrange("p (q t) -> p q t", q=4)
                        self.cp("vector" if half == 0 else "scalar", sg[:, half * 4:half * 4 + 4, c * 128:(c + 1) * 128], src)
                self.dma("sync", dr["xs"][:, :, blk * BLK:(blk + 1) * BLK], sg[:], f"xst{blk % 2}")
            xi = xin[0]
            self.dma("sync", xi[0:NS, :], dr["x_sample"], "xin0")
            p = self.ps()
            for dc in range(8):
                self.tr(p[:, dc * NS:(dc + 1) * NS], xi[0:NS, dc * 128:(dc + 1) * 128], self.ident[0:NS, 0:NS])
            sg = stg[0]
            self.cp("vector", sg[:, :, 0:NS], p[:, 0:8 * NS].rearrange("p (q t) -> p q t", q=8))
            self.dma("sync", dr["xs"][:, :, SEQ:NTOK], sg[:, :, 0:NS], "xst0")

    def norm_block(self, xb, hT, sq, rstd, ntok, wcol):
        for dc in range(8):
            self.act(sq[:, dc, :ntok], xb[:, dc, :ntok], AF.Square)
        p = self.ps()
        for dc in range(8):
            self.mm(p[:, :ntok], self.onesb[:], sq[:, dc, :ntok], start=(dc == 0), stop=(dc == 7))
        self.act(rstd[:, :ntok], p[:, :ntok], AF.Sqrt, bias=EPS, scale=1.0 / D)
        self.recip(rstd[:, :ntok], rstd[:, :ntok])
        for dc in range(8):
            self.stt("vector" if dc % 2 else "gpsimd", hT[:, dc, :ntok], xb[:, dc, :ntok], self.pk[:, wcol + dc:wcol + dc + 1],
                     rstd[:, :ntok], ALU.mult, ALU.mult)

    def proj_fm(self, p, wi, col0, ncols, hT, ntok, pcol0=0):
        for kc in range(8):
            self.mm(p[:ncols, pcol0:pcol0 + ntok], wi[:, kc, col0:col0 + ncols], hT[:, kc, :ntok], start=(kc == 0), stop=(kc == 7))

    def hybrid_phase(self, li, layer):
        dr, cst, pk = self.dr, self.cst, self.pk
        with contextlib.ExitStack() as ph:
            sb = lambda n, s, d: self.sb(ph, n, s, d)
            wi = sb("wi", [128, 8, HYB_IN], BF16)
            wo = sb("wo", [128, 8, D], BF16)
            for kc in range(8):
                self.dma("gpsimd", wi[:, kc, :], dr["w_in_hyb"][li, kc * 128:(kc + 1) * 128, :], "wi")
            for kc in range(8):
                self.dma("gpsimd", wo[:, kc, :], dr["w_out_hyb"][li, kc * 128:(kc + 1) * 128, :], "wo")
            for kc in range(8):
                col = PK[f"gnw{li}"] if kc < 4 else PK[f"rnw{li}"] + kc - 4
                self.ts("vector", wo[:, kc, :], wo[:, kc, :], pk[:, col:col + 1], ALU.mult)
            Sg = sb("Sg", [128, 4, 128], F32)
            Sgb = sb("Sgb", [128, 4, 128], BF16)
            Sr = sb("Sr", [64, 4, 128], F32)
            Srb = sb("Srb", [64, 4, 128], BF16)
            hist = sb("hist", [128, 12, 3], F32)
            for t_ in (Sg, Sgb, Sr, Srb, hist):
                self.memset("gpsimd", t_[:], 0.0)
            with contextlib.ExitStack() as bs:
                self.hybrid_prompt(li, layer, bs, wi, wo, Sg, Sgb, Sr, Srb, hist)
            self.dma("sync", dr["p_gdn"][li].rearrange("h k v -> k h v"), Sg[:], "pst")
            self.dma("sync", dr["p_ret"][li].rearrange("h k v -> k h v"), Sr[:], "pst")
            if self.do_sample:
                self.S.barrier()
                with contextlib.ExitStack() as bs:
                    self.hybrid_sample(li, layer, bs, wi, wo)

    def hybrid_prompt(self, li, layer, bs, wi, wo, Sg, Sgb, Sr, Srb, hist):
        dr, cst, pk = self.dr, self.cst, self.pk
        sb = lambda n, s, d: self.sb(bs, n, s, d)
        xb = sb("xb", [128, 8, BLK], F32)
        hT = sb("hT", [128, 8, BLK], BF16)
        sq = sb("sq", [128, 8, BLK], BF16)
        rstd = sb("rstd", [128, BLK], F32)
        rope = sb("rope", [128, 2, BLK], F32)
        ctmp = [sb(f"ctmp{i}", [128, BLK + 3], F32) for i in range(3)]
        cacc = [sb(f"cacc{i}", [128, BLK], F32) for i in range(3)]
        qkf = sb("qkf", [128, 8, BLK], BF16)
        qkT = sb("qkT", [128, 8, BLK], BF16)
        vT = sb("vT", [128, 4, BLK], BF16)
        szT = sb("szT", [128, 4, BLK], BF16)
        sgT = sb("sgT", [128, 4, BLK], BF16)
        rawb = sb("rawb", [64, 8, BLK], BF16)
        rot = sb("rot", [64, 8, BLK], BF16)
        rt1 = [sb(f"rt1{i}", [128, BLK], F32) for i in range(2)]
        rt2 = [sb(f"rt2{i}", [128, BLK], F32) for i in range(2)]
        smallf = sb("smallf", [128, 160], F32)
        v_tok = sb("v_tok", [128, 2, 512], BF16)
        kg_tok = sb("kg_tok", [128, 2, 512], BF16)
        kd_tok = sb("kd_tok", [128, 2, 512], BF16)
        vb_tok = sb("vb_tok", [128, 2, 512], BF16)
        kdr_tok = sb("kdr_tok", [128, 2, 256], BF16)
        decT = sb("decT", [128, 2, 512], F32)
        decS = sb("decS", [128, 512], F32)
        EgB = sb("EgB", [128, 512], F32)
        qgT = sb("qgT", [128, 2, 512], BF16)
        QK = sb("QK", [128, 2, 512], BF16)
        SR = sb("SR", [128, 512], BF16)
        qgr = sb("qgr", [64, 4, 128], BF16)
        NX = [[sb(f"NX{c}{k}", [128, 512], F32) for k in range(2)] for c in range(2)]
        NXT = [[sb(f"NXT{c}{k}", [128, 512], F32) for k in range(2)] for c in range(2)]
        NP = [[sb(f"NP{c}{k}", [128, 512], F32) for k in range(2)] for c in range(2)]
        TTb = sb("TTb", [128, 2, 512], BF16)
        nw0T = sb("nw0T", [128, 2, 512], BF16)
        delta = sb("delta", [128, 512], BF16)
        oT = sb("oT", [128, 512], F32)
        ob16 = sq[:, 2:4, :].rearrange("p a b -> p (a b)")
        osq = sq[:, 0:2, :].rearrange("p a b -> p (a b)")
        orr = sb("orr", [128, 512], F32)
        otmp = sb("otmp", [128, 512], F32)
        gU = orr[:].rearrange("p (h i) -> p h i", h=4)
        dtmp = otmp[:].rearrange("p (h i) -> p h i", h=4)
        mixT = sb("mixT", [128, 8, BLK], BF16)
        beta = smallf[:, 0:8]
        negbeta = smallf[:, 8:16]
        xg = smallf[:, 16:24]
        ax = smallf[:, 24:32]
        g_t = smallf[:, 32:40]
        gcum = smallf[:, 40:48]
        gend = smallf[:, 48:56]
        eg = smallf[:, 56:64]
        wdec = smallf[:, 64:72]
        egend = smallf[:, 72:80]
        nblk = SEQ // BLK
        dtb = pk[:, PK[f"gdtb{li}"]:PK[f"gdtb{li}"] + 4]
        nA = self.negA[:, li * 4:(li + 1) * 4]
        def body(blk):
            t0 = blk * BLK
            self.dma("sync", xb[:], dr["xs"][:, :, t0:t0 + BLK], "xld")
            self.dma("sync", rope[:, 0, :], dr["rope"][0, :, t0:t0 + BLK], "rope")
            self.dma("sync", rope[:, 1, :], dr["rope"][1, :, t0:t0 + BLK], "rope")
            self.norm_block(xb, hT, sq, rstd, BLK, PK["nmix"] + layer * 8)
            for g3 in range(4):
                chs = [g3 * 3 + u for u in range(3)]
                pp = {}
                for ch in chs:
                    pp[ch] = self.ps()
                    self.proj_fm(pp[ch], wi, ch * 128, 128, hT, BLK)
                for ch in chs:
                    self.cp("scalar", ctmp[ch % 3][:, 3:3 + BLK], pp[ch][:, :BLK])
                    self.cp("gpsimd", ctmp[ch % 3][:, 0:3], hist[:, ch, :])
                for ch in chs:
                    wc = PK[f"gcw{li}"] + ch * 4
                    self.ts("vector", cacc[ch % 3][:], ctmp[ch % 3][:, 0:BLK], pk[:, wc:wc + 1], ALU.mult)
                for tp in range(1, 4):
                    for ch in chs:
                        wc = PK[f"gcw{li}"] + ch * 4
                        self.stt("vector", cacc[ch % 3][:], ctmp[ch % 3][:, tp:tp + BLK], pk[:, wc + tp:wc + tp + 1], cacc[ch % 3][:], ALU.mult, ALU.add)
                for ch in chs:
                    self.cp("gpsimd", hist[:, ch, :], ctmp[ch % 3][:, BLK:BLK + 3])
                    if ch < 8:
                        self.act(qkf[:, ch, :], cacc[ch % 3][:], AF.Silu)
                    else:
                        self.act(vT[:, ch - 8, :], cacc[ch % 3][:], AF.Silu)
            if self.chk(1):
                return
            for pr in range(4):
                p = self.ps()
                for u in range(2):
                    ch = pr * 2 + u
                    self.act(sq[:, ch, :], qkf[:, ch, :], AF.Square)
                    self.mm(p[:, u * BLK:(u + 1) * BLK], self.onesb[:], sq[:, ch, :])
                rr = rt1[pr % 2]
                rr2 = rt2[pr % 2]
                self.act(rr[:], p[:, 0:BLK], AF.Sqrt, bias=EPS, scale=1.0)
                self.act(rr2[:], p[:, BLK:2 * BLK], AF.Sqrt, bias=EPS, scale=1.0)
                self.recip(rr[:], rr[:])
                self.recip(rr2[:], rr2[:])
                for u, r_ in ((0, rr), (1, rr2)):
                    ch = pr * 2 + u
                    sc = 128.0 ** -0.5 if ch < 4 else 1.0
                    self.stt(self.ve(), qkT[:, ch, :], qkf[:, ch, :], sc, r_[:], ALU.mult, ALU.mult)
            if self.chk(2):
                return
            for h in range(4):
                p = self.ps()
                self.proj_fm(p, wi, 1536 + h * 128, 128, hT, BLK)
                self.act(szT[:, h, :], p[:, :BLK], AF.Silu)
                p = self.ps()
                self.proj_fm(p, wi, 3080 + h * 128, 128, hT, BLK)
                self.act(sgT[:, h, :], p[:, :BLK], AF.Silu)
            if self.chk(3):
                return
            pbg = self.ps()
            for c in range(2):
                for kc in range(8):
                    self.mm(pbg[:, c * 8:(c + 1) * 8], hT[:, kc, c * 128:(c + 1) * 128], wi[:, kc, 2048:2056], start=(kc == 0), stop=(kc == 7))
            pbg3 = pbg[:, 0:16].rearrange("p (c e) -> p c e", c=2)
            b3 = beta.rearrange("p (c h) -> p c h", c=2)
            self.act(b3, pbg3[:, :, 0:4], AF.Sigmoid)
            self.ts("vector", negbeta, beta, -1.0, ALU.mult)
            xg3 = xg.rearrange("p (c h) -> p c h", c=2)
            self.tt("vector", xg3, pbg3[:, :, 4:8], dtb.unsqueeze(1).to_broadcast([128, 2, 4]), ALU.add)
            self.act(ax, xg, AF.Abs)
            self.act(ax, ax, AF.Exp, scale=-1.0)
            self.act(ax, ax, AF.Ln, bias=1.0, scale=1.0)
            self.stt("vector", g_t, xg, 0.0, ax, ALU.max, ALU.add)
            g3 = g_t.rearrange("p (c h) -> p c h", c=2)
            self.tt("vector", g3, g3, nA.unsqueeze(1).to_broadcast([128, 2, 4]), ALU.mult)
            pg = self.ps()
            self.mm(pg[:, 0:8], self.U, g_t)
            self.mm(pg[:, 8:16], self.onesf, g_t)
            self.cp("vector", gcum, pg[:, 0:8])
            self.cp("vector", gend, pg[:, 8:16])
            self.act(eg, gcum, AF.Exp)
            self.tt("vector", wdec, gend, gcum, ALU.subtract)
            self.act(wdec, wdec, AF.Exp)
            self.act(egend, gend, AF.Exp)
            if self.chk(4):
                return
            for j2 in range(4):
                p = self.ps()
                for u in range(2):
                    j = j2 * 2 + u
                    self.proj_fm(p, wi, 2056 + j * 64, 64, hT, BLK, pcol0=u * BLK)
                self.cp("scalar", rawb[:, j2 * 2:j2 * 2 + 2, :], p[0:64, :].rearrange("p (u t) -> p u t", u=2))
            for j2 in range(4):
                p = self.ps()
                for u in range(2):
                    j = j2 * 2 + u
                    self.mm(p[0:64, u * BLK:(u + 1) * BLK], self.PTb[0:64, 0:64], rawb[:, j, :])
                sc = 1.0 if j2 < 2 else 0.125
                for u in range(2):
                    j = j2 * 2 + u
                    self.stt("vector", rt1[u][0:64, :], rawb[:, j, :], sc, rope[0:64, 0, :], ALU.mult, ALU.mult)
                    self.stt("vector", rt2[u][0:64, :], p[0:64, u * BLK:(u + 1) * BLK], sc, rope[0:64, 1, :], ALU.mult, ALU.mult)
                    self.tt("gpsimd", rot[:, j, :], rt1[u][0:64, :], rt2[u][0:64, :], ALU.add)
            if self.chk(5):
                return
            for c in range(2):
                p = self.ps()
                for kc in range(8):
                    self.mm(p[:, :], hT[:, kc, c * 128:(c + 1) * 128], wi[:, kc, 2568:3080], start=(kc == 0), stop=(kc == 7))
                self.cp("scalar", vb_tok[:, c, :], p[:, :])
            if self.chk(6):
                return
            for c in range(2):
                cs = slice(c * 128, (c + 1) * 128)
                pb = self.psbf()
                for h in range(4):
                    self.tr(pb[:, h * 128:(h + 1) * 128], qkT[:, 4 + h, cs], self.identb[:])
                src = pb[:, 0:512].rearrange("p (h k) -> p h k", h=4)
                self.tt("vector", kg_tok[:, c, :].rearrange("p (h k) -> p h k", h=4), src,
                        eg[:, c * 4:(c + 1) * 4].unsqueeze(2).to_broadcast([128, 4, 128]), ALU.mult)
                self.tt("vector", kd_tok[:, c, :].rearrange("p (h k) -> p h k", h=4), src,
                        wdec[:, c * 4:(c + 1) * 4].unsqueeze(2).to_broadcast([128, 4, 128]), ALU.mult)
                for h in range(4):
                    self.tr(pb[:, 512 + h * 128:512 + (h + 1) * 128], vT[:, h, cs], self.identb[:])
                self.cp("scalar", v_tok[:, c, :], pb[:, 512:1024])
            if self.chk(7):
                return
            for c in range(2):
                cs = slice(c * 128, (c + 1) * 128)
                self.tt("vector", gU[:], self.U.unsqueeze(1).to_broadcast([128, 4, 128]),
                        g_t[:, c * 4:(c + 1) * 4].unsqueeze(2).to_broadcast([128, 4, 128]), ALU.mult)
                pA = self.ps()
                pD = self.ps()
                for h in range(4):
                    hs = slice(h * 128, (h + 1) * 128)
                    self.mm(pA[:, hs], self.onesf, gU[:, h, :])
                    self.mm(pD[:, hs], self.onesf, gU[:, h, :], start=True, stop=False)
                    self.mm(pD[:, hs], gU[:, h, :], self.nonesf, start=False, stop=True)
                self.act(EgB[:], pA[:], AF.Exp)
                self.tt("vector", dtmp[:], pD[:].rearrange("p (h i) -> p h i", h=4),
                        cst[:, C_NEG:C_NEG + 128].unsqueeze(1).to_broadcast([128, 4, 128]), ALU.add)
                self.act(decT[:, c, :], dtmp[:].rearrange("p h i -> p (h i)"), AF.Exp)
                self.tt("vector", decS[:].rearrange("p (h i) -> p h i", h=4), decT[:, c, :].rearrange("p (h i) -> p h i", h=4),
                        cst[:, C_STRICT:C_STRICT + 128].unsqueeze(1).to_broadcast([128, 4, 128]), ALU.mult)
                self.tt("vector", qgT[:, c, :].rearrange("p (h i) -> p h i", h=4), qkT[:, 0:4, cs],
                        EgB[:].rearrange("p (h i) -> p h i", h=4), ALU.mult)
                pK = self.ps()
                pQ = self.ps()
                for h in range(4):
                    hs = slice(h * 128, (h + 1) * 128)
                    self.mm(pK[:, hs], qkT[:, 4 + h, cs], qkT[:, 4 + h, cs])
                    self.mm(pQ[:, hs], qkT[:, 4 + h, cs], qkT[:, h, cs])
                for h in range(4):
                    hs = slice(h * 128, (h + 1) * 128)
                    self.stt("vector", NX[c][0][:, hs], pK[:, hs], negbeta[:, c * 4 + h:c * 4 + h + 1], decS[:, hs], ALU.mult, ALU.mult)
                self.tt("vector", QK[:, c, :], pQ[:], decT[:, c, :], ALU.mult)
            if self.chk(8):
                return
            for c in range(2):
                pT = self.ps()
                for h in range(4):
                    hs = slice(h * 128, (h + 1) * 128)
                    self.tr(pT[:, hs], NX[c][0][:, hs], self.ident)
                self.cp("scalar", NXT[c][0][:], pT[:])
                self.tt("vector", NP[c][0][:].rearrange("p (h i) -> p h i", h=4), NX[c][0][:].rearrange("p (h i) -> p h i", h=4),
                        self.ident.unsqueeze(1).to_broadcast([128, 4, 128]), ALU.add)
            cur = 0
            for s in range(1, 7):
                nxt = 1 - cur
                for c in range(2):
                    X, XT, P_ = NX[c][cur], NXT[c][cur], NP[c][cur]
                    Xn, XTn, Pn = NX[c][nxt], NXT[c][nxt], NP[c][nxt]
                    pXT = self.ps()
                    for h in range(4):
                        hs = slice(h * 128, (h + 1) * 128)
                        self.mm(pXT[:, hs], X[:, hs], XT[:, hs])
                    self.cp("scalar", XTn[:], pXT[:])
                    if s < 6:
                        pX = self.ps()
                        for h in range(4):
                            hs = slice(h * 128, (h + 1) * 128)
                            self.mm(pX[:, hs], XT[:, hs], X[:, hs])
                        self.cp("scalar", Xn[:], pX[:])
                    pP = self.ps()
                    for h in range(4):
                        hs = slice(h * 128, (h + 1) * 128)
                        self.mm(pP[:, hs], XTn[:, hs], P_[:, hs])
                    self.tt("vector", Pn[:], pP[:], P_[:], ALU.add)
                cur = nxt
            for c in range(2):
                self.cp("scalar", TTb[:, c, :], NP[c][cur][:])
                pW = self.ps()
                for h in range(4):
                    hs = slice(h * 128, (h + 1) * 128)
                    self.mm(pW[:, hs], kg_tok[:, c, hs], TTb[:, c, hs])
                self.act(nw0T[:, c, :], pW[:], AF.Copy, scale=-1.0)
            if self.chk(9):
                return
            for c in range(2):
                cs = slice(c * 128, (c + 1) * 128)
                pd = self.ps()
                for h in range(4):
                    hs = slice(h * 128, (h + 1) * 128)
                    self.mm(pd[:, hs], TTb[:, c, hs], v_tok[:, c, hs], start=True, stop=False)
                    self.mm(pd[:, hs], nw0T[:, c, hs], Sgb[:, h, :], start=False, stop=True)
                self.tt("vector", delta[:].rearrange("p (h v) -> p h v", h=4), pd[:].rearrange("p (h v) -> p h v", h=4),
                        beta[:, c * 4:(c + 1) * 4].unsqueeze(2).to_broadcast([128, 4, 128]), ALU.mult)
                py = self.ps()
                pS = self.ps()
                for h in range(4):
                    hs = slice(h * 128, (h + 1) * 128)
                    self.mm(py[:, hs], Sgb[:, h, :], qgT[:, c, hs], start=True, stop=False)
                    self.mm(py[:, hs], delta[:, hs], QK[:, c, hs], start=False, stop=True)
                    self.mm(pS[:, hs], kd_tok[:, c, hs], delta[:, hs])
                self.cp("scalar", oT[:], py[:])
                for h in range(4):
                    hs = slice(h * 128, (h + 1) * 128)
                    self.stt("vector", Sg[:, h, :], Sg[:, h, :], egend[:, c * 4 + h:c * 4 + h + 1], pS[:, hs], ALU.mult, ALU.add)
                self.cp("scalar", Sgb[:], Sg[:])
                self.act(osq, oT[:], AF.Square)
                pn = self.ps()
                for h in range(4):
                    hs = slice(h * 128, (h + 1) * 128)
                    self.mm(pn[:, hs], self.onesb[:], osq[:, hs])
                self.act(orr[:], pn[:], AF.Sqrt, bias=EPS, scale=1.0 / 128)
                self.recip(orr[:], orr[:])
                self.tt("vector", otmp[:], oT[:], orr[:], ALU.mult)
                self.tt("vector", mixT[:, 0:4, cs], otmp[:].rearrange("p (h i) -> p h i", h=4), szT[:, :, cs], ALU.mult)
                pSc = self.ps()
                for h in range(4):
                    hs = slice(h * 128, (h + 1) * 128)
                    self.mm(pSc[:, hs], rot[:, 4 + h, cs], rot[:, h, cs])
                self.tt("vector", SR[:], pSc[:], cst[:, C_DMT:C_DMT + 512], ALU.mult)
                pb = self.psbf()
                for h in range(4):
                    self.tr(pb[:, h * 64:(h + 1) * 64], rot[:, 4 + h, cs], self.identb[0:64, 0:64])
                self.tt("vector", kdr_tok[:, c, :], pb[:, 0:256], cst[:, C_KDEC:C_KDEC + 256], ALU.mult)
                self.tt("vector", qgr[:], rot[:, 0:4, cs], cst[0:64, C_QDEC:C_QDEC + 512].rearrange("p (r i) -> p r i", r=4), ALU.mult)
                py = self.ps()
                pS = self.ps()
                for h in range(4):
                    hs = slice(h * 128, (h + 1) * 128)
                    self.mm(py[:, hs], vb_tok[:, c, hs], SR[:, hs], start=True, stop=False)
                    self.mm(py[:, hs], Srb[:, h, :], qgr[:, h, :], start=False, stop=True)
                    self.mm(pS[0:64, hs], kdr_tok[:, c, h * 64:(h + 1) * 64], vb_tok[:, c, hs])
                self.cp("scalar", oT[:], py[:])
                self.cp("scalar", ob16, oT[:])
                for h in range(4):
                    hs = slice(h * 128, (h + 1) * 128)
                    self.stt("vector", Sr[:, h, :], Sr[:, h, :], GAM[h] ** 128, pS[0:64, hs], ALU.mult, ALU.add)
                self.cp("scalar", Srb[:], Sr[:])
                pm = self.ps()
                for h in range(4):
                    hs = slice(h * 128, (h + 1) * 128)
                    self.mm(pm[:, hs], self.onesb[:], ob16[:, hs])
                self.stt("vector", otmp[:], pm[:], -1.0 / 128, oT[:], ALU.mult, ALU.add)
                self.act(osq, otmp[:], AF.Square)
                pn = self.ps()
                for h in range(4):
                    hs = slice(h * 128, (h + 1) * 128)
                    self.mm(pn[:, hs], self.onesb[:], osq[:, hs])
                self.act(orr[:], pn[:], AF.Sqrt, bias=EPS, scale=1.0 / 128)
                self.recip(orr[:], orr[:])
                self.tt("vector", otmp[:], otmp[:], orr[:], ALU.mult)
                self.tt("vector", mixT[:, 4:8, cs], otmp[:].rearrange("p (h i) -> p h i", h=4), sgT[:, :, cs], ALU.mult)
            if self.chk(10):
                return
            for dc in range(8):
                p = self.ps()
                for kc in range(8):
                    self.mm(p[:, :BLK], wo[:, kc, dc * 128:(dc + 1) * 128], mixT[:, kc, :], start=(kc == 0), stop=(kc == 7))
                self.tt("vector", xb[:, dc, :], p[:, :BLK], xb[:, dc, :], ALU.add)
            self.dma("sync", dr["xs"][:, :, t0:t0 + BLK], xb[:], "xst")
            if blk == nblk - 1:
                for cb in range(3):
                    p = self.ps()
                    for kc in range(8):
                        self.mm(p[0:3, :], hT[:, kc, BLK - 3:BLK], wi[:, kc, cb * 512:(cb + 1) * 512], start=(kc == 0), stop=(kc == 7))
                    cs3 = (NX[0][0], NX[0][1], NX[1][0])[cb]
                    self.cp("vector", cs3[0:3, :], p[0:3, :])
                    self.dma("sync", dr["p_gdn_conv"][li, :, cb * 512:(cb + 1) * 512], cs3[0:3, :], "pst")
        for blk in range(nblk if not self.chk(0) else 1):
            body(blk)

    def red(self, eng, out, in_):
        self.S.add(eng, lambda e: e.tensor_reduce(out=out, in_=in_, axis=AX.X, op=ALU.add), reads=[in_], writes=[out])

    def softplus16(self, out, x, tmp):
        self.act(tmp, x, AF.Abs)
        self.act(tmp, tmp, AF.Exp, scale=-1.0)
        self.act(tmp, tmp, AF.Ln, bias=1.0, scale=1.0)
        self.stt("vector", out, x, 0.0, tmp, ALU.max, ALU.add)

    def sample_common(self, bs, layer, wi, ncols):
        dr = self.dr
        sb = lambda n, s, d: self.sb(bs, n, s, d)
        xbs = sb("xbs", [128, 8, NS], F32)
        hTs = sb("hTs", [128, 8, NS], BF16)
        sqs = sb("sqs", [128, 8, NS], BF16)
        rstds = sb("rstds", [128, NS], F32)
        prj = sb("prj", [NS, ncols], F32)
        self.dma("sync", xbs[:], dr["xs"][:, :, SEQ:NTOK], "xld")
        self.norm_block(xbs, hTs, sqs, rstds, NS, PK["nmix"] + layer * 8)
        c0 = 0
        k = 0
        while c0 < ncols:
            n = min(512, ncols - c0)
            p = self.ps()
            for kc in range(8):
                self.mm(p[0:NS, 0:n], hTs[:, kc, :], wi[:, kc, c0:c0 + n], start=(kc == 0), stop=(kc == 7))
            self.cp("vector" if k % 2 else "scalar", prj[:, c0:c0 + n], p[0:NS, 0:n])
            c0 += n
            k += 1
        return xbs, prj

    def sample_outproj(self, bs, xbs, mixs, nk, wo_get):
        dr = self.dr
        sb = lambda n, s, d: self.sb(bs, n, s, d)
        mixTs = sb("mixTs", [128, nk, NS], BF16)
        for k0 in range(0, nk, 8):
            p = self.ps()
            for k in range(k0, min(nk, k0 + 8)):
                self.tr(p[:, (k - k0) * NS:(k - k0 + 1) * NS], mixs[:, k * 128:(k + 1) * 128], self.ident[0:NS, 0:NS])
            n = min(nk, k0 + 8) - k0
            self.cp("vector", mixTs[:, k0:k0 + n, :], p[:, 0:n * NS].rearrange("p (k t) -> p k t", k=n))
        po = self.ps()
        for dc in range(8):
            for kc in range(nk):
                self.mm(po[:, dc * NS:(dc + 1) * NS], wo_get(kc, dc), mixTs[:, kc, :], start=(kc == 0), stop=(kc == nk - 1))
        self.tt("vector", xbs[:], po[:, 0:8 * NS].rearrange("p (k t) -> p k t", k=8), xbs[:], ALU.add)
        self.dma("sync", dr["xs"][:, :, SEQ:NTOK], xbs[:], "xst")

    def hybrid_sample(self, li, layer, bs, wi, wo):
        dr, cst, pk = self.dr, self.cst, self.pk
        sb = lambda n, s, d: self.sb(bs, n, s, d)
        xbs, prj = self.sample_common(bs, layer, wi, HYB_IN)
        id16 = cst[0:NS, C_ID16:C_ID16 + 256].rearrange("p (a b) -> p a b", a=16)
        id16f = cst[:, C_ID16:C_ID16 + 256].rearrange("p (a b) -> p a b", a=16)
        cw = sb("cw", [NS, 4, 512], F32)
        cbuf = sb("cbuf", [NS, 3, 512], F32)
        qkv = sb("qkv", [NS, 1536], F32)
        ctm = sb("ctm", [NS, 512], F32)
        stb = sb("stb", [128, NS, 4, 128], F32)
        kqm = sb("kqm", [128, 8, NS, NS], F32)
        ktm = [sb(f"ktm{i}", [NS, NS, 128], F32) for i in range(2)]
        qkTs = sb("qkTs", [128, 8, NS], F32)
        sm = sb("sms", [NS, 256], F32)
        t1 = sb("st1", [NS, 512], F32)
        t2 = sb("st2", [NS, 512], F32)
        dl = sb("sdl", [NS, 512], F32)
        mixs = sb("mixs", [NS, 1024], F32)
        egB = sb("egB", [128, 64], F32)
        egm = sb("egm", [NS, NS, 4], F32)
        qr = sb("qr", [NS, 4, 64], F32)
        kr = sb("kr", [NS, 4, 64], F32)
        for b in range(NS):
            self.dma("sync", stb[:, b, :, :], dr["state_gdn"][li, b].rearrange("h k v -> k h v"), "stld")
        for pc in range(3):
            c0 = pc * 512
            self.dma("sync", cw[:], dr["gdn_conv_w"][li, :, c0:c0 + 512].partition_broadcast(NS), "cwld")
            self.dma("sync", cbuf[:], dr["state_gdn_conv"][li, :, :, c0:c0 + 512], "cbld")
            self.tt("vector", ctm[:], prj[:, c0:c0 + 512], cw[:, 3, :], ALU.mult)
            for tp in range(3):
                self.tt("gpsimd", t1[:], cbuf[:, tp, :], cw[:, tp, :], ALU.mult)
                self.tt("vector", ctm[:], ctm[:], t1[:], ALU.add)
            self.act(qkv[:, c0:c0 + 512], ctm[:], AF.Silu)
            self.dma("sync", dr["s_gdn_conv"][li, :, 0:2, c0:c0 + 512], cbuf[:, 1:3, :], "cvst")
        self.dma("sync", dr["s_gdn_conv"][li, :, 2, :], prj[:, 0:1536], "cvst")
        beta = sm[:, 0:4]
        xg = sm[:, 4:8]
        tmp4 = sm[:, 8:12]
        g_ = sm[:, 12:16]
        eg = sm[:, 16:20]
        ss = sm[:, 20:28]
        qk = sm[:, 28:32]
        ss2 = sm[:, 32:36]
        rs2 = sm[:, 36:40]
        self.act(beta, prj[:, 2048:2052], AF.Sigmoid)
        self.tt("vector", xg, prj[:, 2052:2056], pk[0:NS, PK[f"gdtb{li}"]:PK[f"gdtb{li}"] + 4], ALU.add)
        self.softplus16(g_, xg, tmp4)
        self.tt("vector", g_, g_, self.negA[0:NS, li * 4:(li + 1) * 4], ALU.mult)
        self.act(eg, g_, AF.Exp)
        qk3 = qkv[:, 0:1024].rearrange("p (h k) -> p h k", h=8)
        self.tt("vector", t1[:], qkv[:, 0:512], qkv[:, 0:512], ALU.mult)
        self.tt("gpsimd", t2[:], qkv[:, 512:1024], qkv[:, 512:1024], ALU.mult)
        self.red("vector", ss[:, 0:4], t1[:].rearrange("p (h k) -> p h k", h=4))
        self.red("vector", ss[:, 4:8], t2[:].rearrange("p (h k) -> p h k", h=4))
        self.act(ss, ss, AF.Sqrt, bias=EPS, scale=1.0)
        self.recip(ss, ss)
        self.ts("vector", ss[:, 0:4], ss[:, 0:4], 128.0 ** -0.5, ALU.mult)
        self.tt("vector", qk3, qk3, ss.unsqueeze(2).to_broadcast([NS, 8, 128]), ALU.mult)
        self.tt("vector", t1[:], qkv[:, 0:512], qkv[:, 512:1024], ALU.mult)
        self.red("vector", qk, t1[:].rearrange("p (h k) -> p h k", h=4))
        p = self.ps()
        for j in range(8):
            self.tr(p[:, j * NS:(j + 1) * NS], qkv[:, j * 128:(j + 1) * 128], self.ident[0:NS, 0:NS])
        self.cp("vector", qkTs[:], p[:, 0:8 * NS].rearrange("p (j t) -> p j t", j=8))
        for j in range(8):
            self.tt("vector" if j % 2 else "gpsimd", kqm[:, j, :, :], qkTs[:, j, :].unsqueeze(1).to_broadcast([128, NS, NS]), id16f, ALU.mult)
        pk_ = self.ps()
        pq_ = self.ps()
        for h in range(4):
            hs = slice(h * 128, (h + 1) * 128)
            for b in range(NS):
                self.mm(pk_[0:NS, hs], kqm[:, 4 + h, b, :], stb[:, b, h, :], start=(b == 0), stop=(b == NS - 1))
                self.mm(pq_[0:NS, hs], kqm[:, h, b, :], stb[:, b, h, :], start=(b == 0), stop=(b == NS - 1))
        v3 = qkv[:, 1024:1536].rearrange("p (h v) -> p h v", h=4)
        eg3 = eg.unsqueeze(2).to_broadcast([NS, 4, 128])
        t13 = t1[:].rearrange("p (h v) -> p h v", h=4)
        t23 = t2[:].rearrange("p (h v) -> p h v", h=4)
        dl3 = dl[:].rearrange("p (h v) -> p h v", h=4)
        self.tt("vector", t13, pk_[0:NS, :].rearrange("p (h v) -> p h v", h=4), eg3, ALU.mult)
        self.tt("vector", t13, v3, t13, ALU.subtract)
        self.tt("vector", dl3, t13, beta.unsqueeze(2).to_broadcast([NS, 4, 128]), ALU.mult)
        self.tt("vector", t23, pq_[0:NS, :].rearrange("p (h v) -> p h v", h=4), eg3, ALU.mult)
        self.tt("vector", t13, dl3, qk.unsqueeze(2).to_broadcast([NS, 4, 128]), ALU.mult)
        self.tt("vector", t23, t23, t13, ALU.add)
        self.tt("gpsimd", t13, t23, t23, ALU.mult)
        self.red("vector", ss2, t13)
        self.act(rs2, ss2, AF.Sqrt, bias=EPS, scale=1.0 / 128)
        self.recip(rs2, rs2)
        self.tt("vector", t23, t23, rs2.unsqueeze(2).to_broadcast([NS, 4, 128]), ALU.mult)
        self.act(t1[:], prj[:, 1536:2048], AF.Silu)
        self.tt("vector", mixs[:, 0:512], t2[:], t1[:], ALU.mult)
        self.tt("vector", egm[:], eg.unsqueeze(1).to_broadcast([NS, NS, 4]),
                self.id16col(cst).to_broadcast([NS, NS, 4]), ALU.mult)
        pe = self.ps()
        self.mm(pe[:, 0:64], self.onesf[0:NS, :], egm[:].rearrange("p a h -> p (a h)"))
        self.cp("vector", egB[:], pe[:, 0:64])
        for h in range(4):
            kt = ktm[h % 2]
            self.tt("gpsimd", kt[:], qkv[:, 512 + h * 128:512 + (h + 1) * 128].unsqueeze(1).to_broadcast([NS, NS, 128]),
                    self.id16col(cst).to_broadcast([NS, NS, 128]), ALU.mult)
            for b4 in range(NS // 4):
                pu = self.ps()
                for u in range(4):
                    b = b4 * 4 + u
                    self.mm(pu[:, u * 128:(u + 1) * 128], kt[:, b, :], dl[:, h * 128:(h + 1) * 128])
                for u in range(4):
                    b = b4 * 4 + u
                    self.stt("vector", stb[:, b, h, :], stb[:, b, h, :], egB[:, b * 4 + h:b * 4 + h + 1], pu[:, u * 128:(u + 1) * 128],
                             ALU.mult, ALU.add)
        for b in range(NS):
            self.dma("sync", dr["s_gdn"][li, b].rearrange("h k v -> k h v"), stb[:, b, :, :], "stst")
        cosr = cst[0:NS, C_ROPES:C_ROPES + 32].unsqueeze(1).to_broadcast([NS, 4, 32])
        sinr = cst[0:NS, C_ROPES + 32:C_ROPES + 64].unsqueeze(1).to_broadcast([NS, 4, 32])
        ra = sb("ra", [NS, 4, 32], F32)
        rb = sb("rb", [NS, 4, 32], F32)
        for (src0, dst, sc) in ((2056, qr, 1.0), (2312, kr, 0.125)):
            src = prj[:, src0:src0 + 256].rearrange("p (h k) -> p h k", h=4)
            x1, x2 = src[:, :, 0:32], src[:, :, 32:64]
            self.tt("vector", ra[:], x1, cosr, ALU.mult)
            self.tt("vector", rb[:], x2, sinr, ALU.mult)
            self.tt("vector", dst[:, :, 0:32], ra[:], rb[:], ALU.subtract)
            self.tt("vector", ra[:], x2, cosr, ALU.mult)
            self.tt("vector", rb[:], x1, sinr, ALU.mult)
            self.tt("vector", dst[:, :, 32:64], ra[:], rb[:], ALU.add)
            if sc != 1.0:
                self.ts("vector", dst[:], dst[:], sc, ALU.mult)
        for b in range(NS):
            self.dma("sync", stb[0:64, b, :, :], dr["state_ret"][li, b].rearrange("h k v -> k h v"), "stld")
        qrT = sb("qrT", [64, 4, NS], F32)
        qrm = kqm[0:64, 0:4, :, :]
        krm = [ktm[i][:, :, 0:64] for i in range(2)]
        p = self.ps()
        for h in range(4):
            self.tr(p[0:64, h * NS:(h + 1) * NS], qr[:, h, :], self.ident[0:NS, 0:NS])
        self.cp("vector", qrT[:], p[0:64, 0:4 * NS].rearrange("p (h t) -> p h t", h=4))
        for h in range(4):
            self.tt("vector", qrm[:, h, :, :], qrT[:, h, :].unsqueeze(1).to_broadcast([64, NS, NS]), id16f[0:64], ALU.mult)
        pq_ = self.ps()
        for h in range(4):
            hs = slice(h * 128, (h + 1) * 128)
            for b in range(NS):
                self.mm(pq_[0:NS, hs], qrm[:, h, b, :], stb[0:64, b, h, :], start=(b == 0), stop=(b == NS - 1))
        vb3 = prj[:, 2568:3080].rearrange("p (h v) -> p h v", h=4)
        qkr = sm[:, 40:44]
        mean = sm[:, 44:48]
        var = sm[:, 48:52]
        self.tt("vector", dl[:, 0:256].rearrange("p (h k) -> p h k", h=4), qr[:], kr[:], ALU.mult)
        self.red("vector", qkr, dl[:, 0:256].rearrange("p (h k) -> p h k", h=4))
        self.tt("vector", t13, vb3, qkr.unsqueeze(2).to_broadcast([NS, 4, 128]), ALU.mult)
        for h in range(4):
            self.stt("vector", t2[:, h * 128:(h + 1) * 128], pq_[0:NS, h * 128:(h + 1) * 128], GAM[h], t1[:, h * 128:(h + 1) * 128], ALU.mult, ALU.add)
        self.red("vector", mean, t23)
        self.ts("vector", mean, mean, -1.0 / 128, ALU.mult)
        self.tt("vector", t23, t23, mean.unsqueeze(2).to_broadcast([NS, 4, 128]), ALU.add)
        self.tt("gpsimd", t13, t23, t23, ALU.mult)
        self.red("vector", var, t13)
        self.act(var, var, AF.Sqrt, bias=EPS, scale=1.0 / 128)
        self.recip(var, var)
        self.tt("vector", t23, t23, var.unsqueeze(2).to_broadcast([NS, 4, 128]), ALU.mult)
        self.act(t1[:], prj[:, 3080:3592], AF.Silu)
        self.tt("vector", mixs[:, 512:1024], t2[:], t1[:], ALU.mult)
        for h in range(4):
            km = krm[h % 2]
            self.tt("gpsimd", km, kr[:, h, :].unsqueeze(1).to_broadcast([NS, NS, 64]), self.id16col(cst).to_broadcast([NS, NS, 64]), ALU.mult)
            for b4 in range(NS // 4):
                pu = self.ps()
                for u in range(4):
                    b = b4 * 4 + u
                    self.mm(pu[0:64, u * 128:(u + 1) * 128], km[:, b, :], prj[:, 2568 + h * 128:2568 + (h + 1) * 128])
                for u in range(4):
                    b = b4 * 4 + u
                    self.stt("vector", stb[0:64, b, h, :], stb[0:64, b, h, :], GAM[h], pu[0:64, u * 128:(u + 1) * 128], ALU.mult, ALU.add)
        for b in range(NS):
            self.dma("sync", dr["s_ret"][li, b].rearrange("h k v -> k h v"), stb[0:64, b, :, :], "stst")
        self.sample_outproj(bs, xbs, mixs, 8, lambda kc, dc: wo[:, kc, dc * 128:(dc + 1) * 128])

    def id16col(self, cst):
        return self.ident[0:NS, 0:NS].unsqueeze(2)

    def ssd_phase(self, li, layer):
        dr, cst, pk = self.dr, self.cst, self.pk
        with contextlib.ExitStack() as ph:
            sb = lambda n, s, d: self.sb(ph, n, s, d)
            wi = sb("wis", [128, 8, SSM_IN], BF16)
            wo = [sb(f"wos{i}", [128, 2048], BF16) for i in range(2)]
            wosd = self.nc.dram_tensor(f"wos_bf{li}", [8, 128, 16, 128], BF16, kind="Internal").ap()
            self.wosd = wosd
            for kc in range(8):
                self.dma("gpsimd", wi[:, kc, :], dr["w_in_ssm"][li, kc * 128:(kc + 1) * 128, :], "wi")
            for kc in range(16):
                wb = wo[(kc // 2) % 2][:, (kc % 2) * 1024:(kc % 2 + 1) * 1024]
                self.dma("gpsimd", wb, dr["w_out_ssm"][li, kc * 128:(kc + 1) * 128, :], f"wog{kc % 4}")
                col = PK[f"snw{li}"] + kc
                self.ts("vector", wb, wb, pk[:, col:col + 1], ALU.mult)
                self.dma("sync", wosd[:, :, kc, :].rearrange("dc p d -> p dc d"), wb.rearrange("p (dc d) -> p dc d", dc=8), f"wost{kc % 4}")
            S_ = sb("S", [128, 2048], F32)
            Sbz = sb("Sbz", [128, 32, 128], BF16)
            Vz = sb("Vz", [128, 32, 128], BF16)
            hist = sb("hists", [128, 24, 3], F32)
            for t_ in (S_, Sbz, Vz, hist):
                self.memset("gpsimd", t_[:], 0.0)
            with contextlib.ExitStack() as bs:
                self.ssd_prompt(li, layer, bs, wi, wo, S_, Sbz, Vz, hist)
            self.dma("sync", dr["p_ssm"][li].rearrange("h n d -> n h d"), S_[:].rearrange("p (h d) -> p h d", h=32), "pst")
            if self.do_sample:
                self.S.barrier()
                with contextlib.ExitStack() as bs:
                    self.ssd_sample(li, layer, bs, wi, wo, [Sbz[:].bitcast(F32).rearrange("p a b -> p (a b)"), Vz[:].bitcast(F32).rearrange("p a b -> p (a b)"), S_[:]])

    def ssd_prompt(self, li, layer, bs, wi, wo, S_, Sbz, Vz, hist):
        dr, cst, pk = self.dr, self.cst, self.pk
        sb = lambda n, s, d: self.sb(bs, n, s, d)
        xb = sb("xb", [128, 8, BLK], F32)
        ynT = xb[:].bitcast(BF16).rearrange("p a (b t) -> p (a b) t", b=2)
        hT = sb("hT", [128, 8, BLK], BF16)
        rstd = sb("rstd", [128, BLK], F32)
        ctmp = [sb(f"ctmp{i}", [128, BLK + 3], F32) for i in range(3)]
        cacc = [sb(f"cacc{i}", [128, BLK], F32) for i in range(3)]
        szT = sb("szT", [128, 16, BLK], BF16)
        xsT = sb("xsT", [128, 16, BLK], BF16)
        BCT = sb("BCT", [128, 8, BLK], BF16)
        xpc = [sb(f"xpc{i}", [128, BLK], F32) for i in range(3)]
        smf = sb("smf", [128, 9 * 64], F32)
        vp = sb("vp", [128, 2048], BF16)
        B_tok = sb("B_tok", [128, 512], BF16)
        scM = sb("scM", [128, 512], F32)
        gU = sb("gU", [128, 512], F32)
        E_ = sb("E_", [128, 512], F32)
        dtm = sb("dtm", [128, 512], F32)
        SD = [sb(f"SD{i}", [128, 512], BF16) for i in range(2)]
        CgT = [sb(f"CgT{i}", [128, 512], BF16) for i in range(2)]
        y_sb = sb("y_sb", [128, 16, 128], F32)
        ysq = sb("ysq", [128, 16, 128], BF16)
        sq = ysq[:].rearrange("p (a b) i -> p a (b i)", b=2)
        rg = sb("rg", [128, 512], F32)
        dtr = smf[:, 0:64]
        ax = smf[:, 64:128]
        dt_ = smf[:, 128:192]
        g_t = smf[:, 192:256]
        gcum = smf[:, 256:320]
        gend = smf[:, 320:384]
        wdec = smf[:, 384:448]
        dtw = smf[:, 448:512]
        egend = smf[:, 512:576]
        dtb = pk[:, PK[f"sdtb{li}"]:PK[f"sdtb{li}"] + 32]
        nA = self.negA[:, 8 + li * 32:8 + (li + 1) * 32]
        nblk = SEQ // BLK
        Vzv = Vz[:].rearrange("p (q a) (b d) -> p q a b d", a=2, b=2)
        Sbzv = Sbz[:].rearrange("p (q a) (b d) -> p q a b d", a=2, b=2)

        def body(blk):
            t0 = blk * BLK
            self.dma("sync", xb[:], dr["xs"][:, :, t0:t0 + BLK], "xld")
            self.norm_block(xb, hT, sq, rstd, BLK, PK["nmix"] + layer * 8)
            if self.chk(21):
                return
            for ch in range(16):
                p = self.ps()
                self.proj_fm(p, wi, ch * 128, 128, hT, BLK)
                self.act(szT[:, ch, :], p[:, :BLK], AF.Silu)
            for g3 in range(8):
                chs = [g3 * 3 + u for u in range(3)]
                pp = {}
                for ch in chs:
                    pp[ch] = self.ps()
                    self.proj_fm(pp[ch], wi, 2048 + ch * 128, 128, hT, BLK)
                for ch in chs:
                    self.cp("scalar", ctmp[ch % 3][:, 3:3 + BLK], pp[ch][:, :BLK])
                    self.cp("gpsimd", ctmp[ch % 3][:, 0:3], hist[:, ch, :])
                for ch in chs:
                    wc = PK[f"scw{li}"] + ch * 4
                    bc = PK[f"scb{li}"] + ch
                    self.ts("vector", cacc[ch % 3][:], ctmp[ch % 3][:, 0:BLK], pk[:, wc:wc + 1], ALU.mult, pk[:, bc:bc + 1], ALU.add)
                for tp in range(1, 4):
                    for ch in chs:
                        wc = PK[f"scw{li}"] + ch * 4
                        self.stt("vector", cacc[ch % 3][:], ctmp[ch % 3][:, tp:tp + BLK], pk[:, wc + tp:wc + tp + 1], cacc[ch % 3][:], ALU.mult, ALU.add)
                for ch in chs:
                    self.cp("gpsimd", hist[:, ch, :], ctmp[ch % 3][:, BLK:BLK + 3])
                    dst = xsT[:, ch, :] if ch < 16 else BCT[:, ch - 16, :]
                    self.act(dst, cacc[ch % 3][:], AF.Silu)
            if self.chk(22):
                return
            pdt = self.ps()
            for c in range(2):
                for kc in range(8):
                    self.mm(pdt[:, c * 32:(c + 1) * 32], hT[:, kc, c * 128:(c + 1) * 128], wi[:, kc, 5120:5152], start=(kc == 0), stop=(kc == 7))
            self.tt("vector", dtr.rearrange("p (c h) -> p c h", c=2), pdt[:, 0:64].rearrange("p (c h) -> p c h", c=2),
                    dtb.unsqueeze(1).to_broadcast([128, 2, 32]), ALU.add)
            self.act(ax, dtr, AF.Abs)
            self.act(ax, ax, AF.Exp, scale=-1.0)
            self.act(ax, ax, AF.Ln, bias=1.0, scale=1.0)
            self.stt("vector", dt_, dtr, 0.0, ax, ALU.max, ALU.add)
            self.tt("vector", g_t.rearrange("p (c h) -> p c h", c=2), dt_.rearrange("p (c h) -> p c h", c=2),
                    nA.unsqueeze(1).to_broadcast([128, 2, 32]), ALU.mult)
            pg = self.ps()
            self.mm(pg[:, 0:64], self.U, g_t)
            self.mm(pg[:, 64:128], self.onesf, g_t)
            self.cp("vector", gcum, pg[:, 0:64])
            self.cp("vector", gend, pg[:, 64:128])
            self.tt("vector", wdec, gend, gcum, ALU.subtract)
            self.act(wdec, wdec, AF.Exp)
            self.tt("vector", dtw, dt_, wdec, ALU.mult)
            self.act(egend, gend, AF.Exp)
            if self.chk(23):
                return
            for c in range(2):
                cs = slice(c * 128, (c + 1) * 128)
                for grp in range(4):
                    pb = self.psbf()
                    for q in range(4):
                        self.tr(pb[:, q * 128:(q + 1) * 128], xsT[:, grp * 4 + q, cs], self.identb[:])
                    pbv = pb[:, 0:512].rearrange("p (q a d) -> p q a d", q=4, a=2)
                    dtv = dt_[:, c * 32 + grp * 8:c * 32 + grp * 8 + 8].rearrange("p (q a) -> p q a", a=2)
                    for hh in range(2):
                        self.tt("vector", Vzv[:, grp * 4:(grp + 1) * 4, hh, hh, :], pbv[:, :, hh, :],
                                dtv[:, :, hh].unsqueeze(2).to_broadcast([128, 4, 64]), ALU.mult)
                    self.tt("vector", vp[:, grp * 512:(grp + 1) * 512].rearrange("p (h d) -> p h d", h=8),
                            pb[:, 0:512].rearrange("p (h d) -> p h d", h=8),
                            dtw[:, c * 32 + grp * 8:c * 32 + grp * 8 + 8].unsqueeze(2).to_broadcast([128, 8, 64]), ALU.mult)
                pb = self.psbf()
                for g in range(4):
                    self.tr(pb[:, g * 128:(g + 1) * 128], BCT[:, g, cs], self.identb[:])
                self.cp("scalar", B_tok[:], pb[:, 0:512])
                pS = self.ps()
                for g in range(4):
                    self.mm(pS[:, g * 128:(g + 1) * 128], BCT[:, g, cs], BCT[:, 4 + g, cs])
                self.tt("vector", scM[:].rearrange("p (g i) -> p g i", g=4), pS[:].rearrange("p (g i) -> p g i", g=4),
                        cst[:, C_INCL:C_INCL + 128].unsqueeze(1).to_broadcast([128, 4, 128]), ALU.mult)
                if self.chk(24):
                    return
                py = None
                for quad in range(8):
                    g = quad // 2
                    h0 = quad * 4
                    k2 = quad % 2
                    self.tt("vector", gU[:].rearrange("p (h i) -> p h i", h=4), self.U.unsqueeze(1).to_broadcast([128, 4, 128]),
                            g_t[:, c * 32 + h0:c * 32 + h0 + 4].unsqueeze(2).to_broadcast([128, 4, 128]), ALU.mult)
                    pA = self.ps()
                    for u in range(4):
                        us = slice(u * 128, (u + 1) * 128)
                        self.mm(pA[:, us], self.onesf, gU[:, us])
                    tokw = self.zcol[:, 1:2]
                    self.S.add("scalar", lambda e, o=E_[:], i=pA[:]: e.activation(out=o, in_=i, func=AF.Exp), reads=[pA[:]], writes=[E_[:], tokw])
                    self.tt("vector", CgT[k2][:].rearrange("p (h i) -> p h i", h=4), E_[:].rearrange("p (h i) -> p h i", h=4),
                            BCT[:, 4 + g, cs].unsqueeze(1).to_broadcast([128, 4, 128]), ALU.mult)
                    for u in range(4):
                        us = slice(u * 128, (u + 1) * 128)
                        gc = gcum[:, c * 32 + h0 + u:c * 32 + h0 + u + 1]
                        self.S.add("vector", lambda e, o=dtm[:, us], i=pA[:, us], g_=gc: e.tensor_scalar(out=o, in0=i, scalar1=g_, scalar2=None, op0=ALU.subtract),
                                   reads=[pA[:, us], gc, tokw], writes=[dtm[:, us]])
                    self.ts("vector", dtm[:], dtm[:], 0.0, ALU.min)
                    self.act(dtm[:], dtm[:], AF.Exp)
                    self.tt("gpsimd", SD[k2][:].rearrange("p (h i) -> p h i", h=4), dtm[:].rearrange("p (h i) -> p h i", h=4),
                            scM[:, g * 128:(g + 1) * 128].unsqueeze(1).to_broadcast([128, 4, 128]), ALU.mult)
                    if k2 == 0:
                        py = self.ps()
                    for pr in range(2):
                        slot = k2 * 2 + pr
                        reg = py[:, slot * 128:(slot + 1) * 128]
                        for hh in range(2):
                            u = pr * 2 + hh
                            h = h0 + u
                            us = slice(u * 128, (u + 1) * 128)
                            self.mm(reg, Vz[:, h, :], SD[k2][:, us], start=(hh == 0), stop=False)
                            self.mm(reg, Sbz[:, h, :], CgT[k2][:, us], start=False, stop=(hh == 1))
                    if k2 == 1:
                        for slot in range(4):
                            pair = (quad - 1) * 2 + slot
                            dcol = PK[f"sD{li}"] + pair
                            self.stt("vector", y_sb[:, pair, :], xsT[:, pair, cs], pk[:, dcol:dcol + 1], py[:, slot * 128:(slot + 1) * 128],
                                     ALU.mult, ALU.add)
                if self.chk(25):
                    return
                self.tt("vector", y_sb[:], y_sb[:], szT[:, :, cs], ALU.mult)
                self.act(ysq[:], y_sb[:], AF.Square)
                pn = self.ps()
                for g in range(4):
                    for q in range(4):
                        self.mm(pn[:, g * 128:(g + 1) * 128], self.onesb[:], ysq[:, g * 4 + q, :], start=(q == 0), stop=(q == 3))
                self.act(rg[:], pn[:], AF.Sqrt, bias=EPS, scale=1.0 / 512)
                self.recip(rg[:], rg[:])
                self.tt("vector", ynT[:, :, cs].rearrange("p (g q) i -> p g q i", g=4), y_sb[:].rearrange("p (g q) i -> p g q i", g=4),
                        rg[:].rearrange("p (g i) -> p g i", g=4).unsqueeze(2).to_broadcast([128, 4, 4, 128]), ALU.mult)
                for g in range(4):
                    gs = slice(g * 512, (g + 1) * 512)
                    pU = self.ps()
                    self.mm(pU[:], B_tok[:, g * 128:(g + 1) * 128], vp[:, gs])
                    self.tt("vector", S_[:, gs].rearrange("p (h d) -> p h d", h=8), S_[:, gs].rearrange("p (h d) -> p h d", h=8),
                            egend[:, c * 32 + g * 8:c * 32 + g * 8 + 8].unsqueeze(2).to_broadcast([128, 8, 64]), ALU.mult)
                    self.tt("vector", S_[:, gs], S_[:, gs], pU[:], ALU.add)
                    sv = S_[:, gs].rearrange("p (q a d) -> p q a d", q=4, a=2)
                    for hh in range(2):
                        self.cp("scalar", Sbzv[:, g * 4:(g + 1) * 4, hh, hh, :], sv[:, :, hh, :])
            if self.chk(26):
                return
            def ld(dc):
                wb_ = wo[dc % 2][:].rearrange("p (k d) -> p k d", k=16)
                self.dma("sync", wb_, self.wosd[dc], f"wo{dc % 2}")
                self.dma("sync", xpc[dc % 3][:], dr["xs"][:, dc, t0:t0 + BLK], f"xpl{dc % 3}")
            ld(0)
            ld(1)
            for dc in range(8):
                wb = wo[dc % 2][:].rearrange("p (k d) -> p k d", k=16)
                xp = xpc[dc % 3]
                p = self.ps()
                for kc in range(16):
                    self.mm(p[:, :BLK], wb[:, kc, :], ynT[:, kc, :], start=(kc == 0), stop=(kc == 15))
                self.tt("vector", xp[:], p[:, :BLK], xp[:], ALU.add)
                if dc + 2 < 8:
                    ld(dc + 2)
                self.dma("sync", dr["xs"][:, dc, t0:t0 + BLK], xp[:], f"xps{dc % 3}")
            if blk == nblk - 1:
                for cb in range(6):
                    p = self.ps()
                    for kc in range(8):
                        self.mm(p[0:3, :], hT[:, kc, BLK - 3:BLK], wi[:, kc, 2048 + cb * 512:2048 + (cb + 1) * 512], start=(kc == 0), stop=(kc == 7))
                    c3 = (gU, E_, dtm)[cb % 3]
                    self.cp("vector", c3[0:3, :], p[0:3, :])
                    self.dma("sync", dr["p_ssm_conv"][li, :, cb * 512:(cb + 1) * 512], c3[0:3, :], "pst")
        for blk in range(nblk if not self.chk(0) else 1):
            body(blk)

    def ssd_sample(self, li, layer, bs, wi, wo, Sbufs):
        dr, cst, pk = self.dr, self.cst, self.pk
        sb = lambda n, s, d: self.sb(bs, n, s, d)
        xbs, prj = self.sample_common(bs, layer, wi, SSM_IN)
        id16f = cst[:, C_ID16:C_ID16 + 256].rearrange("p (a b) -> p a b", a=16)
        ident16 = self.ident[0:NS, 0:NS]
        cw = sb("cw", [NS, 4, 512], F32)
        cbuf = sb("cbuf", [NS, 3, 512], F32)
        cbv = sb("cbv", [NS, 512], F32)
        ctm = sb("ctm", [NS, 512], F32)
        t1 = sb("st1", [NS, 512], F32)
        xbc = sb("xbc", [NS, 3072], F32)
        vv = sb("vv", [NS, 2048], F32)
        y_ = sb("ys", [NS, 2048], F32)
        sm = sb("sms", [NS, 256], F32)
        egm = sb("egm", [NS, NS, 32], F32)
        egB = sb("egB", [128, 512], F32)
        CTs = sb("CTs", [128, 4, NS], F32)
        CTm = sb("CTm", [128, 4, NS, NS], F32)
        Bmb = [sb("Bmb0", [NS, 512], F32)] * 2
        for pc in range(6):
            c0 = pc * 512
            self.dma("sync", cw[:], dr["ssm_conv_w"][li, :, c0:c0 + 512].partition_broadcast(NS), "cwld")
            self.dma("sync", cbv[:], dr["ssm_conv_b"][li, c0:c0 + 512].partition_broadcast(NS), "cwld")
            self.dma("sync", cbuf[:], dr["state_ssm_conv"][li, :, :, c0:c0 + 512], "cbld")
            self.tt("vector", ctm[:], prj[:, 2048 + c0:2048 + c0 + 512], cw[:, 3, :], ALU.mult)
            self.tt("vector", ctm[:], ctm[:], cbv[:], ALU.add)
            for tp in range(3):
                self.tt("gpsimd", t1[:], cbuf[:, tp, :], cw[:, tp, :], ALU.mult)
                self.tt("vector", ctm[:], ctm[:], t1[:], ALU.add)
            self.act(xbc[:, c0:c0 + 512], ctm[:], AF.Silu)
            self.dma("sync", dr["s_ssm_conv"][li, :, 0:2, c0:c0 + 512], cbuf[:, 1:3, :], "cvst")
        self.dma("sync", dr["s_ssm_conv"][li, :, 2, :], prj[:, 2048:5120], "cvst")
        dtr = sm[:, 0:32]
        tmp = sm[:, 32:64]
        dt_ = sm[:, 64:96]
        g_ = sm[:, 96:128]
        eg = sm[:, 128:160]
        ss = sm[:, 160:164]
        self.tt("vector", dtr, prj[:, 5120:5152], pk[0:NS, PK[f"sdtb{li}"]:PK[f"sdtb{li}"] + 32], ALU.add)
        self.softplus16(dt_, dtr, tmp)
        self.tt("vector", g_, dt_, self.negA[0:NS, 8 + li * 32:8 + (li + 1) * 32], ALU.mult)
        self.act(eg, g_, AF.Exp)
        xs3 = xbc[:, 0:2048].rearrange("p (h d) -> p h d", h=32)
        self.tt("vector", vv[:].rearrange("p (h d) -> p h d", h=32), xs3, dt_.unsqueeze(2).to_broadcast([NS, 32, 64]), ALU.mult)
        p = self.ps()
        for g in range(4):
            self.tr(p[:, g * NS:(g + 1) * NS], xbc[:, 2560 + g * 128:2560 + (g + 1) * 128], ident16)
        self.cp("vector", CTs[:], p[:, 0:4 * NS].rearrange("p (g t) -> p g t", g=4))
        for g in range(4):
            self.tt("vector" if g % 2 else "gpsimd", CTm[:, g, :, :], CTs[:, g, :].unsqueeze(1).to_broadcast([128, NS, NS]), id16f, ALU.mult)
        self.tt("vector", egm[:], eg.unsqueeze(1).to_broadcast([NS, NS, 32]), ident16.unsqueeze(2).to_broadcast([NS, NS, 32]), ALU.mult)
        pe = self.ps()
        self.mm(pe[:, :], self.onesf[0:NS, :], egm[:].rearrange("p a h -> p (a h)"))
        self.cp("vector", egB[:], pe[:, :])
        psy = self.psf[0:4]
        k = 0
        for b in range(NS):
            Sb = Sbufs[b % 3]
            self.dma("sync", Sb.rearrange("p (h d) -> p h d", h=32), dr["state_ssm"][li, b].rearrange("h n d -> n h d"), f"sld{b % 3}")
            bm = Bmb[b % 2]
            self.ts("vector", bm[:], xbc[:, 2048:2560], ident16[:, b:b + 1], ALU.mult)
            self.tt("gpsimd", Sb.rearrange("p (h d) -> p h d", h=32), Sb.rearrange("p (h d) -> p h d", h=32),
                    egB[:, b * 32:(b + 1) * 32].unsqueeze(2).to_broadcast([128, 32, 64]), ALU.mult)
            for g in range(4):
                gs = slice(g * 512, (g + 1) * 512)
                pu = self.psf[4 + k % 2]
                k += 1
                self.mm(pu[:, :], bm[:, g * 128:(g + 1) * 128], vv[:, gs])
                self.tt("vector", Sb[:, gs], Sb[:, gs], pu[:, :], ALU.add)
                self.mm(psy[g][0:NS, :], CTm[:, g, b, :], Sb[:, gs], start=(b == 0), stop=(b == NS - 1))
            self.dma("sync", dr["s_ssm"][li, b].rearrange("h n d -> n h d"), Sb.rearrange("p (h d) -> p h d", h=32), f"sst{b % 3}")
        for g in range(4):
            self.cp("vector" if g % 2 else "scalar", y_[:, g * 512:(g + 1) * 512], psy[g][0:NS, :])
        y3 = y_[:].rearrange("p (h d) -> p h d", h=32)
        vv3 = vv[:].rearrange("p (h d) -> p h d", h=32)
        self.tt("gpsimd", vv3, xs3, pk[0:NS, PK[f"sDrep{li}"]:PK[f"sDrep{li}"] + 32].unsqueeze(2).to_broadcast([NS, 32, 64]), ALU.mult)
        self.tt("vector", y_[:], y_[:], vv[:], ALU.add)
        for q in range(4):
            self.act(vv[:, q * 512:(q + 1) * 512], prj[:, q * 512:(q + 1) * 512], AF.Silu)
        self.tt("vector", y_[:], y_[:], vv[:], ALU.mult)
        self.tt("gpsimd", vv[:], y_[:], y_[:], ALU.mult)
        self.red("vector", ss, vv[:].rearrange("p (g c) -> p g c", g=4))
        self.act(ss, ss, AF.Sqrt, bias=EPS, scale=1.0 / 512)
        self.recip(ss, ss)
        self.tt("vector", y_[:].rearrange("p (g c) -> p g c", g=4), y_[:].rearrange("p (g c) -> p g c", g=4),
                ss.unsqueeze(2).to_broadcast([NS, 4, 512]), ALU.mult)
        mixTs = egB[:].bitcast(BF16)[:, 0:256].rearrange("p (k t) -> p k t", k=16)
        for k0 in (0, 8):
            p = self.ps()
            for kk in range(8):
                self.tr(p[:, kk * NS:(kk + 1) * NS], y_[:, (k0 + kk) * 128:(k0 + kk + 1) * 128], ident16)
            self.cp("vector", mixTs[:, k0:k0 + 8, :], p[:, 0:8 * NS].rearrange("p (k t) -> p k t", k=8))
        po = self.ps()
        for dc in range(8):
            wb = wo[dc % 2][:].rearrange("p (k d) -> p k d", k=16)
            self.dma("sync", wb, self.wosd[dc], f"wo{dc % 2}")
            for kc in range(16):
                self.mm(po[:, dc * NS:(dc + 1) * NS], wb[:, kc, :], mixTs[:, kc, :], start=(kc == 0), stop=(kc == 15))
        self.tt("vector", xbs[:], po[:, 0:8 * NS].rearrange("p (k t) -> p k t", k=8), xbs[:], ALU.add)
        self.dma("sync", dr["xs"][:, :, SEQ:NTOK], xbs[:], "xst")

    def mlp_phase(self, layer, last):
        dr, cst, pk = self.dr, self.cst, self.pk
        with contextlib.ExitStack() as ph:
            sb = lambda n, s, d: self.sb(ph, n, s, d)
            xall = sb("xall", [128, 8, NTOK], F32)
            hall = sb("hall", [128, 8, NTOK], BF16)
            sq = sb("msq", [128, 8, 512], BF16)
            rstd = sb("mrstd", [128, 512], F32)
            w1 = [sb(f"w1_{i}", [128, 8, 512], BF16) for i in range(2)]
            w2 = [sb(f"w2_{i}", [128, 4, D], BF16) for i in range(2)]
            rl = [sb(f"rl{i}", [128, 512], BF16) for i in range(2)]
            aT = [sb(f"aT{i}", [128, 4, 512], BF16) for i in range(2)]
            for dc in range(8):
                self.dma("sync", xall[:, dc, :], dr["xs"][:, dc, :], "xall")
            tbs = [(i * 512, 512) for i in range(4)] + [(SEQ, NS)]

            def loadw(fb):
                b = fb % 2
                self.dma("gpsimd", w1[b][:], dr["mlp_w1"][layer, :, fb * 512:(fb + 1) * 512].rearrange("(kc p) f -> p kc f", p=128), f"w1_{b}")
                self.dma("gpsimd", w2[b][:], dr["mlp_w2"][layer, fb * 512:(fb + 1) * 512, :].rearrange("(fc p) d -> p fc d", p=128), f"w2_{b}")
            import os
            KM = os.environ.get("KMLP", "full")
            if KM != "load":
                loadw(0)
                for (t0, nt) in tbs:
                    self.norm_block(xall[:, :, t0:t0 + nt], hall[:, :, t0:t0 + nt], sq, rstd, nt, PK["nmlp"] + layer * 8)
            it = 0
            nfb = {"load": 0, "norm": 0, "fb1": 1, "fb1f": 1, "fb2": 2, "fb3": 3}.get(KM, 8)
            if KM in ("load", "norm", "fb1", "fb2", "fb3"):
                last = False
            for fb in range(nfb):
                if fb + 1 < nfb:
                    loadw(fb + 1)
                b = fb % 2
                for (t0, nt) in tbs:
                    a = aT[it % 2]
                    it += 1
                    for fc in range(4):
                        p = self.ps()
                        for kc in range(8):
                            self.mm(p[:, :nt], w1[b][:, kc, fc * 128:(fc + 1) * 128], hall[:, kc, t0:t0 + nt], start=(kc == 0), stop=(kc == 7))
                        r_ = rl[fc % 2]
                        self.act(r_[:, :nt], p[:, :nt], AF.Relu)
                        self.tt("gpsimd", a[:, fc, :nt], r_[:, :nt], r_[:, :nt], ALU.mult)
                    for dc in range(8):
                        p = self.ps()
                        for fc in range(4):
                            self.mm(p[:, :nt], w2[b][:, fc, dc * 128:(dc + 1) * 128], a[:, fc, :nt], start=(fc == 0), stop=(fc == 3))
                        self.tt("vector", xall[:, dc, t0:t0 + nt], p[:, :nt], xall[:, dc, t0:t0 + nt], ALU.add)
            if not last:
                for dc in range(8):
                    self.dma("sync", dr["xs"][:, dc, :], xall[:, dc, :], "xall_st")
            if last:
                if self.debug:
                    for dc in range(8):
                        self.dma("sync", dr["xs"][:, dc, :], xall[:, dc, :], "xall_st")
                yst = [sb(f"yst{i}", [128, D], F32) for i in range(2)]
                yT = [sb(f"yT{i}", [128, 8, 128], F32) for i in range(2)]
                k = 0
                for (t0, nt) in tbs:
                    self.norm_block_f32(xall[:, :, t0:t0 + nt], sq, rstd, nt, PK["nfin"], yT, yst, t0)

    def norm_block_f32(self, xb, sq, rstd, ntok, wcol, yT, yst, t0):
        dr = self.dr
        for dc in range(8):
            self.act(sq[:, dc, :ntok], xb[:, dc, :ntok], AF.Square)
        p = self.ps()
        for dc in range(8):
            self.mm(p[:, :ntok], self.onesb[:], sq[:, dc, :ntok], start=(dc == 0), stop=(dc == 7))
        self.act(rstd[:, :ntok], p[:, :ntok], AF.Sqrt, bias=EPS, scale=1.0 / D)
        self.recip(rstd[:, :ntok], rstd[:, :ntok])
        ntile = (ntok + 127) // 128
        for ti in range(ntile):
            n = min(128, ntok - ti * 128)
            k = (t0 // 128 + ti) % 2
            y_, ys = yT[k], yst[k]
            for dc in range(8):
                self.stt("vector" if dc % 2 else "gpsimd", y_[:, dc, :n], xb[:, dc, ti * 128:ti * 128 + n], self.pk[:, wcol + dc:wcol + dc + 1],
                         rstd[:, ti * 128:ti * 128 + n], ALU.mult, ALU.mult)
            for half in range(2):
                p = self.ps()
                for q in range(4):
                    dc = half * 4 + q
                    self.tr(p[0:n, q * 128:(q + 1) * 128], y_[:, dc, :n], self.ident)
                self.cp("vector" if half == 0 else "scalar", ys[0:n, half * 512:(half + 1) * 512], p[0:n, :])
            if t0 < SEQ:
                self.dma("sync", dr["y_prompt"][t0 + ti * 128:t0 + ti * 128 + n, :], ys[0:n, :], f"yout{k}")
            else:
                self.dma("sync", dr["y_sample"][0:n, :], ys[0:n, :], f"yout{k}")


_CACHE = {}


def _prep_inputs(inp):
    pk = make_pk(inp)
    cst = make_consts()
    rope = make_rope()
    shared = {k: np.ascontiguousarray(inp[k], dtype=np.float32) for k in
              ("w_in_hyb", "w_out_hyb", "w_in_ssm", "w_out_ssm", "mlp_w1", "mlp_w2", "gdn_conv_w", "ssm_conv_w", "ssm_conv_b")}
    maps = []
    for c in range(NCORES):
        m = dict(shared)
        m["pk"] = pk
        m["cst"] = cst
        m["rope"] = rope
        m["x_prompt"] = np.ascontiguousarray(inp["x_prompt"][c])
        m["x_sample"] = np.ascontiguousarray(inp["x_sample"][c * NS:(c + 1) * NS, 0])
        for k in ("state_gdn", "state_gdn_conv", "state_ret", "state_ssm", "state_ssm_conv"):
            m[k] = np.ascontiguousarray(inp[k][:, c * NS:(c + 1) * NS])
        maps.append(m)
    return maps


def kernel(**inp):
    if "nc" not in _CACHE:
        _CACHE["nc"] = Builder().build()
    nc = _CACHE["nc"]
    maps = _prep_inputs(inp)
    res = run_bass_kernel_spmd(nc, maps, core_ids=list(range(NCORES)))
    R = res.results
    y_prompt = np.stack([R[c]["y_prompt"] for c in range(NCORES)], 0)
    y_sample = np.concatenate([R[c]["y_sample"] for c in range(NCORES)], 0)[:, None, :]

    def pcat(name):
        return np.stack([R[c][name] for c in range(NCORES)], 1)

    def scat(name):
        return np.concatenate([R[c][name] for c in range(NCORES)], 1)
    return (y_prompt, y_sample, pcat("p_gdn"), pcat("p_gdn_conv"), pcat("p_ret"), pcat("p_ssm"), pcat("p_ssm_conv"),
            scat("s_gdn"), scat("s_gdn_conv"), scat("s_ret"), scat("s_ssm"), scat("s_ssm_conv"))
```

```python
import contextlib
import math
import numpy as np
import concourse.bass as bass
import concourse.mybir as mybir
from concourse.bass_utils import run_bass_kernel_spmd

F32 = mybir.dt.float32
BF16 = mybir.dt.bfloat16
ALU = mybir.AluOpType
AF = mybir.ActivationFunctionType
AX = mybir.AxisListType

ENGS = ("sync", "scalar", "vector", "gpsimd", "tensor")
EPOCH = 30000
NCORES = 8
SEQ = 2048
NS = 16
NTOK = SEQ + NS
D = 1024
EPS = 1e-6
HYB_IN = 3592
SSM_IN = 5152
BLK = 256


def _esize(dt):
    return 2 if dt == BF16 else 4


def _box(ap):
    dims = ap.ap
    off = ap.offset
    sp = str(ap.space)
    es = _esize(ap.dtype)
    if sp in ("SB", "PSUM"):
        ps = dims[0][0]
        if ps == 0:
            ps = 1 << 30
        p0 = off // ps
        p1 = p0 + dims[0][1]
        f0 = off % ps
        ext = 1
        for st, cnt in dims[1:]:
            ext += (cnt - 1) * abs(st)
        if sp == "PSUM":
            return (sp + ap.name, (p0 // 32) * 32, ((p1 + 31) // 32) * 32, 0, 2048)
        return (sp + ap.name, p0, p1, f0 * es, (f0 + ext) * es)
    ext = 1
    for st, cnt in dims:
        ext += (cnt - 1) * abs(st)
    return (sp + ap.name, 0, 1, off * es, (off + ext) * es)


class Sched:
    def __init__(self, nc):
        self.nc = nc
        self.ops = []
        self.hist = {}
        self.chans = {}
        self.last_eng = {}
        self.pending_barrier = None

    def barrier(self):
        self.pending_barrier = (dict(self.last_eng), dict(self.last_chan_op()))
        self.barrier_seen = set()
        self.hist = {}

    def last_chan_op(self):
        d = {}
        for i, o in enumerate(self.ops):
            if o["chan"] is not None:
                d[o["chan"]] = i
        return d

    def add(self, eng, fn, reads=(), writes=(), chan=None):
        idx = len(self.ops)
        deps = {}
        rb = [_box(a) for a in reads]
        wb = [_box(a) for a in writes]
        for b in rb:
            isps = b[0].startswith("PSUM")
            for r in self.hist.get(b[0], ()):
                if r[0] < b[2] and b[1] < r[1] and r[2] < b[4] and b[3] < r[3]:
                    if r[4]:
                        deps[r[5]] = True
                    elif isps and r[5] < idx and self.ops[r[5]]["eng"] != eng:
                        deps[r[5]] = True
        for b in wb:
            for r in self.hist.get(b[0], ()):
                if r[0] < b[2] and b[1] < r[1] and r[2] < b[4] and b[3] < r[3]:
                    deps.setdefault(r[5], False)
        isdma = chan is not None
        if self.pending_barrier is not None and eng not in self.barrier_seen:
            self.barrier_seen.add(eng)
            le, lc = self.pending_barrier
            for e2, i2 in le.items():
                deps[i2] = True
            for c2, i2 in lc.items():
                deps[i2] = True
        for b in wb:
            lst = self.hist.setdefault(b[0], [])
            lst[:] = [r for r in lst if not (b[1] <= r[0] and r[1] <= b[2] and b[3] <= r[2] and r[3] <= b[4])]
            lst.append((b[1], b[2], b[3], b[4], True, idx))
        for b in rb:
            lst = self.hist.setdefault(b[0], [])
            if not isdma:
                lst[:] = [r for r in lst if not ((not r[4]) and r[5] < idx and self.ops[r[5]]["eng"] == eng and self.ops[r[5]]["chan"] is None
                                                 and b[1] <= r[0] and r[1] <= b[2] and b[3] <= r[2] and r[3] <= b[4])]
            lst.append((b[1], b[2], b[3], b[4], False, idx))
        if isdma:
            self.chans[chan] = self.chans.get(chan, 0) + 1
        else:
            self.last_eng[eng] = idx
        self.ops.append(dict(eng=eng, fn=fn, deps=deps, chan=chan))
        return idx

    def emit(self):
        nc = self.nc
        ops = self.ops
        need = [False] * len(ops)
        for c, o in enumerate(ops):
            kept = {}
            for p, raw in o["deps"].items():
                po = ops[p]
                if po["chan"] is None and po["eng"] == o["eng"] and o["chan"] is None:
                    if o["eng"] == "tensor":
                        continue
                kept[p] = raw
                need[p] = True
            o["deps"] = kept
        sigidx = {}
        cnt = {e: 0 for e in ENGS}
        for i, o in enumerate(ops):
            if o["chan"] is None and need[i]:
                cnt[o["eng"]] += 1
                sigidx[i] = cnt[o["eng"]]
        nep = {e: (cnt[e] + EPOCH - 1) // EPOCH for e in ENGS}
        self.cnt = cnt
        with contextlib.ExitStack() as st:
            esem = {e: [st.enter_context(nc.semaphore(f"s_{e}_{j}")) for j in range(max(1, nep[e]))] for e in ENGS}
            csem = {c: st.enter_context(nc.semaphore(f"c_{c}")) for c in self.chans}
            waited = {e: {} for e in ENGS}
            chan_issued = {c: 0 for c in self.chans}
            chan_tgt = {c: 0 for c in self.chans}
            plan = {e: [] for e in ENGS}
            for i, o in enumerate(ops):
                E = o["eng"]
                w = {}
                for p in o["deps"]:
                    po = ops[p]
                    if po["chan"] is None:
                        s = sigidx[p]
                        key = ("e", po["eng"], (s - 1) // EPOCH)
                        val = (s - 1) % EPOCH + 1
                    else:
                        ch = po["chan"]
                        tgt = chan_issued[ch]
                        chan_tgt[ch] = max(chan_tgt[ch], tgt)
                        key = ("c", ch)
                        val = 16 * tgt
                    if val > w.get(key, 0):
                        w[key] = val
                if o["chan"] is not None:
                    ch = o["chan"]
                    if chan_tgt[ch] > 0:
                        key = ("c", ch)
                        w[key] = max(w.get(key, 0), 16 * chan_tgt[ch])
                    chan_issued[ch] += 1
                waits = []
                for key, val in w.items():
                    if val > waited[E].get(key, 0):
                        waited[E][key] = val
                        sem = csem[key[1]] if key[0] == "c" else esem[key[1]][key[2]]
                        waits.append((sem, val))
                sig = None
                if o["chan"] is not None:
                    sig = (csem[o["chan"]], 16)
                elif i in sigidx:
                    s = sigidx[i]
                    sig = (esem[E][(s - 1) // EPOCH], 1)
                plan[E].append((o["fn"], waits, sig))
            fin = [(csem[ch], 16 * n) for ch, n in chan_issued.items() if n]
            self.n_instr = {e: len(plan[e]) for e in ENGS}
            with nc.Block() as block:
                def mk(E):
                    def body(eng):
                        for fn, waits, sig in plan[E]:
                            for sem, val in waits:
                                eng.wait_ge(sem, val)
                            ins = fn(eng)
                            if sig is not None:
                                ins.then_inc(sig[0], sig[1])
                        if E == "sync":
                            for sem, val in fin:
                                eng.wait_ge(sem, val)
                    return body
                block.sync(mk("sync"))
                block.scalar(mk("scalar"))
                block.vector(mk("vector"))
                block.gpsimd(mk("gpsimd"))
                block.tensor(mk("tensor"))


C_IDENT, C_U, C_NEG, C_INCL, C_STRICT, C_ONES, C_NONES, C_PT = [i * 128 for i in range(8)]
C_DMT = 8 * 128
C_QDEC = C_DMT + 512
C_KDEC = C_QDEC + 512
C_ID16 = C_KDEC + 256
C_ROPES = C_ID16 + 256
C_G128 = C_ROPES + 64
NCST = C_G128 + 4


def make_consts():
    c = np.zeros((128, NCST), np.float64)
    j = np.arange(128)[:, None]
    i = np.arange(128)[None, :]
    c[:, C_IDENT:C_IDENT + 128] = (j == i)
    c[:, C_U:C_U + 128] = (j <= i)
    c[:, C_NEG:C_NEG + 128] = np.where(i >= j, 0.0, -1e30)
    c[:, C_INCL:C_INCL + 128] = (i >= j)
    c[:, C_STRICT:C_STRICT + 128] = (i > j)
    c[:, C_ONES:C_ONES + 128] = 1.0
    c[:, C_NONES:C_NONES + 128] = -1.0
    PT = np.zeros((128, 128))
    for m in range(128):
        if m % 64 < 32:
            PT[m + 32, m] = -1.0
        else:
            PT[m - 32, m] = 1.0
    c[:, C_PT:C_PT + 128] = PT
    gam = 1.0 - 2.0 ** (-5.0 - np.arange(4))
    for h in range(4):
        c[:, C_DMT + h * 128:C_DMT + (h + 1) * 128] = np.where(i >= j, gam[h] ** np.maximum(i - j, 0), 0.0)
        c[:, C_KDEC + h * 64:C_KDEC + (h + 1) * 64] = (gam[h] ** (127 - j))
    for h in range(4):
        c[:, C_QDEC + h * 128:C_QDEC + (h + 1) * 128] = gam[h] ** (i + 1)
    c[:, C_ID16:C_ID16 + 256] = np.eye(16).reshape(1, 256)
    half = 32
    inv = (10000.0 ** (-np.arange(half, dtype=np.float32) / half)).astype(np.float32)
    ang = (np.float32(16384.0) * inv).astype(np.float32).astype(np.float64)
    c[:, C_ROPES:C_ROPES + 32] = np.cos(ang)[None, :]
    c[:, C_ROPES + 32:C_ROPES + 64] = np.sin(ang)[None, :]
    c[:, C_G128:C_G128 + 4] = gam[None, :] ** 128
    return c.astype(np.float32)


GAM = [1.0 - 2.0 ** (-5.0 - h) for h in range(4)]


def make_rope():
    half = 32
    inv = (10000.0 ** (-np.arange(half, dtype=np.float32) / half)).astype(np.float32)
    pos = np.concatenate([np.arange(SEQ), np.full(NS, 16384)]).astype(np.float32)
    ang = (pos[None, :] * inv[:, None]).astype(np.float32).astype(np.float64)
    r = np.zeros((2, 128, NTOK), np.float32)
    for p in range(128):
        r[0, p] = np.cos(ang[p % 32])
        r[1, p] = np.sin(ang[p % 32])
    return r


PK = {}


def _pk_layout():
    off = 0

    def put(name, n):
        nonlocal off
        PK[name] = off
        off += n
    put("nmix", 32)
    put("nmlp", 32)
    put("nfin", 8)
    for i in range(2):
        put(f"gcw{i}", 48)
        put(f"gdtb{i}", 4)
        put(f"galog{i}", 4)
        put(f"gnw{i}", 1)
        put(f"rnw{i}", 4)
        put(f"scw{i}", 96)
        put(f"scb{i}", 24)
        put(f"sdtb{i}", 32)
        put(f"salog{i}", 32)
        put(f"sD{i}", 16)
        put(f"sDrep{i}", 32)
        put(f"snw{i}", 16)
    return off


NPK = _pk_layout()


def make_pk(inp):
    pk = np.zeros((128, NPK), np.float32)

    def fm(v, nch):
        return np.ascontiguousarray(v.reshape(nch, 128).T)
    for l in range(4):
        pk[:, PK["nmix"] + l * 8:PK["nmix"] + (l + 1) * 8] = fm(inp["norm_mix"][l], 8)
        pk[:, PK["nmlp"] + l * 8:PK["nmlp"] + (l + 1) * 8] = fm(inp["norm_mlp"][l], 8)
    pk[:, PK["nfin"]:PK["nfin"] + 8] = fm(inp["norm_final"], 8)
    for i in range(2):
        cw = inp["gdn_conv_w"][i]
        pk[:, PK[f"gcw{i}"]:PK[f"gcw{i}"] + 48] = cw.reshape(4, 12, 128).transpose(2, 1, 0).reshape(128, 48)
        pk[:, PK[f"gdtb{i}"]:PK[f"gdtb{i}"] + 4] = inp["gdn_dt_bias"][i][None, :]
        pk[:, PK[f"galog{i}"]:PK[f"galog{i}"] + 4] = inp["gdn_a_log"][i][None, :]
        pk[:, PK[f"gnw{i}"]] = inp["gdn_norm_w"][i]
        pk[:, PK[f"rnw{i}"]:PK[f"rnw{i}"] + 4] = fm(inp["ret_norm_w"][i], 4)
        sw = inp["ssm_conv_w"][i]
        pk[:, PK[f"scw{i}"]:PK[f"scw{i}"] + 96] = sw.reshape(4, 24, 128).transpose(2, 1, 0).reshape(128, 96)
        pk[:, PK[f"scb{i}"]:PK[f"scb{i}"] + 24] = fm(inp["ssm_conv_b"][i], 24)
        pk[:, PK[f"sdtb{i}"]:PK[f"sdtb{i}"] + 32] = inp["ssm_dt_bias"][i][None, :]
        pk[:, PK[f"salog{i}"]:PK[f"salog{i}"] + 32] = inp["ssm_a_log"][i][None, :]
        pk[:, PK[f"sD{i}"]:PK[f"sD{i}"] + 16] = np.repeat(inp["ssm_d"][i].reshape(16, 2), 64, axis=1).T
        pk[:, PK[f"sDrep{i}"]:PK[f"sDrep{i}"] + 32] = inp["ssm_d"][i][None, :]
        pk[:, PK[f"snw{i}"]:PK[f"snw{i}"] + 16] = fm(inp["ssm_norm_w"][i], 16)
    return pk


class StopBuild(Exception):
    pass


class Builder:
    def chk(self, k):
        import os
        return int(os.environ.get("KH", "99")) == k

    def __init__(self, depth=4, do_sample=True, debug=False):
        self.depth = depth
        self.do_sample = do_sample
        self.debug = debug
        nc = bass.Bass("TRN2", target_bir_lowering=False)
        self.nc = nc
        self.S = Sched(nc)
        self._psi = 0
        self._u = 0

    def mm(self, out, lhsT, rhs, start=True, stop=True):
        self.S.add("tensor", lambda e: e.matmul(out, lhsT=lhsT, rhs=rhs, start=start, stop=stop),
                   reads=[lhsT, rhs], writes=[out])

    def tr(self, out, in_, ident):
        self.S.add("tensor", lambda e: e.transpose(out=out, in_=in_, identity=ident), reads=[in_, ident], writes=[out])

    def act(self, out, in_, func, bias=None, scale=None):
        if func == AF.Sqrt:
            self.act(out, in_, AF.Ln, bias=bias, scale=scale)
            self.act(out, out, AF.Exp, scale=-0.5)
            return
        rd = [in_]
        kw = {}
        if bias is not None:
            kw["bias"] = bias
            if not isinstance(bias, (int, float)):
                rd.append(bias)
        if scale is not None:
            kw["scale"] = scale
            if not isinstance(scale, (int, float)):
                rd.append(scale)
        self.S.add("scalar", lambda e: e.activation(out=out, in_=in_, func=func, **kw), reads=rd, writes=[out])

    def tt(self, eng, out, in0, in1, op):
        self.S.add(eng, lambda e: e.tensor_tensor(out=out, in0=in0, in1=in1, op=op), reads=[in0, in1], writes=[out])

    def ts(self, eng, out, in0, s1, op0, s2=None, op1=None):
        rd = [in0]
        if not isinstance(s1, (int, float)):
            rd.append(s1)
        if s2 is not None and not isinstance(s2, (int, float)):
            rd.append(s2)
        if op1 is None:
            self.S.add(eng, lambda e: e.tensor_scalar(out=out, in0=in0, scalar1=s1, scalar2=None, op0=op0), reads=rd, writes=[out])
        else:
            self.S.add(eng, lambda e: e.tensor_scalar(out=out, in0=in0, scalar1=s1, scalar2=s2, op0=op0, op1=op1), reads=rd, writes=[out])

    def stt(self, eng, out, in0, scalar, in1, op0, op1):
        rd = [in0, in1]
        if not isinstance(scalar, (int, float)):
            rd.append(scalar)
        self.S.add("vector", lambda e: e.scalar_tensor_tensor(out=out, in0=in0, scalar=scalar, in1=in1, op0=op0, op1=op1),
                   reads=rd, writes=[out])

    def cp(self, eng, out, in_):
        if eng == "scalar":
            self.S.add("scalar", lambda e: e.copy(out=out, in_=in_), reads=[in_], writes=[out])
        else:
            self.S.add(eng, lambda e: e.tensor_copy(out=out, in_=in_), reads=[in_], writes=[out])

    def recip(self, out, in_):
        return

    def memset(self, eng, out, val):
        self.S.add(eng, lambda e: e.memset(out, val), writes=[out])

    def dma(self, eng, out, in_, chan):
        self.S.add(eng, lambda e: e.dma_start(out=out, in_=in_), reads=[in_], writes=[out], chan=chan)

    def ps(self):
        self._psi = (self._psi + 1) % len(self.psf)
        return self.psf[self._psi]

    def psbf(self):
        self._u = (self._u + 1) % len(self.psb)
        return self.psb[self._u]

    def ve(self):
        self._u2 = getattr(self, "_u2", 0) + 1
        return "vector" if self._u2 % 2 else "gpsimd"

    def sb(self, st, name, shape, dt):
        self._nid = getattr(self, "_nid", 0) + 1
        return st.enter_context(self.nc.sbuf_tensor(f"s{self._nid}_{name}", shape, dt))

    def build(self):
        nc = self.nc
        dr = {}

        def din(name, shape):
            dr[name] = nc.dram_tensor(name, shape, F32, kind="ExternalInput").ap()

        def dout(name, shape):
            dr[name] = nc.dram_tensor(name, shape, F32, kind="ExternalOutput").ap()
        din("x_prompt", [SEQ, D])
        din("x_sample", [NS, D])
        din("state_gdn", [2, NS, 4, 128, 128])
        din("state_gdn_conv", [2, NS, 3, 1536])
        din("state_ret", [2, NS, 4, 64, 128])
        din("state_ssm", [2, NS, 32, 128, 64])
        din("state_ssm_conv", [2, NS, 3, 3072])
        din("w_in_hyb", [2, D, HYB_IN])
        din("w_out_hyb", [2, D, D])
        din("w_in_ssm", [2, D, SSM_IN])
        din("w_out_ssm", [2, 2048, D])
        din("mlp_w1", [4, D, 4096])
        din("mlp_w2", [4, 4096, D])
        din("gdn_conv_w", [2, 4, 1536])
        din("ssm_conv_w", [2, 4, 3072])
        din("ssm_conv_b", [2, 3072])
        din("pk", [128, NPK])
        din("cst", [128, NCST])
        din("rope", [2, 128, NTOK])
        dout("y_prompt", [SEQ, D])
        dout("y_sample", [NS, D])
        dout("p_gdn", [2, 4, 128, 128])
        dout("p_gdn_conv", [2, 3, 1536])
        dout("p_ret", [2, 4, 64, 128])
        dout("p_ssm", [2, 32, 128, 64])
        dout("p_ssm_conv", [2, 3, 3072])
        dout("s_gdn", [2, NS, 4, 128, 128])
        dout("s_gdn_conv", [2, NS, 3, 1536])
        dout("s_ret", [2, NS, 4, 64, 128])
        dout("s_ssm", [2, NS, 32, 128, 64])
        dout("s_ssm_conv", [2, NS, 3, 3072])
        if self.debug:
            dout("xs", [128, 8, NTOK])
        else:
            dr["xs"] = nc.dram_tensor("xs", [128, 8, NTOK], F32, kind="Internal").ap()
        self.dr = dr
        with contextlib.ExitStack() as top:
            self.psf = [top.enter_context(nc.psum_tensor(f"psf{i}", [128, 512], F32)) for i in range(6)]
            self.psb = [top.enter_context(nc.psum_tensor(f"psb{i}", [128, 1024], BF16)) for i in range(2)]
            cst = self.sb(top, "cst", [128, NCST], F32)
            pk = self.sb(top, "pk", [128, NPK], F32)
            self.cst, self.pk = cst, pk
            self.dma("sync", cst[:], dr["cst"], "cst")
            self.dma("sync", pk[:], dr["pk"], "cst")
            self.identb = self.sb(top, "identb", [128, 128], BF16)
            self.onesb = self.sb(top, "onesb", [128, 128], BF16)
            self.PTb = self.sb(top, "PTb", [128, 128], BF16)
            self.cp("vector", self.identb[:], cst[:, C_IDENT:C_IDENT + 128])
            self.cp("vector", self.onesb[:], cst[:, C_ONES:C_ONES + 128])
            self.cp("vector", self.PTb[:], cst[:, C_PT:C_PT + 128])
            self.ident = cst[:, C_IDENT:C_IDENT + 128]
            self.onesf = cst[:, C_ONES:C_ONES + 128]
            self.nonesf = cst[:, C_NONES:C_NONES + 128]
            self.U = cst[:, C_U:C_U + 128]
            self.zcol = self.sb(top, "zcol", [128, 4], F32)
            self.memset("gpsimd", self.zcol[:], 0.0)
            self.negA = self.sb(top, "negA", [128, 72], F32)
            for i in range(2):
                self.act(self.negA[:, i * 4:(i + 1) * 4], pk[:, PK[f"galog{i}"]:PK[f"galog{i}"] + 4], AF.Exp)
                self.act(self.negA[:, 8 + i * 32:8 + (i + 1) * 32], pk[:, PK[f"salog{i}"]:PK[f"salog{i}"] + 32], AF.Exp)
            self.ts("vector", self.negA[:], self.negA[:], -1.0, ALU.mult)

            self.prologue()
            for layer in range(self.depth):
                self.S.barrier()
                import os
                if "mix" in os.environ.get("KSKIP", ""):
                    pass
                elif layer % 2 == 0:
                    self.hybrid_phase(layer // 2, layer)
                else:
                    self.ssd_phase(layer // 2, layer)
                self.S.barrier()
                if "mlp" not in os.environ.get("KSKIP", ""):
                    self.mlp_phase(layer, last=(layer == self.depth - 1))
            self.S.emit()
        return nc

    def prologue(self):
        dr = self.dr
        with contextlib.ExitStack() as st:
            xin = [self.sb(st, f"xin{i}", [128, D], F32) for i in range(2)]
            stg = [self.sb(st, f"xstg{i}", [128, 8, BLK], F32) for i in range(2)]
            for blk in range(SEQ // BLK):
                sg = stg[blk % 2]
                for c in range(2):
                    t = blk * 2 + c
                    xi = xin[t % 2]
                    self.dma("sync", xi[:], dr["x_prompt"][t * 128:(t + 1) * 128, :], f"xin{t % 2}")
                    for half in range(2):
                        p = self.ps()
                        for q in range(4):
                            dc = half * 4 + q
                            self.tr(p[:, q * 128:(q + 1) * 128], xi[:, dc * 128:(dc + 1) * 128], self.ident)
                        src = p[:].rearrange("p (q t) -> p q t", q=4)
                        self.cp("vector" if half == 0 else "scalar", sg[:, half * 4:half * 4 + 4, c * 128:(c + 1) * 128], src)
                self.dma("sync", dr["xs"][:, :, blk * BLK:(blk + 1) * BLK], sg[:], f"xst{blk % 2}")
            xi = xin[0]
            self.dma("sync", xi[0:NS, :], dr["x_sample"], "xin0")
            p = self.ps()
            for dc in range(8):
                self.tr(p[:, dc * NS:(dc + 1) * NS], xi[0:NS, dc * 128:(dc + 1) * 128], self.ident[0:NS, 0:NS])
            sg = stg[0]
            self.cp("vector", sg[:, :, 0:NS], p[:, 0:8 * NS].rearrange("p (q t) -> p q t", q=8))
            self.dma("sync", dr["xs"][:, :, SEQ:NTOK], sg[:, :, 0:NS], "xst0")

    def norm_block(self, xb, hT, sq, rstd, ntok, wcol):
        for dc in range(8):
            self.act(sq[:, dc, :ntok], xb[:, dc, :ntok], AF.Square)
        p = self.ps()
        for dc in range(8):
            self.mm(p[:, :ntok], self.onesb[:], sq[:, dc, :ntok], start=(dc == 0), stop=(dc == 7))
        self.act(rstd[:, :ntok], p[:, :ntok], AF.Sqrt, bias=EPS, scale=1.0 / D)
        self.recip(rstd[:, :ntok], rstd[:, :ntok])
        for dc in range(8):
            self.stt("vector" if dc % 2 else "gpsimd", hT[:, dc, :ntok], xb[:, dc, :ntok], self.pk[:, wcol + dc:wcol + dc + 1],
                     rstd[:, :ntok], ALU.mult, ALU.mult)

    def proj_fm(self, p, wi, col0, ncols, hT, ntok, pcol0=0):
        for kc in range(8):
            self.mm(p[:ncols, pcol0:pcol0 + ntok], wi[:, kc, col0:col0 + ncols], hT[:, kc, :ntok], start=(kc == 0), stop=(kc == 7))

    def hybrid_phase(self, li, layer):
        dr, cst, pk = self.dr, self.cst, self.pk
        with contextlib.ExitStack() as ph:
            sb = lambda n, s, d: self.sb(ph, n, s, d)
            wi = sb("wi", [128, 8, HYB_IN], BF16)
            wo = sb("wo", [128, 8, D], BF16)
            for (c0, c1, chn) in ((0, 1536, "wi0"), (1536, 2568, "wi1"), (2568, HYB_IN, "wi2")):
                for kc in range(8):
                    self.dma("gpsimd", wi[:, kc, c0:c1], dr["w_in_hyb"][li, kc * 128:(kc + 1) * 128, c0:c1], chn)
            for kc in range(8):
                self.dma("gpsimd", wo[:, kc, :], dr["w_out_hyb"][li, kc * 128:(kc + 1) * 128, :], "wo")
            for kc in range(8):
                col = PK[f"gnw{li}"] if kc < 4 else PK[f"rnw{li}"] + kc - 4
                self.ts("vector", wo[:, kc, :], wo[:, kc, :], pk[:, col:col + 1], ALU.mult)
            Sg = sb("Sg", [128, 4, 128], F32)
            Sgb = sb("Sgb", [128, 4, 128], BF16)
            Sr = sb("Sr", [64, 4, 128], F32)
            Srb = sb("Srb", [64, 4, 128], BF16)
            hist = sb("hist", [128, 12, 3], F32)
            for t_ in (Sg, Sgb, Sr, Srb, hist):
                self.memset("gpsimd", t_[:], 0.0)
            with contextlib.ExitStack() as bs:
                self.hybrid_prompt(li, layer, bs, wi, wo, Sg, Sgb, Sr, Srb, hist)
            self.dma("sync", dr["p_gdn"][li].rearrange("h k v -> k h v"), Sg[:], "pst")
            self.dma("sync", dr["p_ret"][li].rearrange("h k v -> k h v"), Sr[:], "pst")
            if self.do_sample:
                self.S.barrier()
                with contextlib.ExitStack() as bs:
                    self.hybrid_sample(li, layer, bs, wi, wo)

    def hybrid_prompt(self, li, layer, bs, wi, wo, Sg, Sgb, Sr, Srb, hist):
        dr, cst, pk = self.dr, self.cst, self.pk
        sb = lambda n, s, d: self.sb(bs, n, s, d)
        xb = sb("xb", [128, 8, BLK], F32)
        hT = sb("hT", [128, 8, BLK], BF16)
        sq = sb("sq", [128, 8, BLK], BF16)
        rstd = sb("rstd", [128, BLK], F32)
        rope = sb("rope", [128, 2, BLK], F32)
        ctmp = [sb(f"ctmp{i}", [128, BLK + 3], F32) for i in range(3)]
        cacc = [sb(f"cacc{i}", [128, BLK], F32) for i in range(3)]
        qkf = sb("qkf", [128, 8, BLK], BF16)
        qkT = sb("qkT", [128, 8, BLK], BF16)
        vT = sb("vT", [128, 4, BLK], BF16)
        szT = sb("szT", [128, 4, BLK], BF16)
        sgT = sb("sgT", [128, 4, BLK], BF16)
        rawb = sb("rawb", [64, 8, BLK], BF16)
        rot = sb("rot", [64, 8, BLK], BF16)
        rt1 = [sb(f"rt1{i}", [128, BLK], F32) for i in range(2)]
        rt2 = [sb(f"rt2{i}", [128, BLK], F32) for i in range(2)]
        smallf = sb("smallf", [128, 160], F32)
        v_tok = sb("v_tok", [128, 2, 512], BF16)
        kg_tok = sb("kg_tok", [128, 2, 512], BF16)
        kd_tok = sb("kd_tok", [128, 2, 512], BF16)
        vb_tok = sb("vb_tok", [128, 2, 512], BF16)
        kdr_tok = sb("kdr_tok", [128, 2, 256], BF16)
        decT = sb("decT", [128, 2, 512], F32)
        decS = sb("decS", [128, 512], F32)
        EgB = sb("EgB", [128, 512], F32)
        qgT = sb("qgT", [128, 2, 512], BF16)
        QK = sb("QK", [128, 2, 512], BF16)
        SR = sb("SR", [128, 512], BF16)
        qgr = sb("qgr", [64, 4, 128], BF16)
        NX = [[sb(f"NX{c}{k}", [128, 512], F32) for k in range(2)] for c in range(2)]
        NXT = [[sb(f"NXT{c}{k}", [128, 512], F32) for k in range(2)] for c in range(2)]
        NP = [[sb(f"NP{c}{k}", [128, 512], F32) for k in range(2)] for c in range(2)]
        TTb = sb("TTb", [128, 2, 512], BF16)
        nw0T = sb("nw0T", [128, 2, 512], BF16)
        delta = sb("delta", [128, 512], BF16)
        oT = sb("oT", [128, 512], F32)
        ob16 = sq[:, 2:4, :].rearrange("p a b -> p (a b)")
        osq = sq[:, 0:2, :].rearrange("p a b -> p (a b)")
        orr = sb("orr", [128, 512], F32)
        otmp = sb("otmp", [128, 512], F32)
        gU = orr[:].rearrange("p (h i) -> p h i", h=4)
        dtmp = otmp[:].rearrange("p (h i) -> p h i", h=4)
        mixT = sb("mixT", [128, 8, BLK], BF16)
        beta = smallf[:, 0:8]
        negbeta = smallf[:, 8:16]
        xg = smallf[:, 16:24]
        ax = smallf[:, 24:32]
        g_t = smallf[:, 32:40]
        gcum = smallf[:, 40:48]
        gend = smallf[:, 48:56]
        eg = smallf[:, 56:64]
        wdec = smallf[:, 64:72]
        egend = smallf[:, 72:80]
        nblk = SEQ // BLK
        dtb = pk[:, PK[f"gdtb{li}"]:PK[f"gdtb{li}"] + 4]
        nA = self.negA[:, li * 4:(li + 1) * 4]
        def body(blk):
            t0 = blk * BLK
            self.dma("sync", xb[:], dr["xs"][:, :, t0:t0 + BLK], "xld")
            self.dma("sync", rope[:, 0, :], dr["rope"][0, :, t0:t0 + BLK], "rope")
            self.dma("sync", rope[:, 1, :], dr["rope"][1, :, t0:t0 + BLK], "rope")
            self.norm_block(xb, hT, sq, rstd, BLK, PK["nmix"] + layer * 8)
            for g3 in range(4):
                chs = [g3 * 3 + u for u in range(3)]
                pp = {}
                for ch in chs:
                    pp[ch] = self.ps()
                    self.proj_fm(pp[ch], wi, ch * 128, 128, hT, BLK)
                for ch in chs:
                    self.cp("scalar", ctmp[ch % 3][:, 3:3 + BLK], pp[ch][:, :BLK])
                    self.cp("gpsimd", ctmp[ch % 3][:, 0:3], hist[:, ch, :])
                for ch in chs:
                    wc = PK[f"gcw{li}"] + ch * 4
                    self.ts("vector", cacc[ch % 3][:], ctmp[ch % 3][:, 0:BLK], pk[:, wc:wc + 1], ALU.mult)
                for tp in range(1, 4):
                    for ch in chs:
                        wc = PK[f"gcw{li}"] + ch * 4
                        self.stt("vector", cacc[ch % 3][:], ctmp[ch % 3][:, tp:tp + BLK], pk[:, wc + tp:wc + tp + 1], cacc[ch % 3][:], ALU.mult, ALU.add)
                for ch in chs:
                    self.cp("gpsimd", hist[:, ch, :], ctmp[ch % 3][:, BLK:BLK + 3])
                    if ch < 8:
                        self.act(qkf[:, ch, :], cacc[ch % 3][:], AF.Silu)
                    else:
                        self.act(vT[:, ch - 8, :], cacc[ch % 3][:], AF.Silu)
            if self.chk(1):
                return
            for pr in range(4):
                p = self.ps()
                for u in range(2):
                    ch = pr * 2 + u
                    self.act(sq[:, ch, :], qkf[:, ch, :], AF.Square)
                    self.mm(p[:, u * BLK:(u + 1) * BLK], self.onesb[:], sq[:, ch, :])
                rr = rt1[pr % 2]
                rr2 = rt2[pr % 2]
                self.act(rr[:], p[:, 0:BLK], AF.Sqrt, bias=EPS, scale=1.0)
                self.act(rr2[:], p[:, BLK:2 * BLK], AF.Sqrt, bias=EPS, scale=1.0)
                self.recip(rr[:], rr[:])
                self.recip(rr2[:], rr2[:])
                for u, r_ in ((0, rr), (1, rr2)):
                    ch = pr * 2 + u
                    sc = 128.0 ** -0.5 if ch < 4 else 1.0
                    self.stt(self.ve(), qkT[:, ch, :], qkf[:, ch, :], sc, r_[:], ALU.mult, ALU.mult)
            if self.chk(2):
                return
            for h in range(4):
                p = self.ps()
                self.proj_fm(p, wi, 1536 + h * 128, 128, hT, BLK)
                self.act(szT[:, h, :], p[:, :BLK], AF.Silu)
                p = self.ps()
                self.proj_fm(p, wi, 3080 + h * 128, 128, hT, BLK)
                self.act(sgT[:, h, :], p[:, :BLK], AF.Silu)
            if self.chk(3):
                return
            pbg = self.ps()
            for c in range(2):
                for kc in range(8):
                    self.mm(pbg[:, c * 8:(c + 1) * 8], hT[:, kc, c * 128:(c + 1) * 128], wi[:, kc, 2048:2056], start=(kc == 0), stop=(kc == 7))
            pbg3 = pbg[:, 0:16].rearrange("p (c e) -> p c e", c=2)
            b3 = beta.rearrange("p (c h) -> p c h", c=2)
            self.act(b3, pbg3[:, :, 0:4], AF.Sigmoid)
            self.ts("vector", negbeta, beta, -1.0, ALU.mult)
            xg3 = xg.rearrange("p (c h) -> p c h", c=2)
            self.tt("vector", xg3, pbg3[:, :, 4:8], dtb.unsqueeze(1).to_broadcast([128, 2, 4]), ALU.add)
            self.act(ax, xg, AF.Abs)
            self.act(ax, ax, AF.Exp, scale=-1.0)
            self.act(ax, ax, AF.Ln, bias=1.0, scale=1.0)
            self.stt("vector", g_t, xg, 0.0, ax, ALU.max, ALU.add)
            g3 = g_t.rearrange("p (c h) -> p c h", c=2)
            self.tt("vector", g3, g3, nA.unsqueeze(1).to_broadcast([128, 2, 4]), ALU.mult)
            pg = self.ps()
            self.mm(pg[:, 0:8], self.U, g_t)
            self.mm(pg[:, 8:16], self.onesf, g_t)
            self.cp("vector", gcum, pg[:, 0:8])
            self.cp("vector", gend, pg[:, 8:16])
            self.act(eg, gcum, AF.Exp)
            self.tt("vector", wdec, gend, gcum, ALU.subtract)
            self.act(wdec, wdec, AF.Exp)
            self.act(egend, gend, AF.Exp)
            if self.chk(4):
                return
            for j2 in range(4):
                p = self.ps()
                for u in range(2):
                    j = j2 * 2 + u
                    self.proj_fm(p, wi, 2056 + j * 64, 64, hT, BLK, pcol0=u * BLK)
                self.cp("scalar", rawb[:, j2 * 2:j2 * 2 + 2, :], p[0:64, :].rearrange("p (u t) -> p u t", u=2))
            for j2 in range(4):
                p = self.ps()
                for u in range(2):
                    j = j2 * 2 + u
                    self.mm(p[0:64, u * BLK:(u + 1) * BLK], self.PTb[0:64, 0:64], rawb[:, j, :])
                sc = 1.0 if j2 < 2 else 0.125
                for u in range(2):
                    j = j2 * 2 + u
                    self.stt("vector", rt1[u][0:64, :], rawb[:, j, :], sc, rope[0:64, 0, :], ALU.mult, ALU.mult)
                    self.stt("vector", rt2[u][0:64, :], p[0:64, u * BLK:(u + 1) * BLK], sc, rope[0:64, 1, :], ALU.mult, ALU.mult)
                    self.tt("gpsimd", rot[:, j, :], rt1[u][0:64, :], rt2[u][0:64, :], ALU.add)
            if self.chk(5):
                return
            for c in range(2):
                p = self.ps()
                for kc in range(8):
                    self.mm(p[:, :], hT[:, kc, c * 128:(c + 1) * 128], wi[:, kc, 2568:3080], start=(kc == 0), stop=(kc == 7))
                self.cp("scalar", vb_tok[:, c, :], p[:, :])
            if self.chk(6):
                return
            for c in range(2):
                cs = slice(c * 128, (c + 1) * 128)
                pb = self.psbf()
                for h in range(4):
                    self.tr(pb[:, h * 128:(h + 1) * 128], qkT[:, 4 + h, cs], self.identb[:])
                src = pb[:, 0:512].rearrange("p (h k) -> p h k", h=4)
                self.tt("vector", kg_tok[:, c, :].rearrange("p (h k) -> p h k", h=4), src,
                        eg[:, c * 4:(c + 1) * 4].unsqueeze(2).to_broadcast([128, 4, 128]), ALU.mult)
                self.tt("vector", kd_tok[:, c, :].rearrange("p (h k) -> p h k", h=4), src,
                        wdec[:, c * 4:(c + 1) * 4].unsqueeze(2).to_broadcast([128, 4, 128]), ALU.mult)
                for h in range(4):
                    self.tr(pb[:, 512 + h * 128:512 + (h + 1) * 128], vT[:, h, cs], self.identb[:])
                self.cp("scalar", v_tok[:, c, :], pb[:, 512:1024])
            if self.chk(7):
                return
            for c in range(2):
                cs = slice(c * 128, (c + 1) * 128)
                self.tt("vector", gU[:], self.U.unsqueeze(1).to_broadcast([128, 4, 128]),
                        g_t[:, c * 4:(c + 1) * 4].unsqueeze(2).to_broadcast([128, 4, 128]), ALU.mult)
                pA = self.ps()
                pD = self.ps()
                for h in range(4):
                    hs = slice(h * 128, (h + 1) * 128)
                    self.mm(pA[:, hs], self.onesf, gU[:, h, :])
                    self.mm(pD[:, hs], self.onesf, gU[:, h, :], start=True, stop=False)
                    self.mm(pD[:, hs], gU[:, h, :], self.nonesf, start=False, stop=True)
                self.act(EgB[:], pA[:], AF.Exp)
                self.tt("vector", dtmp[:], pD[:].rearrange("p (h i) -> p h i", h=4),
                        cst[:, C_NEG:C_NEG + 128].unsqueeze(1).to_broadcast([128, 4, 128]), ALU.add)
                self.act(decT[:, c, :], dtmp[:].rearrange("p h i -> p (h i)"), AF.Exp)
                self.tt("vector", decS[:].rearrange("p (h i) -> p h i", h=4), decT[:, c, :].rearrange("p (h i) -> p h i", h=4),
                        cst[:, C_STRICT:C_STRICT + 128].unsqueeze(1).to_broadcast([128, 4, 128]), ALU.mult)
                self.tt("vector", qgT[:, c, :].rearrange("p (h i) -> p h i", h=4), qkT[:, 0:4, cs],
                        EgB[:].rearrange("p (h i) -> p h i", h=4), ALU.mult)
                pK = self.ps()
                pQ = self.ps()
                for h in range(4):
                    hs = slice(h * 128, (h + 1) * 128)
                    self.mm(pK[:, hs], qkT[:, 4 + h, cs], qkT[:, 4 + h, cs])
                    self.mm(pQ[:, hs], qkT[:, 4 + h, cs], qkT[:, h, cs])
                for h in range(4):
                    hs = slice(h * 128, (h + 1) * 128)
                    self.stt("vector", NX[c][0][:, hs], pK[:, hs], negbeta[:, c * 4 + h:c * 4 + h + 1], decS[:, hs], ALU.mult, ALU.mult)
                self.tt("vector", QK[:, c, :], pQ[:], decT[:, c, :], ALU.mult)
            if self.chk(8):
                return
            for c in range(2):
                pT = self.ps()
                for h in range(4):
                    hs = slice(h * 128, (h + 1) * 128)
                    self.tr(pT[:, hs], NX[c][0][:, hs], self.ident)
                self.cp("scalar", NXT[c][0][:], pT[:])
                self.tt("vector", NP[c][0][:].rearrange("p (h i) -> p h i", h=4), NX[c][0][:].rearrange("p (h i) -> p h i", h=4),
                        self.ident.unsqueeze(1).to_broadcast([128, 4, 128]), ALU.add)
            cur = 0
            for s in range(1, 7):
                nxt = 1 - cur
                for c in range(2):
                    X, XT, P_ = NX[c][cur], NXT[c][cur], NP[c][cur]
                    Xn, XTn, Pn = NX[c][nxt], NXT[c][nxt], NP[c][nxt]
                    pXT = self.ps()
                    for h in range(4):
                        hs = slice(h * 128, (h + 1) * 128)
                        self.mm(pXT[:, hs], X[:, hs], XT[:, hs])
                    self.cp("scalar", XTn[:], pXT[:])
                    if s < 6:
                        pX = self.ps()
                        for h in range(4):
                            hs = slice(h * 128, (h + 1) * 128)
                            self.mm(pX[:, hs], XT[:, hs], X[:, hs])
                        self.cp("scalar", Xn[:], pX[:])
                    pP = self.ps()
                    for h in range(4):
                        hs = slice(h * 128, (h + 1) * 128)
                        self.mm(pP[:, hs], XTn[:, hs], P_[:, hs])
                    self.tt("vector", Pn[:], pP[:], P_[:], ALU.add)
                cur = nxt
            for c in range(2):
                self.cp("scalar", TTb[:, c, :], NP[c][cur][:])
                pW = self.ps()
                for h in range(4):
                    hs = slice(h * 128, (h + 1) * 128)
                    self.mm(pW[:, hs], kg_tok[:, c, hs], TTb[:, c, hs])
                self.act(nw0T[:, c, :], pW[:], AF.Copy, scale=-1.0)
            if self.chk(9):
                return
            for c in range(2):
                cs = slice(c * 128, (c + 1) * 128)
                pd = self.ps()
                for h in range(4):
                    hs = slice(h * 128, (h + 1) * 128)
                    self.mm(pd[:, hs], TTb[:, c, hs], v_tok[:, c, hs], start=True, stop=False)
                    self.mm(pd[:, hs], nw0T[:, c, hs], Sgb[:, h, :], start=False, stop=True)
                self.tt("vector", delta[:].rearrange("p (h v) -> p h v", h=4), pd[:].rearrange("p (h v) -> p h v", h=4),
                        beta[:, c * 4:(c + 1) * 4].unsqueeze(2).to_broadcast([128, 4, 128]), ALU.mult)
                py = self.ps()
                pS = self.ps()
                for h in range(4):
                    hs = slice(h * 128, (h + 1) * 128)
                    self.mm(py[:, hs], Sgb[:, h, :], qgT[:, c, hs], start=True, stop=False)
                    self.mm(py[:, hs], delta[:, hs], QK[:, c, hs], start=False, stop=True)
                    self.mm(pS[:, hs], kd_tok[:, c, hs], delta[:, hs])
                self.cp("scalar", oT[:], py[:])
                for h in range(4):
                    hs = slice(h * 128, (h + 1) * 128)
                    self.stt("vector", Sg[:, h, :], Sg[:, h, :], egend[:, c * 4 + h:c * 4 + h + 1], pS[:, hs], ALU.mult, ALU.add)
                self.cp("scalar", Sgb[:], Sg[:])
                self.act(osq, oT[:], AF.Square)
                pn = self.ps()
                for h in range(4):
                    hs = slice(h * 128, (h + 1) * 128)
                    self.mm(pn[:, hs], self.onesb[:], osq[:, hs])
                self.act(orr[:], pn[:], AF.Sqrt, bias=EPS, scale=1.0 / 128)
                self.recip(orr[:], orr[:])
                self.tt("vector", otmp[:], oT[:], orr[:], ALU.mult)
                self.tt("vector", mixT[:, 0:4, cs], otmp[:].rearrange("p (h i) -> p h i", h=4), szT[:, :, cs], ALU.mult)
                pSc = self.ps()
                for h in range(4):
                    hs = slice(h * 128, (h + 1) * 128)
                    self.mm(pSc[:, hs], rot[:, 4 + h, cs], rot[:, h, cs])
                self.tt("vector", SR[:], pSc[:], cst[:, C_DMT:C_DMT + 512], ALU.mult)
                pb = self.psbf()
                for h in range(4):
                    self.tr(pb[:, h * 64:(h + 1) * 64], rot[:, 4 + h, cs], self.identb[0:64, 0:64])
                self.tt("vector", kdr_tok[:, c, :], pb[:, 0:256], cst[:, C_KDEC:C_KDEC + 256], ALU.mult)
                self.tt("vector", qgr[:], rot[:, 0:4, cs], cst[0:64, C_QDEC:C_QDEC + 512].rearrange("p (r i) -> p r i", r=4), ALU.mult)
                py = self.ps()
                pS = self.ps()
                for h in range(4):
                    hs = slice(h * 128, (h + 1) * 128)
                    self.mm(py[:, hs], vb_tok[:, c, hs], SR[:, hs], start=True, stop=False)
                    self.mm(py[:, hs], Srb[:, h, :], qgr[:, h, :], start=False, stop=True)
                    self.mm(pS[0:64, hs], kdr_tok[:, c, h * 64:(h + 1) * 64], vb_tok[:, c, hs])
                self.cp("scalar", oT[:], py[:])
                self.cp("scalar", ob16, oT[:])
                for h in range(4):
                    hs = slice(h * 128, (h + 1) * 128)
                    self.stt("vector", Sr[:, h, :], Sr[:, h, :], GAM[h] ** 128, pS[0:64, hs], ALU.mult, ALU.add)
                self.cp("scalar", Srb[:], Sr[:])
                pm = self.ps()
                for h in range(4):
                    hs = slice(h * 128, (h + 1) * 128)
                    self.mm(pm[:, hs], self.onesb[:], ob16[:, hs])
                self.stt("vector", otmp[:], pm[:], -1.0 / 128, oT[:], ALU.mult, ALU.add)
                self.act(osq, otmp[:], AF.Square)
                pn = self.ps()
                for h in range(4):
                    hs = slice(h * 128, (h + 1) * 128)
                    self.mm(pn[:, hs], self.onesb[:], osq[:, hs])
                self.act(orr[:], pn[:], AF.Sqrt, bias=EPS, scale=1.0 / 128)
                self.recip(orr[:], orr[:])
                self.tt("vector", otmp[:], otmp[:], orr[:], ALU.mult)
                self.tt("vector", mixT[:, 4:8, cs], otmp[:].rearrange("p (h i) -> p h i", h=4), sgT[:, :, cs], ALU.mult)
            if self.chk(10):
                return
            for dc in range(8):
                p = self.ps()
                for kc in range(8):
                    self.mm(p[:, :BLK], wo[:, kc, dc * 128:(dc + 1) * 128], mixT[:, kc, :], start=(kc == 0), stop=(kc == 7))
                self.tt("vector", xb[:, dc, :], p[:, :BLK], xb[:, dc, :], ALU.add)
            self.dma("sync", dr["xs"][:, :, t0:t0 + BLK], xb[:], "xst")
            if blk == nblk - 1:
                for cb in range(3):
                    p = self.ps()
                    for kc in range(8):
                        self.mm(p[0:3, :], hT[:, kc, BLK - 3:BLK], wi[:, kc, cb * 512:(cb + 1) * 512], start=(kc == 0), stop=(kc == 7))
                    cs3 = (NX[0][0], NX[0][1], NX[1][0])[cb]
                    self.cp("vector", cs3[0:3, :], p[0:3, :])
                    self.dma("sync", dr["p_gdn_conv"][li, :, cb * 512:(cb + 1) * 512], cs3[0:3, :], "pst")
        for blk in range(nblk if not self.chk(0) else 1):
            body(blk)

    def red(self, eng, out, in_):
        self.S.add(eng, lambda e: e.tensor_reduce(out=out, in_=in_, axis=AX.X, op=ALU.add), reads=[in_], writes=[out])

    def softplus16(self, out, x, tmp):
        self.act(tmp, x, AF.Abs)
        self.act(tmp, tmp, AF.Exp, scale=-1.0)
        self.act(tmp, tmp, AF.Ln, bias=1.0, scale=1.0)
        self.stt("vector", out, x, 0.0, tmp, ALU.max, ALU.add)

    def sample_common(self, bs, layer, wi, ncols):
        dr = self.dr
        sb = lambda n, s, d: self.sb(bs, n, s, d)
        xbs = sb("xbs", [128, 8, NS], F32)
        hTs = sb("hTs", [128, 8, NS], BF16)
        sqs = sb("sqs", [128, 8, NS], BF16)
        rstds = sb("rstds", [128, NS], F32)
        prj = sb("prj", [NS, ncols], F32)
        self.dma("sync", xbs[:], dr["xs"][:, :, SEQ:NTOK], "xld")
        self.norm_block(xbs, hTs, sqs, rstds, NS, PK["nmix"] + layer * 8)
        c0 = 0
        k = 0
        while c0 < ncols:
            n = min(512, ncols - c0)
            p = self.ps()
            for kc in range(8):
                self.mm(p[0:NS, 0:n], hTs[:, kc, :], wi[:, kc, c0:c0 + n], start=(kc == 0), stop=(kc == 7))
            self.cp("vector" if k % 2 else "scalar", prj[:, c0:c0 + n], p[0:NS, 0:n])
            c0 += n
            k += 1
        return xbs, prj

    def sample_outproj(self, bs, xbs, mixs, nk, wo_get):
        dr = self.dr
        sb = lambda n, s, d: self.sb(bs, n, s, d)
        mixTs = sb("mixTs", [128, nk, NS], BF16)
        for k0 in range(0, nk, 8):
            p = self.ps()
            for k in range(k0, min(nk, k0 + 8)):
                self.tr(p[:, (k - k0) * NS:(k - k0 + 1) * NS], mixs[:, k * 128:(k + 1) * 128], self.ident[0:NS, 0:NS])
            n = min(nk, k0 + 8) - k0
            self.cp("vector", mixTs[:, k0:k0 + n, :], p[:, 0:n * NS].rearrange("p (k t) -> p k t", k=n))
        po = self.ps()
        for dc in range(8):
            for kc in range(nk):
                self.mm(po[:, dc * NS:(dc + 1) * NS], wo_get(kc, dc), mixTs[:, kc, :], start=(kc == 0), stop=(kc == nk - 1))
        self.tt("vector", xbs[:], po[:, 0:8 * NS].rearrange("p (k t) -> p k t", k=8), xbs[:], ALU.add)
        self.dma("sync", dr["xs"][:, :, SEQ:NTOK], xbs[:], "xst")

    def hybrid_sample(self, li, layer, bs, wi, wo):
        dr, cst, pk = self.dr, self.cst, self.pk
        sb = lambda n, s, d: self.sb(bs, n, s, d)
        xbs, prj = self.sample_common(bs, layer, wi, HYB_IN)
        id16 = cst[0:NS, C_ID16:C_ID16 + 256].rearrange("p (a b) -> p a b", a=16)
        id16f = cst[:, C_ID16:C_ID16 + 256].rearrange("p (a b) -> p a b", a=16)
        cw = sb("cw", [NS, 4, 512], F32)
        cbuf = sb("cbuf", [NS, 3, 512], F32)
        qkv = sb("qkv", [NS, 1536], F32)
        ctm = sb("ctm", [NS, 512], F32)
        stb = sb("stb", [128, NS, 4, 128], F32)
        kqm = sb("kqm", [128, 8, NS, NS], F32)
        ktm = [sb(f"ktm{i}", [NS, NS, 128], F32) for i in range(2)]
        qkTs = sb("qkTs", [128, 8, NS], F32)
        sm = sb("sms", [NS, 256], F32)
        t1 = sb("st1", [NS, 512], F32)
        t2 = sb("st2", [NS, 512], F32)
        dl = sb("sdl", [NS, 512], F32)
        mixs = sb("mixs", [NS, 1024], F32)
        egB = sb("egB", [128, 64], F32)
        egm = sb("egm", [NS, NS, 4], F32)
        qr = sb("qr", [NS, 4, 64], F32)
        kr = sb("kr", [NS, 4, 64], F32)
        for b in range(NS):
            self.dma("sync", stb[:, b, :, :], dr["state_gdn"][li, b].rearrange("h k v -> k h v"), "stld")
        for pc in range(3):
            c0 = pc * 512
            self.dma("sync", cw[:], dr["gdn_conv_w"][li, :, c0:c0 + 512].partition_broadcast(NS), "cwld")
            self.dma("sync", cbuf[:], dr["state_gdn_conv"][li, :, :, c0:c0 + 512], "cbld")
            self.tt("vector", ctm[:], prj[:, c0:c0 + 512], cw[:, 3, :], ALU.mult)
            for tp in range(3):
                self.tt("gpsimd", t1[:], cbuf[:, tp, :], cw[:, tp, :], ALU.mult)
                self.tt("vector", ctm[:], ctm[:], t1[:], ALU.add)
            self.act(qkv[:, c0:c0 + 512], ctm[:], AF.Silu)
            self.dma("sync", dr["s_gdn_conv"][li, :, 0:2, c0:c0 + 512], cbuf[:, 1:3, :], "cvst")
        self.dma("sync", dr["s_gdn_conv"][li, :, 2, :], prj[:, 0:1536], "cvst")
        beta = sm[:, 0:4]
        xg = sm[:, 4:8]
        tmp4 = sm[:, 8:12]
        g_ = sm[:, 12:16]
        eg = sm[:, 16:20]
        ss = sm[:, 20:28]
        qk = sm[:, 28:32]
        ss2 = sm[:, 32:36]
        rs2 = sm[:, 36:40]
        self.act(beta, prj[:, 2048:2052], AF.Sigmoid)
        self.tt("vector", xg, prj[:, 2052:2056], pk[0:NS, PK[f"gdtb{li}"]:PK[f"gdtb{li}"] + 4], ALU.add)
        self.softplus16(g_, xg, tmp4)
        self.tt("vector", g_, g_, self.negA[0:NS, li * 4:(li + 1) * 4], ALU.mult)
        self.act(eg, g_, AF.Exp)
        qk3 = qkv[:, 0:1024].rearrange("p (h k) -> p h k", h=8)
        self.tt("vector", t1[:], qkv[:, 0:512], qkv[:, 0:512], ALU.mult)
        self.tt("gpsimd", t2[:], qkv[:, 512:1024], qkv[:, 512:1024], ALU.mult)
        self.red("vector", ss[:, 0:4], t1[:].rearrange("p (h k) -> p h k", h=4))
        self.red("vector", ss[:, 4:8], t2[:].rearrange("p (h k) -> p h k", h=4))
        self.act(ss, ss, AF.Sqrt, bias=EPS, scale=1.0)
        self.recip(ss, ss)
        self.ts("vector", ss[:, 0:4], ss[:, 0:4], 128.0 ** -0.5, ALU.mult)
        self.tt("vector", qk3, qk3, ss.unsqueeze(2).to_broadcast([NS, 8, 128]), ALU.mult)
        self.tt("vector", t1[:], qkv[:, 0:512], qkv[:, 512:1024], ALU.mult)
        self.red("vector", qk, t1[:].rearrange("p (h k) -> p h k", h=4))
        p = self.ps()
        for j in range(8):
            self.tr(p[:, j * NS:(j + 1) * NS], qkv[:, j * 128:(j + 1) * 128], self.ident[0:NS, 0:NS])
        self.cp("vector", qkTs[:], p[:, 0:8 * NS].rearrange("p (j t) -> p j t", j=8))
        for j in range(8):
            self.tt("vector" if j % 2 else "gpsimd", kqm[:, j, :, :], qkTs[:, j, :].unsqueeze(1).to_broadcast([128, NS, NS]), id16f, ALU.mult)
        pk_ = self.ps()
        pq_ = self.ps()
        for h in range(4):
            hs = slice(h * 128, (h + 1) * 128)
            for b in range(NS):
                self.mm(pk_[0:NS, hs], kqm[:, 4 + h, b, :], stb[:, b, h, :], start=(b == 0), stop=(b == NS - 1))
                self.mm(pq_[0:NS, hs], kqm[:, h, b, :], stb[:, b, h, :], start=(b == 0), stop=(b == NS - 1))
        v3 = qkv[:, 1024:1536].rearrange("p (h v) -> p h v", h=4)
        eg3 = eg.unsqueeze(2).to_broadcast([NS, 4, 128])
        t13 = t1[:].rearrange("p (h v) -> p h v", h=4)
        t23 = t2[:].rearrange("p (h v) -> p h v", h=4)
        dl3 = dl[:].rearrange("p (h v) -> p h v", h=4)
        self.tt("vector", t13, pk_[0:NS, :].rearrange("p (h v) -> p h v", h=4), eg3, ALU.mult)
        self.tt("vector", t13, v3, t13, ALU.subtract)
        self.tt("vector", dl3, t13, beta.unsqueeze(2).to_broadcast([NS, 4, 128]), ALU.mult)
        self.tt("vector", t23, pq_[0:NS, :].rearrange("p (h v) -> p h v", h=4), eg3, ALU.mult)
        self.tt("vector", t13, dl3, qk.unsqueeze(2).to_broadcast([NS, 4, 128]), ALU.mult)
        self.tt("vector", t23, t23, t13, ALU.add)
        self.tt("gpsimd", t13, t23, t23, ALU.mult)
        self.red("vector", ss2, t13)
        self.act(rs2, ss2, AF.Sqrt, bias=EPS, scale=1.0 / 128)
        self.recip(rs2, rs2)
        self.tt("vector", t23, t23, rs2.unsqueeze(2).to_broadcast([NS, 4, 128]), ALU.mult)
        self.act(t1[:], prj[:, 1536:2048], AF.Silu)
        self.tt("vector", mixs[:, 0:512], t2[:], t1[:], ALU.mult)
        self.tt("vector", egm[:], eg.unsqueeze(1).to_broadcast([NS, NS, 4]),
                self.id16col(cst).to_broadcast([NS, NS, 4]), ALU.mult)
        pe = self.ps()
        self.mm(pe[:, 0:64], self.onesf[0:NS, :], egm[:].rearrange("p a h -> p (a h)"))
        self.cp("vector", egB[:], pe[:, 0:64])
        for h in range(4):
            kt = ktm[h % 2]
            self.tt("gpsimd", kt[:], qkv[:, 512 + h * 128:512 + (h + 1) * 128].unsqueeze(1).to_broadcast([NS, NS, 128]),
                    self.id16col(cst).to_broadcast([NS, NS, 128]), ALU.mult)
            for b4 in range(NS // 4):
                pu = self.ps()
                for u in range(4):
                    b = b4 * 4 + u
                    self.mm(pu[:, u * 128:(u + 1) * 128], kt[:, b, :], dl[:, h * 128:(h + 1) * 128])
                for u in range(4):
                    b = b4 * 4 + u
                    self.stt("vector", stb[:, b, h, :], stb[:, b, h, :], egB[:, b * 4 + h:b * 4 + h + 1], pu[:, u * 128:(u + 1) * 128],
                             ALU.mult, ALU.add)
        for b in range(NS):
            self.dma("sync", dr["s_gdn"][li, b].rearrange("h k v -> k h v"), stb[:, b, :, :], "stst")
        cosr = cst[0:NS, C_ROPES:C_ROPES + 32].unsqueeze(1).to_broadcast([NS, 4, 32])
        sinr = cst[0:NS, C_ROPES + 32:C_ROPES + 64].unsqueeze(1).to_broadcast([NS, 4, 32])
        ra = sb("ra", [NS, 4, 32], F32)
        rb = sb("rb", [NS, 4, 32], F32)
        for (src0, dst, sc) in ((2056, qr, 1.0), (2312, kr, 0.125)):
            src = prj[:, src0:src0 + 256].rearrange("p (h k) -> p h k", h=4)
            x1, x2 = src[:, :, 0:32], src[:, :, 32:64]
            self.tt("vector", ra[:], x1, cosr, ALU.mult)
            self.tt("vector", rb[:], x2, sinr, ALU.mult)
            self.tt("vector", dst[:, :, 0:32], ra[:], rb[:], ALU.subtract)
            self.tt("vector", ra[:], x2, cosr, ALU.mult)
            self.tt("vector", rb[:], x1, sinr, ALU.mult)
            self.tt("vector", dst[:, :, 32:64], ra[:], rb[:], ALU.add)
            if sc != 1.0:
                self.ts("vector", dst[:], dst[:], sc, ALU.mult)
        for b in range(NS):
            self.dma("sync", stb[0:64, b, :, :], dr["state_ret"][li, b].rearrange("h k v -> k h v"), "stld")
        qrT = sb("qrT", [64, 4, NS], F32)
        qrm = kqm[0:64, 0:4, :, :]
        krm = [ktm[i][:, :, 0:64] for i in range(2)]
        p = self.ps()
        for h in range(4):
            self.tr(p[0:64, h * NS:(h + 1) * NS], qr[:, h, :], self.ident[0:NS, 0:NS])
        self.cp("vector", qrT[:], p[0:64, 0:4 * NS].rearrange("p (h t) -> p h t", h=4))
        for h in range(4):
            self.tt("vector", qrm[:, h, :, :], qrT[:, h, :].unsqueeze(1).to_broadcast([64, NS, NS]), id16f[0:64], ALU.mult)
        pq_ = self.ps()
        for h in range(4):
            hs = slice(h * 128, (h + 1) * 128)
            for b in range(NS):
                self.mm(pq_[0:NS, hs], qrm[:, h, b, :], stb[0:64, b, h, :], start=(b == 0), stop=(b == NS - 1))
        vb3 = prj[:, 2568:3080].rearrange("p (h v) -> p h v", h=4)
        qkr = sm[:, 40:44]
        mean = sm[:, 44:48]
        var = sm[:, 48:52]
        self.tt("vector", dl[:, 0:256].rearrange("p (h k) -> p h k", h=4), qr[:], kr[:], ALU.mult)
        self.red("vector", qkr, dl[:, 0:256].rearrange("p (h k) -> p h k", h=4))
        self.tt("vector", t13, vb3, qkr.unsqueeze(2).to_broadcast([NS, 4, 128]), ALU.mult)
        for h in range(4):
            self.stt("vector", t2[:, h * 128:(h + 1) * 128], pq_[0:NS, h * 128:(h + 1) * 128], GAM[h], t1[:, h * 128:(h + 1) * 128], ALU.mult, ALU.add)
        self.red("vector", mean, t23)
        self.ts("vector", mean, mean, -1.0 / 128, ALU.mult)
        self.tt("vector", t23, t23, mean.unsqueeze(2).to_broadcast([NS, 4, 128]), ALU.add)
        self.tt("gpsimd", t13, t23, t23, ALU.mult)
        self.red("vector", var, t13)
        self.act(var, var, AF.Sqrt, bias=EPS, scale=1.0 / 128)
        self.recip(var, var)
        self.tt("vector", t23, t23, var.unsqueeze(2).to_broadcast([NS, 4, 128]), ALU.mult)
        self.act(t1[:], prj[:, 3080:3592], AF.Silu)
        self.tt("vector", mixs[:, 512:1024], t2[:], t1[:], ALU.mult)
        for h in range(4):
            km = krm[h % 2]
            self.tt("gpsimd", km, kr[:, h, :].unsqueeze(1).to_broadcast([NS, NS, 64]), self.id16col(cst).to_broadcast([NS, NS, 64]), ALU.mult)
            for b4 in range(NS // 4):
                pu = self.ps()
                for u in range(4):
                    b = b4 * 4 + u
                    self.mm(pu[0:64, u * 128:(u + 1) * 128], km[:, b, :], prj[:, 2568 + h * 128:2568 + (h + 1) * 128])
                for u in range(4):
                    b = b4 * 4 + u
                    self.stt("vector", stb[0:64, b, h, :], stb[0:64, b, h, :], GAM[h], pu[0:64, u * 128:(u + 1) * 128], ALU.mult, ALU.add)
        for b in range(NS):
            self.dma("sync", dr["s_ret"][li, b].rearrange("h k v -> k h v"), stb[0:64, b, :, :], "stst")
        self.sample_outproj(bs, xbs, mixs, 8, lambda kc, dc: wo[:, kc, dc * 128:(dc + 1) * 128])

    def id16col(self, cst):
        return self.ident[0:NS, 0:NS].unsqueeze(2)

    def ssd_phase(self, li, layer):
        dr, cst, pk = self.dr, self.cst, self.pk
        with contextlib.ExitStack() as ph:
            sb = lambda n, s, d: self.sb(ph, n, s, d)
            wi = sb("wis", [128, 8, SSM_IN], BF16)
            wo = [sb(f"wos{i}", [128, 2048], BF16) for i in range(2)]
            wosd = self.nc.dram_tensor(f"wos_bf{li}", [8, 128, 16, 128], BF16, kind="Internal").ap()
            self.wosd = wosd
            for (c0, c1, chn) in ((0, 1024, "wi0"), (1024, 2048, "wi1"), (2048, 3584, "wi2"), (3584, SSM_IN, "wi3")):
                for kc in range(8):
                    self.dma("gpsimd", wi[:, kc, c0:c1], dr["w_in_ssm"][li, kc * 128:(kc + 1) * 128, c0:c1], chn)
            for kc in range(16):
                wb = wo[(kc // 2) % 2][:, (kc % 2) * 1024:(kc % 2 + 1) * 1024]
                self.dma("gpsimd", wb, dr["w_out_ssm"][li, kc * 128:(kc + 1) * 128, :], f"wog{kc % 4}")
                col = PK[f"snw{li}"] + kc
                self.ts("vector", wb, wb, pk[:, col:col + 1], ALU.mult)
                self.dma("sync", wosd[:, :, kc, :].rearrange("dc p d -> p dc d"), wb.rearrange("p (dc d) -> p dc d", dc=8), f"wost{kc % 4}")
            S_ = sb("S", [128, 2048], F32)
            Sbz = sb("Sbz", [128, 32, 128], BF16)
            Vz = sb("Vz", [128, 32, 128], BF16)
            hist = sb("hists", [128, 24, 3], F32)
            for t_ in (S_, Sbz, Vz, hist):
                self.memset("gpsimd", t_[:], 0.0)
            with contextlib.ExitStack() as bs:
                self.ssd_prompt(li, layer, bs, wi, wo, S_, Sbz, Vz, hist)
            self.dma("sync", dr["p_ssm"][li].rearrange("h n d -> n h d"), S_[:].rearrange("p (h d) -> p h d", h=32), "pst")
            if self.do_sample:
                self.S.barrier()
                with contextlib.ExitStack() as bs:
                    self.ssd_sample(li, layer, bs, wi, wo, [Sbz[:].bitcast(F32).rearrange("p a b -> p (a b)"), Vz[:].bitcast(F32).rearrange("p a b -> p (a b)"), S_[:]])

    def ssd_prompt(self, li, layer, bs, wi, wo, S_, Sbz, Vz, hist):
        dr, cst, pk = self.dr, self.cst, self.pk
        sb = lambda n, s, d: self.sb(bs, n, s, d)
        xb = sb("xb", [128, 8, BLK], F32)
        ynT = xb[:].bitcast(BF16).rearrange("p a (b t) -> p (a b) t", b=2)
        hT = sb("hT", [128, 8, BLK], BF16)
        rstd = sb("rstd", [128, BLK], F32)
        ctmp = [sb(f"ctmp{i}", [128, BLK + 3], F32) for i in range(3)]
        cacc = [sb(f"cacc{i}", [128, BLK], F32) for i in range(3)]
        szT = sb("szT", [128, 16, BLK], BF16)
        xsT = sb("xsT", [128, 16, BLK], BF16)
        BCT = sb("BCT", [128, 8, BLK], BF16)
        xpc = [sb(f"xpc{i}", [128, BLK], F32) for i in range(3)]
        smf = sb("smf", [128, 9 * 64], F32)
        vp = sb("vp", [128, 2048], BF16)
        B_tok = sb("B_tok", [128, 512], BF16)
        scM = sb("scM", [128, 512], F32)
        gU = sb("gU", [128, 512], F32)
        E_ = sb("E_", [128, 512], F32)
        dtm = sb("dtm", [128, 512], F32)
        SD = [sb(f"SD{i}", [128, 512], BF16) for i in range(2)]
        CgT = [sb(f"CgT{i}", [128, 512], BF16) for i in range(2)]
        y_sb = sb("y_sb", [128, 16, 128], F32)
        ysq = sb("ysq", [128, 16, 128], BF16)
        sq = ysq[:].rearrange("p (a b) i -> p a (b i)", b=2)
        rg = sb("rg", [128, 512], F32)
        dtr = smf[:, 0:64]
        ax = smf[:, 64:128]
        dt_ = smf[:, 128:192]
        g_t = smf[:, 192:256]
        gcum = smf[:, 256:320]
        gend = smf[:, 320:384]
        wdec = smf[:, 384:448]
        dtw = smf[:, 448:512]
        egend = smf[:, 512:576]
        dtb = pk[:, PK[f"sdtb{li}"]:PK[f"sdtb{li}"] + 32]
        nA = self.negA[:, 8 + li * 32:8 + (li + 1) * 32]
        nblk = SEQ // BLK
        Vzv = Vz[:].rearrange("p (q a) (b d) -> p q a b d", a=2, b=2)
        Sbzv = Sbz[:].rearrange("p (q a) (b d) -> p q a b d", a=2, b=2)

        def body(blk):
            t0 = blk * BLK
            self.dma("sync", xb[:], dr["xs"][:, :, t0:t0 + BLK], "xld")
            self.norm_block(xb, hT, sq, rstd, BLK, PK["nmix"] + layer * 8)
            if self.chk(21):
                return
            for ch in range(16):
                p = self.ps()
                self.proj_fm(p, wi, ch * 128, 128, hT, BLK)
                self.act(szT[:, ch, :], p[:, :BLK], AF.Silu)
            for g3 in range(8):
                chs = [g3 * 3 + u for u in range(3)]
                pp = {}
                for ch in chs:
                    pp[ch] = self.ps()
                    self.proj_fm(pp[ch], wi, 2048 + ch * 128, 128, hT, BLK)
                for ch in chs:
                    self.cp("scalar", ctmp[ch % 3][:, 3:3 + BLK], pp[ch][:, :BLK])
                    self.cp("gpsimd", ctmp[ch % 3][:, 0:3], hist[:, ch, :])
                for ch in chs:
                    wc = PK[f"scw{li}"] + ch * 4
                    bc = PK[f"scb{li}"] + ch
                    self.ts("vector", cacc[ch % 3][:], ctmp[ch % 3][:, 0:BLK], pk[:, wc:wc + 1], ALU.mult, pk[:, bc:bc + 1], ALU.add)
                for tp in range(1, 4):
                    for ch in chs:
                        wc = PK[f"scw{li}"] + ch * 4
                        self.stt("vector", cacc[ch % 3][:], ctmp[ch % 3][:, tp:tp + BLK], pk[:, wc + tp:wc + tp + 1], cacc[ch % 3][:], ALU.mult, ALU.add)
                for ch in chs:
                    self.cp("gpsimd", hist[:, ch, :], ctmp[ch % 3][:, BLK:BLK + 3])
                    dst = xsT[:, ch, :] if ch < 16 else BCT[:, ch - 16, :]
                    self.act(dst, cacc[ch % 3][:], AF.Silu)
            if self.chk(22):
                return
            pdt = self.ps()
            for c in range(2):
                for kc in range(8):
                    self.mm(pdt[:, c * 32:(c + 1) * 32], hT[:, kc, c * 128:(c + 1) * 128], wi[:, kc, 5120:5152], start=(kc == 0), stop=(kc == 7))
            self.tt("vector", dtr.rearrange("p (c h) -> p c h", c=2), pdt[:, 0:64].rearrange("p (c h) -> p c h", c=2),
                    dtb.unsqueeze(1).to_broadcast([128, 2, 32]), ALU.add)
            self.act(ax, dtr, AF.Abs)
            self.act(ax, ax, AF.Exp, scale=-1.0)
            self.act(ax, ax, AF.Ln, bias=1.0, scale=1.0)
            self.stt("vector", dt_, dtr, 0.0, ax, ALU.max, ALU.add)
            self.tt("vector", g_t.rearrange("p (c h) -> p c h", c=2), dt_.rearrange("p (c h) -> p c h", c=2),
                    nA.unsqueeze(1).to_broadcast([128, 2, 32]), ALU.mult)
            pg = self.ps()
            self.mm(pg[:, 0:64], self.U, g_t)
            self.mm(pg[:, 64:128], self.onesf, g_t)
            self.cp("vector", gcum, pg[:, 0:64])
            self.cp("vector", gend, pg[:, 64:128])
            self.tt("vector", wdec, gend, gcum, ALU.subtract)
            self.act(wdec, wdec, AF.Exp)
            self.tt("vector", dtw, dt_, wdec, ALU.mult)
            self.act(egend, gend, AF.Exp)
            if self.chk(23):
                return
            for c in range(2):
                cs = slice(c * 128, (c + 1) * 128)
                for grp in range(4):
                    pb = self.psbf()
                    for q in range(4):
                        self.tr(pb[:, q * 128:(q + 1) * 128], xsT[:, grp * 4 + q, cs], self.identb[:])
                    pbv = pb[:, 0:512].rearrange("p (q a d) -> p q a d", q=4, a=2)
                    dtv = dt_[:, c * 32 + grp * 8:c * 32 + grp * 8 + 8].rearrange("p (q a) -> p q a", a=2)
                    for hh in range(2):
                        self.tt("vector", Vzv[:, grp * 4:(grp + 1) * 4, hh, hh, :], pbv[:, :, hh, :],
                                dtv[:, :, hh].unsqueeze(2).to_broadcast([128, 4, 64]), ALU.mult)
                    self.tt("vector", vp[:, grp * 512:(grp + 1) * 512].rearrange("p (h d) -> p h d", h=8),
                            pb[:, 0:512].rearrange("p (h d) -> p h d", h=8),
                            dtw[:, c * 32 + grp * 8:c * 32 + grp * 8 + 8].unsqueeze(2).to_broadcast([128, 8, 64]), ALU.mult)
                pb = self.psbf()
                for g in range(4):
                    self.tr(pb[:, g * 128:(g + 1) * 128], BCT[:, g, cs], self.identb[:])
                self.cp("scalar", B_tok[:], pb[:, 0:512])
                pS = self.ps()
                for g in range(4):
                    self.mm(pS[:, g * 128:(g + 1) * 128], BCT[:, g, cs], BCT[:, 4 + g, cs])
                self.tt("vector", scM[:].rearrange("p (g i) -> p g i", g=4), pS[:].rearrange("p (g i) -> p g i", g=4),
                        cst[:, C_INCL:C_INCL + 128].unsqueeze(1).to_broadcast([128, 4, 128]), ALU.mult)
                if self.chk(24):
                    return
                py = None
                for quad in range(8):
                    g = quad // 2
                    h0 = quad * 4
                    k2 = quad % 2
                    self.tt("vector", gU[:].rearrange("p (h i) -> p h i", h=4), self.U.unsqueeze(1).to_broadcast([128, 4, 128]),
                            g_t[:, c * 32 + h0:c * 32 + h0 + 4].unsqueeze(2).to_broadcast([128, 4, 128]), ALU.mult)
                    pA = self.ps()
                    for u in range(4):
                        us = slice(u * 128, (u + 1) * 128)
                        self.mm(pA[:, us], self.onesf, gU[:, us])
                    tokw = self.zcol[:, 1:2]
                    self.S.add("scalar", lambda e, o=E_[:], i=pA[:]: e.activation(out=o, in_=i, func=AF.Exp), reads=[pA[:]], writes=[E_[:], tokw])
                    self.tt("vector", CgT[k2][:].rearrange("p (h i) -> p h i", h=4), E_[:].rearrange("p (h i) -> p h i", h=4),
                            BCT[:, 4 + g, cs].unsqueeze(1).to_broadcast([128, 4, 128]), ALU.mult)
                    for u in range(4):
                        us = slice(u * 128, (u + 1) * 128)
                        gc = gcum[:, c * 32 + h0 + u:c * 32 + h0 + u + 1]
                        self.S.add("vector", lambda e, o=dtm[:, us], i=pA[:, us], g_=gc: e.tensor_scalar(out=o, in0=i, scalar1=g_, scalar2=None, op0=ALU.subtract),
                                   reads=[pA[:, us], gc, tokw], writes=[dtm[:, us]])
                    self.ts("vector", dtm[:], dtm[:], 0.0, ALU.min)
                    self.act(dtm[:], dtm[:], AF.Exp)
                    self.tt("gpsimd", SD[k2][:].rearrange("p (h i) -> p h i", h=4), dtm[:].rearrange("p (h i) -> p h i", h=4),
                            scM[:, g * 128:(g + 1) * 128].unsqueeze(1).to_broadcast([128, 4, 128]), ALU.mult)
                    if k2 == 0:
                        py = self.ps()
                    for pr in range(2):
                        slot = k2 * 2 + pr
                        reg = py[:, slot * 128:(slot + 1) * 128]
                        for hh in range(2):
                            u = pr * 2 + hh
                            h = h0 + u
                            us = slice(u * 128, (u + 1) * 128)
                            self.mm(reg, Vz[:, h, :], SD[k2][:, us], start=(hh == 0), stop=False)
                            self.mm(reg, Sbz[:, h, :], CgT[k2][:, us], start=False, stop=(hh == 1))
                    if k2 == 1:
                        for slot in range(4):
                            pair = (quad - 1) * 2 + slot
                            dcol = PK[f"sD{li}"] + pair
                            self.stt("vector", y_sb[:, pair, :], xsT[:, pair, cs], pk[:, dcol:dcol + 1], py[:, slot * 128:(slot + 1) * 128],
                                     ALU.mult, ALU.add)
                if self.chk(25):
                    return
                self.tt("vector", y_sb[:], y_sb[:], szT[:, :, cs], ALU.mult)
                self.act(ysq[:], y_sb[:], AF.Square)
                pn = self.ps()
                for g in range(4):
                    for q in range(4):
                        self.mm(pn[:, g * 128:(g + 1) * 128], self.onesb[:], ysq[:, g * 4 + q, :], start=(q == 0), stop=(q == 3))
                self.act(rg[:], pn[:], AF.Sqrt, bias=EPS, scale=1.0 / 512)
                self.recip(rg[:], rg[:])
                self.tt("vector", ynT[:, :, cs].rearrange("p (g q) i -> p g q i", g=4), y_sb[:].rearrange("p (g q) i -> p g q i", g=4),
                        rg[:].rearrange("p (g i) -> p g i", g=4).unsqueeze(2).to_broadcast([128, 4, 4, 128]), ALU.mult)
                for g in range(4):
                    gs = slice(g * 512, (g + 1) * 512)
                    pU = self.ps()
                    self.mm(pU[:], B_tok[:, g * 128:(g + 1) * 128], vp[:, gs])
                    self.tt("vector", S_[:, gs].rearrange("p (h d) -> p h d", h=8), S_[:, gs].rearrange("p (h d) -> p h d", h=8),
                            egend[:, c * 32 + g * 8:c * 32 + g * 8 + 8].unsqueeze(2).to_broadcast([128, 8, 64]), ALU.mult)
                    self.tt("vector", S_[:, gs], S_[:, gs], pU[:], ALU.add)
                    sv = S_[:, gs].rearrange("p (q a d) -> p q a d", q=4, a=2)
                    for hh in range(2):
                        self.cp("scalar", Sbzv[:, g * 4:(g + 1) * 4, hh, hh, :], sv[:, :, hh, :])
            if self.chk(26):
                return
            def ld(dc):
                wb_ = wo[dc % 2][:].rearrange("p (k d) -> p k d", k=16)
                self.dma("sync", wb_, self.wosd[dc], f"wo{dc % 2}")
                self.dma("sync", xpc[dc % 3][:], dr["xs"][:, dc, t0:t0 + BLK], f"xpl{dc % 3}")
            ld(0)
            ld(1)
            for dc in range(8):
                wb = wo[dc % 2][:].rearrange("p (k d) -> p k d", k=16)
                xp = xpc[dc % 3]
                p = self.ps()
                for kc in range(16):
                    self.mm(p[:, :BLK], wb[:, kc, :], ynT[:, kc, :], start=(kc == 0), stop=(kc == 15))
                self.tt("vector", xp[:], p[:, :BLK], xp[:], ALU.add)
                if dc + 2 < 8:
                    ld(dc + 2)
                self.dma("sync", dr["xs"][:, dc, t0:t0 + BLK], xp[:], f"xps{dc % 3}")
            if blk == nblk - 1:
                for cb in range(6):
                    p = self.ps()
                    for kc in range(8):
                        self.mm(p[0:3, :], hT[:, kc, BLK - 3:BLK], wi[:, kc, 2048 + cb * 512:2048 + (cb + 1) * 512], start=(kc == 0), stop=(kc == 7))
                    c3 = (gU, E_, dtm)[cb % 3]
                    self.cp("vector", c3[0:3, :], p[0:3, :])
                    self.dma("sync", dr["p_ssm_conv"][li, :, cb * 512:(cb + 1) * 512], c3[0:3, :], "pst")
        for blk in range(nblk if not self.chk(0) else 1):
            body(blk)

    def ssd_sample(self, li, layer, bs, wi, wo, Sbufs):
        dr, cst, pk = self.dr, self.cst, self.pk
        sb = lambda n, s, d: self.sb(bs, n, s, d)
        xbs, prj = self.sample_common(bs, layer, wi, SSM_IN)
        id16f = cst[:, C_ID16:C_ID16 + 256].rearrange("p (a b) -> p a b", a=16)
        ident16 = self.ident[0:NS, 0:NS]
        cw = sb("cw", [NS, 4, 512], F32)
        cbuf = sb("cbuf", [NS, 3, 512], F32)
        cbv = sb("cbv", [NS, 512], F32)
        ctm = sb("ctm", [NS, 512], F32)
        t1 = sb("st1", [NS, 512], F32)
        xbc = sb("xbc", [NS, 3072], F32)
        vv = sb("vv", [NS, 2048], F32)
        y_ = sb("ys", [NS, 2048], F32)
        sm = sb("sms", [NS, 256], F32)
        egm = sb("egm", [NS, NS, 32], F32)
        egB = sb("egB", [128, 512], F32)
        CTs = sb("CTs", [128, 4, NS], F32)
        CTm = sb("CTm", [128, 4, NS, NS], F32)
        Bmb = [sb("Bmb0", [NS, 512], F32)] * 2
        for pc in range(6):
            c0 = pc * 512
            self.dma("sync", cw[:], dr["ssm_conv_w"][li, :, c0:c0 + 512].partition_broadcast(NS), "cwld")
            self.dma("sync", cbv[:], dr["ssm_conv_b"][li, c0:c0 + 512].partition_broadcast(NS), "cwld")
            self.dma("sync", cbuf[:], dr["state_ssm_conv"][li, :, :, c0:c0 + 512], "cbld")
            self.tt("vector", ctm[:], prj[:, 2048 + c0:2048 + c0 + 512], cw[:, 3, :], ALU.mult)
            self.tt("vector", ctm[:], ctm[:], cbv[:], ALU.add)
            for tp in range(3):
                self.tt("gpsimd", t1[:], cbuf[:, tp, :], cw[:, tp, :], ALU.mult)
                self.tt("vector", ctm[:], ctm[:], t1[:], ALU.add)
            self.act(xbc[:, c0:c0 + 512], ctm[:], AF.Silu)
            self.dma("sync", dr["s_ssm_conv"][li, :, 0:2, c0:c0 + 512], cbuf[:, 1:3, :], "cvst")
        self.dma("sync", dr["s_ssm_conv"][li, :, 2, :], prj[:, 2048:5120], "cvst")
        dtr = sm[:, 0:32]
        tmp = sm[:, 32:64]
        dt_ = sm[:, 64:96]
        g_ = sm[:, 96:128]
        eg = sm[:, 128:160]
        ss = sm[:, 160:164]
        self.tt("vector", dtr, prj[:, 5120:5152], pk[0:NS, PK[f"sdtb{li}"]:PK[f"sdtb{li}"] + 32], ALU.add)
        self.softplus16(dt_, dtr, tmp)
        self.tt("vector", g_, dt_, self.negA[0:NS, 8 + li * 32:8 + (li + 1) * 32], ALU.mult)
        self.act(eg, g_, AF.Exp)
        xs3 = xbc[:, 0:2048].rearrange("p (h d) -> p h d", h=32)
        self.tt("vector", vv[:].rearrange("p (h d) -> p h d", h=32), xs3, dt_.unsqueeze(2).to_broadcast([NS, 32, 64]), ALU.mult)
        p = self.ps()
        for g in range(4):
            self.tr(p[:, g * NS:(g + 1) * NS], xbc[:, 2560 + g * 128:2560 + (g + 1) * 128], ident16)
        self.cp("vector", CTs[:], p[:, 0:4 * NS].rearrange("p (g t) -> p g t", g=4))
        for g in range(4):
            self.tt("vector" if g % 2 else "gpsimd", CTm[:, g, :, :], CTs[:, g, :].unsqueeze(1).to_broadcast([128, NS, NS]), id16f, ALU.mult)
        self.tt("vector", egm[:], eg.unsqueeze(1).to_broadcast([NS, NS, 32]), ident16.unsqueeze(2).to_broadcast([NS, NS, 32]), ALU.mult)
        pe = self.ps()
        self.mm(pe[:, :], self.onesf[0:NS, :], egm[:].rearrange("p a h -> p (a h)"))
        self.cp("vector", egB[:], pe[:, :])
        psy = self.psf[0:4]
        k = 0
        for b in range(NS):
            Sb = Sbufs[b % 3]
            self.dma("sync", Sb.rearrange("p (h d) -> p h d", h=32), dr["state_ssm"][li, b].rearrange("h n d -> n h d"), f"sld{b % 3}")
            bm = Bmb[b % 2]
            self.ts("vector", bm[:], xbc[:, 2048:2560], ident16[:, b:b + 1], ALU.mult)
            self.tt("gpsimd", Sb.rearrange("p (h d) -> p h d", h=32), Sb.rearrange("p (h d) -> p h d", h=32),
                    egB[:, b * 32:(b + 1) * 32].unsqueeze(2).to_broadcast([128, 32, 64]), ALU.mult)
            for g in range(4):
                gs = slice(g * 512, (g + 1) * 512)
                pu = self.psf[4 + k % 2]
                k += 1
                self.mm(pu[:, :], bm[:, g * 128:(g + 1) * 128], vv[:, gs])
                self.tt("vector", Sb[:, gs], Sb[:, gs], pu[:, :], ALU.add)
                self.mm(psy[g][0:NS, :], CTm[:, g, b, :], Sb[:, gs], start=(b == 0), stop=(b == NS - 1))
            self.dma("sync", dr["s_ssm"][li, b].rearrange("h n d -> n h d"), Sb.rearrange("p (h d) -> p h d", h=32), f"sst{b % 3}")
        for g in range(4):
            self.cp("vector" if g % 2 else "scalar", y_[:, g * 512:(g + 1) * 512], psy[g][0:NS, :])
        y3 = y_[:].rearrange("p (h d) -> p h d", h=32)
        vv3 = vv[:].rearrange("p (h d) -> p h d", h=32)
        self.tt("gpsimd", vv3, xs3, pk[0:NS, PK[f"sDrep{li}"]:PK[f"sDrep{li}"] + 32].unsqueeze(2).to_broadcast([NS, 32, 64]), ALU.mult)
        self.tt("vector", y_[:], y_[:], vv[:], ALU.add)
        for q in range(4):
            self.act(vv[:, q * 512:(q + 1) * 512], prj[:, q * 512:(q + 1) * 512], AF.Silu)
        self.tt("vector", y_[:], y_[:], vv[:], ALU.mult)
        self.tt("gpsimd", vv[:], y_[:], y_[:], ALU.mult)
        self.red("vector", ss, vv[:].rearrange("p (g c) -> p g c", g=4))
        self.act(ss, ss, AF.Sqrt, bias=EPS, scale=1.0 / 512)
        self.recip(ss, ss)
        self.tt("vector", y_[:].rearrange("p (g c) -> p g c", g=4), y_[:].rearrange("p (g c) -> p g c", g=4),
                ss.unsqueeze(2).to_broadcast([NS, 4, 512]), ALU.mult)
        mixTs = egB[:].bitcast(BF16)[:, 0:256].rearrange("p (k t) -> p k t", k=16)
        for k0 in (0, 8):
            p = self.ps()
            for kk in range(8):
                self.tr(p[:, kk * NS:(kk + 1) * NS], y_[:, (k0 + kk) * 128:(k0 + kk + 1) * 128], ident16)
            self.cp("vector", mixTs[:, k0:k0 + 8, :], p[:, 0:8 * NS].rearrange("p (k t) -> p k t", k=8))
        po = self.ps()
        for dc in range(8):
            wb = wo[dc % 2][:].rearrange("p (k d) -> p k d", k=16)
            self.dma("sync", wb, self.wosd[dc], f"wo{dc % 2}")
            for kc in range(16):
                self.mm(po[:, dc * NS:(dc + 1) * NS], wb[:, kc, :], mixTs[:, kc, :], start=(kc == 0), stop=(kc == 15))
        self.tt("vector", xbs[:], po[:, 0:8 * NS].rearrange("p (k t) -> p k t", k=8), xbs[:], ALU.add)
        self.dma("sync", dr["xs"][:, :, SEQ:NTOK], xbs[:], "xst")

    def mlp_phase(self, layer, last):
        dr, cst, pk = self.dr, self.cst, self.pk
        with contextlib.ExitStack() as ph:
            sb = lambda n, s, d: self.sb(ph, n, s, d)
            xall = sb("xall", [128, 8, NTOK], F32)
            hall = sb("hall", [128, 8, NTOK], BF16)
            sq = sb("msq", [128, 8, 512], BF16)
            rstd = sb("mrstd", [128, 512], F32)
            w1 = [sb(f"w1_{i}", [128, 8, 512], BF16) for i in range(2)]
            w2 = [sb(f"w2_{i}", [128, 4, D], BF16) for i in range(2)]
            rl = [sb(f"rl{i}", [128, 512], BF16) for i in range(2)]
            aT = [sb(f"aT{i}", [128, 4, 512], BF16) for i in range(2)]
            for dc in range(8):
                self.dma("sync", xall[:, dc, :], dr["xs"][:, dc, :], "xall")
            tbs = [(i * 512, 512) for i in range(4)] + [(SEQ, NS)]

            def loadw(fb):
                b = fb % 2
                self.dma("gpsimd", w1[b][:], dr["mlp_w1"][layer, :, fb * 512:(fb + 1) * 512].rearrange("(kc p) f -> p kc f", p=128), f"w1_{b}")
                self.dma("gpsimd", w2[b][:], dr["mlp_w2"][layer, fb * 512:(fb + 1) * 512, :].rearrange("(fc p) d -> p fc d", p=128), f"w2_{b}")
            import os
            KM = os.environ.get("KMLP", "full")
            if KM != "load":
                loadw(0)
                for (t0, nt) in tbs:
                    self.norm_block(xall[:, :, t0:t0 + nt], hall[:, :, t0:t0 + nt], sq, rstd, nt, PK["nmlp"] + layer * 8)
            it = 0
            nfb = {"load": 0, "norm": 0, "fb1": 1, "fb1f": 1, "fb2": 2, "fb3": 3}.get(KM, 8)
            if KM in ("load", "norm", "fb1", "fb2", "fb3"):
                last = False
            for fb in range(nfb):
                if fb + 1 < nfb:
                    loadw(fb + 1)
                b = fb % 2
                for (t0, nt) in tbs:
                    a = aT[it % 2]
                    it += 1
                    for fc in range(4):
                        p = self.ps()
                        for kc in range(8):
                            self.mm(p[:, :nt], w1[b][:, kc, fc * 128:(fc + 1) * 128], hall[:, kc, t0:t0 + nt], start=(kc == 0), stop=(kc == 7))
                        r_ = rl[fc % 2]
                        self.act(r_[:, :nt], p[:, :nt], AF.Relu)
                        self.tt("gpsimd", a[:, fc, :nt], r_[:, :nt], r_[:, :nt], ALU.mult)
                    for dc in range(8):
                        p = self.ps()
                        for fc in range(4):
                            self.mm(p[:, :nt], w2[b][:, fc, dc * 128:(dc + 1) * 128], a[:, fc, :nt], start=(fc == 0), stop=(fc == 3))
                        self.tt("vector", xall[:, dc, t0:t0 + nt], p[:, :nt], xall[:, dc, t0:t0 + nt], ALU.add)
            if not last:
                for dc in range(8):
                    self.dma("sync", dr["xs"][:, dc, :], xall[:, dc, :], "xall_st")
            if last:
                if self.debug:
                    for dc in range(8):
                        self.dma("sync", dr["xs"][:, dc, :], xall[:, dc, :], "xall_st")
                yst = [sb(f"yst{i}", [128, D], F32) for i in range(2)]
                yT = [sb(f"yT{i}", [128, 8, 128], F32) for i in range(2)]
                k = 0
                for (t0, nt) in tbs:
                    self.norm_block_f32(xall[:, :, t0:t0 + nt], sq, rstd, nt, PK["nfin"], yT, yst, t0)

    def norm_block_f32(self, xb, sq, rstd, ntok, wcol, yT, yst, t0):
        dr = self.dr
        for dc in range(8):
            self.act(sq[:, dc, :ntok], xb[:, dc, :ntok], AF.Square)
        p = self.ps()
        for dc in range(8):
            self.mm(p[:, :ntok], self.onesb[:], sq[:, dc, :ntok], start=(dc == 0), stop=(dc == 7))
        self.act(rstd[:, :ntok], p[:, :ntok], AF.Sqrt, bias=EPS, scale=1.0 / D)
        self.recip(rstd[:, :ntok], rstd[:, :ntok])
        ntile = (ntok + 127) // 128
        for ti in range(ntile):
            n = min(128, ntok - ti * 128)
            k = (t0 // 128 + ti) % 2
            y_, ys = yT[k], yst[k]
            for dc in range(8):
                self.stt("vector" if dc % 2 else "gpsimd", y_[:, dc, :n], xb[:, dc, ti * 128:ti * 128 + n], self.pk[:, wcol + dc:wcol + dc + 1],
                         rstd[:, ti * 128:ti * 128 + n], ALU.mult, ALU.mult)
            for half in range(2):
                p = self.ps()
                for q in range(4):
                    dc = half * 4 + q
                    self.tr(p[0:n, q * 128:(q + 1) * 128], y_[:, dc, :n], self.ident)
                self.cp("vector" if half == 0 else "scalar", ys[0:n, half * 512:(half + 1) * 512], p[0:n, :])
            if t0 < SEQ:
                self.dma("sync", dr["y_prompt"][t0 + ti * 128:t0 + ti * 128 + n, :], ys[0:n, :], f"yout{k}")
            else:
                self.dma("sync", dr["y_sample"][0:n, :], ys[0:n, :], f"yout{k}")


_CACHE = {}


def _prep_inputs(inp):
    pk = make_pk(inp)
    cst = make_consts()
    rope = make_rope()
    shared = {k: np.ascontiguousarray(inp[k], dtype=np.float32) for k in
              ("w_in_hyb", "w_out_hyb", "w_in_ssm", "w_out_ssm", "mlp_w1", "mlp_w2", "gdn_conv_w", "ssm_conv_w", "ssm_conv_b")}
    maps = []
    for c in range(NCORES):
        m = dict(shared)
        m["pk"] = pk
        m["cst"] = cst
        m["rope"] = rope
        m["x_prompt"] = np.ascontiguousarray(inp["x_prompt"][c])
        m["x_sample"] = np.ascontiguousarray(inp["x_sample"][c * NS:(c + 1) * NS, 0])
        for k in ("state_gdn", "state_gdn_conv", "state_ret", "state_ssm", "state_ssm_conv"):
            m[k] = np.ascontiguousarray(inp[k][:, c * NS:(c + 1) * NS])
        maps.append(m)
    return maps


def kernel(**inp):
    if "nc" not in _CACHE:
        _CACHE["nc"] = Builder().build()
    nc = _CACHE["nc"]
    maps = _prep_inputs(inp)
    res = run_bass_kernel_spmd(nc, maps, core_ids=list(range(NCORES)))
    R = res.results
    y_prompt = np.stack([R[c]["y_prompt"] for c in range(NCORES)], 0)
    y_sample = np.concatenate([R[c]["y_sample"] for c in range(NCORES)], 0)[:, None, :]

    def pcat(name):
        return np.stack([R[c][name] for c in range(NCORES)], 1)

    def scat(name):
        return np.concatenate([R[c][name] for c in range(NCORES)], 1)
    return (y_prompt, y_sample, pcat("p_gdn"), pcat("p_gdn_conv"), pcat("p_ret"), pcat("p_ssm"), pcat("p_ssm_conv"),
            scat("s_gdn"), scat("s_gdn_conv"), scat("s_ret"), scat("s_ssm"), scat("s_ssm_conv"))
```

```python
import contextlib
import math
import numpy as np
import concourse.bass as bass
import concourse.mybir as mybir
from concourse.bass_utils import run_bass_kernel_spmd

F32 = mybir.dt.float32
BF16 = mybir.dt.bfloat16
ALU = mybir.AluOpType
AF = mybir.ActivationFunctionType
AX = mybir.AxisListType

ENGS = ("sync", "scalar", "vector", "gpsimd", "tensor")
EPOCH = 30000
NCORES = 8
SEQ = 2048
NS = 16
NTOK = SEQ + NS
D = 1024
EPS = 1e-6
HYB_IN = 3592
SSM_IN = 5152
BLK = 256


def _esize(dt):
    return 2 if dt == BF16 else 4


def _box(ap):
    dims = ap.ap
    off = ap.offset
    sp = str(ap.space)
    es = _esize(ap.dtype)
    if sp in ("SB", "PSUM"):
        ps = dims[0][0]
        if ps == 0:
            ps = 1 << 30
        p0 = off // ps
        p1 = p0 + dims[0][1]
        f0 = off % ps
        ext = 1
        for st, cnt in dims[1:]:
            ext += (cnt - 1) * abs(st)
        if sp == "PSUM":
            return (sp + ap.name, (p0 // 32) * 32, ((p1 + 31) // 32) * 32, 0, 2048)
        return (sp + ap.name, p0, p1, f0 * es, (f0 + ext) * es)
    ext = 1
    for st, cnt in dims:
        ext += (cnt - 1) * abs(st)
    return (sp + ap.name, 0, 1, off * es, (off + ext) * es)


class Sched:
    def __init__(self, nc):
        self.nc = nc
        self.ops = []
        self.hist = {}
        self.chans = {}
        self.last_eng = {}
        self.pending_barrier = None

    def barrier(self):
        self.pending_barrier = (dict(self.last_eng), dict(self.last_chan_op()))
        self.barrier_seen = set()
        self.hist = {}

    def last_chan_op(self):
        d = {}
        for i, o in enumerate(self.ops):
            if o["chan"] is not None:
                d[o["chan"]] = i
        return d

    def add(self, eng, fn, reads=(), writes=(), chan=None):
        idx = len(self.ops)
        deps = {}
        rb = [_box(a) for a in reads]
        wb = [_box(a) for a in writes]
        for b in rb:
            isps = b[0].startswith("PSUM")
            for r in self.hist.get(b[0], ()):
                if r[0] < b[2] and b[1] < r[1] and r[2] < b[4] and b[3] < r[3]:
                    if r[4]:
                        deps[r[5]] = True
                    elif isps and r[5] < idx and self.ops[r[5]]["eng"] != eng:
                        deps[r[5]] = True
        for b in wb:
            for r in self.hist.get(b[0], ()):
                if r[0] < b[2] and b[1] < r[1] and r[2] < b[4] and b[3] < r[3]:
                    deps.setdefault(r[5], False)
        isdma = chan is not None
        if self.pending_barrier is not None and eng not in self.barrier_seen:
            self.barrier_seen.add(eng)
            le, lc = self.pending_barrier
            for e2, i2 in le.items():
                deps[i2] = True
            for c2, i2 in lc.items():
                deps[i2] = True
        for b in wb:
            lst = self.hist.setdefault(b[0], [])
            lst[:] = [r for r in lst if not (b[1] <= r[0] and r[1] <= b[2] and b[3] <= r[2] and r[3] <= b[4])]
            lst.append((b[1], b[2], b[3], b[4], True, idx))
        for b in rb:
            lst = self.hist.setdefault(b[0], [])
            if not isdma:
                lst[:] = [r for r in lst if not ((not r[4]) and r[5] < idx and self.ops[r[5]]["eng"] == eng and self.ops[r[5]]["chan"] is None
                                                 and b[1] <= r[0] and r[1] <= b[2] and b[3] <= r[2] and r[3] <= b[4])]
            lst.append((b[1], b[2], b[3], b[4], False, idx))
        if isdma:
            self.chans[chan] = self.chans.get(chan, 0) + 1
        else:
            self.last_eng[eng] = idx
        self.ops.append(dict(eng=eng, fn=fn, deps=deps, chan=chan))
        return idx

    def emit(self):
        nc = self.nc
        ops = self.ops
        need = [False] * len(ops)
        for c, o in enumerate(ops):
            kept = {}
            for p, raw in o["deps"].items():
                po = ops[p]
                if po["chan"] is None and po["eng"] == o["eng"] and o["chan"] is None:
                    if o["eng"] == "tensor":
                        continue
                kept[p] = raw
                need[p] = True
            o["deps"] = kept
        sigidx = {}
        cnt = {e: 0 for e in ENGS}
        for i, o in enumerate(ops):
            if o["chan"] is None and need[i]:
                cnt[o["eng"]] += 1
                sigidx[i] = cnt[o["eng"]]
        nep = {e: (cnt[e] + EPOCH - 1) // EPOCH for e in ENGS}
        self.cnt = cnt
        with contextlib.ExitStack() as st:
            esem = {e: [st.enter_context(nc.semaphore(f"s_{e}_{j}")) for j in range(max(1, nep[e]))] for e in ENGS}
            csem = {c: st.enter_context(nc.semaphore(f"c_{c}")) for c in self.chans}
            waited = {e: {} for e in ENGS}
            chan_issued = {c: 0 for c in self.chans}
            chan_tgt = {c: 0 for c in self.chans}
            plan = {e: [] for e in ENGS}
            for i, o in enumerate(ops):
                E = o["eng"]
                w = {}
                for p in o["deps"]:
                    po = ops[p]
                    if po["chan"] is None:
                        s = sigidx[p]
                        key = ("e", po["eng"], (s - 1) // EPOCH)
                        val = (s - 1) % EPOCH + 1
                    else:
                        ch = po["chan"]
                        tgt = chan_issued[ch]
                        chan_tgt[ch] = max(chan_tgt[ch], tgt)
                        key = ("c", ch)
                        val = 16 * tgt
                    if val > w.get(key, 0):
                        w[key] = val
                if o["chan"] is not None:
                    ch = o["chan"]
                    if chan_tgt[ch] > 0:
                        key = ("c", ch)
                        w[key] = max(w.get(key, 0), 16 * chan_tgt[ch])
                    chan_issued[ch] += 1
                waits = []
                for key, val in w.items():
                    if val > waited[E].get(key, 0):
                        waited[E][key] = val
                        sem = csem[key[1]] if key[0] == "c" else esem[key[1]][key[2]]
                        waits.append((sem, val))
                sig = None
                if o["chan"] is not None:
                    sig = (csem[o["chan"]], 16)
                elif i in sigidx:
                    s = sigidx[i]
                    sig = (esem[E][(s - 1) // EPOCH], 1)
                plan[E].append((o["fn"], waits, sig))
            fin = [(csem[ch], 16 * n) for ch, n in chan_issued.items() if n]
            self.n_instr = {e: len(plan[e]) for e in ENGS}
            with nc.Block() as block:
                def mk(E):
                    def body(eng):
                        for fn, waits, sig in plan[E]:
                            for sem, val in waits:
                                eng.wait_ge(sem, val)
                            ins = fn(eng)
                            if sig is not None:
                                ins.then_inc(sig[0], sig[1])
                        if E == "sync":
                            for sem, val in fin:
                                eng.wait_ge(sem, val)
                    return body
                block.sync(mk("sync"))
                block.scalar(mk("scalar"))
                block.vector(mk("vector"))
                block.gpsimd(mk("gpsimd"))
                block.tensor(mk("tensor"))


C_IDENT, C_U, C_NEG, C_INCL, C_STRICT, C_ONES, C_NONES, C_PT = [i * 128 for i in range(8)]
C_DMT = 8 * 128
C_QDEC = C_DMT + 512
C_KDEC = C_QDEC + 512
C_ID16 = C_KDEC + 256
C_ROPES = C_ID16 + 256
C_G128 = C_ROPES + 64
NCST = C_G128 + 4


def make_consts():
    c = np.zeros((128, NCST), np.float64)
    j = np.arange(128)[:, None]
    i = np.arange(128)[None, :]
    c[:, C_IDENT:C_IDENT + 128] = (j == i)
    c[:, C_U:C_U + 128] = (j <= i)
    c[:, C_NEG:C_NEG + 128] = np.where(i >= j, 0.0, -1e30)
    c[:, C_INCL:C_INCL + 128] = (i >= j)
    c[:, C_STRICT:C_STRICT + 128] = (i > j)
    c[:, C_ONES:C_ONES + 128] = 1.0
    c[:, C_NONES:C_NONES + 128] = -1.0
    PT = np.zeros((128, 128))
    for m in range(128):
        if m % 64 < 32:
            PT[m + 32, m] = -1.0
        else:
            PT[m - 32, m] = 1.0
    c[:, C_PT:C_PT + 128] = PT
    gam = 1.0 - 2.0 ** (-5.0 - np.arange(4))
    for h in range(4):
        c[:, C_DMT + h * 128:C_DMT + (h + 1) * 128] = np.where(i >= j, gam[h] ** np.maximum(i - j, 0), 0.0)
        c[:, C_KDEC + h * 64:C_KDEC + (h + 1) * 64] = (gam[h] ** (127 - j))
    for h in range(4):
        c[:, C_QDEC + h * 128:C_QDEC + (h + 1) * 128] = gam[h] ** (i + 1)
    c[:, C_ID16:C_ID16 + 256] = np.eye(16).reshape(1, 256)
    half = 32
    inv = (10000.0 ** (-np.arange(half, dtype=np.float32) / half)).astype(np.float32)
    ang = (np.float32(16384.0) * inv).astype(np.float32).astype(np.float64)
    c[:, C_ROPES:C_ROPES + 32] = np.cos(ang)[None, :]
    c[:, C_ROPES + 32:C_ROPES + 64] = np.sin(ang)[None, :]
    c[:, C_G128:C_G128 + 4] = gam[None, :] ** 128
    return c.astype(np.float32)


GAM = [1.0 - 2.0 ** (-5.0 - h) for h in range(4)]


def make_rope():
    half = 32
    inv = (10000.0 ** (-np.arange(half, dtype=np.float32) / half)).astype(np.float32)
    pos = np.concatenate([np.arange(SEQ), np.full(NS, 16384)]).astype(np.float32)
    ang = (pos[None, :] * inv[:, None]).astype(np.float32).astype(np.float64)
    r = np.zeros((2, 128, NTOK), np.float32)
    for p in range(128):
        r[0, p] = np.cos(ang[p % 32])
        r[1, p] = np.sin(ang[p % 32])
    return r


PK = {}


def _pk_layout():
    off = 0

    def put(name, n):
        nonlocal off
        PK[name] = off
        off += n
    put("nmix", 32)
    put("nmlp", 32)
    put("nfin", 8)
    for i in range(2):
        put(f"gcw{i}", 48)
        put(f"gdtb{i}", 4)
        put(f"galog{i}", 4)
        put(f"gnw{i}", 1)
        put(f"rnw{i}", 4)
        put(f"scw{i}", 96)
        put(f"scb{i}", 24)
        put(f"sdtb{i}", 32)
        put(f"salog{i}", 32)
        put(f"sD{i}", 16)
        put(f"sDrep{i}", 32)
        put(f"snw{i}", 16)
    return off


NPK = _pk_layout()


def make_pk(inp):
    pk = np.zeros((128, NPK), np.float32)

    def fm(v, nch):
        return np.ascontiguousarray(v.reshape(nch, 128).T)
    for l in range(4):
        pk[:, PK["nmix"] + l * 8:PK["nmix"] + (l + 1) * 8] = fm(inp["norm_mix"][l], 8)
        pk[:, PK["nmlp"] + l * 8:PK["nmlp"] + (l + 1) * 8] = fm(inp["norm_mlp"][l], 8)
    pk[:, PK["nfin"]:PK["nfin"] + 8] = fm(inp["norm_final"], 8)
    for i in range(2):
        cw = inp["gdn_conv_w"][i]
        pk[:, PK[f"gcw{i}"]:PK[f"gcw{i}"] + 48] = cw.reshape(4, 12, 128).transpose(2, 1, 0).reshape(128, 48)
        pk[:, PK[f"gdtb{i}"]:PK[f"gdtb{i}"] + 4] = inp["gdn_dt_bias"][i][None, :]
        pk[:, PK[f"galog{i}"]:PK[f"galog{i}"] + 4] = inp["gdn_a_log"][i][None, :]
        pk[:, PK[f"gnw{i}"]] = inp["gdn_norm_w"][i]
        pk[:, PK[f"rnw{i}"]:PK[f"rnw{i}"] + 4] = fm(inp["ret_norm_w"][i], 4)
        sw = inp["ssm_conv_w"][i]
        pk[:, PK[f"scw{i}"]:PK[f"scw{i}"] + 96] = sw.reshape(4, 24, 128).transpose(2, 1, 0).reshape(128, 96)
        pk[:, PK[f"scb{i}"]:PK[f"scb{i}"] + 24] = fm(inp["ssm_conv_b"][i], 24)
        pk[:, PK[f"sdtb{i}"]:PK[f"sdtb{i}"] + 32] = inp["ssm_dt_bias"][i][None, :]
        pk[:, PK[f"salog{i}"]:PK[f"salog{i}"] + 32] = inp["ssm_a_log"][i][None, :]
        pk[:, PK[f"sD{i}"]:PK[f"sD{i}"] + 16] = np.repeat(inp["ssm_d"][i].reshape(16, 2), 64, axis=1).T
        pk[:, PK[f"sDrep{i}"]:PK[f"sDrep{i}"] + 32] = inp["ssm_d"][i][None, :]
        pk[:, PK[f"snw{i}"]:PK[f"snw{i}"] + 16] = fm(inp["ssm_norm_w"][i], 16)
    return pk


class StopBuild(Exception):
    pass


class Builder:
    def chk(self, k):
        import os
        return int(os.environ.get("KH", "99")) == k

    def __init__(self, depth=4, do_sample=True, debug=False):
        self.depth = depth
        self.do_sample = do_sample
        self.debug = debug
        nc = bass.Bass("TRN2", target_bir_lowering=False)
        self.nc = nc
        self.S = Sched(nc)
        self._psi = 0
        self._u = 0

    def mm(self, out, lhsT, rhs, start=True, stop=True):
        self.S.add("tensor", lambda e: e.matmul(out, lhsT=lhsT, rhs=rhs, start=start, stop=stop),
                   reads=[lhsT, rhs], writes=[out])

    def tr(self, out, in_, ident):
        self.S.add("tensor", lambda e: e.transpose(out=out, in_=in_, identity=ident), reads=[in_, ident], writes=[out])

    def act(self, out, in_, func, bias=None, scale=None):
        if func == AF.Sqrt:
            self.act(out, in_, AF.Ln, bias=bias, scale=scale)
            self.act(out, out, AF.Exp, scale=-0.5)
            return
        rd = [in_]
        kw = {}
        if bias is not None:
            kw["bias"] = bias
            if not isinstance(bias, (int, float)):
                rd.append(bias)
        if scale is not None:
            kw["scale"] = scale
            if not isinstance(scale, (int, float)):
                rd.append(scale)
        self.S.add("scalar", lambda e: e.activation(out=out, in_=in_, func=func, **kw), reads=rd, writes=[out])

    def tt(self, eng, out, in0, in1, op):
        self.S.add(eng, lambda e: e.tensor_tensor(out=out, in0=in0, in1=in1, op=op), reads=[in0, in1], writes=[out])

    def ts(self, eng, out, in0, s1, op0, s2=None, op1=None):
        rd = [in0]
        if not isinstance(s1, (int, float)):
            rd.append(s1)
        if s2 is not None and not isinstance(s2, (int, float)):
            rd.append(s2)
        if op1 is None:
            self.S.add(eng, lambda e: e.tensor_scalar(out=out, in0=in0, scalar1=s1, scalar2=None, op0=op0), reads=rd, writes=[out])
        else:
            self.S.add(eng, lambda e: e.tensor_scalar(out=out, in0=in0, scalar1=s1, scalar2=s2, op0=op0, op1=op1), reads=rd, writes=[out])

    def stt(self, eng, out, in0, scalar, in1, op0, op1):
        rd = [in0, in1]
        if not isinstance(scalar, (int, float)):
            rd.append(scalar)
        self.S.add("vector", lambda e: e.scalar_tensor_tensor(out=out, in0=in0, scalar=scalar, in1=in1, op0=op0, op1=op1),
                   reads=rd, writes=[out])

    def cp(self, eng, out, in_):
        if eng == "scalar":
            self.S.add("scalar", lambda e: e.copy(out=out, in_=in_), reads=[in_], writes=[out])
        else:
            self.S.add(eng, lambda e: e.tensor_copy(out=out, in_=in_), reads=[in_], writes=[out])

    def recip(self, out, in_):
        return

    def memset(self, eng, out, val):
        self.S.add(eng, lambda e: e.memset(out, val), writes=[out])

    def dma(self, eng, out, in_, chan):
        self.S.add(eng, lambda e: e.dma_start(out=out, in_=in_), reads=[in_], writes=[out], chan=chan)

    def ps(self):
        self._psi = (self._psi + 1) % len(self.psf)
        return self.psf[self._psi]

    def psbf(self):
        self._u = (self._u + 1) % len(self.psb)
        return self.psb[self._u]

    def ve(self):
        self._u2 = getattr(self, "_u2", 0) + 1
        return "vector" if self._u2 % 2 else "gpsimd"

    def sb(self, st, name, shape, dt):
        self._nid = getattr(self, "_nid", 0) + 1
        return st.enter_context(self.nc.sbuf_tensor(f"s{self._nid}_{name}", shape, dt))

    def build(self):
        nc = self.nc
        dr = {}

        def din(name, shape):
            dr[name] = nc.dram_tensor(name, shape, F32, kind="ExternalInput").ap()

        def dout(name, shape):
            dr[name] = nc.dram_tensor(name, shape, F32, kind="ExternalOutput").ap()
        din("x_prompt", [SEQ, D])
        din("x_sample", [NS, D])
        din("state_gdn", [2, NS, 4, 128, 128])
        din("state_gdn_conv", [2, NS, 3, 1536])
        din("state_ret", [2, NS, 4, 64, 128])
        din("state_ssm", [2, NS, 32, 128, 64])
        din("state_ssm_conv", [2, NS, 3, 3072])
        din("w_in_hyb", [2, D, HYB_IN])
        din("w_out_hyb", [2, D, D])
        din("w_in_ssm", [2, D, SSM_IN])
        din("w_out_ssm", [2, 2048, D])
        din("mlp_w1", [4, D, 4096])
        din("mlp_w2", [4, 4096, D])
        din("gdn_conv_w", [2, 4, 1536])
        din("ssm_conv_w", [2, 4, 3072])
        din("ssm_conv_b", [2, 3072])
        din("pk", [128, NPK])
        din("cst", [128, NCST])
        din("rope", [2, 128, NTOK])
        dout("y_prompt", [SEQ, D])
        dout("y_sample", [NS, D])
        dout("p_gdn", [2, 4, 128, 128])
        dout("p_gdn_conv", [2, 3, 1536])
        dout("p_ret", [2, 4, 64, 128])
        dout("p_ssm", [2, 32, 128, 64])
        dout("p_ssm_conv", [2, 3, 3072])
        dout("s_gdn", [2, NS, 4, 128, 128])
        dout("s_gdn_conv", [2, NS, 3, 1536])
        dout("s_ret", [2, NS, 4, 64, 128])
        dout("s_ssm", [2, NS, 32, 128, 64])
        dout("s_ssm_conv", [2, NS, 3, 3072])
        if self.debug:
            dout("xs", [128, 8, NTOK])
        else:
            dr["xs"] = nc.dram_tensor("xs", [128, 8, NTOK], F32, kind="Internal").ap()
        self.dr = dr
        with contextlib.ExitStack() as top:
            self.psf = [top.enter_context(nc.psum_tensor(f"psf{i}", [128, 512], F32)) for i in range(6)]
            self.psb = [top.enter_context(nc.psum_tensor(f"psb{i}", [128, 1024], BF16)) for i in range(2)]
            cst = self.sb(top, "cst", [128, NCST], F32)
            pk = self.sb(top, "pk", [128, NPK], F32)
            self.cst, self.pk = cst, pk
            self.dma("sync", cst[:], dr["cst"], "cst")
            self.dma("sync", pk[:], dr["pk"], "cst")
            self.identb = self.sb(top, "identb", [128, 128], BF16)
            self.onesb = self.sb(top, "onesb", [128, 128], BF16)
            self.PTb = self.sb(top, "PTb", [128, 128], BF16)
            self.cp("vector", self.identb[:], cst[:, C_IDENT:C_IDENT + 128])
            self.cp("vector", self.onesb[:], cst[:, C_ONES:C_ONES + 128])
            self.cp("vector", self.PTb[:], cst[:, C_PT:C_PT + 128])
            self.ident = cst[:, C_IDENT:C_IDENT + 128]
            self.onesf = cst[:, C_ONES:C_ONES + 128]
            self.nonesf = cst[:, C_NONES:C_NONES + 128]
            self.U = cst[:, C_U:C_U + 128]
            self.zcol = self.sb(top, "zcol", [128, 4], F32)
            self.memset("gpsimd", self.zcol[:], 0.0)
            self.negA = self.sb(top, "negA", [128, 72], F32)
            for i in range(2):
                self.act(self.negA[:, i * 4:(i + 1) * 4], pk[:, PK[f"galog{i}"]:PK[f"galog{i}"] + 4], AF.Exp)
                self.act(self.negA[:, 8 + i * 32:8 + (i + 1) * 32], pk[:, PK[f"salog{i}"]:PK[f"salog{i}"] + 32], AF.Exp)
            self.ts("vector", self.negA[:], self.negA[:], -1.0, ALU.mult)

            self.prologue()
            for layer in range(self.depth):
                self.S.barrier()
                import os
                if "mix" in os.environ.get("KSKIP", ""):
                    pass
                elif layer % 2 == 0:
                    self.hybrid_phase(layer // 2, layer)
                else:
                    self.ssd_phase(layer // 2, layer)
                self.S.barrier()
                if "mlp" not in os.environ.get("KSKIP", ""):
                    self.mlp_phase(layer, last=(layer == self.depth - 1))
            self.S.emit()
        return nc

    def prologue(self):
        dr = self.dr
        with contextlib.ExitStack() as st:
            xin = [self.sb(st, f"xin{i}", [128, D], F32) for i in range(2)]
            stg = [self.sb(st, f"xstg{i}", [128, 8, BLK], F32) for i in range(2)]
            for blk in range(SEQ // BLK):
                sg = stg[blk % 2]
                for c in range(2):
                    t = blk * 2 + c
                    xi = xin[t % 2]
                    self.dma("sync", xi[:], dr["x_prompt"][t * 128:(t + 1) * 128, :], f"xin{t % 2}")
                    for half in range(2):
                        p = self.ps()
                        for q in range(4):
                            dc = half * 4 + q
                            self.tr(p[:, q * 128:(q + 1) * 128], xi[:, dc * 128:(dc + 1) * 128], self.ident)
                        src = p[:].rearrange("p (q t) -> p q t", q=4)
                        self.cp("vector" if half == 0 else "scalar", sg[:, half * 4:half * 4 + 4, c * 128:(c + 1) * 128], src)
                self.dma("sync", dr["xs"][:, :, blk * BLK:(blk + 1) * BLK], sg[:], f"xst{blk % 2}")
            xi = xin[0]
            self.dma("sync", xi[0:NS, :], dr["x_sample"], "xin0")
            p = self.ps()
            for dc in range(8):
                self.tr(p[:, dc * NS:(dc + 1) * NS], xi[0:NS, dc * 128:(dc + 1) * 128], self.ident[0:NS, 0:NS])
            sg = stg[0]
            self.cp("vector", sg[:, :, 0:NS], p[:, 0:8 * NS].rearrange("p (q t) -> p q t", q=8))
            self.dma("sync", dr["xs"][:, :, SEQ:NTOK], sg[:, :, 0:NS], "xst0")

    def norm_block(self, xb, hT, sq, rstd, ntok, wcol):
        for dc in range(8):
            self.act(sq[:, dc, :ntok], xb[:, dc, :ntok], AF.Square)
        p = self.ps()
        for dc in range(8):
            self.mm(p[:, :ntok], self.onesb[:], sq[:, dc, :ntok], start=(dc == 0), stop=(dc == 7))
        self.act(rstd[:, :ntok], p[:, :ntok], AF.Sqrt, bias=EPS, scale=1.0 / D)
        self.recip(rstd[:, :ntok], rstd[:, :ntok])
        for dc in range(8):
            self.stt("vector" if dc % 2 else "gpsimd", hT[:, dc, :ntok], xb[:, dc, :ntok], self.pk[:, wcol + dc:wcol + dc + 1],
                     rstd[:, :ntok], ALU.mult, ALU.mult)

    def proj_fm(self, p, wi, col0, ncols, hT, ntok, pcol0=0):
        for kc in range(8):
            self.mm(p[:ncols, pcol0:pcol0 + ntok], wi[:, kc, col0:col0 + ncols], hT[:, kc, :ntok], start=(kc == 0), stop=(kc == 7))

    def hybrid_phase(self, li, layer):
        dr, cst, pk = self.dr, self.cst, self.pk
        with contextlib.ExitStack() as ph:
            sb = lambda n, s, d: self.sb(ph, n, s, d)
            wi = sb("wi", [128, 8, HYB_IN], BF16)
            wo = sb("wo", [128, 8, D], BF16)
            for kc in range(8):
                self.dma("gpsimd", wi[:, kc, :], dr["w_in_hyb"][li, kc * 128:(kc + 1) * 128, :], "wi")
            for kc in range(8):
                self.dma("gpsimd", wo[:, kc, :], dr["w_out_hyb"][li, kc * 128:(kc + 1) * 128, :], "wo")
            for kc in range(8):
                col = PK[f"gnw{li}"] if kc < 4 else PK[f"rnw{li}"] + kc - 4
                self.ts("vector", wo[:, kc, :], wo[:, kc, :], pk[:, col:col + 1], ALU.mult)
            Sg = sb("Sg", [128, 4, 128], F32)
            Sgb = sb("Sgb", [128, 4, 128], BF16)
            Sr = sb("Sr", [64, 4, 128], F32)
            Srb = sb("Srb", [64, 4, 128], BF16)
            hist = sb("hist", [128, 12, 3], F32)
            for t_ in (Sg, Sgb, Sr, Srb, hist):
                self.memset("gpsimd", t_[:], 0.0)
            with contextlib.ExitStack() as bs:
                self.hybrid_prompt(li, layer, bs, wi, wo, Sg, Sgb, Sr, Srb, hist)
            self.dma("sync", dr["p_gdn"][li].rearrange("h k v -> k h v"), Sg[:], "pst")
            self.dma("sync", dr["p_ret"][li].rearrange("h k v -> k h v"), Sr[:], "pst")
            if self.do_sample:
                self.S.barrier()
                with contextlib.ExitStack() as bs:
                    self.hybrid_sample(li, layer, bs, wi, wo)

    def hybrid_prompt(self, li, layer, bs, wi, wo, Sg, Sgb, Sr, Srb, hist):
        dr, cst, pk = self.dr, self.cst, self.pk
        sb = lambda n, s, d: self.sb(bs, n, s, d)
        xb = sb("xb", [128, 8, BLK], F32)
        hT = sb("hT", [128, 8, BLK], BF16)
        sq = sb("sq", [128, 8, BLK], BF16)
        rstd = sb("rstd", [128, BLK], F32)
        rope = sb("rope", [128, 2, BLK], F32)
        ctmp = [sb(f"ctmp{i}", [128, BLK + 3], F32) for i in range(3)]
        cacc = [sb(f"cacc{i}", [128, BLK], F32) for i in range(3)]
        qkf = sb("qkf", [128, 8, BLK], BF16)
        qkT = sb("qkT", [128, 8, BLK], BF16)
        vT = sb("vT", [128, 4, BLK], BF16)
        szT = sb("szT", [128, 4, BLK], BF16)
        sgT = sb("sgT", [128, 4, BLK], BF16)
        rawb = sb("rawb", [64, 8, BLK], BF16)
        rot = sb("rot", [64, 8, BLK], BF16)
        rt1 = [sb(f"rt1{i}", [128, BLK], F32) for i in range(2)]
        rt2 = [sb(f"rt2{i}", [128, BLK], F32) for i in range(2)]
        smallf = sb("smallf", [128, 160], F32)
        v_tok = sb("v_tok", [128, 2, 512], BF16)
        kg_tok = sb("kg_tok", [128, 2, 512], BF16)
        kd_tok = sb("kd_tok", [128, 2, 512], BF16)
        vb_tok = sb("vb_tok", [128, 2, 512], BF16)
        kdr_tok = sb("kdr_tok", [128, 2, 256], BF16)
        decT = sb("decT", [128, 2, 512], F32)
        decS = sb("decS", [128, 512], F32)
        EgB = sb("EgB", [128, 512], F32)
        qgT = sb("qgT", [128, 2, 512], BF16)
        QK = sb("QK", [128, 2, 512], BF16)
        SR = sb("SR", [128, 512], BF16)
        qgr = sb("qgr", [64, 4, 128], BF16)
        NX = [[sb(f"NX{c}{k}", [128, 512], F32) for k in range(2)] for c in range(2)]
        NXT = [[sb(f"NXT{c}{k}", [128, 512], F32) for k in range(2)] for c in range(2)]
        NP = [[sb(f"NP{c}{k}", [128, 512], F32) for k in range(2)] for c in range(2)]
        TTb = sb("TTb", [128, 2, 512], BF16)
        nw0T = sb("nw0T", [128, 2, 512], BF16)
        delta = sb("delta", [128, 512], BF16)
        oT = sb("oT", [128, 512], F32)
        ob16 = sq[:, 2:4, :].rearrange("p a b -> p (a b)")
        osq = sq[:, 0:2, :].rearrange("p a b -> p (a b)")
        orr = sb("orr", [128, 512], F32)
        otmp = sb("otmp", [128, 512], F32)
        gU = orr[:].rearrange("p (h i) -> p h i", h=4)
        dtmp = otmp[:].rearrange("p (h i) -> p h i", h=4)
        mixT = sb("mixT", [128, 8, BLK], BF16)
        beta = smallf[:, 0:8]
        negbeta = smallf[:, 8:16]
        xg = smallf[:, 16:24]
        ax = smallf[:, 24:32]
        g_t = smallf[:, 32:40]
        gcum = smallf[:, 40:48]
        gend = smallf[:, 48:56]
        eg = smallf[:, 56:64]
        wdec = smallf[:, 64:72]
        egend = smallf[:, 72:80]
        nblk = SEQ // BLK
        dtb = pk[:, PK[f"gdtb{li}"]:PK[f"gdtb{li}"] + 4]
        nA = self.negA[:, li * 4:(li + 1) * 4]
        def body(blk):
            t0 = blk * BLK
            self.dma("sync", xb[:], dr["xs"][:, :, t0:t0 + BLK], "xld")
            self.dma("sync", rope[:, 0, :], dr["rope"][0, :, t0:t0 + BLK], "rope")
            self.dma("sync", rope[:, 1, :], dr["rope"][1, :, t0:t0 + BLK], "rope")
            self.norm_block(xb, hT, sq, rstd, BLK, PK["nmix"] + layer * 8)
            for g3 in range(4):
                chs = [g3 * 3 + u for u in range(3)]
                pp = {}
                for ch in chs:
                    pp[ch] = self.ps()
                    self.proj_fm(pp[ch], wi, ch * 128, 128, hT, BLK)
                for ch in chs:
                    self.cp("scalar", ctmp[ch % 3][:, 3:3 + BLK], pp[ch][:, :BLK])
                    self.cp("gpsimd", ctmp[ch % 3][:, 0:3], hist[:, ch, :])
                for ch in chs:
                    wc = PK[f"gcw{li}"] + ch * 4
                    self.ts("vector", cacc[ch % 3][:], ctmp[ch % 3][:, 0:BLK], pk[:, wc:wc + 1], ALU.mult)
                for tp in range(1, 4):
                    for ch in chs:
                        wc = PK[f"gcw{li}"] + ch * 4
                        self.stt("vector", cacc[ch % 3][:], ctmp[ch % 3][:, tp:tp + BLK], pk[:, wc + tp:wc + tp + 1], cacc[ch % 3][:], ALU.mult, ALU.add)
                for ch in chs:
                    self.cp("gpsimd", hist[:, ch, :], ctmp[ch % 3][:, BLK:BLK + 3])
                    if ch < 8:
                        self.act(qkf[:, ch, :], cacc[ch % 3][:], AF.Silu)
                    else:
                        self.act(vT[:, ch - 8, :], cacc[ch % 3][:], AF.Silu)
            if self.chk(1):
                return
            for pr in range(4):
                p = self.ps()
                for u in range(2):
                    ch = pr * 2 + u
                    self.act(sq[:, ch, :], qkf[:, ch, :], AF.Square)
                    self.mm(p[:, u * BLK:(u + 1) * BLK], self.onesb[:], sq[:, ch, :])
                rr = rt1[pr % 2]
                rr2 = rt2[pr % 2]
                self.act(rr[:], p[:, 0:BLK], AF.Sqrt, bias=EPS, scale=1.0)
                self.act(rr2[:], p[:, BLK:2 * BLK], AF.Sqrt, bias=EPS, scale=1.0)
                self.recip(rr[:], rr[:])
                self.recip(rr2[:], rr2[:])
                for u, r_ in ((0, rr), (1, rr2)):
                    ch = pr * 2 + u
                    sc = 128.0 ** -0.5 if ch < 4 else 1.0
                    self.stt(self.ve(), qkT[:, ch, :], qkf[:, ch, :], sc, r_[:], ALU.mult, ALU.mult)
            if self.chk(2):
                return
            for h in range(4):
                p = self.ps()
                self.proj_fm(p, wi, 1536 + h * 128, 128, hT, BLK)
                self.act(szT[:, h, :], p[:, :BLK], AF.Silu)
                p = self.ps()
                self.proj_fm(p, wi, 3080 + h * 128, 128, hT, BLK)
                self.act(sgT[:, h, :], p[:, :BLK], AF.Silu)
            if self.chk(3):
                return
            pbg = self.ps()
            for c in range(2):
                for kc in range(8):
                    self.mm(pbg[:, c * 8:(c + 1) * 8], hT[:, kc, c * 128:(c + 1) * 128], wi[:, kc, 2048:2056], start=(kc == 0), stop=(kc == 7))
            pbg3 = pbg[:, 0:16].rearrange("p (c e) -> p c e", c=2)
            b3 = beta.rearrange("p (c h) -> p c h", c=2)
            self.act(b3, pbg3[:, :, 0:4], AF.Sigmoid)
            self.ts("vector", negbeta, beta, -1.0, ALU.mult)
            xg3 = xg.rearrange("p (c h) -> p c h", c=2)
            self.tt("vector", xg3, pbg3[:, :, 4:8], dtb.unsqueeze(1).to_broadcast([128, 2, 4]), ALU.add)
            self.act(ax, xg, AF.Abs)
            self.act(ax, ax, AF.Exp, scale=-1.0)
            self.act(ax, ax, AF.Ln, bias=1.0, scale=1.0)
            self.stt("vector", g_t, xg, 0.0, ax, ALU.max, ALU.add)
            g3 = g_t.rearrange("p (c h) -> p c h", c=2)
            self.tt("vector", g3, g3, nA.unsqueeze(1).to_broadcast([128, 2, 4]), ALU.mult)
            pg = self.ps()
            self.mm(pg[:, 0:8], self.U, g_t)
            self.mm(pg[:, 8:16], self.onesf, g_t)
            self.cp("vector", gcum, pg[:, 0:8])
            self.cp("vector", gend, pg[:, 8:16])
            self.act(eg, gcum, AF.Exp)
            self.tt("vector", wdec, gend, gcum, ALU.subtract)
            self.act(wdec, wdec, AF.Exp)
            self.act(egend, gend, AF.Exp)
            if self.chk(4):
                return
            for j2 in range(4):
                p = self.ps()
                for u in range(2):
                    j = j2 * 2 + u
                    self.proj_fm(p, wi, 2056 + j * 64, 64, hT, BLK, pcol0=u * BLK)
                self.cp("scalar", rawb[:, j2 * 2:j2 * 2 + 2, :], p[0:64, :].rearrange("p (u t) -> p u t", u=2))
            for j2 in range(4):
                p = self.ps()
                for u in range(2):
                    j = j2 * 2 + u
                    self.mm(p[0:64, u * BLK:(u + 1) * BLK], self.PTb[0:64, 0:64], rawb[:, j, :])
                sc = 1.0 if j2 < 2 else 0.125
                for u in range(2):
                    j = j2 * 2 + u
                    self.stt("vector", rt1[u][0:64, :], rawb[:, j, :], sc, rope[0:64, 0, :], ALU.mult, ALU.mult)
                    self.stt("vector", rt2[u][0:64, :], p[0:64, u * BLK:(u + 1) * BLK], sc, rope[0:64, 1, :], ALU.mult, ALU.mult)
                    self.tt("gpsimd", rot[:, j, :], rt1[u][0:64, :], rt2[u][0:64, :], ALU.add)
            if self.chk(5):
                return
            for c in range(2):
                p = self.ps()
                for kc in range(8):
                    self.mm(p[:, :], hT[:, kc, c * 128:(c + 1) * 128], wi[:, kc, 2568:3080], start=(kc == 0), stop=(kc == 7))
                self.cp("scalar", vb_tok[:, c, :], p[:, :])
            if self.chk(6):
                return
            for c in range(2):
                cs = slice(c * 128, (c + 1) * 128)
                pb = self.psbf()
                for h in range(4):
                    self.tr(pb[:, h * 128:(h + 1) * 128], qkT[:, 4 + h, cs], self.identb[:])
                src = pb[:, 0:512].rearrange("p (h k) -> p h k", h=4)
                self.tt("vector", kg_tok[:, c, :].rearrange("p (h k) -> p h k", h=4), src,
                        eg[:, c * 4:(c + 1) * 4].unsqueeze(2).to_broadcast([128, 4, 128]), ALU.mult)
                self.tt("vector", kd_tok[:, c, :].rearrange("p (h k) -> p h k", h=4), src,
                        wdec[:, c * 4:(c + 1) * 4].unsqueeze(2).to_broadcast([128, 4, 128]), ALU.mult)
                for h in range(4):
                    self.tr(pb[:, 512 + h * 128:512 + (h + 1) * 128], vT[:, h, cs], self.identb[:])
                self.cp("scalar", v_tok[:, c, :], pb[:, 512:1024])
            if self.chk(7):
                return
            for c in range(2):
                cs = slice(c * 128, (c + 1) * 128)
                self.tt("vector", gU[:], self.U.unsqueeze(1).to_broadcast([128, 4, 128]),
                        g_t[:, c * 4:(c + 1) * 4].unsqueeze(2).to_broadcast([128, 4, 128]), ALU.mult)
                pA = self.ps()
                pD = self.ps()
                for h in range(4):
                    hs = slice(h * 128, (h + 1) * 128)
                    self.mm(pD[:, hs], self.onesf, gU[:, h, :], start=True, stop=False)
                    self.mm(pD[:, hs], gU[:, h, :], self.nonesf, start=False, stop=True)
                self.mm(pA[:, :], self.onesf, gU[:].rearrange("p h i -> p (h i)"))
                self.act(EgB[:], pA[:], AF.Exp)
                self.tt("vector", dtmp[:], pD[:].rearrange("p (h i) -> p h i", h=4),
                        cst[:, C_NEG:C_NEG + 128].unsqueeze(1).to_broadcast([128, 4, 128]), ALU.add)
                self.act(decT[:, c, :], dtmp[:].rearrange("p h i -> p (h i)"), AF.Exp)
                self.tt("vector", decS[:].rearrange("p (h i) -> p h i", h=4), decT[:, c, :].rearrange("p (h i) -> p h i", h=4),
                        cst[:, C_STRICT:C_STRICT + 128].unsqueeze(1).to_broadcast([128, 4, 128]), ALU.mult)
                self.tt("vector", qgT[:, c, :].rearrange("p (h i) -> p h i", h=4), qkT[:, 0:4, cs],
                        EgB[:].rearrange("p (h i) -> p h i", h=4), ALU.mult)
                pK = self.ps()
                pQ = self.ps()
                for h in range(4):
                    hs = slice(h * 128, (h + 1) * 128)
                    self.mm(pK[:, hs], qkT[:, 4 + h, cs], qkT[:, 4 + h, cs])
                    self.mm(pQ[:, hs], qkT[:, 4 + h, cs], qkT[:, h, cs])
                for h in range(4):
                    hs = slice(h * 128, (h + 1) * 128)
                    self.stt("vector", NX[c][0][:, hs], pK[:, hs], negbeta[:, c * 4 + h:c * 4 + h + 1], decS[:, hs], ALU.mult, ALU.mult)
                self.tt("vector", QK[:, c, :], pQ[:], decT[:, c, :], ALU.mult)
            if self.chk(8):
                return
            for c in range(2):
                pT = self.ps()
                for h in range(4):
                    hs = slice(h * 128, (h + 1) * 128)
                    self.tr(pT[:, hs], NX[c][0][:, hs], self.ident)
                self.cp("scalar", NXT[c][0][:], pT[:])
                self.tt("vector", NP[c][0][:].rearrange("p (h i) -> p h i", h=4), NX[c][0][:].rearrange("p (h i) -> p h i", h=4),
                        self.ident.unsqueeze(1).to_broadcast([128, 4, 128]), ALU.add)
            cur = 0
            for s in range(1, 7):
                nxt = 1 - cur
                for c in range(2):
                    X, XT, P_ = NX[c][cur], NXT[c][cur], NP[c][cur]
                    Xn, XTn, Pn = NX[c][nxt], NXT[c][nxt], NP[c][nxt]
                    pXT = self.ps()
                    for h in range(4):
                        hs = slice(h * 128, (h + 1) * 128)
                        self.mm(pXT[:, hs], X[:, hs], XT[:, hs])
                    self.cp("scalar", XTn[:], pXT[:])
                    if s < 6:
                        pX = self.ps()
                        for h in range(4):
                            hs = slice(h * 128, (h + 1) * 128)
                            self.mm(pX[:, hs], XT[:, hs], X[:, hs])
                        self.cp("scalar", Xn[:], pX[:])
                    pP = self.ps()
                    for h in range(4):
                        hs = slice(h * 128, (h + 1) * 128)
                        self.mm(pP[:, hs], XTn[:, hs], P_[:, hs])
                    self.tt("vector", Pn[:], pP[:], P_[:], ALU.add)
                cur = nxt
            for c in range(2):
                self.cp("scalar", TTb[:, c, :], NP[c][cur][:])
                pW = self.ps()
                for h in range(4):
                    hs = slice(h * 128, (h + 1) * 128)
                    self.mm(pW[:, hs], kg_tok[:, c, hs], TTb[:, c, hs])
                self.act(nw0T[:, c, :], pW[:], AF.Copy, scale=-1.0)
            if self.chk(9):
                return
            for c in range(2):
                cs = slice(c * 128, (c + 1) * 128)
                pd = self.ps()
                for h in range(4):
                    hs = slice(h * 128, (h + 1) * 128)
                    self.mm(pd[:, hs], TTb[:, c, hs], v_tok[:, c, hs], start=True, stop=False)
                    self.mm(pd[:, hs], nw0T[:, c, hs], Sgb[:, h, :], start=False, stop=True)
                self.tt("vector", delta[:].rearrange("p (h v) -> p h v", h=4), pd[:].rearrange("p (h v) -> p h v", h=4),
                        beta[:, c * 4:(c + 1) * 4].unsqueeze(2).to_broadcast([128, 4, 128]), ALU.mult)
                py = self.ps()
                pS = self.ps()
                for h in range(4):
                    hs = slice(h * 128, (h + 1) * 128)
                    self.mm(py[:, hs], Sgb[:, h, :], qgT[:, c, hs], start=True, stop=False)
                    self.mm(py[:, hs], delta[:, hs], QK[:, c, hs], start=False, stop=True)
                    self.mm(pS[:, hs], kd_tok[:, c, hs], delta[:, hs])
                self.cp("scalar", oT[:], py[:])
                for h in range(4):
                    hs = slice(h * 128, (h + 1) * 128)
                    self.stt("vector", Sg[:, h, :], Sg[:, h, :], egend[:, c * 4 + h:c * 4 + h + 1], pS[:, hs], ALU.mult, ALU.add)
                self.cp("scalar", Sgb[:], Sg[:])
                self.act(osq, oT[:], AF.Square)
                pn = self.ps()
                self.mm(pn[:, :], self.onesb[:], osq)
                self.act(orr[:], pn[:], AF.Sqrt, bias=EPS, scale=1.0 / 128)
                self.recip(orr[:], orr[:])
                self.tt("vector", otmp[:], oT[:], orr[:], ALU.mult)
                self.tt("vector", mixT[:, 0:4, cs], otmp[:].rearrange("p (h i) -> p h i", h=4), szT[:, :, cs], ALU.mult)
                pSc = self.ps()
                for h in range(4):
                    hs = slice(h * 128, (h + 1) * 128)
                    self.mm(pSc[:, hs], rot[:, 4 + h, cs], rot[:, h, cs])
                self.tt("vector", SR[:], pSc[:], cst[:, C_DMT:C_DMT + 512], ALU.mult)
                pb = self.psbf()
                for h in range(4):
                    self.tr(pb[:, h * 64:(h + 1) * 64], rot[:, 4 + h, cs], self.identb[0:64, 0:64])
                self.tt("vector", kdr_tok[:, c, :], pb[:, 0:256], cst[:, C_KDEC:C_KDEC + 256], ALU.mult)
                self.tt("vector", qgr[:], rot[:, 0:4, cs], cst[0:64, C_QDEC:C_QDEC + 512].rearrange("p (r i) -> p r i", r=4), ALU.mult)
                py = self.ps()
                pS = self.ps()
                for h in range(4):
                    hs = slice(h * 128, (h + 1) * 128)
                    self.mm(py[:, hs], vb_tok[:, c, hs], SR[:, hs], start=True, stop=False)
                    self.mm(py[:, hs], Srb[:, h, :], qgr[:, h, :], start=False, stop=True)
                    self.mm(pS[0:64, hs], kdr_tok[:, c, h * 64:(h + 1) * 64], vb_tok[:, c, hs])
                self.cp("scalar", oT[:], py[:])
                self.cp("scalar", ob16, oT[:])
                for h in range(4):
                    hs = slice(h * 128, (h + 1) * 128)
                    self.stt("vector", Sr[:, h, :], Sr[:, h, :], GAM[h] ** 128, pS[0:64, hs], ALU.mult, ALU.add)
                self.cp("scalar", Srb[:], Sr[:])
                pm = self.ps()
                self.mm(pm[:, :], self.onesb[:], ob16)
                self.stt("vector", otmp[:], pm[:], -1.0 / 128, oT[:], ALU.mult, ALU.add)
                self.act(osq, otmp[:], AF.Square)
                pn = self.ps()
                self.mm(pn[:, :], self.onesb[:], osq)
                self.act(orr[:], pn[:], AF.Sqrt, bias=EPS, scale=1.0 / 128)
                self.recip(orr[:], orr[:])
                self.tt("vector", otmp[:], otmp[:], orr[:], ALU.mult)
                self.tt("vector", mixT[:, 4:8, cs], otmp[:].rearrange("p (h i) -> p h i", h=4), sgT[:, :, cs], ALU.mult)
            if self.chk(10):
                return
            for dc in range(8):
                p = self.ps()
                for kc in range(8):
                    self.mm(p[:, :BLK], wo[:, kc, dc * 128:(dc + 1) * 128], mixT[:, kc, :], start=(kc == 0), stop=(kc == 7))
                self.tt("vector", xb[:, dc, :], p[:, :BLK], xb[:, dc, :], ALU.add)
            self.dma("sync", dr["xs"][:, :, t0:t0 + BLK], xb[:], "xst")
            if blk == nblk - 1:
                for cb in range(3):
                    p = self.ps()
                    for kc in range(8):
                        self.mm(p[0:3, :], hT[:, kc, BLK - 3:BLK], wi[:, kc, cb * 512:(cb + 1) * 512], start=(kc == 0), stop=(kc == 7))
                    cs3 = (NX[0][0], NX[0][1], NX[1][0])[cb]
                    self.cp("vector", cs3[0:3, :], p[0:3, :])
                    self.dma("sync", dr["p_gdn_conv"][li, :, cb * 512:(cb + 1) * 512], cs3[0:3, :], "pst")
        for blk in range(nblk if not self.chk(0) else 1):
            body(blk)

    def red(self, eng, out, in_):
        self.S.add(eng, lambda e: e.tensor_reduce(out=out, in_=in_, axis=AX.X, op=ALU.add), reads=[in_], writes=[out])

    def softplus16(self, out, x, tmp):
        self.act(tmp, x, AF.Abs)
        self.act(tmp, tmp, AF.Exp, scale=-1.0)
        self.act(tmp, tmp, AF.Ln, bias=1.0, scale=1.0)
        self.stt("vector", out, x, 0.0, tmp, ALU.max, ALU.add)

    def sample_common(self, bs, layer, wi, ncols):
        dr = self.dr
        sb = lambda n, s, d: self.sb(bs, n, s, d)
        xbs = sb("xbs", [128, 8, NS], F32)
        hTs = sb("hTs", [128, 8, NS], BF16)
        sqs = sb("sqs", [128, 8, NS], BF16)
        rstds = sb("rstds", [128, NS], F32)
        prj = sb("prj", [NS, ncols], F32)
        self.dma("sync", xbs[:], dr["xs"][:, :, SEQ:NTOK], "xld")
        self.norm_block(xbs, hTs, sqs, rstds, NS, PK["nmix"] + layer * 8)
        c0 = 0
        k = 0
        while c0 < ncols:
            n = min(512, ncols - c0)
            p = self.ps()
            for kc in range(8):
                self.mm(p[0:NS, 0:n], hTs[:, kc, :], wi[:, kc, c0:c0 + n], start=(kc == 0), stop=(kc == 7))
            self.cp("vector" if k % 2 else "scalar", prj[:, c0:c0 + n], p[0:NS, 0:n])
            c0 += n
            k += 1
        return xbs, prj

    def sample_outproj(self, bs, xbs, mixs, nk, wo_get):
        dr = self.dr
        sb = lambda n, s, d: self.sb(bs, n, s, d)
        mixTs = sb("mixTs", [128, nk, NS], BF16)
        for k0 in range(0, nk, 8):
            p = self.ps()
            for k in range(k0, min(nk, k0 + 8)):
                self.tr(p[:, (k - k0) * NS:(k - k0 + 1) * NS], mixs[:, k * 128:(k + 1) * 128], self.ident[0:NS, 0:NS])
            n = min(nk, k0 + 8) - k0
            self.cp("vector", mixTs[:, k0:k0 + n, :], p[:, 0:n * NS].rearrange("p (k t) -> p k t", k=n))
        po = self.ps()
        for dc in range(8):
            for kc in range(nk):
                self.mm(po[:, dc * NS:(dc + 1) * NS], wo_get(kc, dc), mixTs[:, kc, :], start=(kc == 0), stop=(kc == nk - 1))
        self.tt("vector", xbs[:], po[:, 0:8 * NS].rearrange("p (k t) -> p k t", k=8), xbs[:], ALU.add)
        self.dma("sync", dr["xs"][:, :, SEQ:NTOK], xbs[:], "xst")

    def hybrid_sample(self, li, layer, bs, wi, wo):
        dr, cst, pk = self.dr, self.cst, self.pk
        sb = lambda n, s, d: self.sb(bs, n, s, d)
        xbs, prj = self.sample_common(bs, layer, wi, HYB_IN)
        id16 = cst[0:NS, C_ID16:C_ID16 + 256].rearrange("p (a b) -> p a b", a=16)
        id16f = cst[:, C_ID16:C_ID16 + 256].rearrange("p (a b) -> p a b", a=16)
        cw = sb("cw", [NS, 4, 512], F32)
        cbuf = sb("cbuf", [NS, 3, 512], F32)
        qkv = sb("qkv", [NS, 1536], F32)
        ctm = sb("ctm", [NS, 512], F32)
        stb = sb("stb", [128, NS, 4, 128], F32)
        kqm = sb("kqm", [128, 8, NS, NS], F32)
        ktm = [sb(f"ktm{i}", [NS, NS, 128], F32) for i in range(2)]
        qkTs = sb("qkTs", [128, 8, NS], F32)
        sm = sb("sms", [NS, 256], F32)
        t1 = sb("st1", [NS, 512], F32)
        t2 = sb("st2", [NS, 512], F32)
        dl = sb("sdl", [NS, 512], F32)
        mixs = sb("mixs", [NS, 1024], F32)
        egB = sb("egB", [128, 64], F32)
        egm = sb("egm", [NS, NS, 4], F32)
        qr = sb("qr", [NS, 4, 64], F32)
        kr = sb("kr", [NS, 4, 64], F32)
        for b in range(NS):
            self.dma("sync", stb[:, b, :, :], dr["state_gdn"][li, b].rearrange("h k v -> k h v"), "stld")
        for pc in range(3):
            c0 = pc * 512
            self.dma("sync", cw[:], dr["gdn_conv_w"][li, :, c0:c0 + 512].partition_broadcast(NS), "cwld")
            self.dma("sync", cbuf[:], dr["state_gdn_conv"][li, :, :, c0:c0 + 512], "cbld")
            self.tt("vector", ctm[:], prj[:, c0:c0 + 512], cw[:, 3, :], ALU.mult)
            for tp in range(3):
                self.tt("gpsimd", t1[:], cbuf[:, tp, :], cw[:, tp, :], ALU.mult)
                self.tt("vector", ctm[:], ctm[:], t1[:], ALU.add)
            self.act(qkv[:, c0:c0 + 512], ctm[:], AF.Silu)
            self.dma("sync", dr["s_gdn_conv"][li, :, 0:2, c0:c0 + 512], cbuf[:, 1:3, :], "cvst")
        self.dma("sync", dr["s_gdn_conv"][li, :, 2, :], prj[:, 0:1536], "cvst")
        beta = sm[:, 0:4]
        xg = sm[:, 4:8]
        tmp4 = sm[:, 8:12]
        g_ = sm[:, 12:16]
        eg = sm[:, 16:20]
        ss = sm[:, 20:28]
        qk = sm[:, 28:32]
        ss2 = sm[:, 32:36]
        rs2 = sm[:, 36:40]
        self.act(beta, prj[:, 2048:2052], AF.Sigmoid)
        self.tt("vector", xg, prj[:, 2052:2056], pk[0:NS, PK[f"gdtb{li}"]:PK[f"gdtb{li}"] + 4], ALU.add)
        self.softplus16(g_, xg, tmp4)
        self.tt("vector", g_, g_, self.negA[0:NS, li * 4:(li + 1) * 4], ALU.mult)
        self.act(eg, g_, AF.Exp)
        qk3 = qkv[:, 0:1024].rearrange("p (h k) -> p h k", h=8)
        self.tt("vector", t1[:], qkv[:, 0:512], qkv[:, 0:512], ALU.mult)
        self.tt("gpsimd", t2[:], qkv[:, 512:1024], qkv[:, 512:1024], ALU.mult)
        self.red("vector", ss[:, 0:4], t1[:].rearrange("p (h k) -> p h k", h=4))
        self.red("vector", ss[:, 4:8], t2[:].rearrange("p (h k) -> p h k", h=4))
        self.act(ss, ss, AF.Sqrt, bias=EPS, scale=1.0)
        self.recip(ss, ss)
        self.ts("vector", ss[:, 0:4], ss[:, 0:4], 128.0 ** -0.5, ALU.mult)
        self.tt("vector", qk3, qk3, ss.unsqueeze(2).to_broadcast([NS, 8, 128]), ALU.mult)
        self.tt("vector", t1[:], qkv[:, 0:512], qkv[:, 512:1024], ALU.mult)
        self.red("vector", qk, t1[:].rearrange("p (h k) -> p h k", h=4))
        p = self.ps()
        for j in range(8):
            self.tr(p[:, j * NS:(j + 1) * NS], qkv[:, j * 128:(j + 1) * 128], self.ident[0:NS, 0:NS])
        self.cp("vector", qkTs[:], p[:, 0:8 * NS].rearrange("p (j t) -> p j t", j=8))
        for j in range(8):
            self.tt("vector" if j % 2 else "gpsimd", kqm[:, j, :, :], qkTs[:, j, :].unsqueeze(1).to_broadcast([128, NS, NS]), id16f, ALU.mult)
        pk_ = self.ps()
        pq_ = self.ps()
        for h in range(4):
            hs = slice(h * 128, (h + 1) * 128)
            for b in range(NS):
                self.mm(pk_[0:NS, hs], kqm[:, 4 + h, b, :], stb[:, b, h, :], start=(b == 0), stop=(b == NS - 1))
                self.mm(pq_[0:NS, hs], kqm[:, h, b, :], stb[:, b, h, :], start=(b == 0), stop=(b == NS - 1))
        v3 = qkv[:, 1024:1536].rearrange("p (h v) -> p h v", h=4)
        eg3 = eg.unsqueeze(2).to_broadcast([NS, 4, 128])
        t13 = t1[:].rearrange("p (h v) -> p h v", h=4)
        t23 = t2[:].rearrange("p (h v) -> p h v", h=4)
        dl3 = dl[:].rearrange("p (h v) -> p h v", h=4)
        self.tt("vector", t13, pk_[0:NS, :].rearrange("p (h v) -> p h v", h=4), eg3, ALU.mult)
        self.tt("vector", t13, v3, t13, ALU.subtract)
        self.tt("vector", dl3, t13, beta.unsqueeze(2).to_broadcast([NS, 4, 128]), ALU.mult)
        self.tt("vector", t23, pq_[0:NS, :].rearrange("p (h v) -> p h v", h=4), eg3, ALU.mult)
        self.tt("vector", t13, dl3, qk.unsqueeze(2).to_broadcast([NS, 4, 128]), ALU.mult)
        self.tt("vector", t23, t23, t13, ALU.add)
        self.tt("gpsimd", t13, t23, t23, ALU.mult)
        self.red("vector", ss2, t13)
        self.act(rs2, ss2, AF.Sqrt, bias=EPS, scale=1.0 / 128)
        self.recip(rs2, rs2)
        self.tt("vector", t23, t23, rs2.unsqueeze(2).to_broadcast([NS, 4, 128]), ALU.mult)
        self.act(t1[:], prj[:, 1536:2048], AF.Silu)
        self.tt("vector", mixs[:, 0:512], t2[:], t1[:], ALU.mult)
        self.tt("vector", egm[:], eg.unsqueeze(1).to_broadcast([NS, NS, 4]),
                self.id16col(cst).to_broadcast([NS, NS, 4]), ALU.mult)
        pe = self.ps()
        self.mm(pe[:, 0:64], self.onesf[0:NS, :], egm[:].rearrange("p a h -> p (a h)"))
        self.cp("vector", egB[:], pe[:, 0:64])
        for h in range(4):
            kt = ktm[h % 2]
            self.tt("gpsimd", kt[:], qkv[:, 512 + h * 128:512 + (h + 1) * 128].unsqueeze(1).to_broadcast([NS, NS, 128]),
                    self.id16col(cst).to_broadcast([NS, NS, 128]), ALU.mult)
            for b4 in range(NS // 4):
                pu = self.ps()
                for u in range(4):
                    b = b4 * 4 + u
                    self.mm(pu[:, u * 128:(u + 1) * 128], kt[:, b, :], dl[:, h * 128:(h + 1) * 128])
                for u in range(4):
                    b = b4 * 4 + u
                    self.stt("vector", stb[:, b, h, :], stb[:, b, h, :], egB[:, b * 4 + h:b * 4 + h + 1], pu[:, u * 128:(u + 1) * 128],
                             ALU.mult, ALU.add)
        for b in range(NS):
            self.dma("sync", dr["s_gdn"][li, b].rearrange("h k v -> k h v"), stb[:, b, :, :], "stst")
        cosr = cst[0:NS, C_ROPES:C_ROPES + 32].unsqueeze(1).to_broadcast([NS, 4, 32])
        sinr = cst[0:NS, C_ROPES + 32:C_ROPES + 64].unsqueeze(1).to_broadcast([NS, 4, 32])
        ra = sb("ra", [NS, 4, 32], F32)
        rb = sb("rb", [NS, 4, 32], F32)
        for (src0, dst, sc) in ((2056, qr, 1.0), (2312, kr, 0.125)):
            src = prj[:, src0:src0 + 256].rearrange("p (h k) -> p h k", h=4)
            x1, x2 = src[:, :, 0:32], src[:, :, 32:64]
            self.tt("vector", ra[:], x1, cosr, ALU.mult)
            self.tt("vector", rb[:], x2, sinr, ALU.mult)
            self.tt("vector", dst[:, :, 0:32], ra[:], rb[:], ALU.subtract)
            self.tt("vector", ra[:], x2, cosr, ALU.mult)
            self.tt("vector", rb[:], x1, sinr, ALU.mult)
            self.tt("vector", dst[:, :, 32:64], ra[:], rb[:], ALU.add)
            if sc != 1.0:
                self.ts("vector", dst[:], dst[:], sc, ALU.mult)
        for b in range(NS):
            self.dma("sync", stb[0:64, b, :, :], dr["state_ret"][li, b].rearrange("h k v -> k h v"), "stld")
        qrT = sb("qrT", [64, 4, NS], F32)
        qrm = kqm[0:64, 0:4, :, :]
        krm = [ktm[i][:, :, 0:64] for i in range(2)]
        p = self.ps()
        for h in range(4):
            self.tr(p[0:64, h * NS:(h + 1) * NS], qr[:, h, :], self.ident[0:NS, 0:NS])
        self.cp("vector", qrT[:], p[0:64, 0:4 * NS].rearrange("p (h t) -> p h t", h=4))
        for h in range(4):
            self.tt("vector", qrm[:, h, :, :], qrT[:, h, :].unsqueeze(1).to_broadcast([64, NS, NS]), id16f[0:64], ALU.mult)
        pq_ = self.ps()
        for h in range(4):
            hs = slice(h * 128, (h + 1) * 128)
            for b in range(NS):
                self.mm(pq_[0:NS, hs], qrm[:, h, b, :], stb[0:64, b, h, :], start=(b == 0), stop=(b == NS - 1))
        vb3 = prj[:, 2568:3080].rearrange("p (h v) -> p h v", h=4)
        qkr = sm[:, 40:44]
        mean = sm[:, 44:48]
        var = sm[:, 48:52]
        self.tt("vector", dl[:, 0:256].rearrange("p (h k) -> p h k", h=4), qr[:], kr[:], ALU.mult)
        self.red("vector", qkr, dl[:, 0:256].rearrange("p (h k) -> p h k", h=4))
        self.tt("vector", t13, vb3, qkr.unsqueeze(2).to_broadcast([NS, 4, 128]), ALU.mult)
        for h in range(4):
            self.stt("vector", t2[:, h * 128:(h + 1) * 128], pq_[0:NS, h * 128:(h + 1) * 128], GAM[h], t1[:, h * 128:(h + 1) * 128], ALU.mult, ALU.add)
        self.red("vector", mean, t23)
        self.ts("vector", mean, mean, -1.0 / 128, ALU.mult)
        self.tt("vector", t23, t23, mean.unsqueeze(2).to_broadcast([NS, 4, 128]), ALU.add)
        self.tt("gpsimd", t13, t23, t23, ALU.mult)
        self.red("vector", var, t13)
        self.act(var, var, AF.Sqrt, bias=EPS, scale=1.0 / 128)
        self.recip(var, var)
        self.tt("vector", t23, t23, var.unsqueeze(2).to_broadcast([NS, 4, 128]), ALU.mult)
        self.act(t1[:], prj[:, 3080:3592], AF.Silu)
        self.tt("vector", mixs[:, 512:1024], t2[:], t1[:], ALU.mult)
        for h in range(4):
            km = krm[h % 2]
            self.tt("gpsimd", km, kr[:, h, :].unsqueeze(1).to_broadcast([NS, NS, 64]), self.id16col(cst).to_broadcast([NS, NS, 64]), ALU.mult)
            for b4 in range(NS // 4):
                pu = self.ps()
                for u in range(4):
                    b = b4 * 4 + u
                    self.mm(pu[0:64, u * 128:(u + 1) * 128], km[:, b, :], prj[:, 2568 + h * 128:2568 + (h + 1) * 128])
                for u in range(4):
                    b = b4 * 4 + u
                    self.stt("vector", stb[0:64, b, h, :], stb[0:64, b, h, :], GAM[h], pu[0:64, u * 128:(u + 1) * 128], ALU.mult, ALU.add)
        for b in range(NS):
            self.dma("sync", dr["s_ret"][li, b].rearrange("h k v -> k h v"), stb[0:64, b, :, :], "stst")
        self.sample_outproj(bs, xbs, mixs, 8, lambda kc, dc: wo[:, kc, dc * 128:(dc + 1) * 128])

    def id16col(self, cst):
        return self.ident[0:NS, 0:NS].unsqueeze(2)

    def ssd_phase(self, li, layer):
        dr, cst, pk = self.dr, self.cst, self.pk
        with contextlib.ExitStack() as ph:
            sb = lambda n, s, d: self.sb(ph, n, s, d)
            wi = sb("wis", [128, 8, SSM_IN], BF16)
            wo = [sb(f"wos{i}", [128, 2048], BF16) for i in range(2)]
            wosd = self.nc.dram_tensor(f"wos_bf{li}", [8, 128, 16, 128], BF16, kind="Internal").ap()
            self.wosd = wosd
            for kc in range(8):
                self.dma("gpsimd", wi[:, kc, :], dr["w_in_ssm"][li, kc * 128:(kc + 1) * 128, :], "wi")
            for kc in range(16):
                wb = wo[(kc // 2) % 2][:, (kc % 2) * 1024:(kc % 2 + 1) * 1024]
                self.dma("gpsimd", wb, dr["w_out_ssm"][li, kc * 128:(kc + 1) * 128, :], f"wog{kc % 4}")
                col = PK[f"snw{li}"] + kc
                self.ts("vector", wb, wb, pk[:, col:col + 1], ALU.mult)
                self.dma("sync", wosd[:, :, kc, :].rearrange("dc p d -> p dc d"), wb.rearrange("p (dc d) -> p dc d", dc=8), f"wost{kc % 4}")
            S_ = sb("S", [128, 2048], F32)
            Sbz = sb("Sbz", [128, 32, 128], BF16)
            Vz = sb("Vz", [128, 32, 128], BF16)
            hist = sb("hists", [128, 24, 3], F32)
            for t_ in (S_, Sbz, Vz, hist):
                self.memset("gpsimd", t_[:], 0.0)
            with contextlib.ExitStack() as bs:
                self.ssd_prompt(li, layer, bs, wi, wo, S_, Sbz, Vz, hist)
            self.dma("sync", dr["p_ssm"][li].rearrange("h n d -> n h d"), S_[:].rearrange("p (h d) -> p h d", h=32), "pst")
            if self.do_sample:
                self.S.barrier()
                with contextlib.ExitStack() as bs:
                    self.ssd_sample(li, layer, bs, wi, wo, [Sbz[:].bitcast(F32).rearrange("p a b -> p (a b)"), Vz[:].bitcast(F32).rearrange("p a b -> p (a b)"), S_[:]])

    def ssd_prompt(self, li, layer, bs, wi, wo, S_, Sbz, Vz, hist):
        dr, cst, pk = self.dr, self.cst, self.pk
        sb = lambda n, s, d: self.sb(bs, n, s, d)
        xb = sb("xb", [128, 8, BLK], F32)
        ynT = xb[:].bitcast(BF16).rearrange("p a (b t) -> p (a b) t", b=2)
        hT = sb("hT", [128, 8, BLK], BF16)
        rstd = sb("rstd", [128, BLK], F32)
        ctmp = [sb(f"ctmp{i}", [128, BLK + 3], F32) for i in range(3)]
        cacc = [sb(f"cacc{i}", [128, BLK], F32) for i in range(3)]
        szT = sb("szT", [128, 16, BLK], BF16)
        xsT = sb("xsT", [128, 16, BLK], BF16)
        BCT = sb("BCT", [128, 8, BLK], BF16)
        xpc = [sb(f"xpc{i}", [128, BLK], F32) for i in range(3)]
        smf = sb("smf", [128, 9 * 64], F32)
        vp = sb("vp", [128, 2048], BF16)
        B_tok = sb("B_tok", [128, 512], BF16)
        scM = sb("scM", [128, 512], F32)
        gU = sb("gU", [128, 512], F32)
        E_ = sb("E_", [128, 512], F32)
        dtm = sb("dtm", [128, 512], F32)
        SD = [sb(f"SD{i}", [128, 512], BF16) for i in range(2)]
        CgT = [sb(f"CgT{i}", [128, 512], BF16) for i in range(2)]
        y_sb = sb("y_sb", [128, 16, 128], F32)
        ysq = sb("ysq", [128, 16, 128], BF16)
        sq = ysq[:].rearrange("p (a b) i -> p a (b i)", b=2)
        rg = sb("rg", [128, 512], F32)
        dtr = smf[:, 0:64]
        ax = smf[:, 64:128]
        dt_ = smf[:, 128:192]
        g_t = smf[:, 192:256]
        gcum = smf[:, 256:320]
        gend = smf[:, 320:384]
        wdec = smf[:, 384:448]
        dtw = smf[:, 448:512]
        egend = smf[:, 512:576]
        dtb = pk[:, PK[f"sdtb{li}"]:PK[f"sdtb{li}"] + 32]
        nA = self.negA[:, 8 + li * 32:8 + (li + 1) * 32]
        nblk = SEQ // BLK
        Vzv = Vz[:].rearrange("p (q a) (b d) -> p q a b d", a=2, b=2)
        Sbzv = Sbz[:].rearrange("p (q a) (b d) -> p q a b d", a=2, b=2)

        def body(blk):
            t0 = blk * BLK
            self.dma("sync", xb[:], dr["xs"][:, :, t0:t0 + BLK], "xld")
            self.norm_block(xb, hT, sq, rstd, BLK, PK["nmix"] + layer * 8)
            if self.chk(21):
                return
            for ch in range(16):
                p = self.ps()
                self.proj_fm(p, wi, ch * 128, 128, hT, BLK)
                self.act(szT[:, ch, :], p[:, :BLK], AF.Silu)
            for g3 in range(8):
                chs = [g3 * 3 + u for u in range(3)]
                pp = {}
                for ch in chs:
                    pp[ch] = self.ps()
                    self.proj_fm(pp[ch], wi, 2048 + ch * 128, 128, hT, BLK)
                for ch in chs:
                    self.cp("scalar", ctmp[ch % 3][:, 3:3 + BLK], pp[ch][:, :BLK])
                    self.cp("gpsimd", ctmp[ch % 3][:, 0:3], hist[:, ch, :])
                for ch in chs:
                    wc = PK[f"scw{li}"] + ch * 4
                    bc = PK[f"scb{li}"] + ch
                    self.ts("vector", cacc[ch % 3][:], ctmp[ch % 3][:, 0:BLK], pk[:, wc:wc + 1], ALU.mult, pk[:, bc:bc + 1], ALU.add)
                for tp in range(1, 4):
                    for ch in chs:
                        wc = PK[f"scw{li}"] + ch * 4
                        self.stt("vector", cacc[ch % 3][:], ctmp[ch % 3][:, tp:tp + BLK], pk[:, wc + tp:wc + tp + 1], cacc[ch % 3][:], ALU.mult, ALU.add)
                for ch in chs:
                    self.cp("gpsimd", hist[:, ch, :], ctmp[ch % 3][:, BLK:BLK + 3])
                    dst = xsT[:, ch, :] if ch < 16 else BCT[:, ch - 16, :]
                    self.act(dst, cacc[ch % 3][:], AF.Silu)
            if self.chk(22):
                return
            pdt = self.ps()
            for c in range(2):
                for kc in range(8):
                    self.mm(pdt[:, c * 32:(c + 1) * 32], hT[:, kc, c * 128:(c + 1) * 128], wi[:, kc, 5120:5152], start=(kc == 0), stop=(kc == 7))
            self.tt("vector", dtr.rearrange("p (c h) -> p c h", c=2), pdt[:, 0:64].rearrange("p (c h) -> p c h", c=2),
                    dtb.unsqueeze(1).to_broadcast([128, 2, 32]), ALU.add)
            self.act(ax, dtr, AF.Abs)
            self.act(ax, ax, AF.Exp, scale=-1.0)
            self.act(ax, ax, AF.Ln, bias=1.0, scale=1.0)
            self.stt("vector", dt_, dtr, 0.0, ax, ALU.max, ALU.add)
            self.tt("vector", g_t.rearrange("p (c h) -> p c h", c=2), dt_.rearrange("p (c h) -> p c h", c=2),
                    nA.unsqueeze(1).to_broadcast([128, 2, 32]), ALU.mult)
            pg = self.ps()
            self.mm(pg[:, 0:64], self.U, g_t)
            self.mm(pg[:, 64:128], self.onesf, g_t)
            self.cp("vector", gcum, pg[:, 0:64])
            self.cp("vector", gend, pg[:, 64:128])
            self.tt("vector", wdec, gend, gcum, ALU.subtract)
            self.act(wdec, wdec, AF.Exp)
            self.tt("vector", dtw, dt_, wdec, ALU.mult)
            self.act(egend, gend, AF.Exp)
            if self.chk(23):
                return
            for c in range(2):
                cs = slice(c * 128, (c + 1) * 128)
                for grp in range(4):
                    pb = self.psbf()
                    for q in range(4):
                        self.tr(pb[:, q * 128:(q + 1) * 128], xsT[:, grp * 4 + q, cs], self.identb[:])
                    pbv = pb[:, 0:512].rearrange("p (q a d) -> p q a d", q=4, a=2)
                    dtv = dt_[:, c * 32 + grp * 8:c * 32 + grp * 8 + 8].rearrange("p (q a) -> p q a", a=2)
                    for hh in range(2):
                        self.tt("vector", Vzv[:, grp * 4:(grp + 1) * 4, hh, hh, :], pbv[:, :, hh, :],
                                dtv[:, :, hh].unsqueeze(2).to_broadcast([128, 4, 64]), ALU.mult)
                    self.tt("vector", vp[:, grp * 512:(grp + 1) * 512].rearrange("p (h d) -> p h d", h=8),
                            pb[:, 0:512].rearrange("p (h d) -> p h d", h=8),
                            dtw[:, c * 32 + grp * 8:c * 32 + grp * 8 + 8].unsqueeze(2).to_broadcast([128, 8, 64]), ALU.mult)
                pb = self.psbf()
                for g in range(4):
                    self.tr(pb[:, g * 128:(g + 1) * 128], BCT[:, g, cs], self.identb[:])
                self.cp("scalar", B_tok[:], pb[:, 0:512])
                pS = self.ps()
                for g in range(4):
                    self.mm(pS[:, g * 128:(g + 1) * 128], BCT[:, g, cs], BCT[:, 4 + g, cs])
                self.tt("vector", scM[:].rearrange("p (g i) -> p g i", g=4), pS[:].rearrange("p (g i) -> p g i", g=4),
                        cst[:, C_INCL:C_INCL + 128].unsqueeze(1).to_broadcast([128, 4, 128]), ALU.mult)
                if self.chk(24):
                    return
                py = None
                for quad in range(8):
                    g = quad // 2
                    h0 = quad * 4
                    k2 = quad % 2
                    self.tt("vector", gU[:].rearrange("p (h i) -> p h i", h=4), self.U.unsqueeze(1).to_broadcast([128, 4, 128]),
                            g_t[:, c * 32 + h0:c * 32 + h0 + 4].unsqueeze(2).to_broadcast([128, 4, 128]), ALU.mult)
                    pA = self.ps()
                    self.mm(pA[:, :], self.onesf, gU[:, :])
                    tokw = self.zcol[:, 1:2]
                    self.S.add("scalar", lambda e, o=E_[:], i=pA[:]: e.activation(out=o, in_=i, func=AF.Exp), reads=[pA[:]], writes=[E_[:], tokw])
                    self.tt("vector", CgT[k2][:].rearrange("p (h i) -> p h i", h=4), E_[:].rearrange("p (h i) -> p h i", h=4),
                            BCT[:, 4 + g, cs].unsqueeze(1).to_broadcast([128, 4, 128]), ALU.mult)
                    for u in range(4):
                        us = slice(u * 128, (u + 1) * 128)
                        gc = gcum[:, c * 32 + h0 + u:c * 32 + h0 + u + 1]
                        self.S.add("vector", lambda e, o=dtm[:, us], i=pA[:, us], g_=gc: e.tensor_scalar(out=o, in0=i, scalar1=g_, scalar2=None, op0=ALU.subtract),
                                   reads=[pA[:, us], gc, tokw], writes=[dtm[:, us]])
                    self.ts("vector", dtm[:], dtm[:], 0.0, ALU.min)
                    self.act(dtm[:], dtm[:], AF.Exp)
                    self.tt("gpsimd", SD[k2][:].rearrange("p (h i) -> p h i", h=4), dtm[:].rearrange("p (h i) -> p h i", h=4),
                            scM[:, g * 128:(g + 1) * 128].unsqueeze(1).to_broadcast([128, 4, 128]), ALU.mult)
                    if k2 == 0:
                        py = self.ps()
                    for pr in range(2):
                        slot = k2 * 2 + pr
                        reg = py[:, slot * 128:(slot + 1) * 128]
                        for hh in range(2):
                            u = pr * 2 + hh
                            h = h0 + u
                            us = slice(u * 128, (u + 1) * 128)
                            self.mm(reg, Vz[:, h, :], SD[k2][:, us], start=(hh == 0), stop=False)
                            self.mm(reg, Sbz[:, h, :], CgT[k2][:, us], start=False, stop=(hh == 1))
                    if k2 == 1:
                        for slot in range(4):
                            pair = (quad - 1) * 2 + slot
                            dcol = PK[f"sD{li}"] + pair
                            self.stt("vector", y_sb[:, pair, :], xsT[:, pair, cs], pk[:, dcol:dcol + 1], py[:, slot * 128:(slot + 1) * 128],
                                     ALU.mult, ALU.add)
                if self.chk(25):
                    return
                self.tt("vector", y_sb[:], y_sb[:], szT[:, :, cs], ALU.mult)
                self.act(ysq[:], y_sb[:], AF.Square)
                pn = self.ps()
                for g in range(4):
                    for q in range(4):
                        self.mm(pn[:, g * 128:(g + 1) * 128], self.onesb[:], ysq[:, g * 4 + q, :], start=(q == 0), stop=(q == 3))
                self.act(rg[:], pn[:], AF.Sqrt, bias=EPS, scale=1.0 / 512)
                self.recip(rg[:], rg[:])
                self.tt("vector", ynT[:, :, cs].rearrange("p (g q) i -> p g q i", g=4), y_sb[:].rearrange("p (g q) i -> p g q i", g=4),
                        rg[:].rearrange("p (g i) -> p g i", g=4).unsqueeze(2).to_broadcast([128, 4, 4, 128]), ALU.mult)
                for g in range(4):
                    gs = slice(g * 512, (g + 1) * 512)
                    pU = self.ps()
                    self.mm(pU[:], B_tok[:, g * 128:(g + 1) * 128], vp[:, gs])
                    self.tt("vector", S_[:, gs].rearrange("p (h d) -> p h d", h=8), S_[:, gs].rearrange("p (h d) -> p h d", h=8),
                            egend[:, c * 32 + g * 8:c * 32 + g * 8 + 8].unsqueeze(2).to_broadcast([128, 8, 64]), ALU.mult)
                    self.tt("vector", S_[:, gs], S_[:, gs], pU[:], ALU.add)
                    sv = S_[:, gs].rearrange("p (q a d) -> p q a d", q=4, a=2)
                    for hh in range(2):
                        self.cp("scalar", Sbzv[:, g * 4:(g + 1) * 4, hh, hh, :], sv[:, :, hh, :])
            if self.chk(26):
                return
            def ld(dc):
                wb_ = wo[dc % 2][:].rearrange("p (k d) -> p k d", k=16)
                self.dma("sync", wb_, self.wosd[dc], f"wo{dc % 2}")
                self.dma("sync", xpc[dc % 3][:], dr["xs"][:, dc, t0:t0 + BLK], f"xpl{dc % 3}")
            ld(0)
            ld(1)
            for dc in range(8):
                wb = wo[dc % 2][:].rearrange("p (k d) -> p k d", k=16)
                xp = xpc[dc % 3]
                p = self.ps()
                for kc in range(16):
                    self.mm(p[:, :BLK], wb[:, kc, :], ynT[:, kc, :], start=(kc == 0), stop=(kc == 15))
                self.tt("vector", xp[:], p[:, :BLK], xp[:], ALU.add)
                if dc + 2 < 8:
                    ld(dc + 2)
                self.dma("sync", dr["xs"][:, dc, t0:t0 + BLK], xp[:], f"xps{dc % 3}")
            if blk == nblk - 1:
                for cb in range(6):
                    p = self.ps()
                    for kc in range(8):
                        self.mm(p[0:3, :], hT[:, kc, BLK - 3:BLK], wi[:, kc, 2048 + cb * 512:2048 + (cb + 1) * 512], start=(kc == 0), stop=(kc == 7))
                    c3 = (gU, E_, dtm)[cb % 3]
                    self.cp("vector", c3[0:3, :], p[0:3, :])
                    self.dma("sync", dr["p_ssm_conv"][li, :, cb * 512:(cb + 1) * 512], c3[0:3, :], "pst")
        for blk in range(nblk if not self.chk(0) else 1):
            body(blk)

    def ssd_sample(self, li, layer, bs, wi, wo, Sbufs):
        dr, cst, pk = self.dr, self.cst, self.pk
        sb = lambda n, s, d: self.sb(bs, n, s, d)
        xbs, prj = self.sample_common(bs, layer, wi, SSM_IN)
        id16f = cst[:, C_ID16:C_ID16 + 256].rearrange("p (a b) -> p a b", a=16)
        ident16 = self.ident[0:NS, 0:NS]
        cw = sb("cw", [NS, 4, 512], F32)
        cbuf = sb("cbuf", [NS, 3, 512], F32)
        cbv = sb("cbv", [NS, 512], F32)
        ctm = sb("ctm", [NS, 512], F32)
        t1 = sb("st1", [NS, 512], F32)
        xbc = sb("xbc", [NS, 3072], F32)
        vv = sb("vv", [NS, 2048], F32)
        y_ = sb("ys", [NS, 2048], F32)
        sm = sb("sms", [NS, 256], F32)
        egm = sb("egm", [NS, NS, 32], F32)
        egB = sb("egB", [128, 512], F32)
        CTs = sb("CTs", [128, 4, NS], F32)
        CTm = sb("CTm", [128, 4, NS, NS], F32)
        Bmb = [sb("Bmb0", [NS, 512], F32)] * 2
        for pc in range(6):
            c0 = pc * 512
            self.dma("sync", cw[:], dr["ssm_conv_w"][li, :, c0:c0 + 512].partition_broadcast(NS), "cwld")
            self.dma("sync", cbv[:], dr["ssm_conv_b"][li, c0:c0 + 512].partition_broadcast(NS), "cwld")
            self.dma("sync", cbuf[:], dr["state_ssm_conv"][li, :, :, c0:c0 + 512], "cbld")
            self.tt("vector", ctm[:], prj[:, 2048 + c0:2048 + c0 + 512], cw[:, 3, :], ALU.mult)
            self.tt("vector", ctm[:], ctm[:], cbv[:], ALU.add)
            for tp in range(3):
                self.tt("gpsimd", t1[:], cbuf[:, tp, :], cw[:, tp, :], ALU.mult)
                self.tt("vector", ctm[:], ctm[:], t1[:], ALU.add)
            self.act(xbc[:, c0:c0 + 512], ctm[:], AF.Silu)
            self.dma("sync", dr["s_ssm_conv"][li, :, 0:2, c0:c0 + 512], cbuf[:, 1:3, :], "cvst")
        self.dma("sync", dr["s_ssm_conv"][li, :, 2, :], prj[:, 2048:5120], "cvst")
        dtr = sm[:, 0:32]
        tmp = sm[:, 32:64]
        dt_ = sm[:, 64:96]
        g_ = sm[:, 96:128]
        eg = sm[:, 128:160]
        ss = sm[:, 160:164]
        self.tt("vector", dtr, prj[:, 5120:5152], pk[0:NS, PK[f"sdtb{li}"]:PK[f"sdtb{li}"] + 32], ALU.add)
        self.softplus16(dt_, dtr, tmp)
        self.tt("vector", g_, dt_, self.negA[0:NS, 8 + li * 32:8 + (li + 1) * 32], ALU.mult)
        self.act(eg, g_, AF.Exp)
        xs3 = xbc[:, 0:2048].rearrange("p (h d) -> p h d", h=32)
        self.tt("vector", vv[:].rearrange("p (h d) -> p h d", h=32), xs3, dt_.unsqueeze(2).to_broadcast([NS, 32, 64]), ALU.mult)
        p = self.ps()
        for g in range(4):
            self.tr(p[:, g * NS:(g + 1) * NS], xbc[:, 2560 + g * 128:2560 + (g + 1) * 128], ident16)
        self.cp("vector", CTs[:], p[:, 0:4 * NS].rearrange("p (g t) -> p g t", g=4))
        for g in range(4):
            self.tt("vector" if g % 2 else "gpsimd", CTm[:, g, :, :], CTs[:, g, :].unsqueeze(1).to_broadcast([128, NS, NS]), id16f, ALU.mult)
        self.tt("vector", egm[:], eg.unsqueeze(1).to_broadcast([NS, NS, 32]), ident16.unsqueeze(2).to_broadcast([NS, NS, 32]), ALU.mult)
        pe = self.ps()
        self.mm(pe[:, :], self.onesf[0:NS, :], egm[:].rearrange("p a h -> p (a h)"))
        self.cp("vector", egB[:], pe[:, :])
        psy = self.psf[0:4]
        k = 0
        for b in range(NS):
            Sb = Sbufs[b % 3]
            self.dma("sync", Sb.rearrange("p (h d) -> p h d", h=32), dr["state_ssm"][li, b].rearrange("h n d -> n h d"), f"sld{b % 3}")
            bm = Bmb[b % 2]
            self.ts("vector", bm[:], xbc[:, 2048:2560], ident16[:, b:b + 1], ALU.mult)
            self.tt("gpsimd", Sb.rearrange("p (h d) -> p h d", h=32), Sb.rearrange("p (h d) -> p h d", h=32),
                    egB[:, b * 32:(b + 1) * 32].unsqueeze(2).to_broadcast([128, 32, 64]), ALU.mult)
            for g in range(4):
                gs = slice(g * 512, (g + 1) * 512)
                pu = self.psf[4 + k % 2]
                k += 1
                self.mm(pu[:, :], bm[:, g * 128:(g + 1) * 128], vv[:, gs])
                self.tt("vector", Sb[:, gs], Sb[:, gs], pu[:, :], ALU.add)
                self.mm(psy[g][0:NS, :], CTm[:, g, b, :], Sb[:, gs], start=(b == 0), stop=(b == NS - 1))
            self.dma("sync", dr["s_ssm"][li, b].rearrange("h n d -> n h d"), Sb.rearrange("p (h d) -> p h d", h=32), f"sst{b % 3}")
        for g in range(4):
            self.cp("vector" if g % 2 else "scalar", y_[:, g * 512:(g + 1) * 512], psy[g][0:NS, :])
        y3 = y_[:].rearrange("p (h d) -> p h d", h=32)
        vv3 = vv[:].rearrange("p (h d) -> p h d", h=32)
        self.tt("gpsimd", vv3, xs3, pk[0:NS, PK[f"sDrep{li}"]:PK[f"sDrep{li}"] + 32].unsqueeze(2).to_broadcast([NS, 32, 64]), ALU.mult)
        self.tt("vector", y_[:], y_[:], vv[:], ALU.add)
        for q in range(4):
            self.act(vv[:, q * 512:(q + 1) * 512], prj[:, q * 512:(q + 1) * 512], AF.Silu)
        self.tt("vector", y_[:], y_[:], vv[:], ALU.mult)
        self.tt("gpsimd", vv[:], y_[:], y_[:], ALU.mult)
        self.red("vector", ss, vv[:].rearrange("p (g c) -> p g c", g=4))
        self.act(ss, ss, AF.Sqrt, bias=EPS, scale=1.0 / 512)
        self.recip(ss, ss)
        self.tt("vector", y_[:].rearrange("p (g c) -> p g c", g=4), y_[:].rearrange("p (g c) -> p g c", g=4),
                ss.unsqueeze(2).to_broadcast([NS, 4, 512]), ALU.mult)
        mixTs = egB[:].bitcast(BF16)[:, 0:256].rearrange("p (k t) -> p k t", k=16)
        for k0 in (0, 8):
            p = self.ps()
            for kk in range(8):
                self.tr(p[:, kk * NS:(kk + 1) * NS], y_[:, (k0 + kk) * 128:(k0 + kk + 1) * 128], ident16)
            self.cp("vector", mixTs[:, k0:k0 + 8, :], p[:, 0:8 * NS].rearrange("p (k t) -> p k t", k=8))
        po = self.ps()
        for dc in range(8):
            wb = wo[dc % 2][:].rearrange("p (k d) -> p k d", k=16)
            self.dma("sync", wb, self.wosd[dc], f"wo{dc % 2}")
            for kc in range(16):
                self.mm(po[:, dc * NS:(dc + 1) * NS], wb[:, kc, :], mixTs[:, kc, :], start=(kc == 0), stop=(kc == 15))
        self.tt("vector", xbs[:], po[:, 0:8 * NS].rearrange("p (k t) -> p k t", k=8), xbs[:], ALU.add)
        self.dma("sync", dr["xs"][:, :, SEQ:NTOK], xbs[:], "xst")

    def mlp_phase(self, layer, last):
        dr, cst, pk = self.dr, self.cst, self.pk
        with contextlib.ExitStack() as ph:
            sb = lambda n, s, d: self.sb(ph, n, s, d)
            xall = sb("xall", [128, 8, NTOK], F32)
            hall = sb("hall", [128, 8, NTOK], BF16)
            sq = sb("msq", [128, 8, 512], BF16)
            rstd = sb("mrstd", [128, 512], F32)
            w1 = [sb(f"w1_{i}", [128, 8, 512], BF16) for i in range(2)]
            w2 = [sb(f"w2_{i}", [128, 4, D], BF16) for i in range(2)]
            rl = [sb(f"rl{i}", [128, 512], BF16) for i in range(2)]
            aT = [sb(f"aT{i}", [128, 4, 512], BF16) for i in range(2)]
            for dc in range(8):
                self.dma("sync", xall[:, dc, :], dr["xs"][:, dc, :], "xall")
            tbs = [(i * 512, 512) for i in range(4)] + [(SEQ, NS)]

            def loadw(fb):
                b = fb % 2
                self.dma("gpsimd", w1[b][:], dr["mlp_w1"][layer, :, fb * 512:(fb + 1) * 512].rearrange("(kc p) f -> p kc f", p=128), f"w1_{b}")
                self.dma("gpsimd", w2[b][:], dr["mlp_w2"][layer, fb * 512:(fb + 1) * 512, :].rearrange("(fc p) d -> p fc d", p=128), f"w2_{b}")
            import os
            KM = os.environ.get("KMLP", "full")
            if KM != "load":
                loadw(0)
                for (t0, nt) in tbs:
                    self.norm_block(xall[:, :, t0:t0 + nt], hall[:, :, t0:t0 + nt], sq, rstd, nt, PK["nmlp"] + layer * 8)
            it = 0
            nfb = {"load": 0, "norm": 0, "fb1": 1, "fb1f": 1, "fb2": 2, "fb3": 3}.get(KM, 8)
            if KM in ("load", "norm", "fb1", "fb2", "fb3"):
                last = False
            for fb in range(nfb):
                if fb + 1 < nfb:
                    loadw(fb + 1)
                b = fb % 2
                for (t0, nt) in tbs:
                    a = aT[it % 2]
                    it += 1
                    for fc in range(4):
                        p = self.ps()
                        for kc in range(8):
                            self.mm(p[:, :nt], w1[b][:, kc, fc * 128:(fc + 1) * 128], hall[:, kc, t0:t0 + nt], start=(kc == 0), stop=(kc == 7))
                        r_ = rl[fc % 2]
                        self.act(r_[:, :nt], p[:, :nt], AF.Relu)
                        self.tt("gpsimd", a[:, fc, :nt], r_[:, :nt], r_[:, :nt], ALU.mult)
                    for dc in range(8):
                        p = self.ps()
                        for fc in range(4):
                            self.mm(p[:, :nt], w2[b][:, fc, dc * 128:(dc + 1) * 128], a[:, fc, :nt], start=(fc == 0), stop=(fc == 3))
                        self.tt("vector", xall[:, dc, t0:t0 + nt], p[:, :nt], xall[:, dc, t0:t0 + nt], ALU.add)
            if not last:
                for dc in range(8):
                    self.dma("sync", dr["xs"][:, dc, :], xall[:, dc, :], "xall_st")
            if last:
                if self.debug:
                    for dc in range(8):
                        self.dma("sync", dr["xs"][:, dc, :], xall[:, dc, :], "xall_st")
                yst = [sb(f"yst{i}", [128, D], F32) for i in range(2)]
                yT = [sb(f"yT{i}", [128, 8, 128], F32) for i in range(2)]
                k = 0
                for (t0, nt) in tbs:
                    self.norm_block_f32(xall[:, :, t0:t0 + nt], sq, rstd, nt, PK["nfin"], yT, yst, t0)

    def norm_block_f32(self, xb, sq, rstd, ntok, wcol, yT, yst, t0):
        dr = self.dr
        for dc in range(8):
            self.act(sq[:, dc, :ntok], xb[:, dc, :ntok], AF.Square)
        p = self.ps()
        for dc in range(8):
            self.mm(p[:, :ntok], self.onesb[:], sq[:, dc, :ntok], start=(dc == 0), stop=(dc == 7))
        self.act(rstd[:, :ntok], p[:, :ntok], AF.Sqrt, bias=EPS, scale=1.0 / D)
        self.recip(rstd[:, :ntok], rstd[:, :ntok])
        ntile = (ntok + 127) // 128
        for ti in range(ntile):
            n = min(128, ntok - ti * 128)
            k = (t0 // 128 + ti) % 2
            y_, ys = yT[k], yst[k]
            for dc in range(8):
                self.stt("vector" if dc % 2 else "gpsimd", y_[:, dc, :n], xb[:, dc, ti * 128:ti * 128 + n], self.pk[:, wcol + dc:wcol + dc + 1],
                         rstd[:, ti * 128:ti * 128 + n], ALU.mult, ALU.mult)
            for half in range(2):
                p = self.ps()
                for q in range(4):
                    dc = half * 4 + q
                    self.tr(p[0:n, q * 128:(q + 1) * 128], y_[:, dc, :n], self.ident)
                self.cp("vector" if half == 0 else "scalar", ys[0:n, half * 512:(half + 1) * 512], p[0:n, :])
            if t0 < SEQ:
                self.dma("sync", dr["y_prompt"][t0 + ti * 128:t0 + ti * 128 + n, :], ys[0:n, :], f"yout{k}")
            else:
                self.dma("sync", dr["y_sample"][0:n, :], ys[0:n, :], f"yout{k}")


_CACHE = {}


def _prep_inputs(inp):
    pk = make_pk(inp)
    cst = make_consts()
    rope = make_rope()
    shared = {k: np.ascontiguousarray(inp[k], dtype=np.float32) for k in
              ("w_in_hyb", "w_out_hyb", "w_in_ssm", "w_out_ssm", "mlp_w1", "mlp_w2", "gdn_conv_w", "ssm_conv_w", "ssm_conv_b")}
    maps = []
    for c in range(NCORES):
        m = dict(shared)
        m["pk"] = pk
        m["cst"] = cst
        m["rope"] = rope
        m["x_prompt"] = np.ascontiguousarray(inp["x_prompt"][c])
        m["x_sample"] = np.ascontiguousarray(inp["x_sample"][c * NS:(c + 1) * NS, 0])
        for k in ("state_gdn", "state_gdn_conv", "state_ret", "state_ssm", "state_ssm_conv"):
            m[k] = np.ascontiguousarray(inp[k][:, c * NS:(c + 1) * NS])
        maps.append(m)
    return maps


def kernel(**inp):
    if "nc" not in _CACHE:
        _CACHE["nc"] = Builder().build()
    nc = _CACHE["nc"]
    maps = _prep_inputs(inp)
    res = run_bass_kernel_spmd(nc, maps, core_ids=list(range(NCORES)))
    R = res.results
    y_prompt = np.stack([R[c]["y_prompt"] for c in range(NCORES)], 0)
    y_sample = np.concatenate([R[c]["y_sample"] for c in range(NCORES)], 0)[:, None, :]

    def pcat(name):
        return np.stack([R[c][name] for c in range(NCORES)], 1)

    def scat(name):
        return np.concatenate([R[c][name] for c in range(NCORES)], 1)
    return (y_prompt, y_sample, pcat("p_gdn"), pcat("p_gdn_conv"), pcat("p_ret"), pcat("p_ssm"), pcat("p_ssm_conv"),
            scat("s_gdn"), scat("s_gdn_conv"), scat("s_ret"), scat("s_ssm"), scat("s_ssm_conv"))
```

```python
import contextlib
import math
import numpy as np
import concourse.bass as bass
import concourse.mybir as mybir
from concourse.bass_utils import run_bass_kernel_spmd

F32 = mybir.dt.float32
BF16 = mybir.dt.bfloat16
ALU = mybir.AluOpType
AF = mybir.ActivationFunctionType
AX = mybir.AxisListType

ENGS = ("sync", "scalar", "vector", "gpsimd", "tensor")
EPOCH = 30000
NCORES = 8
SEQ = 2048
NS = 16
NTOK = SEQ + NS
D = 1024
EPS = 1e-6
HYB_IN = 3592
SSM_IN = 5152
BLK = 256


def _esize(dt):
    return 2 if dt == BF16 else 4


def _box(ap):
    dims = ap.ap
    off = ap.offset
    sp = str(ap.space)
    es = _esize(ap.dtype)
    if sp in ("SB", "PSUM"):
        ps = dims[0][0]
        if ps == 0:
            ps = 1 << 30
        p0 = off // ps
        p1 = p0 + dims[0][1]
        f0 = off % ps
        ext = 1
        for st, cnt in dims[1:]:
            ext += (cnt - 1) * abs(st)
        if sp == "PSUM":
            return (sp + ap.name, (p0 // 32) * 32, ((p1 + 31) // 32) * 32, 0, 2048)
        return (sp + ap.name, p0, p1, f0 * es, (f0 + ext) * es)
    ext = 1
    for st, cnt in dims:
        ext += (cnt - 1) * abs(st)
    return (sp + ap.name, 0, 1, off * es, (off + ext) * es)


class Sched:
    def __init__(self, nc):
        self.nc = nc
        self.ops = []
        self.hist = {}
        self.chans = {}
        self.last_eng = {}
        self.pending_barrier = None

    def barrier(self):
        self.pending_barrier = (dict(self.last_eng), dict(self.last_chan_op()))
        self.barrier_seen = set()
        self.hist = {}

    def last_chan_op(self):
        d = {}
        for i, o in enumerate(self.ops):
            if o["chan"] is not None:
                d[o["chan"]] = i
        return d

    def add(self, eng, fn, reads=(), writes=(), chan=None):
        idx = len(self.ops)
        deps = {}
        rb = [_box(a) for a in reads]
        wb = [_box(a) for a in writes]
        for b in rb:
            isps = b[0].startswith("PSUM")
            for r in self.hist.get(b[0], ()):
                if r[0] < b[2] and b[1] < r[1] and r[2] < b[4] and b[3] < r[3]:
                    if r[4]:
                        deps[r[5]] = True
                    elif isps and r[5] < idx and self.ops[r[5]]["eng"] != eng:
                        deps[r[5]] = True
        for b in wb:
            for r in self.hist.get(b[0], ()):
                if r[0] < b[2] and b[1] < r[1] and r[2] < b[4] and b[3] < r[3]:
                    deps.setdefault(r[5], False)
        isdma = chan is not None
        if self.pending_barrier is not None and eng not in self.barrier_seen:
            self.barrier_seen.add(eng)
            le, lc = self.pending_barrier
            for e2, i2 in le.items():
                deps[i2] = True
            for c2, i2 in lc.items():
                deps[i2] = True
        for b in wb:
            lst = self.hist.setdefault(b[0], [])
            lst[:] = [r for r in lst if not (b[1] <= r[0] and r[1] <= b[2] and b[3] <= r[2] and r[3] <= b[4])]
            lst.append((b[1], b[2], b[3], b[4], True, idx))
        for b in rb:
            lst = self.hist.setdefault(b[0], [])
            if not isdma:
                lst[:] = [r for r in lst if not ((not r[4]) and r[5] < idx and self.ops[r[5]]["eng"] == eng and self.ops[r[5]]["chan"] is None
                                                 and b[1] <= r[0] and r[1] <= b[2] and b[3] <= r[2] and r[3] <= b[4])]
            lst.append((b[1], b[2], b[3], b[4], False, idx))
        if isdma:
            self.chans[chan] = self.chans.get(chan, 0) + 1
        else:
            self.last_eng[eng] = idx
        self.ops.append(dict(eng=eng, fn=fn, deps=deps, chan=chan))
        return idx

    def emit(self):
        nc = self.nc
        ops = self.ops
        need = [False] * len(ops)
        for c, o in enumerate(ops):
            kept = {}
            for p, raw in o["deps"].items():
                po = ops[p]
                if po["chan"] is None and po["eng"] == o["eng"] and o["chan"] is None:
                    if o["eng"] == "tensor":
                        continue
                kept[p] = raw
                need[p] = True
            o["deps"] = kept
        sigidx = {}
        cnt = {e: 0 for e in ENGS}
        for i, o in enumerate(ops):
            if o["chan"] is None and need[i]:
                cnt[o["eng"]] += 1
                sigidx[i] = cnt[o["eng"]]
        nep = {e: (cnt[e] + EPOCH - 1) // EPOCH for e in ENGS}
        self.cnt = cnt
        with contextlib.ExitStack() as st:
            esem = {e: [st.enter_context(nc.semaphore(f"s_{e}_{j}")) for j in range(max(1, nep[e]))] for e in ENGS}
            csem = {c: st.enter_context(nc.semaphore(f"c_{c}")) for c in self.chans}
            waited = {e: {} for e in ENGS}
            chan_issued = {c: 0 for c in self.chans}
            chan_tgt = {c: 0 for c in self.chans}
            plan = {e: [] for e in ENGS}
            for i, o in enumerate(ops):
                E = o["eng"]
                w = {}
                for p in o["deps"]:
                    po = ops[p]
                    if po["chan"] is None:
                        s = sigidx[p]
                        key = ("e", po["eng"], (s - 1) // EPOCH)
                        val = (s - 1) % EPOCH + 1
                    else:
                        ch = po["chan"]
                        tgt = chan_issued[ch]
                        chan_tgt[ch] = max(chan_tgt[ch], tgt)
                        key = ("c", ch)
                        val = 16 * tgt
                    if val > w.get(key, 0):
                        w[key] = val
                if o["chan"] is not None:
                    ch = o["chan"]
                    if chan_tgt[ch] > 0:
                        key = ("c", ch)
                        w[key] = max(w.get(key, 0), 16 * chan_tgt[ch])
                    chan_issued[ch] += 1
                waits = []
                for key, val in w.items():
                    if val > waited[E].get(key, 0):
                        waited[E][key] = val
                        sem = csem[key[1]] if key[0] == "c" else esem[key[1]][key[2]]
                        waits.append((sem, val))
                sig = None
                if o["chan"] is not None:
                    sig = (csem[o["chan"]], 16)
                elif i in sigidx:
                    s = sigidx[i]
                    sig = (esem[E][(s - 1) // EPOCH], 1)
                plan[E].append((o["fn"], waits, sig))
            fin = [(csem[ch], 16 * n) for ch, n in chan_issued.items() if n]
            self.n_instr = {e: len(plan[e]) for e in ENGS}
            with nc.Block() as block:
                def mk(E):
                    def body(eng):
                        for fn, waits, sig in plan[E]:
                            for sem, val in waits:
                                eng.wait_ge(sem, val)
                            ins = fn(eng)
                            if sig is not None:
                                ins.then_inc(sig[0], sig[1])
                        if E == "sync":
                            for sem, val in fin:
                                eng.wait_ge(sem, val)
                    return body
                block.sync(mk("sync"))
                block.scalar(mk("scalar"))
                block.vector(mk("vector"))
                block.gpsimd(mk("gpsimd"))
                block.tensor(mk("tensor"))


C_IDENT, C_U, C_NEG, C_INCL, C_STRICT, C_ONES, C_NONES, C_PT = [i * 128 for i in range(8)]
C_DMT = 8 * 128
C_QDEC = C_DMT + 512
C_KDEC = C_QDEC + 512
C_ID16 = C_KDEC + 256
C_ROPES = C_ID16 + 256
C_G128 = C_ROPES + 64
NCST = C_G128 + 4


def make_consts():
    c = np.zeros((128, NCST), np.float64)
    j = np.arange(128)[:, None]
    i = np.arange(128)[None, :]
    c[:, C_IDENT:C_IDENT + 128] = (j == i)
    c[:, C_U:C_U + 128] = (j <= i)
    c[:, C_NEG:C_NEG + 128] = np.where(i >= j, 0.0, -1e30)
    c[:, C_INCL:C_INCL + 128] = (i >= j)
    c[:, C_STRICT:C_STRICT + 128] = (i > j)
    c[:, C_ONES:C_ONES + 128] = 1.0
    c[:, C_NONES:C_NONES + 128] = -1.0
    PT = np.zeros((128, 128))
    for m in range(128):
        if m % 64 < 32:
            PT[m + 32, m] = -1.0
        else:
            PT[m - 32, m] = 1.0
    c[:, C_PT:C_PT + 128] = PT
    gam = 1.0 - 2.0 ** (-5.0 - np.arange(4))
    for h in range(4):
        c[:, C_DMT + h * 128:C_DMT + (h + 1) * 128] = np.where(i >= j, gam[h] ** np.maximum(i - j, 0), 0.0)
        c[:, C_KDEC + h * 64:C_KDEC + (h + 1) * 64] = (gam[h] ** (127 - j))
    for h in range(4):
        c[:, C_QDEC + h * 128:C_QDEC + (h + 1) * 128] = gam[h] ** (i + 1)
    c[:, C_ID16:C_ID16 + 256] = np.eye(16).reshape(1, 256)
    half = 32
    inv = (10000.0 ** (-np.arange(half, dtype=np.float32) / half)).astype(np.float32)
    ang = (np.float32(16384.0) * inv).astype(np.float32).astype(np.float64)
    c[:, C_ROPES:C_ROPES + 32] = np.cos(ang)[None, :]
    c[:, C_ROPES + 32:C_ROPES + 64] = np.sin(ang)[None, :]
    c[:, C_G128:C_G128 + 4] = gam[None, :] ** 128
    return c.astype(np.float32)


GAM = [1.0 - 2.0 ** (-5.0 - h) for h in range(4)]


def make_rope():
    half = 32
    inv = (10000.0 ** (-np.arange(half, dtype=np.float32) / half)).astype(np.float32)
    pos = np.concatenate([np.arange(SEQ), np.full(NS, 16384)]).astype(np.float32)
    ang = (pos[None, :] * inv[:, None]).astype(np.float32).astype(np.float64)
    r = np.zeros((2, 128, NTOK), np.float32)
    for p in range(128):
        r[0, p] = np.cos(ang[p % 32])
        r[1, p] = np.sin(ang[p % 32])
    return r


PK = {}


def _pk_layout():
    off = 0

    def put(name, n):
        nonlocal off
        PK[name] = off
        off += n
    put("nmix", 32)
    put("nmlp", 32)
    put("nfin", 8)
    for i in range(2):
        put(f"gcw{i}", 48)
        put(f"gdtb{i}", 4)
        put(f"galog{i}", 4)
        put(f"gnw{i}", 1)
        put(f"rnw{i}", 4)
        put(f"scw{i}", 96)
        put(f"scb{i}", 24)
        put(f"sdtb{i}", 32)
        put(f"salog{i}", 32)
        put(f"sD{i}", 16)
        put(f"sDrep{i}", 32)
        put(f"snw{i}", 16)
    return off


NPK = _pk_layout()


def make_pk(inp):
    pk = np.zeros((128, NPK), np.float32)

    def fm(v, nch):
        return np.ascontiguousarray(v.reshape(nch, 128).T)
    for l in range(4):
        pk[:, PK["nmix"] + l * 8:PK["nmix"] + (l + 1) * 8] = fm(inp["norm_mix"][l], 8)
        pk[:, PK["nmlp"] + l * 8:PK["nmlp"] + (l + 1) * 8] = fm(inp["norm_mlp"][l], 8)
    pk[:, PK["nfin"]:PK["nfin"] + 8] = fm(inp["norm_final"], 8)
    for i in range(2):
        cw = inp["gdn_conv_w"][i]
        pk[:, PK[f"gcw{i}"]:PK[f"gcw{i}"] + 48] = cw.reshape(4, 12, 128).transpose(2, 1, 0).reshape(128, 48)
        pk[:, PK[f"gdtb{i}"]:PK[f"gdtb{i}"] + 4] = inp["gdn_dt_bias"][i][None, :]
        pk[:, PK[f"galog{i}"]:PK[f"galog{i}"] + 4] = inp["gdn_a_log"][i][None, :]
        pk[:, PK[f"gnw{i}"]] = inp["gdn_norm_w"][i]
        pk[:, PK[f"rnw{i}"]:PK[f"rnw{i}"] + 4] = fm(inp["ret_norm_w"][i], 4)
        sw = inp["ssm_conv_w"][i]
        pk[:, PK[f"scw{i}"]:PK[f"scw{i}"] + 96] = sw.reshape(4, 24, 128).transpose(2, 1, 0).reshape(128, 96)
        pk[:, PK[f"scb{i}"]:PK[f"scb{i}"] + 24] = fm(inp["ssm_conv_b"][i], 24)
        pk[:, PK[f"sdtb{i}"]:PK[f"sdtb{i}"] + 32] = inp["ssm_dt_bias"][i][None, :]
        pk[:, PK[f"salog{i}"]:PK[f"salog{i}"] + 32] = inp["ssm_a_log"][i][None, :]
        pk[:, PK[f"sD{i}"]:PK[f"sD{i}"] + 16] = np.repeat(inp["ssm_d"][i].reshape(16, 2), 64, axis=1).T
        pk[:, PK[f"sDrep{i}"]:PK[f"sDrep{i}"] + 32] = inp["ssm_d"][i][None, :]
        pk[:, PK[f"snw{i}"]:PK[f"snw{i}"] + 16] = fm(inp["ssm_norm_w"][i], 16)
    return pk


class StopBuild(Exception):
    pass


class Builder:
    def chk(self, k):
        import os
        return int(os.environ.get("KH", "99")) == k

    def __init__(self, depth=4, do_sample=True, debug=False):
        self.depth = depth
        self.do_sample = do_sample
        self.debug = debug
        nc = bass.Bass("TRN2", target_bir_lowering=False)
        self.nc = nc
        self.S = Sched(nc)
        self._psi = 0
        self._u = 0

    def mm(self, out, lhsT, rhs, start=True, stop=True):
        self.S.add("tensor", lambda e: e.matmul(out, lhsT=lhsT, rhs=rhs, start=start, stop=stop),
                   reads=[lhsT, rhs], writes=[out])

    def tr(self, out, in_, ident):
        self.S.add("tensor", lambda e: e.transpose(out=out, in_=in_, identity=ident), reads=[in_, ident], writes=[out])

    def act(self, out, in_, func, bias=None, scale=None):
        if func == AF.Sqrt:
            self.act(out, in_, AF.Ln, bias=bias, scale=scale)
            self.act(out, out, AF.Exp, scale=-0.5)
            return
        rd = [in_]
        kw = {}
        if bias is not None:
            kw["bias"] = bias
            if not isinstance(bias, (int, float)):
                rd.append(bias)
        if scale is not None:
            kw["scale"] = scale
            if not isinstance(scale, (int, float)):
                rd.append(scale)
        self.S.add("scalar", lambda e: e.activation(out=out, in_=in_, func=func, **kw), reads=rd, writes=[out])

    def tt(self, eng, out, in0, in1, op):
        self.S.add(eng, lambda e: e.tensor_tensor(out=out, in0=in0, in1=in1, op=op), reads=[in0, in1], writes=[out])

    def ts(self, eng, out, in0, s1, op0, s2=None, op1=None):
        rd = [in0]
        if not isinstance(s1, (int, float)):
            rd.append(s1)
        if s2 is not None and not isinstance(s2, (int, float)):
            rd.append(s2)
        if op1 is None:
            self.S.add(eng, lambda e: e.tensor_scalar(out=out, in0=in0, scalar1=s1, scalar2=None, op0=op0), reads=rd, writes=[out])
        else:
            self.S.add(eng, lambda e: e.tensor_scalar(out=out, in0=in0, scalar1=s1, scalar2=s2, op0=op0, op1=op1), reads=rd, writes=[out])

    def stt(self, eng, out, in0, scalar, in1, op0, op1):
        rd = [in0, in1]
        if not isinstance(scalar, (int, float)):
            rd.append(scalar)
        self.S.add("vector", lambda e: e.scalar_tensor_tensor(out=out, in0=in0, scalar=scalar, in1=in1, op0=op0, op1=op1),
                   reads=rd, writes=[out])

    def cp(self, eng, out, in_):
        if eng == "scalar":
            self.S.add("scalar", lambda e: e.copy(out=out, in_=in_), reads=[in_], writes=[out])
        else:
            self.S.add(eng, lambda e: e.tensor_copy(out=out, in_=in_), reads=[in_], writes=[out])

    def recip(self, out, in_):
        return

    def memset(self, eng, out, val):
        self.S.add(eng, lambda e: e.memset(out, val), writes=[out])

    def dma(self, eng, out, in_, chan):
        self.S.add(eng, lambda e: e.dma_start(out=out, in_=in_), reads=[in_], writes=[out], chan=chan)

    def ps(self):
        self._psi = (self._psi + 1) % len(self.psf)
        return self.psf[self._psi]

    def psbf(self):
        self._u = (self._u + 1) % len(self.psb)
        return self.psb[self._u]

    def ve(self):
        self._u2 = getattr(self, "_u2", 0) + 1
        return "vector" if self._u2 % 2 else "gpsimd"

    def sb(self, st, name, shape, dt):
        self._nid = getattr(self, "_nid", 0) + 1
        return st.enter_context(self.nc.sbuf_tensor(f"s{self._nid}_{name}", shape, dt))

    def build(self):
        nc = self.nc
        dr = {}

        def din(name, shape):
            dr[name] = nc.dram_tensor(name, shape, F32, kind="ExternalInput").ap()

        def dout(name, shape):
            dr[name] = nc.dram_tensor(name, shape, F32, kind="ExternalOutput").ap()
        din("x_prompt", [SEQ, D])
        din("x_sample", [NS, D])
        din("state_gdn", [2, NS, 4, 128, 128])
        din("state_gdn_conv", [2, NS, 3, 1536])
        din("state_ret", [2, NS, 4, 64, 128])
        din("state_ssm", [2, NS, 32, 128, 64])
        din("state_ssm_conv", [2, NS, 3, 3072])
        din("w_in_hyb", [2, D, HYB_IN])
        din("w_out_hyb", [2, D, D])
        din("w_in_ssm", [2, D, SSM_IN])
        din("w_out_ssm", [2, 2048, D])
        din("mlp_w1", [4, D, 4096])
        din("mlp_w2", [4, 4096, D])
        din("gdn_conv_w", [2, 4, 1536])
        din("ssm_conv_w", [2, 4, 3072])
        din("ssm_conv_b", [2, 3072])
        din("pk", [128, NPK])
        din("cst", [128, NCST])
        din("rope", [2, 128, NTOK])
        dout("y_prompt", [SEQ, D])
        dout("y_sample", [NS, D])
        dout("p_gdn", [2, 4, 128, 128])
        dout("p_gdn_conv", [2, 3, 1536])
        dout("p_ret", [2, 4, 64, 128])
        dout("p_ssm", [2, 32, 128, 64])
        dout("p_ssm_conv", [2, 3, 3072])
        dout("s_gdn", [2, NS, 4, 128, 128])
        dout("s_gdn_conv", [2, NS, 3, 1536])
        dout("s_ret", [2, NS, 4, 64, 128])
        dout("s_ssm", [2, NS, 32, 128, 64])
        dout("s_ssm_conv", [2, NS, 3, 3072])
        if self.debug:
            dout("xs", [128, 8, NTOK])
        else:
            dr["xs"] = nc.dram_tensor("xs", [128, 8, NTOK], F32, kind="Internal").ap()
        self.dr = dr
        with contextlib.ExitStack() as top:
            self.psf = [top.enter_context(nc.psum_tensor(f"psf{i}", [128, 512], F32)) for i in range(6)]
            self.psb = [top.enter_context(nc.psum_tensor(f"psb{i}", [128, 1024], BF16)) for i in range(2)]
            cst = self.sb(top, "cst", [128, NCST], F32)
            pk = self.sb(top, "pk", [128, NPK], F32)
            self.cst, self.pk = cst, pk
            self.dma("sync", cst[:], dr["cst"], "cst")
            self.dma("sync", pk[:], dr["pk"], "cst")
            self.identb = self.sb(top, "identb", [128, 128], BF16)
            self.onesb = self.sb(top, "onesb", [128, 128], BF16)
            self.PTb = self.sb(top, "PTb", [128, 128], BF16)
            self.cp("vector", self.identb[:], cst[:, C_IDENT:C_IDENT + 128])
            self.cp("vector", self.onesb[:], cst[:, C_ONES:C_ONES + 128])
            self.cp("vector", self.PTb[:], cst[:, C_PT:C_PT + 128])
            self.ident = cst[:, C_IDENT:C_IDENT + 128]
            self.onesf = cst[:, C_ONES:C_ONES + 128]
            self.nonesf = cst[:, C_NONES:C_NONES + 128]
            self.U = cst[:, C_U:C_U + 128]
            self.zcol = self.sb(top, "zcol", [128, 4], F32)
            self.memset("gpsimd", self.zcol[:], 0.0)
            self.negA = self.sb(top, "negA", [128, 72], F32)
            for i in range(2):
                self.act(self.negA[:, i * 4:(i + 1) * 4], pk[:, PK[f"galog{i}"]:PK[f"galog{i}"] + 4], AF.Exp)
                self.act(self.negA[:, 8 + i * 32:8 + (i + 1) * 32], pk[:, PK[f"salog{i}"]:PK[f"salog{i}"] + 32], AF.Exp)
            self.ts("vector", self.negA[:], self.negA[:], -1.0, ALU.mult)

            self.prologue()
            for layer in range(self.depth):
                self.S.barrier()
                import os
                if "mix" in os.environ.get("KSKIP", ""):
                    pass
                elif layer % 2 == 0:
                    self.hybrid_phase(layer // 2, layer)
                else:
                    self.ssd_phase(layer // 2, layer)
                self.S.barrier()
                if "mlp" not in os.environ.get("KSKIP", ""):
                    self.mlp_phase(layer, last=(layer == self.depth - 1))
            self.S.emit()
        return nc

    def prologue(self):
        dr = self.dr
        with contextlib.ExitStack() as st:
            xin = [self.sb(st, f"xin{i}", [128, D], F32) for i in range(2)]
            stg = [self.sb(st, f"xstg{i}", [128, 8, BLK], F32) for i in range(2)]
            for blk in range(SEQ // BLK):
                sg = stg[blk % 2]
                for c in range(2):
                    t = blk * 2 + c
                    xi = xin[t % 2]
                    self.dma("sync", xi[:], dr["x_prompt"][t * 128:(t + 1) * 128, :], f"xin{t % 2}")
                    for half in range(2):
                        p = self.ps()
                        for q in range(4):
                            dc = half * 4 + q
                            self.tr(p[:, q * 128:(q + 1) * 128], xi[:, dc * 128:(dc + 1) * 128], self.ident)
                        src = p[:].rearrange("p (q t) -> p q t", q=4)
                        self.cp("vector" if half == 0 else "scalar", sg[:, half * 4:half * 4 + 4, c * 128:(c + 1) * 128], src)
                self.dma("sync", dr["xs"][:, :, blk * BLK:(blk + 1) * BLK], sg[:], f"xst{blk % 2}")
            xi = xin[0]
            self.dma("sync", xi[0:NS, :], dr["x_sample"], "xin0")
            p = self.ps()
            for dc in range(8):
                self.tr(p[:, dc * NS:(dc + 1) * NS], xi[0:NS, dc * 128:(dc + 1) * 128], self.ident[0:NS, 0:NS])
            sg = stg[0]
            self.cp("vector", sg[:, :, 0:NS], p[:, 0:8 * NS].rearrange("p (q t) -> p q t", q=8))
            self.dma("sync", dr["xs"][:, :, SEQ:NTOK], sg[:, :, 0:NS], "xst0")

    def norm_block(self, xb, hT, sq, rstd, ntok, wcol):
        for dc in range(8):
            self.act(sq[:, dc, :ntok], xb[:, dc, :ntok], AF.Square)
        p = self.ps()
        for dc in range(8):
            self.mm(p[:, :ntok], self.onesb[:], sq[:, dc, :ntok], start=(dc == 0), stop=(dc == 7))
        self.act(rstd[:, :ntok], p[:, :ntok], AF.Sqrt, bias=EPS, scale=1.0 / D)
        self.recip(rstd[:, :ntok], rstd[:, :ntok])
        for dc in range(8):
            self.stt("vector" if dc % 2 else "gpsimd", hT[:, dc, :ntok], xb[:, dc, :ntok], self.pk[:, wcol + dc:wcol + dc + 1],
                     rstd[:, :ntok], ALU.mult, ALU.mult)

    def proj_fm(self, p, wi, col0, ncols, hT, ntok, pcol0=0):
        for kc in range(8):
            self.mm(p[:ncols, pcol0:pcol0 + ntok], wi[:, kc, col0:col0 + ncols], hT[:, kc, :ntok], start=(kc == 0), stop=(kc == 7))

    def hybrid_phase(self, li, layer):
        dr, cst, pk = self.dr, self.cst, self.pk
        with contextlib.ExitStack() as ph:
            sb = lambda n, s, d: self.sb(ph, n, s, d)
            wi = sb("wi", [128, 8, HYB_IN], BF16)
            wo = sb("wo", [128, 8, D], BF16)
            for kc in range(8):
                self.dma("gpsimd", wi[:, kc, :], dr["w_in_hyb"][li, kc * 128:(kc + 1) * 128, :], "wi")
            for kc in range(8):
                self.dma("gpsimd", wo[:, kc, :], dr["w_out_hyb"][li, kc * 128:(kc + 1) * 128, :], "wo")
            for kc in range(8):
                col = PK[f"gnw{li}"] if kc < 4 else PK[f"rnw{li}"] + kc - 4
                self.ts("vector", wo[:, kc, :], wo[:, kc, :], pk[:, col:col + 1], ALU.mult)
            Sg = sb("Sg", [128, 4, 128], F32)
            Sgb = sb("Sgb", [128, 4, 128], BF16)
            Sr = sb("Sr", [64, 4, 128], F32)
            Srb = sb("Srb", [64, 4, 128], BF16)
            hist = sb("hist", [128, 12, 3], F32)
            for t_ in (Sg, Sgb, Sr, Srb, hist):
                self.memset("gpsimd", t_[:], 0.0)
            with contextlib.ExitStack() as bs:
                self.hybrid_prompt(li, layer, bs, wi, wo, Sg, Sgb, Sr, Srb, hist)
            self.dma("sync", dr["p_gdn"][li].rearrange("h k v -> k h v"), Sg[:], "pst")
            self.dma("sync", dr["p_ret"][li].rearrange("h k v -> k h v"), Sr[:], "pst")
            if self.do_sample:
                self.S.barrier()
                with contextlib.ExitStack() as bs:
                    self.hybrid_sample(li, layer, bs, wi, wo)

    def hybrid_prompt(self, li, layer, bs, wi, wo, Sg, Sgb, Sr, Srb, hist):
        dr, cst, pk = self.dr, self.cst, self.pk
        sb = lambda n, s, d: self.sb(bs, n, s, d)
        xb = sb("xb", [128, 8, BLK], F32)
        hT = sb("hT", [128, 8, BLK], BF16)
        sq = sb("sq", [128, 8, BLK], BF16)
        rstd = sb("rstd", [128, BLK], F32)
        rope = sb("rope", [128, 2, BLK], F32)
        ctmp = [sb(f"ctmp{i}", [128, BLK + 3], F32) for i in range(3)]
        cacc = [sb(f"cacc{i}", [128, BLK], F32) for i in range(3)]
        qkf = sb("qkf", [128, 8, BLK], BF16)
        qkT = sb("qkT", [128, 8, BLK], BF16)
        vT = sb("vT", [128, 4, BLK], BF16)
        szT = sb("szT", [128, 4, BLK], BF16)
        sgT = sb("sgT", [128, 4, BLK], BF16)
        rawb = sb("rawb", [64, 8, BLK], BF16)
        rot = sb("rot", [64, 8, BLK], BF16)
        rt1 = [sb(f"rt1{i}", [128, BLK], F32) for i in range(2)]
        rt2 = [sb(f"rt2{i}", [128, BLK], F32) for i in range(2)]
        smallf = sb("smallf", [128, 160], F32)
        v_tok = sb("v_tok", [128, 2, 512], BF16)
        kg_tok = sb("kg_tok", [128, 2, 512], BF16)
        kd_tok = sb("kd_tok", [128, 2, 512], BF16)
        vb_tok = sb("vb_tok", [128, 2, 512], BF16)
        kdr_tok = sb("kdr_tok", [128, 2, 256], BF16)
        decT = sb("decT", [128, 2, 512], F32)
        decS = sb("decS", [128, 512], F32)
        EgB = sb("EgB", [128, 512], F32)
        qgT = sb("qgT", [128, 2, 512], BF16)
        QK = sb("QK", [128, 2, 512], BF16)
        SR = sb("SR", [128, 512], BF16)
        qgr = sb("qgr", [64, 4, 128], BF16)
        NX = [[sb(f"NX{c}{k}", [128, 512], F32) for k in range(2)] for c in range(2)]
        NXT = [[sb(f"NXT{c}{k}", [128, 512], F32) for k in range(2)] for c in range(2)]
        NP = [[sb(f"NP{c}{k}", [128, 512], F32) for k in range(2)] for c in range(2)]
        TTb = sb("TTb", [128, 2, 512], BF16)
        nw0T = sb("nw0T", [128, 2, 512], BF16)
        delta = sb("delta", [128, 512], BF16)
        oT = sb("oT", [128, 512], F32)
        ob16 = sq[:, 2:4, :].rearrange("p a b -> p (a b)")
        osq = sq[:, 0:2, :].rearrange("p a b -> p (a b)")
        orr = sb("orr", [128, 512], F32)
        otmp = sb("otmp", [128, 512], F32)
        gU = orr[:].rearrange("p (h i) -> p h i", h=4)
        dtmp = otmp[:].rearrange("p (h i) -> p h i", h=4)
        mixT = sb("mixT", [128, 8, BLK], BF16)
        beta = smallf[:, 0:8]
        negbeta = smallf[:, 8:16]
        xg = smallf[:, 16:24]
        ax = smallf[:, 24:32]
        g_t = smallf[:, 32:40]
        gcum = smallf[:, 40:48]
        gend = smallf[:, 48:56]
        eg = smallf[:, 56:64]
        wdec = smallf[:, 64:72]
        egend = smallf[:, 72:80]
        nblk = SEQ // BLK
        dtb = pk[:, PK[f"gdtb{li}"]:PK[f"gdtb{li}"] + 4]
        nA = self.negA[:, li * 4:(li + 1) * 4]
        def body(blk):
            t0 = blk * BLK
            self.dma("sync", xb[:], dr["xs"][:, :, t0:t0 + BLK], "xld")
            self.dma("sync", rope[:, 0, :], dr["rope"][0, :, t0:t0 + BLK], "rope")
            self.dma("sync", rope[:, 1, :], dr["rope"][1, :, t0:t0 + BLK], "rope")
            self.norm_block(xb, hT, sq, rstd, BLK, PK["nmix"] + layer * 8)
            for g3 in range(4):
                chs = [g3 * 3 + u for u in range(3)]
                pp = {}
                for ch in chs:
                    pp[ch] = self.ps()
                    self.proj_fm(pp[ch], wi, ch * 128, 128, hT, BLK)
                for ch in chs:
                    self.cp("scalar", ctmp[ch % 3][:, 3:3 + BLK], pp[ch][:, :BLK])
                    self.cp("gpsimd", ctmp[ch % 3][:, 0:3], hist[:, ch, :])
                for ch in chs:
                    wc = PK[f"gcw{li}"] + ch * 4
                    self.ts("vector", cacc[ch % 3][:], ctmp[ch % 3][:, 0:BLK], pk[:, wc:wc + 1], ALU.mult)
                for tp in range(1, 4):
                    for ch in chs:
                        wc = PK[f"gcw{li}"] + ch * 4
                        self.stt("vector", cacc[ch % 3][:], ctmp[ch % 3][:, tp:tp + BLK], pk[:, wc + tp:wc + tp + 1], cacc[ch % 3][:], ALU.mult, ALU.add)
                for ch in chs:
                    self.cp("gpsimd", hist[:, ch, :], ctmp[ch % 3][:, BLK:BLK + 3])
                    if ch < 8:
                        self.act(qkf[:, ch, :], cacc[ch % 3][:], AF.Silu)
                    else:
                        self.act(vT[:, ch - 8, :], cacc[ch % 3][:], AF.Silu)
            if self.chk(1):
                return
            for pr in range(4):
                p = self.ps()
                for u in range(2):
                    ch = pr * 2 + u
                    self.act(sq[:, ch, :], qkf[:, ch, :], AF.Square)
                    self.mm(p[:, u * BLK:(u + 1) * BLK], self.onesb[:], sq[:, ch, :])
                rr = rt1[pr % 2]
                rr2 = rt2[pr % 2]
                self.act(rr[:], p[:, 0:BLK], AF.Sqrt, bias=EPS, scale=1.0)
                self.act(rr2[:], p[:, BLK:2 * BLK], AF.Sqrt, bias=EPS, scale=1.0)
                self.recip(rr[:], rr[:])
                self.recip(rr2[:], rr2[:])
                for u, r_ in ((0, rr), (1, rr2)):
                    ch = pr * 2 + u
                    sc = 128.0 ** -0.5 if ch < 4 else 1.0
                    self.stt(self.ve(), qkT[:, ch, :], qkf[:, ch, :], sc, r_[:], ALU.mult, ALU.mult)
            if self.chk(2):
                return
            for h in range(4):
                p = self.ps()
                self.proj_fm(p, wi, 1536 + h * 128, 128, hT, BLK)
                self.act(szT[:, h, :], p[:, :BLK], AF.Silu)
                p = self.ps()
                self.proj_fm(p, wi, 3080 + h * 128, 128, hT, BLK)
                self.act(sgT[:, h, :], p[:, :BLK], AF.Silu)
            if self.chk(3):
                return
            pbg = self.ps()
            for c in range(2):
                for kc in range(8):
                    self.mm(pbg[:, c * 8:(c + 1) * 8], hT[:, kc, c * 128:(c + 1) * 128], wi[:, kc, 2048:2056], start=(kc == 0), stop=(kc == 7))
            pbg3 = pbg[:, 0:16].rearrange("p (c e) -> p c e", c=2)
            b3 = beta.rearrange("p (c h) -> p c h", c=2)
            self.act(b3, pbg3[:, :, 0:4], AF.Sigmoid)
            self.ts("vector", negbeta, beta, -1.0, ALU.mult)
            xg3 = xg.rearrange("p (c h) -> p c h", c=2)
            self.tt("vector", xg3, pbg3[:, :, 4:8], dtb.unsqueeze(1).to_broadcast([128, 2, 4]), ALU.add)
            self.act(ax, xg, AF.Abs)
            self.act(ax, ax, AF.Exp, scale=-1.0)
            self.act(ax, ax, AF.Ln, bias=1.0, scale=1.0)
            self.stt("vector", g_t, xg, 0.0, ax, ALU.max, ALU.add)
            g3 = g_t.rearrange("p (c h) -> p c h", c=2)
            self.tt("vector", g3, g3, nA.unsqueeze(1).to_broadcast([128, 2, 4]), ALU.mult)
            pg = self.ps()
            self.mm(pg[:, 0:8], self.U, g_t)
            self.mm(pg[:, 8:16], self.onesf, g_t)
            self.cp("vector", gcum, pg[:, 0:8])
            self.cp("vector", gend, pg[:, 8:16])
            self.act(eg, gcum, AF.Exp)
            self.tt("vector", wdec, gend, gcum, ALU.subtract)
            self.act(wdec, wdec, AF.Exp)
            self.act(egend, gend, AF.Exp)
            if self.chk(4):
                return
            for j2 in range(4):
                p = self.ps()
                for u in range(2):
                    j = j2 * 2 + u
                    self.proj_fm(p, wi, 2056 + j * 64, 64, hT, BLK, pcol0=u * BLK)
                self.cp("scalar", rawb[:, j2 * 2:j2 * 2 + 2, :], p[0:64, :].rearrange("p (u t) -> p u t", u=2))
            for j2 in range(4):
                p = self.ps()
                for u in range(2):
                    j = j2 * 2 + u
                    self.mm(p[0:64, u * BLK:(u + 1) * BLK], self.PTb[0:64, 0:64], rawb[:, j, :])
                sc = 1.0 if j2 < 2 else 0.125
                for u in range(2):
                    j = j2 * 2 + u
                    self.stt("vector", rt1[u][0:64, :], rawb[:, j, :], sc, rope[0:64, 0, :], ALU.mult, ALU.mult)
                    self.stt("vector", rt2[u][0:64, :], p[0:64, u * BLK:(u + 1) * BLK], sc, rope[0:64, 1, :], ALU.mult, ALU.mult)
                    self.tt("gpsimd", rot[:, j, :], rt1[u][0:64, :], rt2[u][0:64, :], ALU.add)
            if self.chk(5):
                return
            for c in range(2):
                p = self.ps()
                for kc in range(8):
                    self.mm(p[:, :], hT[:, kc, c * 128:(c + 1) * 128], wi[:, kc, 2568:3080], start=(kc == 0), stop=(kc == 7))
                self.cp("scalar", vb_tok[:, c, :], p[:, :])
            if self.chk(6):
                return
            for c in range(2):
                cs = slice(c * 128, (c + 1) * 128)
                pb = self.psbf()
                for h in range(4):
                    self.tr(pb[:, h * 128:(h + 1) * 128], qkT[:, 4 + h, cs], self.identb[:])
                src = pb[:, 0:512].rearrange("p (h k) -> p h k", h=4)
                self.tt("vector", kg_tok[:, c, :].rearrange("p (h k) -> p h k", h=4), src,
                        eg[:, c * 4:(c + 1) * 4].unsqueeze(2).to_broadcast([128, 4, 128]), ALU.mult)
                self.tt("vector", kd_tok[:, c, :].rearrange("p (h k) -> p h k", h=4), src,
                        wdec[:, c * 4:(c + 1) * 4].unsqueeze(2).to_broadcast([128, 4, 128]), ALU.mult)
                for h in range(4):
                    self.tr(pb[:, 512 + h * 128:512 + (h + 1) * 128], vT[:, h, cs], self.identb[:])
                self.cp("scalar", v_tok[:, c, :], pb[:, 512:1024])
            if self.chk(7):
                return
            for c in range(2):
                cs = slice(c * 128, (c + 1) * 128)
                self.tt("vector", gU[:], self.U.unsqueeze(1).to_broadcast([128, 4, 128]),
                        g_t[:, c * 4:(c + 1) * 4].unsqueeze(2).to_broadcast([128, 4, 128]), ALU.mult)
                pA = self.ps()
                pD = self.ps()
                for h in range(4):
                    hs = slice(h * 128, (h + 1) * 128)
                    self.mm(pD[:, hs], self.onesf, gU[:, h, :], start=True, stop=False)
                    self.mm(pD[:, hs], gU[:, h, :], self.nonesf, start=False, stop=True)
                self.mm(pA[:, :], self.onesf, gU[:].rearrange("p h i -> p (h i)"))
                self.act(EgB[:], pA[:], AF.Exp)
                self.tt("vector", dtmp[:], pD[:].rearrange("p (h i) -> p h i", h=4),
                        cst[:, C_NEG:C_NEG + 128].unsqueeze(1).to_broadcast([128, 4, 128]), ALU.add)
                self.act(decT[:, c, :], dtmp[:].rearrange("p h i -> p (h i)"), AF.Exp)
                self.tt("vector", decS[:].rearrange("p (h i) -> p h i", h=4), decT[:, c, :].rearrange("p (h i) -> p h i", h=4),
                        cst[:, C_STRICT:C_STRICT + 128].unsqueeze(1).to_broadcast([128, 4, 128]), ALU.mult)
                self.tt("vector", qgT[:, c, :].rearrange("p (h i) -> p h i", h=4), qkT[:, 0:4, cs],
                        EgB[:].rearrange("p (h i) -> p h i", h=4), ALU.mult)
                pK = self.ps()
                pQ = self.ps()
                for h in range(4):
                    hs = slice(h * 128, (h + 1) * 128)
                    self.mm(pK[:, hs], qkT[:, 4 + h, cs], qkT[:, 4 + h, cs])
                    self.mm(pQ[:, hs], qkT[:, 4 + h, cs], qkT[:, h, cs])
                for h in range(4):
                    hs = slice(h * 128, (h + 1) * 128)
                    self.stt("vector", NX[c][0][:, hs], pK[:, hs], negbeta[:, c * 4 + h:c * 4 + h + 1], decS[:, hs], ALU.mult, ALU.mult)
                self.tt("vector", QK[:, c, :], pQ[:], decT[:, c, :], ALU.mult)
            if self.chk(8):
                return
            for c in range(2):
                pT = self.ps()
                for h in range(4):
                    hs = slice(h * 128, (h + 1) * 128)
                    self.tr(pT[:, hs], NX[c][0][:, hs], self.ident)
                self.cp("scalar", NXT[c][0][:], pT[:])
                self.tt("vector", NP[c][0][:].rearrange("p (h i) -> p h i", h=4), NX[c][0][:].rearrange("p (h i) -> p h i", h=4),
                        self.ident.unsqueeze(1).to_broadcast([128, 4, 128]), ALU.add)
            cur = 0
            for s in range(1, 7):
                nxt = 1 - cur
                for c in range(2):
                    X, XT, P_ = NX[c][cur], NXT[c][cur], NP[c][cur]
                    Xn, XTn, Pn = NX[c][nxt], NXT[c][nxt], NP[c][nxt]
                    pXT = self.ps()
                    for h in range(4):
                        hs = slice(h * 128, (h + 1) * 128)
                        self.mm(pXT[:, hs], X[:, hs], XT[:, hs])
                    self.cp("scalar", XTn[:], pXT[:])
                    if s < 6:
                        pX = self.ps()
                        for h in range(4):
                            hs = slice(h * 128, (h + 1) * 128)
                            self.mm(pX[:, hs], XT[:, hs], X[:, hs])
                        self.cp("scalar", Xn[:], pX[:])
                    pP = self.ps()
                    for h in range(4):
                        hs = slice(h * 128, (h + 1) * 128)
                        self.mm(pP[:, hs], XTn[:, hs], P_[:, hs])
                    self.tt("vector", Pn[:], pP[:], P_[:], ALU.add)
                cur = nxt
            for c in range(2):
                self.cp("scalar", TTb[:, c, :], NP[c][cur][:])
                pW = self.ps()
                for h in range(4):
                    hs = slice(h * 128, (h + 1) * 128)
                    self.mm(pW[:, hs], kg_tok[:, c, hs], TTb[:, c, hs])
                self.act(nw0T[:, c, :], pW[:], AF.Copy, scale=-1.0)
            if self.chk(9):
                return
            for c in range(2):
                cs = slice(c * 128, (c + 1) * 128)
                pd = self.ps()
                for h in range(4):
                    hs = slice(h * 128, (h + 1) * 128)
                    self.mm(pd[:, hs], TTb[:, c, hs], v_tok[:, c, hs], start=True, stop=False)
                    self.mm(pd[:, hs], nw0T[:, c, hs], Sgb[:, h, :], start=False, stop=True)
                self.tt("vector", delta[:].rearrange("p (h v) -> p h v", h=4), pd[:].rearrange("p (h v) -> p h v", h=4),
                        beta[:, c * 4:(c + 1) * 4].unsqueeze(2).to_broadcast([128, 4, 128]), ALU.mult)
                py = self.ps()
                pS = self.ps()
                for h in range(4):
                    hs = slice(h * 128, (h + 1) * 128)
                    self.mm(py[:, hs], Sgb[:, h, :], qgT[:, c, hs], start=True, stop=False)
                    self.mm(py[:, hs], delta[:, hs], QK[:, c, hs], start=False, stop=True)
                    self.mm(pS[:, hs], kd_tok[:, c, hs], delta[:, hs])
                self.cp("scalar", oT[:], py[:])
                for h in range(4):
                    hs = slice(h * 128, (h + 1) * 128)
                    self.stt("vector", Sg[:, h, :], Sg[:, h, :], egend[:, c * 4 + h:c * 4 + h + 1], pS[:, hs], ALU.mult, ALU.add)
                self.cp("scalar", Sgb[:], Sg[:])
                self.act(osq, oT[:], AF.Square)
                pn = self.ps()
                self.mm(pn[:, :], self.onesb[:], osq)
                self.act(orr[:], pn[:], AF.Sqrt, bias=EPS, scale=1.0 / 128)
                self.recip(orr[:], orr[:])
                self.tt("vector", otmp[:], oT[:], orr[:], ALU.mult)
                self.tt("vector", mixT[:, 0:4, cs], otmp[:].rearrange("p (h i) -> p h i", h=4), szT[:, :, cs], ALU.mult)
                pSc = self.ps()
                for h in range(4):
                    hs = slice(h * 128, (h + 1) * 128)
                    self.mm(pSc[:, hs], rot[:, 4 + h, cs], rot[:, h, cs])
                self.tt("vector", SR[:], pSc[:], cst[:, C_DMT:C_DMT + 512], ALU.mult)
                pb = self.psbf()
                for h in range(4):
                    self.tr(pb[:, h * 64:(h + 1) * 64], rot[:, 4 + h, cs], self.identb[0:64, 0:64])
                self.tt("vector", kdr_tok[:, c, :], pb[:, 0:256], cst[:, C_KDEC:C_KDEC + 256], ALU.mult)
                self.tt("vector", qgr[:], rot[:, 0:4, cs], cst[0:64, C_QDEC:C_QDEC + 512].rearrange("p (r i) -> p r i", r=4), ALU.mult)
                py = self.ps()
                pS = self.ps()
                for h in range(4):
                    hs = slice(h * 128, (h + 1) * 128)
                    self.mm(py[:, hs], vb_tok[:, c, hs], SR[:, hs], start=True, stop=False)
                    self.mm(py[:, hs], Srb[:, h, :], qgr[:, h, :], start=False, stop=True)
                    self.mm(pS[0:64, hs], kdr_tok[:, c, h * 64:(h + 1) * 64], vb_tok[:, c, hs])
                self.cp("scalar", oT[:], py[:])
                self.cp("scalar", ob16, oT[:])
                for h in range(4):
                    hs = slice(h * 128, (h + 1) * 128)
                    self.stt("vector", Sr[:, h, :], Sr[:, h, :], GAM[h] ** 128, pS[0:64, hs], ALU.mult, ALU.add)
                self.cp("scalar", Srb[:], Sr[:])
                pm = self.ps()
                self.mm(pm[:, :], self.onesb[:], ob16)
                self.stt("vector", otmp[:], pm[:], -1.0 / 128, oT[:], ALU.mult, ALU.add)
                self.act(osq, otmp[:], AF.Square)
                pn = self.ps()
                self.mm(pn[:, :], self.onesb[:], osq)
                self.act(orr[:], pn[:], AF.Sqrt, bias=EPS, scale=1.0 / 128)
                self.recip(orr[:], orr[:])
                self.tt("vector", otmp[:], otmp[:], orr[:], ALU.mult)
                self.tt("vector", mixT[:, 4:8, cs], otmp[:].rearrange("p (h i) -> p h i", h=4), sgT[:, :, cs], ALU.mult)
            if self.chk(10):
                return
            for dc in range(8):
                p = self.ps()
                for kc in range(8):
                    self.mm(p[:, :BLK], wo[:, kc, dc * 128:(dc + 1) * 128], mixT[:, kc, :], start=(kc == 0), stop=(kc == 7))
                self.tt("vector", xb[:, dc, :], p[:, :BLK], xb[:, dc, :], ALU.add)
            self.dma("sync", dr["xs"][:, :, t0:t0 + BLK], xb[:], "xst")
            if blk == nblk - 1:
                for cb in range(3):
                    p = self.ps()
                    for kc in range(8):
                        self.mm(p[0:3, :], hT[:, kc, BLK - 3:BLK], wi[:, kc, cb * 512:(cb + 1) * 512], start=(kc == 0), stop=(kc == 7))
                    cs3 = (NX[0][0], NX[0][1], NX[1][0])[cb]
                    self.cp("vector", cs3[0:3, :], p[0:3, :])
                    self.dma("sync", dr["p_gdn_conv"][li, :, cb * 512:(cb + 1) * 512], cs3[0:3, :], "pst")
        for blk in range(nblk if not self.chk(0) else 1):
            body(blk)

    def red(self, eng, out, in_):
        self.S.add(eng, lambda e: e.tensor_reduce(out=out, in_=in_, axis=AX.X, op=ALU.add), reads=[in_], writes=[out])

    def softplus16(self, out, x, tmp):
        self.act(tmp, x, AF.Abs)
        self.act(tmp, tmp, AF.Exp, scale=-1.0)
        self.act(tmp, tmp, AF.Ln, bias=1.0, scale=1.0)
        self.stt("vector", out, x, 0.0, tmp, ALU.max, ALU.add)

    def sample_common(self, bs, layer, wi, ncols):
        dr = self.dr
        sb = lambda n, s, d: self.sb(bs, n, s, d)
        xbs = sb("xbs", [128, 8, NS], F32)
        hTs = sb("hTs", [128, 8, NS], BF16)
        sqs = sb("sqs", [128, 8, NS], BF16)
        rstds = sb("rstds", [128, NS], F32)
        prj = sb("prj", [NS, ncols], F32)
        self.dma("sync", xbs[:], dr["xs"][:, :, SEQ:NTOK], "xld")
        self.norm_block(xbs, hTs, sqs, rstds, NS, PK["nmix"] + layer * 8)
        c0 = 0
        k = 0
        while c0 < ncols:
            n = min(512, ncols - c0)
            p = self.ps()
            for kc in range(8):
                self.mm(p[0:NS, 0:n], hTs[:, kc, :], wi[:, kc, c0:c0 + n], start=(kc == 0), stop=(kc == 7))
            self.cp("vector" if k % 2 else "scalar", prj[:, c0:c0 + n], p[0:NS, 0:n])
            c0 += n
            k += 1
        return xbs, prj

    def sample_outproj(self, bs, xbs, mixs, nk, wo_get):
        dr = self.dr
        sb = lambda n, s, d: self.sb(bs, n, s, d)
        mixTs = sb("mixTs", [128, nk, NS], BF16)
        for k0 in range(0, nk, 8):
            p = self.ps()
            for k in range(k0, min(nk, k0 + 8)):
                self.tr(p[:, (k - k0) * NS:(k - k0 + 1) * NS], mixs[:, k * 128:(k + 1) * 128], self.ident[0:NS, 0:NS])
            n = min(nk, k0 + 8) - k0
            self.cp("vector", mixTs[:, k0:k0 + n, :], p[:, 0:n * NS].rearrange("p (k t) -> p k t", k=n))
        po = self.ps()
        for dc in range(8):
            for kc in range(nk):
                self.mm(po[:, dc * NS:(dc + 1) * NS], wo_get(kc, dc), mixTs[:, kc, :], start=(kc == 0), stop=(kc == nk - 1))
        self.tt("vector", xbs[:], po[:, 0:8 * NS].rearrange("p (k t) -> p k t", k=8), xbs[:], ALU.add)
        self.dma("sync", dr["xs"][:, :, SEQ:NTOK], xbs[:], "xst")

    def hybrid_sample(self, li, layer, bs, wi, wo):
        dr, cst, pk = self.dr, self.cst, self.pk
        sb = lambda n, s, d: self.sb(bs, n, s, d)
        xbs, prj = self.sample_common(bs, layer, wi, HYB_IN)
        id16 = cst[0:NS, C_ID16:C_ID16 + 256].rearrange("p (a b) -> p a b", a=16)
        id16f = cst[:, C_ID16:C_ID16 + 256].rearrange("p (a b) -> p a b", a=16)
        cw = sb("cw", [NS, 4, 512], F32)
        cbuf = sb("cbuf", [NS, 3, 512], F32)
        qkv = sb("qkv", [NS, 1536], F32)
        ctm = sb("ctm", [NS, 512], F32)
        stb = sb("stb", [128, NS, 4, 128], F32)
        kqm = sb("kqm", [128, 8, NS, NS], F32)
        ktm = [sb(f"ktm{i}", [NS, NS, 128], F32) for i in range(2)]
        qkTs = sb("qkTs", [128, 8, NS], F32)
        sm = sb("sms", [NS, 256], F32)
        t1 = sb("st1", [NS, 512], F32)
        t2 = sb("st2", [NS, 512], F32)
        dl = sb("sdl", [NS, 512], F32)
        mixs = sb("mixs", [NS, 1024], F32)
        egB = sb("egB", [128, 64], F32)
        egm = sb("egm", [NS, NS, 4], F32)
        qr = sb("qr", [NS, 4, 64], F32)
        kr = sb("kr", [NS, 4, 64], F32)
        for b in range(NS):
            self.dma("sync", stb[:, b, :, :], dr["state_gdn"][li, b].rearrange("h k v -> k h v"), "stld")
        for pc in range(3):
            c0 = pc * 512
            self.dma("sync", cw[:], dr["gdn_conv_w"][li, :, c0:c0 + 512].partition_broadcast(NS), "cwld")
            self.dma("sync", cbuf[:], dr["state_gdn_conv"][li, :, :, c0:c0 + 512], "cbld")
            self.tt("vector", ctm[:], prj[:, c0:c0 + 512], cw[:, 3, :], ALU.mult)
            for tp in range(3):
                self.tt("gpsimd", t1[:], cbuf[:, tp, :], cw[:, tp, :], ALU.mult)
                self.tt("vector", ctm[:], ctm[:], t1[:], ALU.add)
            self.act(qkv[:, c0:c0 + 512], ctm[:], AF.Silu)
            self.dma("sync", dr["s_gdn_conv"][li, :, 0:2, c0:c0 + 512], cbuf[:, 1:3, :], "cvst")
        self.dma("sync", dr["s_gdn_conv"][li, :, 2, :], prj[:, 0:1536], "cvst")
        beta = sm[:, 0:4]
        xg = sm[:, 4:8]
        tmp4 = sm[:, 8:12]
        g_ = sm[:, 12:16]
        eg = sm[:, 16:20]
        ss = sm[:, 20:28]
        qk = sm[:, 28:32]
        ss2 = sm[:, 32:36]
        rs2 = sm[:, 36:40]
        self.act(beta, prj[:, 2048:2052], AF.Sigmoid)
        self.tt("vector", xg, prj[:, 2052:2056], pk[0:NS, PK[f"gdtb{li}"]:PK[f"gdtb{li}"] + 4], ALU.add)
        self.softplus16(g_, xg, tmp4)
        self.tt("vector", g_, g_, self.negA[0:NS, li * 4:(li + 1) * 4], ALU.mult)
        self.act(eg, g_, AF.Exp)
        qk3 = qkv[:, 0:1024].rearrange("p (h k) -> p h k", h=8)
        self.tt("vector", t1[:], qkv[:, 0:512], qkv[:, 0:512], ALU.mult)
        self.tt("gpsimd", t2[:], qkv[:, 512:1024], qkv[:, 512:1024], ALU.mult)
        self.red("vector", ss[:, 0:4], t1[:].rearrange("p (h k) -> p h k", h=4))
        self.red("vector", ss[:, 4:8], t2[:].rearrange("p (h k) -> p h k", h=4))
        self.act(ss, ss, AF.Sqrt, bias=EPS, scale=1.0)
        self.recip(ss, ss)
        self.ts("vector", ss[:, 0:4], ss[:, 0:4], 128.0 ** -0.5, ALU.mult)
        self.tt("vector", qk3, qk3, ss.unsqueeze(2).to_broadcast([NS, 8, 128]), ALU.mult)
        self.tt("vector", t1[:], qkv[:, 0:512], qkv[:, 512:1024], ALU.mult)
        self.red("vector", qk, t1[:].rearrange("p (h k) -> p h k", h=4))
        p = self.ps()
        for j in range(8):
            self.tr(p[:, j * NS:(j + 1) * NS], qkv[:, j * 128:(j + 1) * 128], self.ident[0:NS, 0:NS])
        self.cp("vector", qkTs[:], p[:, 0:8 * NS].rearrange("p (j t) -> p j t", j=8))
        for j in range(8):
            self.tt("vector" if j % 2 else "gpsimd", kqm[:, j, :, :], qkTs[:, j, :].unsqueeze(1).to_broadcast([128, NS, NS]), id16f, ALU.mult)
        pk_ = self.ps()
        pq_ = self.ps()
        for h in range(4):
            hs = slice(h * 128, (h + 1) * 128)
            for b in range(NS):
                self.mm(pk_[0:NS, hs], kqm[:, 4 + h, b, :], stb[:, b, h, :], start=(b == 0), stop=(b == NS - 1))
                self.mm(pq_[0:NS, hs], kqm[:, h, b, :], stb[:, b, h, :], start=(b == 0), stop=(b == NS - 1))
        v3 = qkv[:, 1024:1536].rearrange("p (h v) -> p h v", h=4)
        eg3 = eg.unsqueeze(2).to_broadcast([NS, 4, 128])
        t13 = t1[:].rearrange("p (h v) -> p h v", h=4)
        t23 = t2[:].rearrange("p (h v) -> p h v", h=4)
        dl3 = dl[:].rearrange("p (h v) -> p h v", h=4)
        self.tt("vector", t13, pk_[0:NS, :].rearrange("p (h v) -> p h v", h=4), eg3, ALU.mult)
        self.tt("vector", t13, v3, t13, ALU.subtract)
        self.tt("vector", dl3, t13, beta.unsqueeze(2).to_broadcast([NS, 4, 128]), ALU.mult)
        self.tt("vector", t23, pq_[0:NS, :].rearrange("p (h v) -> p h v", h=4), eg3, ALU.mult)
        self.tt("vector", t13, dl3, qk.unsqueeze(2).to_broadcast([NS, 4, 128]), ALU.mult)
        self.tt("vector", t23, t23, t13, ALU.add)
        self.tt("gpsimd", t13, t23, t23, ALU.mult)
        self.red("vector", ss2, t13)
        self.act(rs2, ss2, AF.Sqrt, bias=EPS, scale=1.0 / 128)
        self.recip(rs2, rs2)
        self.tt("vector", t23, t23, rs2.unsqueeze(2).to_broadcast([NS, 4, 128]), ALU.mult)
        self.act(t1[:], prj[:, 1536:2048], AF.Silu)
        self.tt("vector", mixs[:, 0:512], t2[:], t1[:], ALU.mult)
        self.tt("vector", egm[:], eg.unsqueeze(1).to_broadcast([NS, NS, 4]),
                self.id16col(cst).to_broadcast([NS, NS, 4]), ALU.mult)
        pe = self.ps()
        self.mm(pe[:, 0:64], self.onesf[0:NS, :], egm[:].rearrange("p a h -> p (a h)"))
        self.cp("vector", egB[:], pe[:, 0:64])
        for h in range(4):
            kt = ktm[h % 2]
            self.tt("gpsimd", kt[:], qkv[:, 512 + h * 128:512 + (h + 1) * 128].unsqueeze(1).to_broadcast([NS, NS, 128]),
                    self.id16col(cst).to_broadcast([NS, NS, 128]), ALU.mult)
            for b4 in range(NS // 4):
                pu = self.ps()
                for u in range(4):
                    b = b4 * 4 + u
                    self.mm(pu[:, u * 128:(u + 1) * 128], kt[:, b, :], dl[:, h * 128:(h + 1) * 128])
                for u in range(4):
                    b = b4 * 4 + u
                    self.stt("vector", stb[:, b, h, :], stb[:, b, h, :], egB[:, b * 4 + h:b * 4 + h + 1], pu[:, u * 128:(u + 1) * 128],
                             ALU.mult, ALU.add)
        for b in range(NS):
            self.dma("sync", dr["s_gdn"][li, b].rearrange("h k v -> k h v"), stb[:, b, :, :], "stst")
        cosr = cst[0:NS, C_ROPES:C_ROPES + 32].unsqueeze(1).to_broadcast([NS, 4, 32])
        sinr = cst[0:NS, C_ROPES + 32:C_ROPES + 64].unsqueeze(1).to_broadcast([NS, 4, 32])
        ra = sb("ra", [NS, 4, 32], F32)
        rb = sb("rb", [NS, 4, 32], F32)
        for (src0, dst, sc) in ((2056, qr, 1.0), (2312, kr, 0.125)):
            src = prj[:, src0:src0 + 256].rearrange("p (h k) -> p h k", h=4)
            x1, x2 = src[:, :, 0:32], src[:, :, 32:64]
            self.tt("vector", ra[:], x1, cosr, ALU.mult)
            self.tt("vector", rb[:], x2, sinr, ALU.mult)
            self.tt("vector", dst[:, :, 0:32], ra[:], rb[:], ALU.subtract)
            self.tt("vector", ra[:], x2, cosr, ALU.mult)
            self.tt("vector", rb[:], x1, sinr, ALU.mult)
            self.tt("vector", dst[:, :, 32:64], ra[:], rb[:], ALU.add)
            if sc != 1.0:
                self.ts("vector", dst[:], dst[:], sc, ALU.mult)
        for b in range(NS):
            self.dma("sync", stb[0:64, b, :, :], dr["state_ret"][li, b].rearrange("h k v -> k h v"), "stld")
        qrT = sb("qrT", [64, 4, NS], F32)
        qrm = kqm[0:64, 0:4, :, :]
        krm = [ktm[i][:, :, 0:64] for i in range(2)]
        p = self.ps()
        for h in range(4):
            self.tr(p[0:64, h * NS:(h + 1) * NS], qr[:, h, :], self.ident[0:NS, 0:NS])
        self.cp("vector", qrT[:], p[0:64, 0:4 * NS].rearrange("p (h t) -> p h t", h=4))
        for h in range(4):
            self.tt("vector", qrm[:, h, :, :], qrT[:, h, :].unsqueeze(1).to_broadcast([64, NS, NS]), id16f[0:64], ALU.mult)
        pq_ = self.ps()
        for h in range(4):
            hs = slice(h * 128, (h + 1) * 128)
            for b in range(NS):
                self.mm(pq_[0:NS, hs], qrm[:, h, b, :], stb[0:64, b, h, :], start=(b == 0), stop=(b == NS - 1))
        vb3 = prj[:, 2568:3080].rearrange("p (h v) -> p h v", h=4)
        qkr = sm[:, 40:44]
        mean = sm[:, 44:48]
        var = sm[:, 48:52]
        self.tt("vector", dl[:, 0:256].rearrange("p (h k) -> p h k", h=4), qr[:], kr[:], ALU.mult)
        self.red("vector", qkr, dl[:, 0:256].rearrange("p (h k) -> p h k", h=4))
        self.tt("vector", t13, vb3, qkr.unsqueeze(2).to_broadcast([NS, 4, 128]), ALU.mult)
        for h in range(4):
            self.stt("vector", t2[:, h * 128:(h + 1) * 128], pq_[0:NS, h * 128:(h + 1) * 128], GAM[h], t1[:, h * 128:(h + 1) * 128], ALU.mult, ALU.add)
        self.red("vector", mean, t23)
        self.ts("vector", mean, mean, -1.0 / 128, ALU.mult)
        self.tt("vector", t23, t23, mean.unsqueeze(2).to_broadcast([NS, 4, 128]), ALU.add)
        self.tt("gpsimd", t13, t23, t23, ALU.mult)
        self.red("vector", var, t13)
        self.act(var, var, AF.Sqrt, bias=EPS, scale=1.0 / 128)
        self.recip(var, var)
        self.tt("vector", t23, t23, var.unsqueeze(2).to_broadcast([NS, 4, 128]), ALU.mult)
        self.act(t1[:], prj[:, 3080:3592], AF.Silu)
        self.tt("vector", mixs[:, 512:1024], t2[:], t1[:], ALU.mult)
        for h in range(4):
            km = krm[h % 2]
            self.tt("gpsimd", km, kr[:, h, :].unsqueeze(1).to_broadcast([NS, NS, 64]), self.id16col(cst).to_broadcast([NS, NS, 64]), ALU.mult)
            for b4 in range(NS // 4):
                pu = self.ps()
                for u in range(4):
                    b = b4 * 4 + u
                    self.mm(pu[0:64, u * 128:(u + 1) * 128], km[:, b, :], prj[:, 2568 + h * 128:2568 + (h + 1) * 128])
                for u in range(4):
                    b = b4 * 4 + u
                    self.stt("vector", stb[0:64, b, h, :], stb[0:64, b, h, :], GAM[h], pu[0:64, u * 128:(u + 1) * 128], ALU.mult, ALU.add)
        for b in range(NS):
            self.dma("sync", dr["s_ret"][li, b].rearrange("h k v -> k h v"), stb[0:64, b, :, :], "stst")
        self.sample_outproj(bs, xbs, mixs, 8, lambda kc, dc: wo[:, kc, dc * 128:(dc + 1) * 128])

    def id16col(self, cst):
        return self.ident[0:NS, 0:NS].unsqueeze(2)

    def ssd_phase(self, li, layer):
        dr, cst, pk = self.dr, self.cst, self.pk
        with contextlib.ExitStack() as ph:
            sb = lambda n, s, d: self.sb(ph, n, s, d)
            wi = sb("wis", [128, 8, SSM_IN], BF16)
            wo = [sb(f"wos{i}", [128, 2048], BF16) for i in range(2)]
            wosd = self.nc.dram_tensor(f"wos_bf{li}", [8, 128, 16, 128], BF16, kind="Internal").ap()
            self.wosd = wosd
            for kc in range(8):
                self.dma("gpsimd", wi[:, kc, :], dr["w_in_ssm"][li, kc * 128:(kc + 1) * 128, :], "wi")
            for kc in range(16):
                wb = wo[(kc // 2) % 2][:, (kc % 2) * 1024:(kc % 2 + 1) * 1024]
                self.dma("gpsimd", wb, dr["w_out_ssm"][li, kc * 128:(kc + 1) * 128, :], f"wog{kc % 4}")
                col = PK[f"snw{li}"] + kc
                self.ts("vector", wb, wb, pk[:, col:col + 1], ALU.mult)
                self.dma("sync", wosd[:, :, kc, :].rearrange("dc p d -> p dc d"), wb.rearrange("p (dc d) -> p dc d", dc=8), f"wost{kc % 4}")
            S_ = sb("S", [128, 2048], F32)
            Sbz = sb("Sbz", [128, 32, 128], BF16)
            Vz = sb("Vz", [128, 32, 128], BF16)
            hist = sb("hists", [128, 24, 3], F32)
            for t_ in (S_, Sbz, Vz, hist):
                self.memset("gpsimd", t_[:], 0.0)
            with contextlib.ExitStack() as bs:
                self.ssd_prompt(li, layer, bs, wi, wo, S_, Sbz, Vz, hist)
            self.dma("sync", dr["p_ssm"][li].rearrange("h n d -> n h d"), S_[:].rearrange("p (h d) -> p h d", h=32), "pst")
            if self.do_sample:
                self.S.barrier()
                with contextlib.ExitStack() as bs:
                    self.ssd_sample(li, layer, bs, wi, wo, [Sbz[:].bitcast(F32).rearrange("p a b -> p (a b)"), Vz[:].bitcast(F32).rearrange("p a b -> p (a b)"), S_[:]])

    def ssd_prompt(self, li, layer, bs, wi, wo, S_, Sbz, Vz, hist):
        dr, cst, pk = self.dr, self.cst, self.pk
        sb = lambda n, s, d: self.sb(bs, n, s, d)
        xb = sb("xb", [128, 8, BLK], F32)
        ynT = xb[:].bitcast(BF16).rearrange("p a (b t) -> p (a b) t", b=2)
        hT = sb("hT", [128, 8, BLK], BF16)
        rstd = sb("rstd", [128, BLK], F32)
        ctmp = [sb(f"ctmp{i}", [128, BLK + 3], F32) for i in range(3)]
        cacc = [sb(f"cacc{i}", [128, BLK], F32) for i in range(3)]
        szT = sb("szT", [128, 16, BLK], BF16)
        xsT = sb("xsT", [128, 16, BLK], BF16)
        BCT = sb("BCT", [128, 8, BLK], BF16)
        xpc = [sb(f"xpc{i}", [128, BLK], F32) for i in range(3)]
        smf = sb("smf", [128, 9 * 64], F32)
        vp = sb("vp", [128, 2048], BF16)
        B_tok = sb("B_tok", [128, 512], BF16)
        scM = sb("scM", [128, 512], F32)
        gU = sb("gU", [128, 512], F32)
        E_ = sb("E_", [128, 512], F32)
        dtm = sb("dtm", [128, 512], F32)
        SD = [sb(f"SD{i}", [128, 512], BF16) for i in range(2)]
        CgT = [sb(f"CgT{i}", [128, 512], BF16) for i in range(2)]
        y_sb = sb("y_sb", [128, 16, 128], F32)
        ysq = sb("ysq", [128, 16, 128], BF16)
        sq = ysq[:].rearrange("p (a b) i -> p a (b i)", b=2)
        rg = sb("rg", [128, 512], F32)
        dtr = smf[:, 0:64]
        ax = smf[:, 64:128]
        dt_ = smf[:, 128:192]
        g_t = smf[:, 192:256]
        gcum = smf[:, 256:320]
        gend = smf[:, 320:384]
        wdec = smf[:, 384:448]
        dtw = smf[:, 448:512]
        egend = smf[:, 512:576]
        dtb = pk[:, PK[f"sdtb{li}"]:PK[f"sdtb{li}"] + 32]
        nA = self.negA[:, 8 + li * 32:8 + (li + 1) * 32]
        nblk = SEQ // BLK
        Vzv = Vz[:].rearrange("p (q a) (b d) -> p q a b d", a=2, b=2)
        Sbzv = Sbz[:].rearrange("p (q a) (b d) -> p q a b d", a=2, b=2)

        def body(blk):
            t0 = blk * BLK
            self.dma("sync", xb[:], dr["xs"][:, :, t0:t0 + BLK], "xld")
            self.norm_block(xb, hT, sq, rstd, BLK, PK["nmix"] + layer * 8)
            if self.chk(21):
                return
            for ch in range(16):
                p = self.ps()
                self.proj_fm(p, wi, ch * 128, 128, hT, BLK)
                self.act(szT[:, ch, :], p[:, :BLK], AF.Silu)
            for g3 in range(8):
                chs = [g3 * 3 + u for u in range(3)]
                pp = {}
                for ch in chs:
                    pp[ch] = self.ps()
                    self.proj_fm(pp[ch], wi, 2048 + ch * 128, 128, hT, BLK)
                for ch in chs:
                    self.cp("scalar", ctmp[ch % 3][:, 3:3 + BLK], pp[ch][:, :BLK])
                    self.cp("gpsimd", ctmp[ch % 3][:, 0:3], hist[:, ch, :])
                for ch in chs:
                    wc = PK[f"scw{li}"] + ch * 4
                    bc = PK[f"scb{li}"] + ch
                    self.ts("vector", cacc[ch % 3][:], ctmp[ch % 3][:, 0:BLK], pk[:, wc:wc + 1], ALU.mult, pk[:, bc:bc + 1], ALU.add)
                for tp in range(1, 4):
                    for ch in chs:
                        wc = PK[f"scw{li}"] + ch * 4
                        self.stt("vector", cacc[ch % 3][:], ctmp[ch % 3][:, tp:tp + BLK], pk[:, wc + tp:wc + tp + 1], cacc[ch % 3][:], ALU.mult, ALU.add)
                for ch in chs:
                    self.cp("gpsimd", hist[:, ch, :], ctmp[ch % 3][:, BLK:BLK + 3])
                    dst = xsT[:, ch, :] if ch < 16 else BCT[:, ch - 16, :]
                    self.act(dst, cacc[ch % 3][:], AF.Silu)
            if self.chk(22):
                return
            pdt = self.ps()
            for c in range(2):
                for kc in range(8):
                    self.mm(pdt[:, c * 32:(c + 1) * 32], hT[:, kc, c * 128:(c + 1) * 128], wi[:, kc, 5120:5152], start=(kc == 0), stop=(kc == 7))
            self.tt("vector", dtr.rearrange("p (c h) -> p c h", c=2), pdt[:, 0:64].rearrange("p (c h) -> p c h", c=2),
                    dtb.unsqueeze(1).to_broadcast([128, 2, 32]), ALU.add)
            self.act(ax, dtr, AF.Abs)
            self.act(ax, ax, AF.Exp, scale=-1.0)
            self.act(ax, ax, AF.Ln, bias=1.0, scale=1.0)
            self.stt("vector", dt_, dtr, 0.0, ax, ALU.max, ALU.add)
            self.tt("vector", g_t.rearrange("p (c h) -> p c h", c=2), dt_.rearrange("p (c h) -> p c h", c=2),
                    nA.unsqueeze(1).to_broadcast([128, 2, 32]), ALU.mult)
            pg = self.ps()
            self.mm(pg[:, 0:64], self.U, g_t)
            self.mm(pg[:, 64:128], self.onesf, g_t)
            self.cp("vector", gcum, pg[:, 0:64])
            self.cp("vector", gend, pg[:, 64:128])
            self.tt("vector", wdec, gend, gcum, ALU.subtract)
            self.act(wdec, wdec, AF.Exp)
            self.tt("vector", dtw, dt_, wdec, ALU.mult)
            self.act(egend, gend, AF.Exp)
            if self.chk(23):
                return
            for c in range(2):
                cs = slice(c * 128, (c + 1) * 128)
                for grp in range(4):
                    pb = self.psbf()
                    for q in range(4):
                        self.tr(pb[:, q * 128:(q + 1) * 128], xsT[:, grp * 4 + q, cs], self.identb[:])
                    pbv = pb[:, 0:512].rearrange("p (q a d) -> p q a d", q=4, a=2)
                    dtv = dt_[:, c * 32 + grp * 8:c * 32 + grp * 8 + 8].rearrange("p (q a) -> p q a", a=2)
                    for hh in range(2):
                        self.tt("vector", Vzv[:, grp * 4:(grp + 1) * 4, hh, hh, :], pbv[:, :, hh, :],
                                dtv[:, :, hh].unsqueeze(2).to_broadcast([128, 4, 64]), ALU.mult)
                    self.tt("vector", vp[:, grp * 512:(grp + 1) * 512].rearrange("p (h d) -> p h d", h=8),
                            pb[:, 0:512].rearrange("p (h d) -> p h d", h=8),
                            dtw[:, c * 32 + grp * 8:c * 32 + grp * 8 + 8].unsqueeze(2).to_broadcast([128, 8, 64]), ALU.mult)
                pb = self.psbf()
                for g in range(4):
                    self.tr(pb[:, g * 128:(g + 1) * 128], BCT[:, g, cs], self.identb[:])
                self.cp("scalar", B_tok[:], pb[:, 0:512])
                pS = self.ps()
                for g in range(4):
                    self.mm(pS[:, g * 128:(g + 1) * 128], BCT[:, g, cs], BCT[:, 4 + g, cs])
                self.tt("vector", scM[:].rearrange("p (g i) -> p g i", g=4), pS[:].rearrange("p (g i) -> p g i", g=4),
                        cst[:, C_INCL:C_INCL + 128].unsqueeze(1).to_broadcast([128, 4, 128]), ALU.mult)
                if self.chk(24):
                    return
                py = None
                for quad in range(8):
                    g = quad // 2
                    h0 = quad * 4
                    k2 = quad % 2
                    self.tt("vector", gU[:].rearrange("p (h i) -> p h i", h=4), self.U.unsqueeze(1).to_broadcast([128, 4, 128]),
                            g_t[:, c * 32 + h0:c * 32 + h0 + 4].unsqueeze(2).to_broadcast([128, 4, 128]), ALU.mult)
                    pA = self.ps()
                    self.mm(pA[:, :], self.onesf, gU[:, :])
                    tokw = self.zcol[:, 1:2]
                    self.S.add("scalar", lambda e, o=E_[:], i=pA[:]: e.activation(out=o, in_=i, func=AF.Exp), reads=[pA[:]], writes=[E_[:], tokw])
                    self.tt("vector", CgT[k2][:].rearrange("p (h i) -> p h i", h=4), E_[:].rearrange("p (h i) -> p h i", h=4),
                            BCT[:, 4 + g, cs].unsqueeze(1).to_broadcast([128, 4, 128]), ALU.mult)
                    for u in range(4):
                        us = slice(u * 128, (u + 1) * 128)
                        gc = gcum[:, c * 32 + h0 + u:c * 32 + h0 + u + 1]
                        self.S.add("vector", lambda e, o=dtm[:, us], i=pA[:, us], g_=gc: e.tensor_scalar(out=o, in0=i, scalar1=g_, scalar2=None, op0=ALU.subtract),
                                   reads=[pA[:, us], gc, tokw], writes=[dtm[:, us]])
                    self.ts("vector", dtm[:], dtm[:], 0.0, ALU.min)
                    self.act(dtm[:], dtm[:], AF.Exp)
                    self.tt("gpsimd", SD[k2][:].rearrange("p (h i) -> p h i", h=4), dtm[:].rearrange("p (h i) -> p h i", h=4),
                            scM[:, g * 128:(g + 1) * 128].unsqueeze(1).to_broadcast([128, 4, 128]), ALU.mult)
                    if k2 == 0:
                        py = self.ps()
                    for pr in range(2):
                        slot = k2 * 2 + pr
                        reg = py[:, slot * 128:(slot + 1) * 128]
                        for hh in range(2):
                            u = pr * 2 + hh
                            h = h0 + u
                            us = slice(u * 128, (u + 1) * 128)
                            self.mm(reg, Vz[:, h, :], SD[k2][:, us], start=(hh == 0), stop=False)
                            self.mm(reg, Sbz[:, h, :], CgT[k2][:, us], start=False, stop=(hh == 1))
                    if k2 == 1:
                        for slot in range(4):
                            pair = (quad - 1) * 2 + slot
                            dcol = PK[f"sD{li}"] + pair
                            self.stt("vector", y_sb[:, pair, :], xsT[:, pair, cs], pk[:, dcol:dcol + 1], py[:, slot * 128:(slot + 1) * 128],
                                     ALU.mult, ALU.add)
                if self.chk(25):
                    return
                self.tt("vector", y_sb[:], y_sb[:], szT[:, :, cs], ALU.mult)
                self.act(ysq[:], y_sb[:], AF.Square)
                pn = self.ps()
                for g in range(4):
                    for q in range(4):
                        self.mm(pn[:, g * 128:(g + 1) * 128], self.onesb[:], ysq[:, g * 4 + q, :], start=(q == 0), stop=(q == 3))
                self.act(rg[:], pn[:], AF.Sqrt, bias=EPS, scale=1.0 / 512)
                self.recip(rg[:], rg[:])
                self.tt("vector", ynT[:, :, cs].rearrange("p (g q) i -> p g q i", g=4), y_sb[:].rearrange("p (g q) i -> p g q i", g=4),
                        rg[:].rearrange("p (g i) -> p g i", g=4).unsqueeze(2).to_broadcast([128, 4, 4, 128]), ALU.mult)
                for g in range(4):
                    gs = slice(g * 512, (g + 1) * 512)
                    pU = self.ps()
                    self.mm(pU[:], B_tok[:, g * 128:(g + 1) * 128], vp[:, gs])
                    self.tt("vector", S_[:, gs].rearrange("p (h d) -> p h d", h=8), S_[:, gs].rearrange("p (h d) -> p h d", h=8),
                            egend[:, c * 32 + g * 8:c * 32 + g * 8 + 8].unsqueeze(2).to_broadcast([128, 8, 64]), ALU.mult)
                    self.tt("vector", S_[:, gs], S_[:, gs], pU[:], ALU.add)
                    sv = S_[:, gs].rearrange("p (q a d) -> p q a d", q=4, a=2)
                    for hh in range(2):
                        self.cp("scalar", Sbzv[:, g * 4:(g + 1) * 4, hh, hh, :], sv[:, :, hh, :])
            if self.chk(26):
                return
            def ld(dc):
                wb_ = wo[dc % 2][:].rearrange("p (k d) -> p k d", k=16)
                self.dma("sync", wb_, self.wosd[dc], f"wo{dc % 2}")
                self.dma("sync", xpc[dc % 3][:], dr["xs"][:, dc, t0:t0 + BLK], f"xpl{dc % 3}")
            ld(0)
            ld(1)
            for dc in range(8):
                wb = wo[dc % 2][:].rearrange("p (k d) -> p k d", k=16)
                xp = xpc[dc % 3]
                p = self.ps()
                for kc in range(16):
                    self.mm(p[:, :BLK], wb[:, kc, :], ynT[:, kc, :], start=(kc == 0), stop=(kc == 15))
                self.tt("vector", xp[:], p[:, :BLK], xp[:], ALU.add)
                if dc + 2 < 8:
                    ld(dc + 2)
                self.dma("sync", dr["xs"][:, dc, t0:t0 + BLK], xp[:], f"xps{dc % 3}")
            if blk == nblk - 1:
                for cb in range(6):
                    p = self.ps()
                    for kc in range(8):
                        self.mm(p[0:3, :], hT[:, kc, BLK - 3:BLK], wi[:, kc, 2048 + cb * 512:2048 + (cb + 1) * 512], start=(kc == 0), stop=(kc == 7))
                    c3 = (gU, E_, dtm)[cb % 3]
                    self.cp("vector", c3[0:3, :], p[0:3, :])
                    self.dma("sync", dr["p_ssm_conv"][li, :, cb * 512:(cb + 1) * 512], c3[0:3, :], "pst")
        for blk in range(nblk if not self.chk(0) else 1):
            body(blk)

    def ssd_sample(self, li, layer, bs, wi, wo, Sbufs):
        dr, cst, pk = self.dr, self.cst, self.pk
        sb = lambda n, s, d: self.sb(bs, n, s, d)
        xbs, prj = self.sample_common(bs, layer, wi, SSM_IN)
        id16f = cst[:, C_ID16:C_ID16 + 256].rearrange("p (a b) -> p a b", a=16)
        ident16 = self.ident[0:NS, 0:NS]
        cw = sb("cw", [NS, 4, 512], F32)
        cbuf = sb("cbuf", [NS, 3, 512], F32)
        cbv = sb("cbv", [NS, 512], F32)
        ctm = sb("ctm", [NS, 512], F32)
        t1 = sb("st1", [NS, 512], F32)
        xbc = sb("xbc", [NS, 3072], F32)
        vv = sb("vv", [NS, 2048], F32)
        y_ = sb("ys", [NS, 2048], F32)
        sm = sb("sms", [NS, 256], F32)
        egm = sb("egm", [NS, NS, 32], F32)
        egB = sb("egB", [128, 512], F32)
        CTs = sb("CTs", [128, 4, NS], F32)
        CTm = sb("CTm", [128, 4, NS, NS], F32)
        Bmb = [sb("Bmb0", [NS, 512], F32)] * 2
        for pc in range(6):
            c0 = pc * 512
            self.dma("sync", cw[:], dr["ssm_conv_w"][li, :, c0:c0 + 512].partition_broadcast(NS), "cwld")
            self.dma("sync", cbv[:], dr["ssm_conv_b"][li, c0:c0 + 512].partition_broadcast(NS), "cwld")
            self.dma("sync", cbuf[:], dr["state_ssm_conv"][li, :, :, c0:c0 + 512], "cbld")
            self.tt("vector", ctm[:], prj[:, 2048 + c0:2048 + c0 + 512], cw[:, 3, :], ALU.mult)
            self.tt("vector", ctm[:], ctm[:], cbv[:], ALU.add)
            for tp in range(3):
                self.tt("gpsimd", t1[:], cbuf[:, tp, :], cw[:, tp, :], ALU.mult)
                self.tt("vector", ctm[:], ctm[:], t1[:], ALU.add)
            self.act(xbc[:, c0:c0 + 512], ctm[:], AF.Silu)
            self.dma("sync", dr["s_ssm_conv"][li, :, 0:2, c0:c0 + 512], cbuf[:, 1:3, :], "cvst")
        self.dma("sync", dr["s_ssm_conv"][li, :, 2, :], prj[:, 2048:5120], "cvst")
        dtr = sm[:, 0:32]
        tmp = sm[:, 32:64]
        dt_ = sm[:, 64:96]
        g_ = sm[:, 96:128]
        eg = sm[:, 128:160]
        ss = sm[:, 160:164]
        self.tt("vector", dtr, prj[:, 5120:5152], pk[0:NS, PK[f"sdtb{li}"]:PK[f"sdtb{li}"] + 32], ALU.add)
        self.softplus16(dt_, dtr, tmp)
        self.tt("vector", g_, dt_, self.negA[0:NS, 8 + li * 32:8 + (li + 1) * 32], ALU.mult)
        self.act(eg, g_, AF.Exp)
        xs3 = xbc[:, 0:2048].rearrange("p (h d) -> p h d", h=32)
        self.tt("vector", vv[:].rearrange("p (h d) -> p h d", h=32), xs3, dt_.unsqueeze(2).to_broadcast([NS, 32, 64]), ALU.mult)
        p = self.ps()
        for g in range(4):
            self.tr(p[:, g * NS:(g + 1) * NS], xbc[:, 2560 + g * 128:2560 + (g + 1) * 128], ident16)
        self.cp("vector", CTs[:], p[:, 0:4 * NS].rearrange("p (g t) -> p g t", g=4))
        for g in range(4):
            self.tt("vector" if g % 2 else "gpsimd", CTm[:, g, :, :], CTs[:, g, :].unsqueeze(1).to_broadcast([128, NS, NS]), id16f, ALU.mult)
        self.tt("vector", egm[:], eg.unsqueeze(1).to_broadcast([NS, NS, 32]), ident16.unsqueeze(2).to_broadcast([NS, NS, 32]), ALU.mult)
        pe = self.ps()
        self.mm(pe[:, :], self.onesf[0:NS, :], egm[:].rearrange("p a h -> p (a h)"))
        self.cp("vector", egB[:], pe[:, :])
        psy = self.psf[0:4]
        k = 0
        for b in range(NS):
            Sb = Sbufs[b % 3]
            self.dma("sync", Sb.rearrange("p (h d) -> p h d", h=32), dr["state_ssm"][li, b].rearrange("h n d -> n h d"), f"sld{b % 3}")
            bm = Bmb[b % 2]
            self.ts("vector", bm[:], xbc[:, 2048:2560], ident16[:, b:b + 1], ALU.mult)
            self.tt("gpsimd", Sb.rearrange("p (h d) -> p h d", h=32), Sb.rearrange("p (h d) -> p h d", h=32),
                    egB[:, b * 32:(b + 1) * 32].unsqueeze(2).to_broadcast([128, 32, 64]), ALU.mult)
            for g in range(4):
                gs = slice(g * 512, (g + 1) * 512)
                pu = self.psf[4 + k % 2]
                k += 1
                self.mm(pu[:, :], bm[:, g * 128:(g + 1) * 128], vv[:, gs])
                self.tt("vector", Sb[:, gs], Sb[:, gs], pu[:, :], ALU.add)
                self.mm(psy[g][0:NS, :], CTm[:, g, b, :], Sb[:, gs], start=(b == 0), stop=(b == NS - 1))
            self.dma("sync", dr["s_ssm"][li, b].rearrange("h n d -> n h d"), Sb.rearrange("p (h d) -> p h d", h=32), f"sst{b % 3}")
        for g in range(4):
            self.cp("vector" if g % 2 else "scalar", y_[:, g * 512:(g + 1) * 512], psy[g][0:NS, :])
        y3 = y_[:].rearrange("p (h d) -> p h d", h=32)
        vv3 = vv[:].rearrange("p (h d) -> p h d", h=32)
        self.tt("gpsimd", vv3, xs3, pk[0:NS, PK[f"sDrep{li}"]:PK[f"sDrep{li}"] + 32].unsqueeze(2).to_broadcast([NS, 32, 64]), ALU.mult)
        self.tt("vector", y_[:], y_[:], vv[:], ALU.add)
        for q in range(4):
            self.act(vv[:, q * 512:(q + 1) * 512], prj[:, q * 512:(q + 1) * 512], AF.Silu)
        self.tt("vector", y_[:], y_[:], vv[:], ALU.mult)
        self.tt("gpsimd", vv[:], y_[:], y_[:], ALU.mult)
        self.red("vector", ss, vv[:].rearrange("p (g c) -> p g c", g=4))
        self.act(ss, ss, AF.Sqrt, bias=EPS, scale=1.0 / 512)
        self.recip(ss, ss)
        self.tt("vector", y_[:].rearrange("p (g c) -> p g c", g=4), y_[:].rearrange("p (g c) -> p g c", g=4),
                ss.unsqueeze(2).to_broadcast([NS, 4, 512]), ALU.mult)
        mixTs = egB[:].bitcast(BF16)[:, 0:256].rearrange("p (k t) -> p k t", k=16)
        for k0 in (0, 8):
            p = self.ps()
            for kk in range(8):
                self.tr(p[:, kk * NS:(kk + 1) * NS], y_[:, (k0 + kk) * 128:(k0 + kk + 1) * 128], ident16)
            self.cp("vector", mixTs[:, k0:k0 + 8, :], p[:, 0:8 * NS].rearrange("p (k t) -> p k t", k=8))
        po = self.ps()
        for dc in range(8):
            wb = wo[dc % 2][:].rearrange("p (k d) -> p k d", k=16)
            self.dma("sync", wb, self.wosd[dc], f"wo{dc % 2}")
            for kc in range(16):
                self.mm(po[:, dc * NS:(dc + 1) * NS], wb[:, kc, :], mixTs[:, kc, :], start=(kc == 0), stop=(kc == 15))
        self.tt("vector", xbs[:], po[:, 0:8 * NS].rearrange("p (k t) -> p k t", k=8), xbs[:], ALU.add)
        self.dma("sync", dr["xs"][:, :, SEQ:NTOK], xbs[:], "xst")

    def mlp_phase(self, layer, last):
        dr, cst, pk = self.dr, self.cst, self.pk
        with contextlib.ExitStack() as ph:
            sb = lambda n, s, d: self.sb(ph, n, s, d)
            xall = sb("xall", [128, 8, NTOK], F32)
            hall = sb("hall", [128, 8, NTOK], BF16)
            sq = sb("msq", [128, 8, 512], BF16)
            rstd = sb("mrstd", [128, 512], F32)
            w1 = [sb(f"w1_{i}", [128, 8, 512], BF16) for i in range(2)]
            w2 = [sb(f"w2_{i}", [128, 4, D], BF16) for i in range(2)]
            rl = [sb(f"rl{i}", [128, 512], BF16) for i in range(2)]
            aT = [sb(f"aT{i}", [128, 4, 512], BF16) for i in range(2)]
            for dc in range(8):
                self.dma("sync", xall[:, dc, :], dr["xs"][:, dc, :], "xall")
            tbs = [(i * 512, 512) for i in range(4)] + [(SEQ, NS)]

            def loadw(fb):
                b = fb % 2
                self.dma("gpsimd", w1[b][:], dr["mlp_w1"][layer, :, fb * 512:(fb + 1) * 512].rearrange("(kc p) f -> p kc f", p=128), f"w1_{b}")
                self.dma("gpsimd", w2[b][:], dr["mlp_w2"][layer, fb * 512:(fb + 1) * 512, :].rearrange("(fc p) d -> p fc d", p=128), f"w2_{b}")
            import os
            KM = os.environ.get("KMLP", "full")
            def prenorm(k):
                t0_, nt_ = tbs[k]
                self.norm_block(xall[:, :, t0_:t0_ + nt_], hall[:, :, t0_:t0_ + nt_], sq, rstd, nt_, PK["nmlp"] + layer * 8)
            next_norm = 0
            if KM != "load":
                loadw(0)
                prenorm(0)
                prenorm(1)
                next_norm = 2
            if last:
                yst = [sb(f"yst{i}", [128, D], F32) for i in range(2)]
                yT = [sb(f"yT{i}", [128, 8, 128], F32) for i in range(2)]
            it = 0
            nfb = {"load": 0, "norm": 0, "fb1": 1, "fb1f": 1, "fb2": 2, "fb3": 3}.get(KM, 8)
            if KM in ("load", "norm", "fb1", "fb2", "fb3"):
                last = False
            for fb in range(nfb):
                if fb + 1 < nfb:
                    loadw(fb + 1)
                b = fb % 2
                for tbi, (t0, nt) in enumerate(tbs):
                    a = aT[it % 2]
                    it += 1
                    for fc in range(4):
                        p = self.ps()
                        for kc in range(8):
                            self.mm(p[:, :nt], w1[b][:, kc, fc * 128:(fc + 1) * 128], hall[:, kc, t0:t0 + nt], start=(kc == 0), stop=(kc == 7))
                        r_ = rl[fc % 2]
                        self.act(r_[:, :nt], p[:, :nt], AF.Relu)
                        self.tt("gpsimd", a[:, fc, :nt], r_[:, :nt], r_[:, :nt], ALU.mult)
                    for dc in range(8):
                        p = self.ps()
                        for fc in range(4):
                            self.mm(p[:, :nt], w2[b][:, fc, dc * 128:(dc + 1) * 128], a[:, fc, :nt], start=(fc == 0), stop=(fc == 3))
                        self.tt("vector", xall[:, dc, t0:t0 + nt], p[:, :nt], xall[:, dc, t0:t0 + nt], ALU.add)
                    if fb == 0 and next_norm < len(tbs):
                        prenorm(next_norm)
                        next_norm += 1
                    if fb == nfb - 1:
                        if (not last) or self.debug:
                            self.dma("sync", dr["xs"][:, :, t0:t0 + nt], xall[:, :, t0:t0 + nt], f"xall_st{tbi % 2}")
                        if last and tbi >= 1:
                            pt0, pnt = tbs[tbi - 1]
                            self.norm_block_f32(xall[:, :, pt0:pt0 + pnt], sq, rstd, pnt, PK["nfin"], yT, yst, pt0)
            if last:
                pt0, pnt = tbs[-1]
                self.norm_block_f32(xall[:, :, pt0:pt0 + pnt], sq, rstd, pnt, PK["nfin"], yT, yst, pt0)

    def norm_block_f32(self, xb, sq, rstd, ntok, wcol, yT, yst, t0):
        dr = self.dr
        for dc in range(8):
            self.act(sq[:, dc, :ntok], xb[:, dc, :ntok], AF.Square)
        p = self.ps()
        for dc in range(8):
            self.mm(p[:, :ntok], self.onesb[:], sq[:, dc, :ntok], start=(dc == 0), stop=(dc == 7))
        self.act(rstd[:, :ntok], p[:, :ntok], AF.Sqrt, bias=EPS, scale=1.0 / D)
        self.recip(rstd[:, :ntok], rstd[:, :ntok])
        ntile = (ntok + 127) // 128
        for ti in range(ntile):
            n = min(128, ntok - ti * 128)
            k = (t0 // 128 + ti) % 2
            y_, ys = yT[k], yst[k]
            for dc in range(8):
                self.stt("vector" if dc % 2 else "gpsimd", y_[:, dc, :n], xb[:, dc, ti * 128:ti * 128 + n], self.pk[:, wcol + dc:wcol + dc + 1],
                         rstd[:, ti * 128:ti * 128 + n], ALU.mult, ALU.mult)
            for half in range(2):
                p = self.ps()
                for q in range(4):
                    dc = half * 4 + q
                    self.tr(p[0:n, q * 128:(q + 1) * 128], y_[:, dc, :n], self.ident)
                self.cp("vector" if half == 0 else "scalar", ys[0:n, half * 512:(half + 1) * 512], p[0:n, :])
            if t0 < SEQ:
                self.dma("sync", dr["y_prompt"][t0 + ti * 128:t0 + ti * 128 + n, :], ys[0:n, :], f"yout{k}")
            else:
                self.dma("sync", dr["y_sample"][0:n, :], ys[0:n, :], f"yout{k}")


_CACHE = {}


def _prep_inputs(inp):
    pk = make_pk(inp)
    cst = make_consts()
    rope = make_rope()
    shared = {k: np.ascontiguousarray(inp[k], dtype=np.float32) for k in
              ("w_in_hyb", "w_out_hyb", "w_in_ssm", "w_out_ssm", "mlp_w1", "mlp_w2", "gdn_conv_w", "ssm_conv_w", "ssm_conv_b")}
    maps = []
    for c in range(NCORES):
        m = dict(shared)
        m["pk"] = pk
        m["cst"] = cst
        m["rope"] = rope
        m["x_prompt"] = np.ascontiguousarray(inp["x_prompt"][c])
        m["x_sample"] = np.ascontiguousarray(inp["x_sample"][c * NS:(c + 1) * NS, 0])
        for k in ("state_gdn", "state_gdn_conv", "state_ret", "state_ssm", "state_ssm_conv"):
            m[k] = np.ascontiguousarray(inp[k][:, c * NS:(c + 1) * NS])
        maps.append(m)
    return maps


def kernel(**inp):
    if "nc" not in _CACHE:
        _CACHE["nc"] = Builder().build()
    nc = _CACHE["nc"]
    maps = _prep_inputs(inp)
    res = run_bass_kernel_spmd(nc, maps, core_ids=list(range(NCORES)))
    R = res.results
    y_prompt = np.stack([R[c]["y_prompt"] for c in range(NCORES)], 0)
    y_sample = np.concatenate([R[c]["y_sample"] for c in range(NCORES)], 0)[:, None, :]

    def pcat(name):
        return np.stack([R[c][name] for c in range(NCORES)], 1)

    def scat(name):
        return np.concatenate([R[c][name] for c in range(NCORES)], 1)
    return (y_prompt, y_sample, pcat("p_gdn"), pcat("p_gdn_conv"), pcat("p_ret"), pcat("p_ssm"), pcat("p_ssm_conv"),
            scat("s_gdn"), scat("s_gdn_conv"), scat("s_ret"), scat("s_ssm"), scat("s_ssm_conv"))
```

```python
import contextlib
import math
import numpy as np
import concourse.bass as bass
import concourse.mybir as mybir
from concourse.bass_utils import run_bass_kernel_spmd

F32 = mybir.dt.float32
BF16 = mybir.dt.bfloat16
ALU = mybir.AluOpType
AF = mybir.ActivationFunctionType
AX = mybir.AxisListType

ENGS = ("sync", "scalar", "vector", "gpsimd", "tensor")
EPOCH = 30000
NCORES = 8
SEQ = 2048
NS = 16
NTOK = SEQ + NS
D = 1024
EPS = 1e-6
HYB_IN = 3592
SSM_IN = 5152
BLK = 256


def _esize(dt):
    return 2 if dt == BF16 else 4


def _box(ap):
    dims = ap.ap
    off = ap.offset
    sp = str(ap.space)
    es = _esize(ap.dtype)
    if sp in ("SB", "PSUM"):
        ps = dims[0][0]
        if ps == 0:
            ps = 1 << 30
        p0 = off // ps
        p1 = p0 + dims[0][1]
        f0 = off % ps
        ext = 1
        for st, cnt in dims[1:]:
            ext += (cnt - 1) * abs(st)
        if sp == "PSUM":
            return (sp + ap.name, (p0 // 32) * 32, ((p1 + 31) // 32) * 32, 0, 2048)
        return (sp + ap.name, p0, p1, f0 * es, (f0 + ext) * es)
    ext = 1
    for st, cnt in dims:
        ext += (cnt - 1) * abs(st)
    return (sp + ap.name, 0, 1, off * es, (off + ext) * es)


class Sched:
    def __init__(self, nc):
        self.nc = nc
        self.ops = []
        self.hist = {}
        self.chans = {}
        self.last_eng = {}
        self.pending_barrier = None

    def barrier(self):
        self.pending_barrier = (dict(self.last_eng), dict(self.last_chan_op()))
        self.barrier_seen = set()
        self.hist = {}

    def last_chan_op(self):
        d = {}
        for i, o in enumerate(self.ops):
            if o["chan"] is not None:
                d[o["chan"]] = i
        return d

    def add(self, eng, fn, reads=(), writes=(), chan=None):
        idx = len(self.ops)
        deps = {}
        rb = [_box(a) for a in reads]
        wb = [_box(a) for a in writes]
        for b in rb:
            isps = b[0].startswith("PSUM")
            for r in self.hist.get(b[0], ()):
                if r[0] < b[2] and b[1] < r[1] and r[2] < b[4] and b[3] < r[3]:
                    if r[4]:
                        deps[r[5]] = True
                    elif isps and r[5] < idx and self.ops[r[5]]["eng"] != eng:
                        deps[r[5]] = True
        for b in wb:
            for r in self.hist.get(b[0], ()):
                if r[0] < b[2] and b[1] < r[1] and r[2] < b[4] and b[3] < r[3]:
                    deps.setdefault(r[5], False)
        isdma = chan is not None
        if self.pending_barrier is not None and eng not in self.barrier_seen:
            self.barrier_seen.add(eng)
            le, lc = self.pending_barrier
            for e2, i2 in le.items():
                deps[i2] = True
            for c2, i2 in lc.items():
                deps[i2] = True
        for b in wb:
            lst = self.hist.setdefault(b[0], [])
            lst[:] = [r for r in lst if not (b[1] <= r[0] and r[1] <= b[2] and b[3] <= r[2] and r[3] <= b[4])]
            lst.append((b[1], b[2], b[3], b[4], True, idx))
        for b in rb:
            lst = self.hist.setdefault(b[0], [])
            if not isdma:
                lst[:] = [r for r in lst if not ((not r[4]) and r[5] < idx and self.ops[r[5]]["eng"] == eng and self.ops[r[5]]["chan"] is None
                                                 and b[1] <= r[0] and r[1] <= b[2] and b[3] <= r[2] and r[3] <= b[4])]
            lst.append((b[1], b[2], b[3], b[4], False, idx))
        if isdma:
            self.chans[chan] = self.chans.get(chan, 0) + 1
        else:
            self.last_eng[eng] = idx
        self.ops.append(dict(eng=eng, fn=fn, deps=deps, chan=chan))
        return idx

    def emit(self):
        nc = self.nc
        ops = self.ops
        need = [False] * len(ops)
        for c, o in enumerate(ops):
            kept = {}
            for p, raw in o["deps"].items():
                po = ops[p]
                if po["chan"] is None and po["eng"] == o["eng"] and o["chan"] is None:
                    if o["eng"] == "tensor":
                        continue
                kept[p] = raw
                need[p] = True
            o["deps"] = kept
        sigidx = {}
        cnt = {e: 0 for e in ENGS}
        for i, o in enumerate(ops):
            if o["chan"] is None and need[i]:
                cnt[o["eng"]] += 1
                sigidx[i] = cnt[o["eng"]]
        nep = {e: (cnt[e] + EPOCH - 1) // EPOCH for e in ENGS}
        self.cnt = cnt
        with contextlib.ExitStack() as st:
            esem = {e: [st.enter_context(nc.semaphore(f"s_{e}_{j}")) for j in range(max(1, nep[e]))] for e in ENGS}
            csem = {c: st.enter_context(nc.semaphore(f"c_{c}")) for c in self.chans}
            waited = {e: {} for e in ENGS}
            chan_issued = {c: 0 for c in self.chans}
            chan_tgt = {c: 0 for c in self.chans}
            plan = {e: [] for e in ENGS}
            for i, o in enumerate(ops):
                E = o["eng"]
                w = {}
                for p in o["deps"]:
                    po = ops[p]
                    if po["chan"] is None:
                        s = sigidx[p]
                        key = ("e", po["eng"], (s - 1) // EPOCH)
                        val = (s - 1) % EPOCH + 1
                    else:
                        ch = po["chan"]
                        tgt = chan_issued[ch]
                        chan_tgt[ch] = max(chan_tgt[ch], tgt)
                        key = ("c", ch)
                        val = 16 * tgt
                    if val > w.get(key, 0):
                        w[key] = val
                if o["chan"] is not None:
                    ch = o["chan"]
                    if chan_tgt[ch] > 0:
                        key = ("c", ch)
                        w[key] = max(w.get(key, 0), 16 * chan_tgt[ch])
                    chan_issued[ch] += 1
                waits = []
                for key, val in w.items():
                    if val > waited[E].get(key, 0):
                        waited[E][key] = val
                        sem = csem[key[1]] if key[0] == "c" else esem[key[1]][key[2]]
                        waits.append((sem, val))
                sig = None
                if o["chan"] is not None:
                    sig = (csem[o["chan"]], 16)
                elif i in sigidx:
                    s = sigidx[i]
                    sig = (esem[E][(s - 1) // EPOCH], 1)
                plan[E].append((o["fn"], waits, sig))
            fin = [(csem[ch], 16 * n) for ch, n in chan_issued.items() if n]
            self.n_instr = {e: len(plan[e]) for e in ENGS}
            with nc.Block() as block:
                def mk(E):
                    def body(eng):
                        for fn, waits, sig in plan[E]:
                            for sem, val in waits:
                                eng.wait_ge(sem, val)
                            ins = fn(eng)
                            if sig is not None:
                                ins.then_inc(sig[0], sig[1])
                        if E == "sync":
                            for sem, val in fin:
                                eng.wait_ge(sem, val)
                    return body
                block.sync(mk("sync"))
                block.scalar(mk("scalar"))
                block.vector(mk("vector"))
                block.gpsimd(mk("gpsimd"))
                block.tensor(mk("tensor"))


C_IDENT, C_U, C_NEG, C_INCL, C_STRICT, C_ONES, C_NONES, C_PT = [i * 128 for i in range(8)]
C_DMT = 8 * 128
C_QDEC = C_DMT + 512
C_KDEC = C_QDEC + 512
C_ID16 = C_KDEC + 256
C_ROPES = C_ID16 + 256
C_G128 = C_ROPES + 64
NCST = C_G128 + 4


def make_consts():
    c = np.zeros((128, NCST), np.float64)
    j = np.arange(128)[:, None]
    i = np.arange(128)[None, :]
    c[:, C_IDENT:C_IDENT + 128] = (j == i)
    c[:, C_U:C_U + 128] = (j <= i)
    c[:, C_NEG:C_NEG + 128] = np.where(i >= j, 0.0, -1e30)
    c[:, C_INCL:C_INCL + 128] = (i >= j)
    c[:, C_STRICT:C_STRICT + 128] = (i > j)
    c[:, C_ONES:C_ONES + 128] = 1.0
    c[:, C_NONES:C_NONES + 128] = -1.0
    PT = np.zeros((128, 128))
    for m in range(128):
        if m % 64 < 32:
            PT[m + 32, m] = -1.0
        else:
            PT[m - 32, m] = 1.0
    c[:, C_PT:C_PT + 128] = PT
    gam = 1.0 - 2.0 ** (-5.0 - np.arange(4))
    for h in range(4):
        c[:, C_DMT + h * 128:C_DMT + (h + 1) * 128] = np.where(i >= j, gam[h] ** np.maximum(i - j, 0), 0.0)
        c[:, C_KDEC + h * 64:C_KDEC + (h + 1) * 64] = (gam[h] ** (127 - j))
    for h in range(4):
        c[:, C_QDEC + h * 128:C_QDEC + (h + 1) * 128] = gam[h] ** (i + 1)
    c[:, C_ID16:C_ID16 + 256] = np.eye(16).reshape(1, 256)
    half = 32
    inv = (10000.0 ** (-np.arange(half, dtype=np.float32) / half)).astype(np.float32)
    ang = (np.float32(16384.0) * inv).astype(np.float32).astype(np.float64)
    c[:, C_ROPES:C_ROPES + 32] = np.cos(ang)[None, :]
    c[:, C_ROPES + 32:C_ROPES + 64] = np.sin(ang)[None, :]
    c[:, C_G128:C_G128 + 4] = gam[None, :] ** 128
    return c.astype(np.float32)


GAM = [1.0 - 2.0 ** (-5.0 - h) for h in range(4)]


def make_rope():
    half = 32
    inv = (10000.0 ** (-np.arange(half, dtype=np.float32) / half)).astype(np.float32)
    pos = np.concatenate([np.arange(SEQ), np.full(NS, 16384)]).astype(np.float32)
    ang = (pos[None, :] * inv[:, None]).astype(np.float32).astype(np.float64)
    r = np.zeros((2, 128, NTOK), np.float32)
    for p in range(128):
        r[0, p] = np.cos(ang[p % 32])
        r[1, p] = np.sin(ang[p % 32])
    return r


PK = {}


def _pk_layout():
    off = 0

    def put(name, n):
        nonlocal off
        PK[name] = off
        off += n
    put("nmix", 32)
    put("nmlp", 32)
    put("nfin", 8)
    for i in range(2):
        put(f"gcw{i}", 48)
        put(f"gdtb{i}", 4)
        put(f"galog{i}", 4)
        put(f"gnw{i}", 1)
        put(f"rnw{i}", 4)
        put(f"scw{i}", 96)
        put(f"scb{i}", 24)
        put(f"sdtb{i}", 32)
        put(f"salog{i}", 32)
        put(f"sD{i}", 16)
        put(f"sDrep{i}", 32)
        put(f"snw{i}", 16)
    return off


NPK = _pk_layout()


def make_pk(inp):
    pk = np.zeros((128, NPK), np.float32)

    def fm(v, nch):
        return np.ascontiguousarray(v.reshape(nch, 128).T)
    for l in range(4):
        pk[:, PK["nmix"] + l * 8:PK["nmix"] + (l + 1) * 8] = fm(inp["norm_mix"][l], 8)
        pk[:, PK["nmlp"] + l * 8:PK["nmlp"] + (l + 1) * 8] = fm(inp["norm_mlp"][l], 8)
    pk[:, PK["nfin"]:PK["nfin"] + 8] = fm(inp["norm_final"], 8)
    for i in range(2):
        cw = inp["gdn_conv_w"][i]
        pk[:, PK[f"gcw{i}"]:PK[f"gcw{i}"] + 48] = cw.reshape(4, 12, 128).transpose(2, 1, 0).reshape(128, 48)
        pk[:, PK[f"gdtb{i}"]:PK[f"gdtb{i}"] + 4] = inp["gdn_dt_bias"][i][None, :]
        pk[:, PK[f"galog{i}"]:PK[f"galog{i}"] + 4] = inp["gdn_a_log"][i][None, :]
        pk[:, PK[f"gnw{i}"]] = inp["gdn_norm_w"][i]
        pk[:, PK[f"rnw{i}"]:PK[f"rnw{i}"] + 4] = fm(inp["ret_norm_w"][i], 4)
        sw = inp["ssm_conv_w"][i]
        pk[:, PK[f"scw{i}"]:PK[f"scw{i}"] + 96] = sw.reshape(4, 24, 128).transpose(2, 1, 0).reshape(128, 96)
        pk[:, PK[f"scb{i}"]:PK[f"scb{i}"] + 24] = fm(inp["ssm_conv_b"][i], 24)
        pk[:, PK[f"sdtb{i}"]:PK[f"sdtb{i}"] + 32] = inp["ssm_dt_bias"][i][None, :]
        pk[:, PK[f"salog{i}"]:PK[f"salog{i}"] + 32] = inp["ssm_a_log"][i][None, :]
        pk[:, PK[f"sD{i}"]:PK[f"sD{i}"] + 16] = np.repeat(inp["ssm_d"][i].reshape(16, 2), 64, axis=1).T
        pk[:, PK[f"sDrep{i}"]:PK[f"sDrep{i}"] + 32] = inp["ssm_d"][i][None, :]
        pk[:, PK[f"snw{i}"]:PK[f"snw{i}"] + 16] = fm(inp["ssm_norm_w"][i], 16)
    return pk


class StopBuild(Exception):
    pass


class Builder:
    def chk(self, k):
        import os
        return int(os.environ.get("KH", "99")) == k

    def __init__(self, depth=4, do_sample=True, debug=False):
        self.depth = depth
        self.do_sample = do_sample
        self.debug = debug
        nc = bass.Bass("TRN2", target_bir_lowering=False)
        self.nc = nc
        self.S = Sched(nc)
        self._psi = 0
        self._u = 0

    def mm(self, out, lhsT, rhs, start=True, stop=True):
        self.S.add("tensor", lambda e: e.matmul(out, lhsT=lhsT, rhs=rhs, start=start, stop=stop),
                   reads=[lhsT, rhs], writes=[out])

    def tr(self, out, in_, ident):
        self.S.add("tensor", lambda e: e.transpose(out=out, in_=in_, identity=ident), reads=[in_, ident], writes=[out])

    def act(self, out, in_, func, bias=None, scale=None):
        if func == AF.Sqrt:
            self.act(out, in_, AF.Ln, bias=bias, scale=scale)
            self.act(out, out, AF.Exp, scale=-0.5)
            return
        rd = [in_]
        kw = {}
        if bias is not None:
            kw["bias"] = bias
            if not isinstance(bias, (int, float)):
                rd.append(bias)
        if scale is not None:
            kw["scale"] = scale
            if not isinstance(scale, (int, float)):
                rd.append(scale)
        self.S.add("scalar", lambda e: e.activation(out=out, in_=in_, func=func, **kw), reads=rd, writes=[out])

    def tt(self, eng, out, in0, in1, op):
        self.S.add(eng, lambda e: e.tensor_tensor(out=out, in0=in0, in1=in1, op=op), reads=[in0, in1], writes=[out])

    def ts(self, eng, out, in0, s1, op0, s2=None, op1=None):
        rd = [in0]
        if not isinstance(s1, (int, float)):
            rd.append(s1)
        if s2 is not None and not isinstance(s2, (int, float)):
            rd.append(s2)
        if op1 is None:
            self.S.add(eng, lambda e: e.tensor_scalar(out=out, in0=in0, scalar1=s1, scalar2=None, op0=op0), reads=rd, writes=[out])
        else:
            self.S.add(eng, lambda e: e.tensor_scalar(out=out, in0=in0, scalar1=s1, scalar2=s2, op0=op0, op1=op1), reads=rd, writes=[out])

    def stt(self, eng, out, in0, scalar, in1, op0, op1):
        rd = [in0, in1]
        if not isinstance(scalar, (int, float)):
            rd.append(scalar)
        self.S.add("vector", lambda e: e.scalar_tensor_tensor(out=out, in0=in0, scalar=scalar, in1=in1, op0=op0, op1=op1),
                   reads=rd, writes=[out])

    def cp(self, eng, out, in_):
        if eng == "scalar":
            self.S.add("scalar", lambda e: e.copy(out=out, in_=in_), reads=[in_], writes=[out])
        else:
            self.S.add(eng, lambda e: e.tensor_copy(out=out, in_=in_), reads=[in_], writes=[out])

    def recip(self, out, in_):
        return

    def memset(self, eng, out, val):
        self.S.add(eng, lambda e: e.memset(out, val), writes=[out])

    def dma(self, eng, out, in_, chan):
        self.S.add(eng, lambda e: e.dma_start(out=out, in_=in_), reads=[in_], writes=[out], chan=chan)

    def ps(self):
        self._psi = (self._psi + 1) % len(self.psf)
        return self.psf[self._psi]

    def psbf(self):
        self._u = (self._u + 1) % len(self.psb)
        return self.psb[self._u]

    def ve(self):
        self._u2 = getattr(self, "_u2", 0) + 1
        return "vector" if self._u2 % 2 else "gpsimd"

    def sb(self, st, name, shape, dt):
        self._nid = getattr(self, "_nid", 0) + 1
        return st.enter_context(self.nc.sbuf_tensor(f"s{self._nid}_{name}", shape, dt))

    def build(self):
        nc = self.nc
        dr = {}

        def din(name, shape):
            dr[name] = nc.dram_tensor(name, shape, F32, kind="ExternalInput").ap()

        def dout(name, shape):
            dr[name] = nc.dram_tensor(name, shape, F32, kind="ExternalOutput").ap()
        din("x_prompt", [SEQ, D])
        din("x_sample", [NS, D])
        din("state_gdn", [2, NS, 4, 128, 128])
        din("state_gdn_conv", [2, NS, 3, 1536])
        din("state_ret", [2, NS, 4, 64, 128])
        din("state_ssm", [2, NS, 32, 128, 64])
        din("state_ssm_conv", [2, NS, 3, 3072])
        din("w_in_hyb", [2, D, HYB_IN])
        din("w_out_hyb", [2, D, D])
        din("w_in_ssm", [2, D, SSM_IN])
        din("w_out_ssm", [2, 2048, D])
        din("mlp_w1", [4, D, 4096])
        din("mlp_w2", [4, 4096, D])
        din("gdn_conv_w", [2, 4, 1536])
        din("ssm_conv_w", [2, 4, 3072])
        din("ssm_conv_b", [2, 3072])
        din("pk", [128, NPK])
        din("cst", [128, NCST])
        din("rope", [2, 128, NTOK])
        dout("y_prompt", [SEQ, D])
        dout("y_sample", [NS, D])
        dout("p_gdn", [2, 4, 128, 128])
        dout("p_gdn_conv", [2, 3, 1536])
        dout("p_ret", [2, 4, 64, 128])
        dout("p_ssm", [2, 32, 128, 64])
        dout("p_ssm_conv", [2, 3, 3072])
        dout("s_gdn", [2, NS, 4, 128, 128])
        dout("s_gdn_conv", [2, NS, 3, 1536])
        dout("s_ret", [2, NS, 4, 64, 128])
        dout("s_ssm", [2, NS, 32, 128, 64])
        dout("s_ssm_conv", [2, NS, 3, 3072])
        if self.debug:
            dout("xs", [128, 8, NTOK])
        else:
            dr["xs"] = nc.dram_tensor("xs", [128, 8, NTOK], F32, kind="Internal").ap()
        self.dr = dr
        with contextlib.ExitStack() as top:
            self.psf = [top.enter_context(nc.psum_tensor(f"psf{i}", [128, 512], F32)) for i in range(6)]
            self.psb = [top.enter_context(nc.psum_tensor(f"psb{i}", [128, 1024], BF16)) for i in range(2)]
            cst = self.sb(top, "cst", [128, NCST], F32)
            pk = self.sb(top, "pk", [128, NPK], F32)
            self.cst, self.pk = cst, pk
            self.dma("sync", cst[:], dr["cst"], "cst")
            self.dma("sync", pk[:], dr["pk"], "cst")
            self.identb = self.sb(top, "identb", [128, 128], BF16)
            self.onesb = self.sb(top, "onesb", [128, 128], BF16)
            self.PTb = self.sb(top, "PTb", [128, 128], BF16)
            self.cp("vector", self.identb[:], cst[:, C_IDENT:C_IDENT + 128])
            self.cp("vector", self.onesb[:], cst[:, C_ONES:C_ONES + 128])
            self.cp("vector", self.PTb[:], cst[:, C_PT:C_PT + 128])
            self.ident = cst[:, C_IDENT:C_IDENT + 128]
            self.onesf = cst[:, C_ONES:C_ONES + 128]
            self.nonesf = cst[:, C_NONES:C_NONES + 128]
            self.U = cst[:, C_U:C_U + 128]
            self.zcol = self.sb(top, "zcol", [128, 4], F32)
            self.memset("gpsimd", self.zcol[:], 0.0)
            self.negA = self.sb(top, "negA", [128, 72], F32)
            for i in range(2):
                self.act(self.negA[:, i * 4:(i + 1) * 4], pk[:, PK[f"galog{i}"]:PK[f"galog{i}"] + 4], AF.Exp)
                self.act(self.negA[:, 8 + i * 32:8 + (i + 1) * 32], pk[:, PK[f"salog{i}"]:PK[f"salog{i}"] + 32], AF.Exp)
            self.ts("vector", self.negA[:], self.negA[:], -1.0, ALU.mult)

            self.prologue()
            for layer in range(self.depth):
                self.S.barrier()
                import os
                if "mix" in os.environ.get("KSKIP", ""):
                    pass
                elif layer % 2 == 0:
                    self.hybrid_phase(layer // 2, layer)
                else:
                    self.ssd_phase(layer // 2, layer)
                self.S.barrier()
                if "mlp" not in os.environ.get("KSKIP", ""):
                    self.mlp_phase(layer, last=(layer == self.depth - 1))
            self.S.emit()
        return nc

    def prologue(self):
        dr = self.dr
        with contextlib.ExitStack() as st:
            xin = [self.sb(st, f"xin{i}", [128, D], F32) for i in range(2)]
            stg = [self.sb(st, f"xstg{i}", [128, 8, BLK], F32) for i in range(2)]
            for blk in range(SEQ // BLK):
                sg = stg[blk % 2]
                for c in range(2):
                    t = blk * 2 + c
                    xi = xin[t % 2]
                    self.dma("sync", xi[:], dr["x_prompt"][t * 128:(t + 1) * 128, :], f"xin{t % 2}")
                    for half in range(2):
                        p = self.ps()
                        for q in range(4):
                            dc = half * 4 + q
                            self.tr(p[:, q * 128:(q + 1) * 128], xi[:, dc * 128:(dc + 1) * 128], self.ident)
                        src = p[:].rearrange("p (q t) -> p q t", q=4)
                        self.cp("vector" if half == 0 else "scalar", sg[:, half * 4:half * 4 + 4, c * 128:(c + 1) * 128], src)
                self.dma("sync", dr["xs"][:, :, blk * BLK:(blk + 1) * BLK], sg[:], f"xst{blk % 2}")
            xi = xin[0]
            self.dma("sync", xi[0:NS, :], dr["x_sample"], "xin0")
            p = self.ps()
            for dc in range(8):
                self.tr(p[:, dc * NS:(dc + 1) * NS], xi[0:NS, dc * 128:(dc + 1) * 128], self.ident[0:NS, 0:NS])
            sg = stg[0]
            self.cp("vector", sg[:, :, 0:NS], p[:, 0:8 * NS].rearrange("p (q t) -> p q t", q=8))
            self.dma("sync", dr["xs"][:, :, SEQ:NTOK], sg[:, :, 0:NS], "xst0")

    def norm_block(self, xb, hT, sq, rstd, ntok, wcol):
        for dc in range(8):
            self.act(sq[:, dc, :ntok], xb[:, dc, :ntok], AF.Square)
        p = self.ps()
        for dc in range(8):
            self.mm(p[:, :ntok], self.onesb[:], sq[:, dc, :ntok], start=(dc == 0), stop=(dc == 7))
        self.act(rstd[:, :ntok], p[:, :ntok], AF.Sqrt, bias=EPS, scale=1.0 / D)
        self.recip(rstd[:, :ntok], rstd[:, :ntok])
        for dc in range(8):
            self.stt("vector" if dc % 2 else "gpsimd", hT[:, dc, :ntok], xb[:, dc, :ntok], self.pk[:, wcol + dc:wcol + dc + 1],
                     rstd[:, :ntok], ALU.mult, ALU.mult)

    def proj_fm(self, p, wi, col0, ncols, hT, ntok, pcol0=0):
        for kc in range(8):
            self.mm(p[:ncols, pcol0:pcol0 + ntok], wi[:, kc, col0:col0 + ncols], hT[:, kc, :ntok], start=(kc == 0), stop=(kc == 7))

    def hybrid_phase(self, li, layer):
        dr, cst, pk = self.dr, self.cst, self.pk
        with contextlib.ExitStack() as ph:
            sb = lambda n, s, d: self.sb(ph, n, s, d)
            wi = sb("wi", [128, 8, HYB_IN], BF16)
            wo = sb("wo", [128, 8, D], BF16)
            for kc in range(8):
                self.dma("gpsimd", wi[:, kc, :], dr["w_in_hyb"][li, kc * 128:(kc + 1) * 128, :], "wi")
            for kc in range(8):
                self.dma("gpsimd", wo[:, kc, :], dr["w_out_hyb"][li, kc * 128:(kc + 1) * 128, :], "wo")
            for kc in range(8):
                col = PK[f"gnw{li}"] if kc < 4 else PK[f"rnw{li}"] + kc - 4
                self.ts("vector", wo[:, kc, :], wo[:, kc, :], pk[:, col:col + 1], ALU.mult)
            Sg = sb("Sg", [128, 4, 128], F32)
            Sgb = sb("Sgb", [128, 4, 128], BF16)
            Sr = sb("Sr", [64, 4, 128], F32)
            Srb = sb("Srb", [64, 4, 128], BF16)
            hist = sb("hist", [128, 12, 3], F32)
            for t_ in (Sg, Sgb, Sr, Srb, hist):
                self.memset("gpsimd", t_[:], 0.0)
            with contextlib.ExitStack() as bs:
                self.hybrid_prompt(li, layer, bs, wi, wo, Sg, Sgb, Sr, Srb, hist)
            self.dma("sync", dr["p_gdn"][li].rearrange("h k v -> k h v"), Sg[:], "pst")
            self.dma("sync", dr["p_ret"][li].rearrange("h k v -> k h v"), Sr[:], "pst")
            if self.do_sample:
                self.S.barrier()
                with contextlib.ExitStack() as bs:
                    self.hybrid_sample(li, layer, bs, wi, wo)

    def hybrid_prompt(self, li, layer, bs, wi, wo, Sg, Sgb, Sr, Srb, hist):
        dr, cst, pk = self.dr, self.cst, self.pk
        sb = lambda n, s, d: self.sb(bs, n, s, d)
        xb = sb("xb", [128, 8, BLK], F32)
        hT = sb("hT", [128, 8, BLK], BF16)
        sq = sb("sq", [128, 8, BLK], BF16)
        rstd = sb("rstd", [128, BLK], F32)
        rope = sb("rope", [128, 2, BLK], F32)
        ctmp = [sb(f"ctmp{i}", [128, BLK + 3], F32) for i in range(3)]
        cacc = [sb(f"cacc{i}", [128, BLK], F32) for i in range(3)]
        qkf = sb("qkf", [128, 8, BLK], BF16)
        qkT = sb("qkT", [128, 8, BLK], BF16)
        vT = sb("vT", [128, 4, BLK], BF16)
        szT = sb("szT", [128, 4, BLK], BF16)
        sgT = sb("sgT", [128, 4, BLK], BF16)
        rawb = sb("rawb", [64, 8, BLK], BF16)
        rot = sb("rot", [64, 8, BLK], BF16)
        rt1 = [sb(f"rt1{i}", [128, BLK], F32) for i in range(2)]
        rt2 = [sb(f"rt2{i}", [128, BLK], F32) for i in range(2)]
        smallf = sb("smallf", [128, 160], F32)
        v_tok = sb("v_tok", [128, 2, 512], BF16)
        kg_tok = sb("kg_tok", [128, 2, 512], BF16)
        kd_tok = sb("kd_tok", [128, 2, 512], BF16)
        vb_tok = sb("vb_tok", [128, 2, 512], BF16)
        kdr_tok = sb("kdr_tok", [128, 2, 256], BF16)
        decT = sb("decT", [128, 2, 512], F32)
        decS = sb("decS", [128, 512], F32)
        EgB = sb("EgB", [128, 512], F32)
        qgT = sb("qgT", [128, 2, 512], BF16)
        QK = sb("QK", [128, 2, 512], BF16)
        SR = sb("SR", [128, 512], BF16)
        qgr = sb("qgr", [64, 4, 128], BF16)
        NX = [[sb(f"NX{c}{k}", [128, 512], F32) for k in range(2)] for c in range(2)]
        NXT = [[sb(f"NXT{c}{k}", [128, 512], F32) for k in range(2)] for c in range(2)]
        NP = [[sb(f"NP{c}{k}", [128, 512], F32) for k in range(2)] for c in range(2)]
        TTb = sb("TTb", [128, 2, 512], BF16)
        nw0T = sb("nw0T", [128, 2, 512], BF16)
        delta = sb("delta", [128, 512], BF16)
        oT = sb("oT", [128, 512], F32)
        ob16 = sq[:, 2:4, :].rearrange("p a b -> p (a b)")
        osq = sq[:, 0:2, :].rearrange("p a b -> p (a b)")
        orr = sb("orr", [128, 512], F32)
        otmp = sb("otmp", [128, 512], F32)
        gU = orr[:].rearrange("p (h i) -> p h i", h=4)
        dtmp = otmp[:].rearrange("p (h i) -> p h i", h=4)
        mixT = sb("mixT", [128, 8, BLK], BF16)
        beta = smallf[:, 0:8]
        negbeta = smallf[:, 8:16]
        xg = smallf[:, 16:24]
        ax = smallf[:, 24:32]
        g_t = smallf[:, 32:40]
        gcum = smallf[:, 40:48]
        gend = smallf[:, 48:56]
        eg = smallf[:, 56:64]
        wdec = smallf[:, 64:72]
        egend = smallf[:, 72:80]
        nblk = SEQ // BLK
        dtb = pk[:, PK[f"gdtb{li}"]:PK[f"gdtb{li}"] + 4]
        nA = self.negA[:, li * 4:(li + 1) * 4]
        def body(blk):
            t0 = blk * BLK
            self.dma("sync", xb[:], dr["xs"][:, :, t0:t0 + BLK], "xld")
            self.dma("sync", rope[:, 0, :], dr["rope"][0, :, t0:t0 + BLK], "rope")
            self.dma("sync", rope[:, 1, :], dr["rope"][1, :, t0:t0 + BLK], "rope")
            self.norm_block(xb, hT, sq, rstd, BLK, PK["nmix"] + layer * 8)
            for g3 in range(4):
                chs = [g3 * 3 + u for u in range(3)]
                pp = {}
                for ch in chs:
                    pp[ch] = self.ps()
                    self.proj_fm(pp[ch], wi, ch * 128, 128, hT, BLK)
                for ch in chs:
                    self.cp("scalar", ctmp[ch % 3][:, 3:3 + BLK], pp[ch][:, :BLK])
                    self.cp("gpsimd", ctmp[ch % 3][:, 0:3], hist[:, ch, :])
                for ch in chs:
                    wc = PK[f"gcw{li}"] + ch * 4
                    self.ts("vector", cacc[ch % 3][:], ctmp[ch % 3][:, 0:BLK], pk[:, wc:wc + 1], ALU.mult)
                for tp in range(1, 4):
                    for ch in chs:
                        wc = PK[f"gcw{li}"] + ch * 4
                        self.stt("vector", cacc[ch % 3][:], ctmp[ch % 3][:, tp:tp + BLK], pk[:, wc + tp:wc + tp + 1], cacc[ch % 3][:], ALU.mult, ALU.add)
                for ch in chs:
                    self.cp("gpsimd", hist[:, ch, :], ctmp[ch % 3][:, BLK:BLK + 3])
                    if ch < 8:
                        self.act(qkf[:, ch, :], cacc[ch % 3][:], AF.Silu)
                    else:
                        self.act(vT[:, ch - 8, :], cacc[ch % 3][:], AF.Silu)
            if self.chk(1):
                return
            for pr in range(4):
                p = self.ps()
                for u in range(2):
                    ch = pr * 2 + u
                    self.act(sq[:, ch, :], qkf[:, ch, :], AF.Square)
                    self.mm(p[:, u * BLK:(u + 1) * BLK], self.onesb[:], sq[:, ch, :])
                rr = rt1[pr % 2]
                rr2 = rt2[pr % 2]
                self.act(rr[:], p[:, 0:BLK], AF.Sqrt, bias=EPS, scale=1.0)
                self.act(rr2[:], p[:, BLK:2 * BLK], AF.Sqrt, bias=EPS, scale=1.0)
                self.recip(rr[:], rr[:])
                self.recip(rr2[:], rr2[:])
                for u, r_ in ((0, rr), (1, rr2)):
                    ch = pr * 2 + u
                    sc = 128.0 ** -0.5 if ch < 4 else 1.0
                    self.stt(self.ve(), qkT[:, ch, :], qkf[:, ch, :], sc, r_[:], ALU.mult, ALU.mult)
            if self.chk(2):
                return
            for h in range(4):
                p = self.ps()
                self.proj_fm(p, wi, 1536 + h * 128, 128, hT, BLK)
                self.act(szT[:, h, :], p[:, :BLK], AF.Silu)
                p = self.ps()
                self.proj_fm(p, wi, 3080 + h * 128, 128, hT, BLK)
                self.act(sgT[:, h, :], p[:, :BLK], AF.Silu)
            if self.chk(3):
                return
            pbg = self.ps()
            for c in range(2):
                for kc in range(8):
                    self.mm(pbg[:, c * 8:(c + 1) * 8], hT[:, kc, c * 128:(c + 1) * 128], wi[:, kc, 2048:2056], start=(kc == 0), stop=(kc == 7))
            pbg3 = pbg[:, 0:16].rearrange("p (c e) -> p c e", c=2)
            b3 = beta.rearrange("p (c h) -> p c h", c=2)
            self.act(b3, pbg3[:, :, 0:4], AF.Sigmoid)
            self.ts("vector", negbeta, beta, -1.0, ALU.mult)
            xg3 = xg.rearrange("p (c h) -> p c h", c=2)
            self.tt("vector", xg3, pbg3[:, :, 4:8], dtb.unsqueeze(1).to_broadcast([128, 2, 4]), ALU.add)
            self.act(ax, xg, AF.Abs)
            self.act(ax, ax, AF.Exp, scale=-1.0)
            self.act(ax, ax, AF.Ln, bias=1.0, scale=1.0)
            self.stt("vector", g_t, xg, 0.0, ax, ALU.max, ALU.add)
            g3 = g_t.rearrange("p (c h) -> p c h", c=2)
            self.tt("vector", g3, g3, nA.unsqueeze(1).to_broadcast([128, 2, 4]), ALU.mult)
            pg = self.ps()
            self.mm(pg[:, 0:8], self.U, g_t)
            self.mm(pg[:, 8:16], self.onesf, g_t)
            self.cp("vector", gcum, pg[:, 0:8])
            self.cp("vector", gend, pg[:, 8:16])
            self.act(eg, gcum, AF.Exp)
            self.tt("vector", wdec, gend, gcum, ALU.subtract)
            self.act(wdec, wdec, AF.Exp)
            self.act(egend, gend, AF.Exp)
            if self.chk(4):
                return
            for j2 in range(4):
                p = self.ps()
                for u in range(2):
                    j = j2 * 2 + u
                    self.proj_fm(p, wi, 2056 + j * 64, 64, hT, BLK, pcol0=u * BLK)
                self.cp("scalar", rawb[:, j2 * 2:j2 * 2 + 2, :], p[0:64, :].rearrange("p (u t) -> p u t", u=2))
            for j2 in range(4):
                p = self.ps()
                for u in range(2):
                    j = j2 * 2 + u
                    self.mm(p[0:64, u * BLK:(u + 1) * BLK], self.PTb[0:64, 0:64], rawb[:, j, :])
                sc = 1.0 if j2 < 2 else 0.125
                for u in range(2):
                    j = j2 * 2 + u
                    self.stt("vector", rt1[u][0:64, :], rawb[:, j, :], sc, rope[0:64, 0, :], ALU.mult, ALU.mult)
                    self.stt("vector", rt2[u][0:64, :], p[0:64, u * BLK:(u + 1) * BLK], sc, rope[0:64, 1, :], ALU.mult, ALU.mult)
                    self.tt("gpsimd", rot[:, j, :], rt1[u][0:64, :], rt2[u][0:64, :], ALU.add)
            if self.chk(5):
                return
            for c in range(2):
                p = self.ps()
                for kc in range(8):
                    self.mm(p[:, :], hT[:, kc, c * 128:(c + 1) * 128], wi[:, kc, 2568:3080], start=(kc == 0), stop=(kc == 7))
                self.cp("scalar", vb_tok[:, c, :], p[:, :])
            if self.chk(6):
                return
            for c in range(2):
                cs = slice(c * 128, (c + 1) * 128)
                pb = self.psbf()
                for h in range(4):
                    self.tr(pb[:, h * 128:(h + 1) * 128], qkT[:, 4 + h, cs], self.identb[:])
                src = pb[:, 0:512].rearrange("p (h k) -> p h k", h=4)
                self.tt("vector", kg_tok[:, c, :].rearrange("p (h k) -> p h k", h=4), src,
                        eg[:, c * 4:(c + 1) * 4].unsqueeze(2).to_broadcast([128, 4, 128]), ALU.mult)
                self.tt("vector", kd_tok[:, c, :].rearrange("p (h k) -> p h k", h=4), src,
                        wdec[:, c * 4:(c + 1) * 4].unsqueeze(2).to_broadcast([128, 4, 128]), ALU.mult)
                for h in range(4):
                    self.tr(pb[:, 512 + h * 128:512 + (h + 1) * 128], vT[:, h, cs], self.identb[:])
                self.cp("scalar", v_tok[:, c, :], pb[:, 512:1024])
            if self.chk(7):
                return
            for c in range(2):
                cs = slice(c * 128, (c + 1) * 128)
                self.tt("vector", gU[:], self.U.unsqueeze(1).to_broadcast([128, 4, 128]),
                        g_t[:, c * 4:(c + 1) * 4].unsqueeze(2).to_broadcast([128, 4, 128]), ALU.mult)
                pA = self.ps()
                pD = self.ps()
                for h in range(4):
                    hs = slice(h * 128, (h + 1) * 128)
                    self.mm(pD[:, hs], self.onesf, gU[:, h, :], start=True, stop=False)
                    self.mm(pD[:, hs], gU[:, h, :], self.nonesf, start=False, stop=True)
                self.mm(pA[:, :], self.onesf, gU[:].rearrange("p h i -> p (h i)"))
                self.act(EgB[:], pA[:], AF.Exp)
                self.tt("vector", dtmp[:], pD[:].rearrange("p (h i) -> p h i", h=4),
                        cst[:, C_NEG:C_NEG + 128].unsqueeze(1).to_broadcast([128, 4, 128]), ALU.add)
                self.act(decT[:, c, :], dtmp[:].rearrange("p h i -> p (h i)"), AF.Exp)
                self.tt("vector", decS[:].rearrange("p (h i) -> p h i", h=4), decT[:, c, :].rearrange("p (h i) -> p h i", h=4),
                        cst[:, C_STRICT:C_STRICT + 128].unsqueeze(1).to_broadcast([128, 4, 128]), ALU.mult)
                self.tt("vector", qgT[:, c, :].rearrange("p (h i) -> p h i", h=4), qkT[:, 0:4, cs],
                        EgB[:].rearrange("p (h i) -> p h i", h=4), ALU.mult)
                pK = self.ps()
                pQ = self.ps()
                for h in range(4):
                    hs = slice(h * 128, (h + 1) * 128)
                    self.mm(pK[:, hs], qkT[:, 4 + h, cs], qkT[:, 4 + h, cs])
                    self.mm(pQ[:, hs], qkT[:, 4 + h, cs], qkT[:, h, cs])
                for h in range(4):
                    hs = slice(h * 128, (h + 1) * 128)
                    self.stt("vector", NX[c][0][:, hs], pK[:, hs], negbeta[:, c * 4 + h:c * 4 + h + 1], decS[:, hs], ALU.mult, ALU.mult)
                self.tt("vector", QK[:, c, :], pQ[:], decT[:, c, :], ALU.mult)
            if self.chk(8):
                return
            for c in range(2):
                pT = self.ps()
                for h in range(4):
                    hs = slice(h * 128, (h + 1) * 128)
                    self.tr(pT[:, hs], NX[c][0][:, hs], self.ident)
                self.cp("scalar", NXT[c][0][:], pT[:])
                self.tt("vector", NP[c][0][:].rearrange("p (h i) -> p h i", h=4), NX[c][0][:].rearrange("p (h i) -> p h i", h=4),
                        self.ident.unsqueeze(1).to_broadcast([128, 4, 128]), ALU.add)
            cur = 0
            for s in range(1, 7):
                nxt = 1 - cur
                for c in range(2):
                    X, XT, P_ = NX[c][cur], NXT[c][cur], NP[c][cur]
                    Xn, XTn, Pn = NX[c][nxt], NXT[c][nxt], NP[c][nxt]
                    pXT = self.ps()
                    for h in range(4):
                        hs = slice(h * 128, (h + 1) * 128)
                        self.mm(pXT[:, hs], X[:, hs], XT[:, hs])
                    self.cp("scalar", XTn[:], pXT[:])
                    if s < 6:
                        pX = self.ps()
                        for h in range(4):
                            hs = slice(h * 128, (h + 1) * 128)
                            self.mm(pX[:, hs], XT[:, hs], X[:, hs])
                        self.cp("scalar", Xn[:], pX[:])
                    pP = self.ps()
                    for h in range(4):
                        hs = slice(h * 128, (h + 1) * 128)
                        self.mm(pP[:, hs], XTn[:, hs], P_[:, hs])
                    self.tt("vector", Pn[:], pP[:], P_[:], ALU.add)
                cur = nxt
            for c in range(2):
                self.cp("scalar", TTb[:, c, :], NP[c][cur][:])
                pW = self.ps()
                for h in range(4):
                    hs = slice(h * 128, (h + 1) * 128)
                    self.mm(pW[:, hs], kg_tok[:, c, hs], TTb[:, c, hs])
                self.act(nw0T[:, c, :], pW[:], AF.Copy, scale=-1.0)
            if self.chk(9):
                return
            for c in range(2):
                cs = slice(c * 128, (c + 1) * 128)
                pd = self.ps()
                for h in range(4):
                    hs = slice(h * 128, (h + 1) * 128)
                    self.mm(pd[:, hs], TTb[:, c, hs], v_tok[:, c, hs], start=True, stop=False)
                    self.mm(pd[:, hs], nw0T[:, c, hs], Sgb[:, h, :], start=False, stop=True)
                self.tt("vector", delta[:].rearrange("p (h v) -> p h v", h=4), pd[:].rearrange("p (h v) -> p h v", h=4),
                        beta[:, c * 4:(c + 1) * 4].unsqueeze(2).to_broadcast([128, 4, 128]), ALU.mult)
                py = self.ps()
                pS = self.ps()
                for h in range(4):
                    hs = slice(h * 128, (h + 1) * 128)
                    self.mm(py[:, hs], Sgb[:, h, :], qgT[:, c, hs], start=True, stop=False)
                    self.mm(py[:, hs], delta[:, hs], QK[:, c, hs], start=False, stop=True)
                    self.mm(pS[:, hs], kd_tok[:, c, hs], delta[:, hs])
                self.cp("scalar", oT[:], py[:])
                for h in range(4):
                    hs = slice(h * 128, (h + 1) * 128)
                    self.stt("vector", Sg[:, h, :], Sg[:, h, :], egend[:, c * 4 + h:c * 4 + h + 1], pS[:, hs], ALU.mult, ALU.add)
                self.cp("scalar", Sgb[:], Sg[:])
                self.act(osq, oT[:], AF.Square)
                pn = self.ps()
                self.mm(pn[:, :], self.onesb[:], osq)
                self.act(orr[:], pn[:], AF.Sqrt, bias=EPS, scale=1.0 / 128)
                self.recip(orr[:], orr[:])
                self.tt("vector", otmp[:], oT[:], orr[:], ALU.mult)
                self.tt("vector", mixT[:, 0:4, cs], otmp[:].rearrange("p (h i) -> p h i", h=4), szT[:, :, cs], ALU.mult)
                pSc = self.ps()
                for h in range(4):
                    hs = slice(h * 128, (h + 1) * 128)
                    self.mm(pSc[:, hs], rot[:, 4 + h, cs], rot[:, h, cs])
                self.tt("vector", SR[:], pSc[:], cst[:, C_DMT:C_DMT + 512], ALU.mult)
                pb = self.psbf()
                for h in range(4):
                    self.tr(pb[:, h * 64:(h + 1) * 64], rot[:, 4 + h, cs], self.identb[0:64, 0:64])
                self.tt("vector", kdr_tok[:, c, :], pb[:, 0:256], cst[:, C_KDEC:C_KDEC + 256], ALU.mult)
                self.tt("vector", qgr[:], rot[:, 0:4, cs], cst[0:64, C_QDEC:C_QDEC + 512].rearrange("p (r i) -> p r i", r=4), ALU.mult)
                py = self.ps()
                pS = self.ps()
                for h in range(4):
                    hs = slice(h * 128, (h + 1) * 128)
                    self.mm(py[:, hs], vb_tok[:, c, hs], SR[:, hs], start=True, stop=False)
                    self.mm(py[:, hs], Srb[:, h, :], qgr[:, h, :], start=False, stop=True)
                    self.mm(pS[0:64, hs], kdr_tok[:, c, h * 64:(h + 1) * 64], vb_tok[:, c, hs])
                self.cp("scalar", oT[:], py[:])
                self.cp("scalar", ob16, oT[:])
                for h in range(4):
                    hs = slice(h * 128, (h + 1) * 128)
                    self.stt("vector", Sr[:, h, :], Sr[:, h, :], GAM[h] ** 128, pS[0:64, hs], ALU.mult, ALU.add)
                self.cp("scalar", Srb[:], Sr[:])
                pm = self.ps()
                self.mm(pm[:, :], self.onesb[:], ob16)
                self.stt("vector", otmp[:], pm[:], -1.0 / 128, oT[:], ALU.mult, ALU.add)
                self.act(osq, otmp[:], AF.Square)
                pn = self.ps()
                self.mm(pn[:, :], self.onesb[:], osq)
                self.act(orr[:], pn[:], AF.Sqrt, bias=EPS, scale=1.0 / 128)
                self.recip(orr[:], orr[:])
                self.tt("vector", otmp[:], otmp[:], orr[:], ALU.mult)
                self.tt("vector", mixT[:, 4:8, cs], otmp[:].rearrange("p (h i) -> p h i", h=4), sgT[:, :, cs], ALU.mult)
            if self.chk(10):
                return
            for dc in range(8):
                p = self.ps()
                for kc in range(8):
                    self.mm(p[:, :BLK], wo[:, kc, dc * 128:(dc + 1) * 128], mixT[:, kc, :], start=(kc == 0), stop=(kc == 7))
                self.tt("vector", xb[:, dc, :], p[:, :BLK], xb[:, dc, :], ALU.add)
            self.dma("sync", dr["xs"][:, :, t0:t0 + BLK], xb[:], "xst")
            if blk == nblk - 1:
                for cb in range(3):
                    p = self.ps()
                    for kc in range(8):
                        self.mm(p[0:3, :], hT[:, kc, BLK - 3:BLK], wi[:, kc, cb * 512:(cb + 1) * 512], start=(kc == 0), stop=(kc == 7))
                    cs3 = (NX[0][0], NX[0][1], NX[1][0])[cb]
                    self.cp("vector", cs3[0:3, :], p[0:3, :])
                    self.dma("sync", dr["p_gdn_conv"][li, :, cb * 512:(cb + 1) * 512], cs3[0:3, :], "pst")
        for blk in range(nblk if not self.chk(0) else 1):
            body(blk)

    def red(self, eng, out, in_):
        self.S.add(eng, lambda e: e.tensor_reduce(out=out, in_=in_, axis=AX.X, op=ALU.add), reads=[in_], writes=[out])

    def softplus16(self, out, x, tmp):
        self.act(tmp, x, AF.Abs)
        self.act(tmp, tmp, AF.Exp, scale=-1.0)
        self.act(tmp, tmp, AF.Ln, bias=1.0, scale=1.0)
        self.stt("vector", out, x, 0.0, tmp, ALU.max, ALU.add)

    def sample_common(self, bs, layer, wi, ncols):
        dr = self.dr
        sb = lambda n, s, d: self.sb(bs, n, s, d)
        xbs = sb("xbs", [128, 8, NS], F32)
        hTs = sb("hTs", [128, 8, NS], BF16)
        sqs = sb("sqs", [128, 8, NS], BF16)
        rstds = sb("rstds", [128, NS], F32)
        prj = sb("prj", [NS, ncols], F32)
        self.dma("sync", xbs[:], dr["xs"][:, :, SEQ:NTOK], "xld")
        self.norm_block(xbs, hTs, sqs, rstds, NS, PK["nmix"] + layer * 8)
        c0 = 0
        k = 0
        while c0 < ncols:
            n = min(512, ncols - c0)
            p = self.ps()
            for kc in range(8):
                self.mm(p[0:NS, 0:n], hTs[:, kc, :], wi[:, kc, c0:c0 + n], start=(kc == 0), stop=(kc == 7))
            self.cp("vector" if k % 2 else "scalar", prj[:, c0:c0 + n], p[0:NS, 0:n])
            c0 += n
            k += 1
        return xbs, prj

    def sample_outproj(self, bs, xbs, mixs, nk, wo_get):
        dr = self.dr
        sb = lambda n, s, d: self.sb(bs, n, s, d)
        mixTs = sb("mixTs", [128, nk, NS], BF16)
        for k0 in range(0, nk, 8):
            p = self.ps()
            for k in range(k0, min(nk, k0 + 8)):
                self.tr(p[:, (k - k0) * NS:(k - k0 + 1) * NS], mixs[:, k * 128:(k + 1) * 128], self.ident[0:NS, 0:NS])
            n = min(nk, k0 + 8) - k0
            self.cp("vector", mixTs[:, k0:k0 + n, :], p[:, 0:n * NS].rearrange("p (k t) -> p k t", k=n))
        po = self.ps()
        for dc in range(8):
            for kc in range(nk):
                self.mm(po[:, dc * NS:(dc + 1) * NS], wo_get(kc, dc), mixTs[:, kc, :], start=(kc == 0), stop=(kc == nk - 1))
        self.tt("vector", xbs[:], po[:, 0:8 * NS].rearrange("p (k t) -> p k t", k=8), xbs[:], ALU.add)
        self.dma("sync", dr["xs"][:, :, SEQ:NTOK], xbs[:], "xst")

    def hybrid_sample(self, li, layer, bs, wi, wo):
        dr, cst, pk = self.dr, self.cst, self.pk
        sb = lambda n, s, d: self.sb(bs, n, s, d)
        xbs, prj = self.sample_common(bs, layer, wi, HYB_IN)
        id16 = cst[0:NS, C_ID16:C_ID16 + 256].rearrange("p (a b) -> p a b", a=16)
        id16f = cst[:, C_ID16:C_ID16 + 256].rearrange("p (a b) -> p a b", a=16)
        cw = sb("cw", [NS, 4, 512], F32)
        cbuf = sb("cbuf", [NS, 3, 512], F32)
        qkv = sb("qkv", [NS, 1536], F32)
        ctm = sb("ctm", [NS, 512], F32)
        stb = sb("stb", [128, NS, 4, 128], F32)
        kqm = sb("kqm", [128, 8, NS, NS], F32)
        ktm = [sb(f"ktm{i}", [NS, NS, 128], F32) for i in range(2)]
        qkTs = sb("qkTs", [128, 8, NS], F32)
        sm = sb("sms", [NS, 256], F32)
        t1 = sb("st1", [NS, 512], F32)
        t2 = sb("st2", [NS, 512], F32)
        dl = sb("sdl", [NS, 512], F32)
        mixs = sb("mixs", [NS, 1024], F32)
        egB = sb("egB", [128, 64], F32)
        egm = sb("egm", [NS, NS, 4], F32)
        qr = sb("qr", [NS, 4, 64], F32)
        kr = sb("kr", [NS, 4, 64], F32)
        for b in range(NS):
            self.dma("sync", stb[:, b, :, :], dr["state_gdn"][li, b].rearrange("h k v -> k h v"), "stld")
        for pc in range(3):
            c0 = pc * 512
            self.dma("sync", cw[:], dr["gdn_conv_w"][li, :, c0:c0 + 512].partition_broadcast(NS), "cwld")
            self.dma("sync", cbuf[:], dr["state_gdn_conv"][li, :, :, c0:c0 + 512], "cbld")
            self.tt("vector", ctm[:], prj[:, c0:c0 + 512], cw[:, 3, :], ALU.mult)
            for tp in range(3):
                self.tt("gpsimd", t1[:], cbuf[:, tp, :], cw[:, tp, :], ALU.mult)
                self.tt("vector", ctm[:], ctm[:], t1[:], ALU.add)
            self.act(qkv[:, c0:c0 + 512], ctm[:], AF.Silu)
            self.dma("sync", dr["s_gdn_conv"][li, :, 0:2, c0:c0 + 512], cbuf[:, 1:3, :], "cvst")
        self.dma("sync", dr["s_gdn_conv"][li, :, 2, :], prj[:, 0:1536], "cvst")
        beta = sm[:, 0:4]
        xg = sm[:, 4:8]
        tmp4 = sm[:, 8:12]
        g_ = sm[:, 12:16]
        eg = sm[:, 16:20]
        ss = sm[:, 20:28]
        qk = sm[:, 28:32]
        ss2 = sm[:, 32:36]
        rs2 = sm[:, 36:40]
        self.act(beta, prj[:, 2048:2052], AF.Sigmoid)
        self.tt("vector", xg, prj[:, 2052:2056], pk[0:NS, PK[f"gdtb{li}"]:PK[f"gdtb{li}"] + 4], ALU.add)
        self.softplus16(g_, xg, tmp4)
        self.tt("vector", g_, g_, self.negA[0:NS, li * 4:(li + 1) * 4], ALU.mult)
        self.act(eg, g_, AF.Exp)
        qk3 = qkv[:, 0:1024].rearrange("p (h k) -> p h k", h=8)
        self.tt("vector", t1[:], qkv[:, 0:512], qkv[:, 0:512], ALU.mult)
        self.tt("gpsimd", t2[:], qkv[:, 512:1024], qkv[:, 512:1024], ALU.mult)
        self.red("vector", ss[:, 0:4], t1[:].rearrange("p (h k) -> p h k", h=4))
        self.red("vector", ss[:, 4:8], t2[:].rearrange("p (h k) -> p h k", h=4))
        self.act(ss, ss, AF.Sqrt, bias=EPS, scale=1.0)
        self.recip(ss, ss)
        self.ts("vector", ss[:, 0:4], ss[:, 0:4], 128.0 ** -0.5, ALU.mult)
        self.tt("vector", qk3, qk3, ss.unsqueeze(2).to_broadcast([NS, 8, 128]), ALU.mult)
        self.tt("vector", t1[:], qkv[:, 0:512], qkv[:, 512:1024], ALU.mult)
        self.red("vector", qk, t1[:].rearrange("p (h k) -> p h k", h=4))
        p = self.ps()
        for j in range(8):
            self.tr(p[:, j * NS:(j + 1) * NS], qkv[:, j * 128:(j + 1) * 128], self.ident[0:NS, 0:NS])
        self.cp("vector", qkTs[:], p[:, 0:8 * NS].rearrange("p (j t) -> p j t", j=8))
        for j in range(8):
            self.tt("vector" if j % 2 else "gpsimd", kqm[:, j, :, :], qkTs[:, j, :].unsqueeze(1).to_broadcast([128, NS, NS]), id16f, ALU.mult)
        pk_ = self.ps()
        pq_ = self.ps()
        for h in range(4):
            hs = slice(h * 128, (h + 1) * 128)
            for b in range(NS):
                self.mm(pk_[0:NS, hs], kqm[:, 4 + h, b, :], stb[:, b, h, :], start=(b == 0), stop=(b == NS - 1))
                self.mm(pq_[0:NS, hs], kqm[:, h, b, :], stb[:, b, h, :], start=(b == 0), stop=(b == NS - 1))
        v3 = qkv[:, 1024:1536].rearrange("p (h v) -> p h v", h=4)
        eg3 = eg.unsqueeze(2).to_broadcast([NS, 4, 128])
        t13 = t1[:].rearrange("p (h v) -> p h v", h=4)
        t23 = t2[:].rearrange("p (h v) -> p h v", h=4)
        dl3 = dl[:].rearrange("p (h v) -> p h v", h=4)
        self.tt("vector", t13, pk_[0:NS, :].rearrange("p (h v) -> p h v", h=4), eg3, ALU.mult)
        self.tt("vector", t13, v3, t13, ALU.subtract)
        self.tt("vector", dl3, t13, beta.unsqueeze(2).to_broadcast([NS, 4, 128]), ALU.mult)
        self.tt("vector", t23, pq_[0:NS, :].rearrange("p (h v) -> p h v", h=4), eg3, ALU.mult)
        self.tt("vector", t13, dl3, qk.unsqueeze(2).to_broadcast([NS, 4, 128]), ALU.mult)
        self.tt("vector", t23, t23, t13, ALU.add)
        self.tt("gpsimd", t13, t23, t23, ALU.mult)
        self.red("vector", ss2, t13)
        self.act(rs2, ss2, AF.Sqrt, bias=EPS, scale=1.0 / 128)
        self.recip(rs2, rs2)
        self.tt("vector", t23, t23, rs2.unsqueeze(2).to_broadcast([NS, 4, 128]), ALU.mult)
        self.act(t1[:], prj[:, 1536:2048], AF.Silu)
        self.tt("vector", mixs[:, 0:512], t2[:], t1[:], ALU.mult)
        self.tt("vector", egm[:], eg.unsqueeze(1).to_broadcast([NS, NS, 4]),
                self.id16col(cst).to_broadcast([NS, NS, 4]), ALU.mult)
        pe = self.ps()
        self.mm(pe[:, 0:64], self.onesf[0:NS, :], egm[:].rearrange("p a h -> p (a h)"))
        self.cp("vector", egB[:], pe[:, 0:64])
        for h in range(4):
            kt = ktm[h % 2]
            self.tt("gpsimd", kt[:], qkv[:, 512 + h * 128:512 + (h + 1) * 128].unsqueeze(1).to_broadcast([NS, NS, 128]),
                    self.id16col(cst).to_broadcast([NS, NS, 128]), ALU.mult)
            for b4 in range(NS // 4):
                pu = self.ps()
                for u in range(4):
                    b = b4 * 4 + u
                    self.mm(pu[:, u * 128:(u + 1) * 128], kt[:, b, :], dl[:, h * 128:(h + 1) * 128])
                for u in range(4):
                    b = b4 * 4 + u
                    self.stt("vector", stb[:, b, h, :], stb[:, b, h, :], egB[:, b * 4 + h:b * 4 + h + 1], pu[:, u * 128:(u + 1) * 128],
                             ALU.mult, ALU.add)
        for b in range(NS):
            self.dma("sync", dr["s_gdn"][li, b].rearrange("h k v -> k h v"), stb[:, b, :, :], "stst")
        cosr = cst[0:NS, C_ROPES:C_ROPES + 32].unsqueeze(1).to_broadcast([NS, 4, 32])
        sinr = cst[0:NS, C_ROPES + 32:C_ROPES + 64].unsqueeze(1).to_broadcast([NS, 4, 32])
        ra = sb("ra", [NS, 4, 32], F32)
        rb = sb("rb", [NS, 4, 32], F32)
        for (src0, dst, sc) in ((2056, qr, 1.0), (2312, kr, 0.125)):
            src = prj[:, src0:src0 + 256].rearrange("p (h k) -> p h k", h=4)
            x1, x2 = src[:, :, 0:32], src[:, :, 32:64]
            self.tt("vector", ra[:], x1, cosr, ALU.mult)
            self.tt("vector", rb[:], x2, sinr, ALU.mult)
            self.tt("vector", dst[:, :, 0:32], ra[:], rb[:], ALU.subtract)
            self.tt("vector", ra[:], x2, cosr, ALU.mult)
            self.tt("vector", rb[:], x1, sinr, ALU.mult)
            self.tt("vector", dst[:, :, 32:64], ra[:], rb[:], ALU.add)
            if sc != 1.0:
                self.ts("vector", dst[:], dst[:], sc, ALU.mult)
        for b in range(NS):
            self.dma("sync", stb[0:64, b, :, :], dr["state_ret"][li, b].rearrange("h k v -> k h v"), "stld")
        qrT = sb("qrT", [64, 4, NS], F32)
        qrm = kqm[0:64, 0:4, :, :]
        krm = [ktm[i][:, :, 0:64] for i in range(2)]
        p = self.ps()
        for h in range(4):
            self.tr(p[0:64, h * NS:(h + 1) * NS], qr[:, h, :], self.ident[0:NS, 0:NS])
        self.cp("vector", qrT[:], p[0:64, 0:4 * NS].rearrange("p (h t) -> p h t", h=4))
        for h in range(4):
            self.tt("vector", qrm[:, h, :, :], qrT[:, h, :].unsqueeze(1).to_broadcast([64, NS, NS]), id16f[0:64], ALU.mult)
        pq_ = self.ps()
        for h in range(4):
            hs = slice(h * 128, (h + 1) * 128)
            for b in range(NS):
                self.mm(pq_[0:NS, hs], qrm[:, h, b, :], stb[0:64, b, h, :], start=(b == 0), stop=(b == NS - 1))
        vb3 = prj[:, 2568:3080].rearrange("p (h v) -> p h v", h=4)
        qkr = sm[:, 40:44]
        mean = sm[:, 44:48]
        var = sm[:, 48:52]
        self.tt("vector", dl[:, 0:256].rearrange("p (h k) -> p h k", h=4), qr[:], kr[:], ALU.mult)
        self.red("vector", qkr, dl[:, 0:256].rearrange("p (h k) -> p h k", h=4))
        self.tt("vector", t13, vb3, qkr.unsqueeze(2).to_broadcast([NS, 4, 128]), ALU.mult)
        for h in range(4):
            self.stt("vector", t2[:, h * 128:(h + 1) * 128], pq_[0:NS, h * 128:(h + 1) * 128], GAM[h], t1[:, h * 128:(h + 1) * 128], ALU.mult, ALU.add)
        self.red("vector", mean, t23)
        self.ts("vector", mean, mean, -1.0 / 128, ALU.mult)
        self.tt("vector", t23, t23, mean.unsqueeze(2).to_broadcast([NS, 4, 128]), ALU.add)
        self.tt("gpsimd", t13, t23, t23, ALU.mult)
        self.red("vector", var, t13)
        self.act(var, var, AF.Sqrt, bias=EPS, scale=1.0 / 128)
        self.recip(var, var)
        self.tt("vector", t23, t23, var.unsqueeze(2).to_broadcast([NS, 4, 128]), ALU.mult)
        self.act(t1[:], prj[:, 3080:3592], AF.Silu)
        self.tt("vector", mixs[:, 512:1024], t2[:], t1[:], ALU.mult)
        for h in range(4):
            km = krm[h % 2]
            self.tt("gpsimd", km, kr[:, h, :].unsqueeze(1).to_broadcast([NS, NS, 64]), self.id16col(cst).to_broadcast([NS, NS, 64]), ALU.mult)
            for b4 in range(NS // 4):
                pu = self.ps()
                for u in range(4):
                    b = b4 * 4 + u
                    self.mm(pu[0:64, u * 128:(u + 1) * 128], km[:, b, :], prj[:, 2568 + h * 128:2568 + (h + 1) * 128])
                for u in range(4):
                    b = b4 * 4 + u
                    self.stt("vector", stb[0:64, b, h, :], stb[0:64, b, h, :], GAM[h], pu[0:64, u * 128:(u + 1) * 128], ALU.mult, ALU.add)
        for b in range(NS):
            self.dma("sync", dr["s_ret"][li, b].rearrange("h k v -> k h v"), stb[0:64, b, :, :], "stst")
        self.sample_outproj(bs, xbs, mixs, 8, lambda kc, dc: wo[:, kc, dc * 128:(dc + 1) * 128])

    def id16col(self, cst):
        return self.ident[0:NS, 0:NS].unsqueeze(2)

    def ssd_phase(self, li, layer):
        dr, cst, pk = self.dr, self.cst, self.pk
        with contextlib.ExitStack() as ph:
            sb = lambda n, s, d: self.sb(ph, n, s, d)
            wi = sb("wis", [128, 8, SSM_IN], BF16)
            wo = [sb(f"wos{i}", [128, 2048], BF16) for i in range(2)]
            wosd = self.nc.dram_tensor(f"wos_bf{li}", [8, 128, 16, 128], BF16, kind="Internal").ap()
            self.wosd = wosd
            for kc in range(8):
                self.dma("gpsimd", wi[:, kc, :], dr["w_in_ssm"][li, kc * 128:(kc + 1) * 128, :], "wi")
            for kc in range(16):
                wb = wo[(kc // 2) % 2][:, (kc % 2) * 1024:(kc % 2 + 1) * 1024]
                self.dma("gpsimd", wb, dr["w_out_ssm"][li, kc * 128:(kc + 1) * 128, :], f"wog{kc % 4}")
                col = PK[f"snw{li}"] + kc
                self.ts("vector", wb, wb, pk[:, col:col + 1], ALU.mult)
                self.dma("sync", wosd[:, :, kc, :].rearrange("dc p d -> p dc d"), wb.rearrange("p (dc d) -> p dc d", dc=8), f"wost{kc % 4}")
            S_ = sb("S", [128, 2048], F32)
            Sbz = sb("Sbz", [128, 32, 128], BF16)
            Vz = sb("Vz", [128, 32, 128], BF16)
            hist = sb("hists", [128, 24, 3], F32)
            for t_ in (S_, Sbz, Vz, hist):
                self.memset("gpsimd", t_[:], 0.0)
            with contextlib.ExitStack() as bs:
                self.ssd_prompt(li, layer, bs, wi, wo, S_, Sbz, Vz, hist)
            self.dma("sync", dr["p_ssm"][li].rearrange("h n d -> n h d"), S_[:].rearrange("p (h d) -> p h d", h=32), "pst")
            if self.do_sample:
                self.S.barrier()
                with contextlib.ExitStack() as bs:
                    self.ssd_sample(li, layer, bs, wi, wo, [Sbz[:].bitcast(F32).rearrange("p a b -> p (a b)"), Vz[:].bitcast(F32).rearrange("p a b -> p (a b)"), S_[:]])

    def ssd_prompt(self, li, layer, bs, wi, wo, S_, Sbz, Vz, hist):
        dr, cst, pk = self.dr, self.cst, self.pk
        sb = lambda n, s, d: self.sb(bs, n, s, d)
        xb = sb("xb", [128, 8, BLK], F32)
        ynT = xb[:].bitcast(BF16).rearrange("p a (b t) -> p (a b) t", b=2)
        hT = sb("hT", [128, 8, BLK], BF16)
        rstd = sb("rstd", [128, BLK], F32)
        ctmp = [sb(f"ctmp{i}", [128, BLK + 3], F32) for i in range(3)]
        cacc = [sb(f"cacc{i}", [128, BLK], F32) for i in range(3)]
        szT = sb("szT", [128, 16, BLK], BF16)
        xsT = sb("xsT", [128, 16, BLK], BF16)
        BCT = sb("BCT", [128, 8, BLK], BF16)
        xpc = [sb(f"xpc{i}", [128, BLK], F32) for i in range(3)]
        smf = sb("smf", [128, 9 * 64], F32)
        vp = sb("vp", [128, 2048], BF16)
        B_tok = sb("B_tok", [128, 512], BF16)
        scM = sb("scM", [128, 512], F32)
        gU = sb("gU", [128, 512], F32)
        E_ = sb("E_", [128, 512], F32)
        dtm = sb("dtm", [128, 512], F32)
        SD = [sb(f"SD{i}", [128, 512], BF16) for i in range(2)]
        CgT = [sb(f"CgT{i}", [128, 512], BF16) for i in range(2)]
        y_sb = sb("y_sb", [128, 16, 128], F32)
        ysq = sb("ysq", [128, 16, 128], BF16)
        sq = ysq[:].rearrange("p (a b) i -> p a (b i)", b=2)
        rg = sb("rg", [128, 512], F32)
        dtr = smf[:, 0:64]
        ax = smf[:, 64:128]
        dt_ = smf[:, 128:192]
        g_t = smf[:, 192:256]
        gcum = smf[:, 256:320]
        gend = smf[:, 320:384]
        wdec = smf[:, 384:448]
        dtw = smf[:, 448:512]
        egend = smf[:, 512:576]
        dtb = pk[:, PK[f"sdtb{li}"]:PK[f"sdtb{li}"] + 32]
        nA = self.negA[:, 8 + li * 32:8 + (li + 1) * 32]
        nblk = SEQ // BLK
        Vzv = Vz[:].rearrange("p (q a) (b d) -> p q a b d", a=2, b=2)
        Sbzv = Sbz[:].rearrange("p (q a) (b d) -> p q a b d", a=2, b=2)

        def body(blk):
            t0 = blk * BLK
            self.dma("sync", xb[:], dr["xs"][:, :, t0:t0 + BLK], "xld")
            self.norm_block(xb, hT, sq, rstd, BLK, PK["nmix"] + layer * 8)
            if self.chk(21):
                return
            for ch in range(16):
                p = self.ps()
                self.proj_fm(p, wi, ch * 128, 128, hT, BLK)
                self.act(szT[:, ch, :], p[:, :BLK], AF.Silu)
            for g3 in range(8):
                chs = [g3 * 3 + u for u in range(3)]
                pp = {}
                for ch in chs:
                    pp[ch] = self.ps()
                    self.proj_fm(pp[ch], wi, 2048 + ch * 128, 128, hT, BLK)
                for ch in chs:
                    self.cp("scalar", ctmp[ch % 3][:, 3:3 + BLK], pp[ch][:, :BLK])
                    self.cp("gpsimd", ctmp[ch % 3][:, 0:3], hist[:, ch, :])
                for ch in chs:
                    wc = PK[f"scw{li}"] + ch * 4
                    bc = PK[f"scb{li}"] + ch
                    self.ts("vector", cacc[ch % 3][:], ctmp[ch % 3][:, 0:BLK], pk[:, wc:wc + 1], ALU.mult, pk[:, bc:bc + 1], ALU.add)
                for tp in range(1, 4):
                    for ch in chs:
                        wc = PK[f"scw{li}"] + ch * 4
                        self.stt("vector", cacc[ch % 3][:], ctmp[ch % 3][:, tp:tp + BLK], pk[:, wc + tp:wc + tp + 1], cacc[ch % 3][:], ALU.mult, ALU.add)
                for ch in chs:
                    self.cp("gpsimd", hist[:, ch, :], ctmp[ch % 3][:, BLK:BLK + 3])
                    dst = xsT[:, ch, :] if ch < 16 else BCT[:, ch - 16, :]
                    self.act(dst, cacc[ch % 3][:], AF.Silu)
            if self.chk(22):
                return
            pdt = self.ps()
            for c in range(2):
                for kc in range(8):
                    self.mm(pdt[:, c * 32:(c + 1) * 32], hT[:, kc, c * 128:(c + 1) * 128], wi[:, kc, 5120:5152], start=(kc == 0), stop=(kc == 7))
            self.tt("vector", dtr.rearrange("p (c h) -> p c h", c=2), pdt[:, 0:64].rearrange("p (c h) -> p c h", c=2),
                    dtb.unsqueeze(1).to_broadcast([128, 2, 32]), ALU.add)
            self.act(ax, dtr, AF.Abs)
            self.act(ax, ax, AF.Exp, scale=-1.0)
            self.act(ax, ax, AF.Ln, bias=1.0, scale=1.0)
            self.stt("vector", dt_, dtr, 0.0, ax, ALU.max, ALU.add)
            self.tt("vector", g_t.rearrange("p (c h) -> p c h", c=2), dt_.rearrange("p (c h) -> p c h", c=2),
                    nA.unsqueeze(1).to_broadcast([128, 2, 32]), ALU.mult)
            pg = self.ps()
            self.mm(pg[:, 0:64], self.U, g_t)
            self.mm(pg[:, 64:128], self.onesf, g_t)
            self.cp("vector", gcum, pg[:, 0:64])
            self.cp("vector", gend, pg[:, 64:128])
            self.tt("vector", wdec, gend, gcum, ALU.subtract)
            self.act(wdec, wdec, AF.Exp)
            self.tt("vector", dtw, dt_, wdec, ALU.mult)
            self.act(egend, gend, AF.Exp)
            if self.chk(23):
                return
            for c in range(2):
                cs = slice(c * 128, (c + 1) * 128)
                for grp in range(4):
                    pb = self.psbf()
                    for q in range(4):
                        self.tr(pb[:, q * 128:(q + 1) * 128], xsT[:, grp * 4 + q, cs], self.identb[:])
                    pbv = pb[:, 0:512].rearrange("p (q a d) -> p q a d", q=4, a=2)
                    dtv = dt_[:, c * 32 + grp * 8:c * 32 + grp * 8 + 8].rearrange("p (q a) -> p q a", a=2)
                    for hh in range(2):
                        self.tt("vector", Vzv[:, grp * 4:(grp + 1) * 4, hh, hh, :], pbv[:, :, hh, :],
                                dtv[:, :, hh].unsqueeze(2).to_broadcast([128, 4, 64]), ALU.mult)
                    self.tt("vector", vp[:, grp * 512:(grp + 1) * 512].rearrange("p (h d) -> p h d", h=8),
                            pb[:, 0:512].rearrange("p (h d) -> p h d", h=8),
                            dtw[:, c * 32 + grp * 8:c * 32 + grp * 8 + 8].unsqueeze(2).to_broadcast([128, 8, 64]), ALU.mult)
                pb = self.psbf()
                for g in range(4):
                    self.tr(pb[:, g * 128:(g + 1) * 128], BCT[:, g, cs], self.identb[:])
                self.cp("scalar", B_tok[:], pb[:, 0:512])
                pS = self.ps()
                for g in range(4):
                    self.mm(pS[:, g * 128:(g + 1) * 128], BCT[:, g, cs], BCT[:, 4 + g, cs])
                self.tt("vector", scM[:].rearrange("p (g i) -> p g i", g=4), pS[:].rearrange("p (g i) -> p g i", g=4),
                        cst[:, C_INCL:C_INCL + 128].unsqueeze(1).to_broadcast([128, 4, 128]), ALU.mult)
                if self.chk(24):
                    return
                py = None
                for quad in range(8):
                    g = quad // 2
                    h0 = quad * 4
                    k2 = quad % 2
                    self.tt("vector", gU[:].rearrange("p (h i) -> p h i", h=4), self.U.unsqueeze(1).to_broadcast([128, 4, 128]),
                            g_t[:, c * 32 + h0:c * 32 + h0 + 4].unsqueeze(2).to_broadcast([128, 4, 128]), ALU.mult)
                    pA = self.ps()
                    self.mm(pA[:, :], self.onesf, gU[:, :])
                    tokw = self.zcol[:, 1:2]
                    self.S.add("scalar", lambda e, o=E_[:], i=pA[:]: e.activation(out=o, in_=i, func=AF.Exp), reads=[pA[:]], writes=[E_[:], tokw])
                    self.tt("vector", CgT[k2][:].rearrange("p (h i) -> p h i", h=4), E_[:].rearrange("p (h i) -> p h i", h=4),
                            BCT[:, 4 + g, cs].unsqueeze(1).to_broadcast([128, 4, 128]), ALU.mult)
                    for u in range(4):
                        us = slice(u * 128, (u + 1) * 128)
                        gc = gcum[:, c * 32 + h0 + u:c * 32 + h0 + u + 1]
                        self.S.add("vector", lambda e, o=dtm[:, us], i=pA[:, us], g_=gc: e.tensor_scalar(out=o, in0=i, scalar1=g_, scalar2=None, op0=ALU.subtract),
                                   reads=[pA[:, us], gc, tokw], writes=[dtm[:, us]])
                    self.ts("vector", dtm[:], dtm[:], 0.0, ALU.min)
                    self.act(dtm[:], dtm[:], AF.Exp)
                    self.tt("gpsimd", SD[k2][:].rearrange("p (h i) -> p h i", h=4), dtm[:].rearrange("p (h i) -> p h i", h=4),
                            scM[:, g * 128:(g + 1) * 128].unsqueeze(1).to_broadcast([128, 4, 128]), ALU.mult)
                    if k2 == 0:
                        py = self.ps()
                    for pr in range(2):
                        slot = k2 * 2 + pr
                        reg = py[:, slot * 128:(slot + 1) * 128]
                        for hh in range(2):
                            u = pr * 2 + hh
                            h = h0 + u
                            us = slice(u * 128, (u + 1) * 128)
                            self.mm(reg, Vz[:, h, :], SD[k2][:, us], start=(hh == 0), stop=False)
                            self.mm(reg, Sbz[:, h, :], CgT[k2][:, us], start=False, stop=(hh == 1))
                    if k2 == 1:
                        for slot in range(4):
                            pair = (quad - 1) * 2 + slot
                            dcol = PK[f"sD{li}"] + pair
                            self.stt("vector", y_sb[:, pair, :], xsT[:, pair, cs], pk[:, dcol:dcol + 1], py[:, slot * 128:(slot + 1) * 128],
                                     ALU.mult, ALU.add)
                if self.chk(25):
                    return
                self.tt("vector", y_sb[:], y_sb[:], szT[:, :, cs], ALU.mult)
                self.act(ysq[:], y_sb[:], AF.Square)
                pn = self.ps()
                for g in range(4):
                    for q in range(4):
                        self.mm(pn[:, g * 128:(g + 1) * 128], self.onesb[:], ysq[:, g * 4 + q, :], start=(q == 0), stop=(q == 3))
                self.act(rg[:], pn[:], AF.Sqrt, bias=EPS, scale=1.0 / 512)
                self.recip(rg[:], rg[:])
                self.tt("vector", ynT[:, :, cs].rearrange("p (g q) i -> p g q i", g=4), y_sb[:].rearrange("p (g q) i -> p g q i", g=4),
                        rg[:].rearrange("p (g i) -> p g i", g=4).unsqueeze(2).to_broadcast([128, 4, 4, 128]), ALU.mult)
                for g in range(4):
                    gs = slice(g * 512, (g + 1) * 512)
                    pU = self.ps()
                    self.mm(pU[:], B_tok[:, g * 128:(g + 1) * 128], vp[:, gs])
                    self.tt("vector", S_[:, gs].rearrange("p (h d) -> p h d", h=8), S_[:, gs].rearrange("p (h d) -> p h d", h=8),
                            egend[:, c * 32 + g * 8:c * 32 + g * 8 + 8].unsqueeze(2).to_broadcast([128, 8, 64]), ALU.mult)
                    self.tt("vector", S_[:, gs], S_[:, gs], pU[:], ALU.add)
                    sv = S_[:, gs].rearrange("p (q a d) -> p q a d", q=4, a=2)
                    for hh in range(2):
                        self.cp("scalar", Sbzv[:, g * 4:(g + 1) * 4, hh, hh, :], sv[:, :, hh, :])
            if self.chk(26):
                return
            def ld(dc):
                wb_ = wo[dc % 2][:].rearrange("p (k d) -> p k d", k=16)
                self.dma("sync", wb_, self.wosd[dc], f"wo{dc % 2}")
                self.dma("sync", xpc[dc % 3][:], dr["xs"][:, dc, t0:t0 + BLK], f"xpl{dc % 3}")
            ld(0)
            ld(1)
            for dc in range(8):
                wb = wo[dc % 2][:].rearrange("p (k d) -> p k d", k=16)
                xp = xpc[dc % 3]
                p = self.ps()
                for kc in range(16):
                    self.mm(p[:, :BLK], wb[:, kc, :], ynT[:, kc, :], start=(kc == 0), stop=(kc == 15))
                self.tt("vector", xp[:], p[:, :BLK], xp[:], ALU.add)
                if dc + 2 < 8:
                    ld(dc + 2)
                self.dma("sync", dr["xs"][:, dc, t0:t0 + BLK], xp[:], f"xps{dc % 3}")
            if blk == nblk - 1:
                for cb in range(6):
                    p = self.ps()
                    for kc in range(8):
                        self.mm(p[0:3, :], hT[:, kc, BLK - 3:BLK], wi[:, kc, 2048 + cb * 512:2048 + (cb + 1) * 512], start=(kc == 0), stop=(kc == 7))
                    c3 = (gU, E_, dtm)[cb % 3]
                    self.cp("vector", c3[0:3, :], p[0:3, :])
                    self.dma("sync", dr["p_ssm_conv"][li, :, cb * 512:(cb + 1) * 512], c3[0:3, :], "pst")
        for blk in range(nblk if not self.chk(0) else 1):
            body(blk)

    def ssd_sample(self, li, layer, bs, wi, wo, Sbufs):
        dr, cst, pk = self.dr, self.cst, self.pk
        sb = lambda n, s, d: self.sb(bs, n, s, d)
        xbs, prj = self.sample_common(bs, layer, wi, SSM_IN)
        id16f = cst[:, C_ID16:C_ID16 + 256].rearrange("p (a b) -> p a b", a=16)
        ident16 = self.ident[0:NS, 0:NS]
        cw = sb("cw", [NS, 4, 512], F32)
        cbuf = sb("cbuf", [NS, 3, 512], F32)
        cbv = sb("cbv", [NS, 512], F32)
        ctm = sb("ctm", [NS, 512], F32)
        t1 = sb("st1", [NS, 512], F32)
        xbc = sb("xbc", [NS, 3072], F32)
        vv = sb("vv", [NS, 2048], F32)
        y_ = sb("ys", [NS, 2048], F32)
        sm = sb("sms", [NS, 256], F32)
        egm = sb("egm", [NS, NS, 32], F32)
        egB = sb("egB", [128, 512], F32)
        CTs = sb("CTs", [128, 4, NS], F32)
        CTm = sb("CTm", [128, 4, NS, NS], F32)
        Bmb = [sb("Bmb0", [NS, 512], F32)] * 2
        for pc in range(6):
            c0 = pc * 512
            self.dma("sync", cw[:], dr["ssm_conv_w"][li, :, c0:c0 + 512].partition_broadcast(NS), "cwld")
            self.dma("sync", cbv[:], dr["ssm_conv_b"][li, c0:c0 + 512].partition_broadcast(NS), "cwld")
            self.dma("sync", cbuf[:], dr["state_ssm_conv"][li, :, :, c0:c0 + 512], "cbld")
            self.tt("vector", ctm[:], prj[:, 2048 + c0:2048 + c0 + 512], cw[:, 3, :], ALU.mult)
            self.tt("vector", ctm[:], ctm[:], cbv[:], ALU.add)
            for tp in range(3):
                self.tt("gpsimd", t1[:], cbuf[:, tp, :], cw[:, tp, :], ALU.mult)
                self.tt("vector", ctm[:], ctm[:], t1[:], ALU.add)
            self.act(xbc[:, c0:c0 + 512], ctm[:], AF.Silu)
            self.dma("sync", dr["s_ssm_conv"][li, :, 0:2, c0:c0 + 512], cbuf[:, 1:3, :], "cvst")
        self.dma("sync", dr["s_ssm_conv"][li, :, 2, :], prj[:, 2048:5120], "cvst")
        dtr = sm[:, 0:32]
        tmp = sm[:, 32:64]
        dt_ = sm[:, 64:96]
        g_ = sm[:, 96:128]
        eg = sm[:, 128:160]
        ss = sm[:, 160:164]
        self.tt("vector", dtr, prj[:, 5120:5152], pk[0:NS, PK[f"sdtb{li}"]:PK[f"sdtb{li}"] + 32], ALU.add)
        self.softplus16(dt_, dtr, tmp)
        self.tt("vector", g_, dt_, self.negA[0:NS, 8 + li * 32:8 + (li + 1) * 32], ALU.mult)
        self.act(eg, g_, AF.Exp)
        xs3 = xbc[:, 0:2048].rearrange("p (h d) -> p h d", h=32)
        self.tt("vector", vv[:].rearrange("p (h d) -> p h d", h=32), xs3, dt_.unsqueeze(2).to_broadcast([NS, 32, 64]), ALU.mult)
        p = self.ps()
        for g in range(4):
            self.tr(p[:, g * NS:(g + 1) * NS], xbc[:, 2560 + g * 128:2560 + (g + 1) * 128], ident16)
        self.cp("vector", CTs[:], p[:, 0:4 * NS].rearrange("p (g t) -> p g t", g=4))
        for g in range(4):
            self.tt("vector" if g % 2 else "gpsimd", CTm[:, g, :, :], CTs[:, g, :].unsqueeze(1).to_broadcast([128, NS, NS]), id16f, ALU.mult)
        self.tt("vector", egm[:], eg.unsqueeze(1).to_broadcast([NS, NS, 32]), ident16.unsqueeze(2).to_broadcast([NS, NS, 32]), ALU.mult)
        pe = self.ps()
        self.mm(pe[:, :], self.onesf[0:NS, :], egm[:].rearrange("p a h -> p (a h)"))
        self.cp("vector", egB[:], pe[:, :])
        psy = self.psf[0:4]
        k = 0
        def ldS(b_):
            self.dma("sync", Sbufs[b_ % 3].rearrange("p (h d) -> p h d", h=32), dr["state_ssm"][li, b_].rearrange("h n d -> n h d"), f"sld{b_ % 3}")
        ldS(0)
        ldS(1)
        for b in range(NS):
            if b + 2 < NS:
                ldS(b + 2)
            Sb = Sbufs[b % 3]
            bm = Bmb[b % 2]
            self.ts("vector", bm[:], xbc[:, 2048:2560], ident16[:, b:b + 1], ALU.mult)
            self.tt("gpsimd", Sb.rearrange("p (h d) -> p h d", h=32), Sb.rearrange("p (h d) -> p h d", h=32),
                    egB[:, b * 32:(b + 1) * 32].unsqueeze(2).to_broadcast([128, 32, 64]), ALU.mult)
            for g in range(4):
                gs = slice(g * 512, (g + 1) * 512)
                pu = self.psf[4 + k % 2]
                k += 1
                self.mm(pu[:, :], bm[:, g * 128:(g + 1) * 128], vv[:, gs])
                self.tt("vector", Sb[:, gs], Sb[:, gs], pu[:, :], ALU.add)
                self.mm(psy[g][0:NS, :], CTm[:, g, b, :], Sb[:, gs], start=(b == 0), stop=(b == NS - 1))
            self.dma("sync", dr["s_ssm"][li, b].rearrange("h n d -> n h d"), Sb.rearrange("p (h d) -> p h d", h=32), f"sst{b % 3}")
        for g in range(4):
            self.cp("vector" if g % 2 else "scalar", y_[:, g * 512:(g + 1) * 512], psy[g][0:NS, :])
        y3 = y_[:].rearrange("p (h d) -> p h d", h=32)
        vv3 = vv[:].rearrange("p (h d) -> p h d", h=32)
        self.tt("gpsimd", vv3, xs3, pk[0:NS, PK[f"sDrep{li}"]:PK[f"sDrep{li}"] + 32].unsqueeze(2).to_broadcast([NS, 32, 64]), ALU.mult)
        self.tt("vector", y_[:], y_[:], vv[:], ALU.add)
        for q in range(4):
            self.act(vv[:, q * 512:(q + 1) * 512], prj[:, q * 512:(q + 1) * 512], AF.Silu)
        self.tt("vector", y_[:], y_[:], vv[:], ALU.mult)
        self.tt("gpsimd", vv[:], y_[:], y_[:], ALU.mult)
        self.red("vector", ss, vv[:].rearrange("p (g c) -> p g c", g=4))
        self.act(ss, ss, AF.Sqrt, bias=EPS, scale=1.0 / 512)
        self.recip(ss, ss)
        self.tt("vector", y_[:].rearrange("p (g c) -> p g c", g=4), y_[:].rearrange("p (g c) -> p g c", g=4),
                ss.unsqueeze(2).to_broadcast([NS, 4, 512]), ALU.mult)
        mixTs = egB[:].bitcast(BF16)[:, 0:256].rearrange("p (k t) -> p k t", k=16)
        for k0 in (0, 8):
            p = self.ps()
            for kk in range(8):
                self.tr(p[:, kk * NS:(kk + 1) * NS], y_[:, (k0 + kk) * 128:(k0 + kk + 1) * 128], ident16)
            self.cp("vector", mixTs[:, k0:k0 + 8, :], p[:, 0:8 * NS].rearrange("p (k t) -> p k t", k=8))
        po = self.ps()
        for dc in range(8):
            wb = wo[dc % 2][:].rearrange("p (k d) -> p k d", k=16)
            self.dma("sync", wb, self.wosd[dc], f"wo{dc % 2}")
            for kc in range(16):
                self.mm(po[:, dc * NS:(dc + 1) * NS], wb[:, kc, :], mixTs[:, kc, :], start=(kc == 0), stop=(kc == 15))
        self.tt("vector", xbs[:], po[:, 0:8 * NS].rearrange("p (k t) -> p k t", k=8), xbs[:], ALU.add)
        self.dma("sync", dr["xs"][:, :, SEQ:NTOK], xbs[:], "xst")

    def mlp_phase(self, layer, last):
        dr, cst, pk = self.dr, self.cst, self.pk
        with contextlib.ExitStack() as ph:
            sb = lambda n, s, d: self.sb(ph, n, s, d)
            xall = sb("xall", [128, 8, NTOK], F32)
            hall = sb("hall", [128, 8, NTOK], BF16)
            sq = sb("msq", [128, 8, 512], BF16)
            rstd = sb("mrstd", [128, 512], F32)
            w1 = [sb(f"w1_{i}", [128, 8, 512], BF16) for i in range(2)]
            w2 = [sb(f"w2_{i}", [128, 4, D], BF16) for i in range(2)]
            rl = [sb(f"rl{i}", [128, 512], BF16) for i in range(2)]
            aT = [sb(f"aT{i}", [128, 4, 512], BF16) for i in range(2)]
            for dc in range(8):
                self.dma("sync", xall[:, dc, :], dr["xs"][:, dc, :], "xall")
            tbs = [(i * 512, 512) for i in range(4)] + [(SEQ, NS)]

            def loadw(fb):
                b = fb % 2
                self.dma("gpsimd", w1[b][:], dr["mlp_w1"][layer, :, fb * 512:(fb + 1) * 512].rearrange("(kc p) f -> p kc f", p=128), f"w1_{b}")
                self.dma("gpsimd", w2[b][:], dr["mlp_w2"][layer, fb * 512:(fb + 1) * 512, :].rearrange("(fc p) d -> p fc d", p=128), f"w2_{b}")
            import os
            KM = os.environ.get("KMLP", "full")
            def prenorm(k):
                t0_, nt_ = tbs[k]
                self.norm_block(xall[:, :, t0_:t0_ + nt_], hall[:, :, t0_:t0_ + nt_], sq, rstd, nt_, PK["nmlp"] + layer * 8)
            next_norm = 0
            if KM != "load":
                loadw(0)
                prenorm(0)
                prenorm(1)
                next_norm = 2
            if last:
                yst = [sb(f"yst{i}", [128, D], F32) for i in range(2)]
                yT = [sb(f"yT{i}", [128, 8, 128], F32) for i in range(2)]
            it = 0
            nfb = {"load": 0, "norm": 0, "fb1": 1, "fb1f": 1, "fb2": 2, "fb3": 3}.get(KM, 8)
            if KM in ("load", "norm", "fb1", "fb2", "fb3"):
                last = False
            for fb in range(nfb):
                if fb + 1 < nfb:
                    loadw(fb + 1)
                b = fb % 2
                for tbi, (t0, nt) in enumerate(tbs):
                    a = aT[it % 2]
                    it += 1
                    for fc in range(4):
                        p = self.ps()
                        for kc in range(8):
                            self.mm(p[:, :nt], w1[b][:, kc, fc * 128:(fc + 1) * 128], hall[:, kc, t0:t0 + nt], start=(kc == 0), stop=(kc == 7))
                        r_ = rl[fc % 2]
                        self.act(r_[:, :nt], p[:, :nt], AF.Relu)
                        self.tt("gpsimd", a[:, fc, :nt], r_[:, :nt], r_[:, :nt], ALU.mult)
                    for dc in range(8):
                        p = self.ps()
                        for fc in range(4):
                            self.mm(p[:, :nt], w2[b][:, fc, dc * 128:(dc + 1) * 128], a[:, fc, :nt], start=(fc == 0), stop=(fc == 3))
                        self.tt("vector", xall[:, dc, t0:t0 + nt], p[:, :nt], xall[:, dc, t0:t0 + nt], ALU.add)
                    if fb == 0 and next_norm < len(tbs):
                        prenorm(next_norm)
                        next_norm += 1
                    if fb == nfb - 1:
                        if (not last) or self.debug:
                            self.dma("sync", dr["xs"][:, :, t0:t0 + nt], xall[:, :, t0:t0 + nt], f"xall_st{tbi % 2}")
                        if last and tbi >= 1:
                            pt0, pnt = tbs[tbi - 1]
                            self.norm_block_f32(xall[:, :, pt0:pt0 + pnt], sq, rstd, pnt, PK["nfin"], yT, yst, pt0)
            if last:
                pt0, pnt = tbs[-1]
                self.norm_block_f32(xall[:, :, pt0:pt0 + pnt], sq, rstd, pnt, PK["nfin"], yT, yst, pt0)

    def norm_block_f32(self, xb, sq, rstd, ntok, wcol, yT, yst, t0):
        dr = self.dr
        for dc in range(8):
            self.act(sq[:, dc, :ntok], xb[:, dc, :ntok], AF.Square)
        p = self.ps()
        for dc in range(8):
            self.mm(p[:, :ntok], self.onesb[:], sq[:, dc, :ntok], start=(dc == 0), stop=(dc == 7))
        self.act(rstd[:, :ntok], p[:, :ntok], AF.Sqrt, bias=EPS, scale=1.0 / D)
        self.recip(rstd[:, :ntok], rstd[:, :ntok])
        ntile = (ntok + 127) // 128
        for ti in range(ntile):
            n = min(128, ntok - ti * 128)
            k = (t0 // 128 + ti) % 2
            y_, ys = yT[k], yst[k]
            for dc in range(8):
                self.stt("vector" if dc % 2 else "gpsimd", y_[:, dc, :n], xb[:, dc, ti * 128:ti * 128 + n], self.pk[:, wcol + dc:wcol + dc + 1],
                         rstd[:, ti * 128:ti * 128 + n], ALU.mult, ALU.mult)
            for half in range(2):
                p = self.ps()
                for q in range(4):
                    dc = half * 4 + q
                    self.tr(p[0:n, q * 128:(q + 1) * 128], y_[:, dc, :n], self.ident)
                self.cp("vector" if half == 0 else "scalar", ys[0:n, half * 512:(half + 1) * 512], p[0:n, :])
            if t0 < SEQ:
                self.dma("sync", dr["y_prompt"][t0 + ti * 128:t0 + ti * 128 + n, :], ys[0:n, :], f"yout{k}")
            else:
                self.dma("sync", dr["y_sample"][0:n, :], ys[0:n, :], f"yout{k}")


_CACHE = {}


def _prep_inputs(inp):
    pk = make_pk(inp)
    cst = make_consts()
    rope = make_rope()
    shared = {k: np.ascontiguousarray(inp[k], dtype=np.float32) for k in
              ("w_in_hyb", "w_out_hyb", "w_in_ssm", "w_out_ssm", "mlp_w1", "mlp_w2", "gdn_conv_w", "ssm_conv_w", "ssm_conv_b")}
    maps = []
    for c in range(NCORES):
        m = dict(shared)
        m["pk"] = pk
        m["cst"] = cst
        m["rope"] = rope
        m["x_prompt"] = np.ascontiguousarray(inp["x_prompt"][c])
        m["x_sample"] = np.ascontiguousarray(inp["x_sample"][c * NS:(c + 1) * NS, 0])
        for k in ("state_gdn", "state_gdn_conv", "state_ret", "state_ssm", "state_ssm_conv"):
            m[k] = np.ascontiguousarray(inp[k][:, c * NS:(c + 1) * NS])
        maps.append(m)
    return maps


def kernel(**inp):
    if "nc" not in _CACHE:
        _CACHE["nc"] = Builder().build()
    nc = _CACHE["nc"]
    maps = _prep_inputs(inp)
    res = run_bass_kernel_spmd(nc, maps, core_ids=list(range(NCORES)))
    R = res.results
    y_prompt = np.stack([R[c]["y_prompt"] for c in range(NCORES)], 0)
    y_sample = np.concatenate([R[c]["y_sample"] for c in range(NCORES)], 0)[:, None, :]

    def pcat(name):
        return np.stack([R[c][name] for c in range(NCORES)], 1)

    def scat(name):
        return np.concatenate([R[c][name] for c in range(NCORES)], 1)
    return (y_prompt, y_sample, pcat("p_gdn"), pcat("p_gdn_conv"), pcat("p_ret"), pcat("p_ssm"), pcat("p_ssm_conv"),
            scat("s_gdn"), scat("s_gdn_conv"), scat("s_ret"), scat("s_ssm"), scat("s_ssm_conv"))
```
